# Optimizing a Trainium2 kernel written in Bass

```python
import math
import jax, jax.numpy as jnp
from jax import lax
import numpy as np

D_MODEL = 1024
BATCH = 8
SEQ = 4096
DEPTH = 2

N_A_LAYERS = DEPTH // 2
N_B_LAYERS = DEPTH - N_A_LAYERS
RMS_EPS = 1e-6
GATED_NORM_EPS = 1e-5

SSM_EXPAND = 2
SSM_D_INNER = SSM_EXPAND * D_MODEL
SSM_HEAD_DIM = 64
SSM_N_HEADS = SSM_D_INNER // SSM_HEAD_DIM
SSM_N_GROUPS = 8
SSM_D_STATE = 128
SSM_CONV = 4
SSM_CHUNK = 128
SSM_CONV_DIM = SSM_D_INNER + 2 * SSM_N_GROUPS * SSM_D_STATE
SSM_IN_DIM = SSM_D_INNER + SSM_CONV_DIM + SSM_N_HEADS

ATT_PATTERNS = ((128, 1), (512, 4), (2048, 16))
ATT_N_GROUPS = len(ATT_PATTERNS)
ATT_HEAD_DIM = 128
ATT_HEADS_PER_GROUP = 8
ATT_KV_HEADS_PER_GROUP = 2
ATT_Q_DIM = ATT_N_GROUPS * ATT_HEADS_PER_GROUP * ATT_HEAD_DIM
ATT_KV_DIM = ATT_N_GROUPS * ATT_KV_HEADS_PER_GROUP * ATT_HEAD_DIM
ATT_OUT_DIM = ATT_HEADS_PER_GROUP * ATT_HEAD_DIM
ROPE_DIM = ATT_HEAD_DIM // 4
ROPE_THETA = 500000.0

FFN_DIM = 2816
FFN_CONV = 3

kernel_name = 'yoco_mamba2_dilated_attn_hybrid'


def _rms_norm(x, w, eps=RMS_EPS):
    xf = x.astype(jnp.float32)
    y = xf * lax.rsqrt(jnp.mean(xf * xf, axis=-1, keepdims=True) + eps)
    return (y * w.astype(jnp.float32)).astype(x.dtype)


def _causal_depthwise_conv(x, w):
    width, ch = w.shape
    return lax.conv_general_dilated(
        x, w[:, None, :].astype(x.dtype), window_strides=(1,),
        padding=((width - 1, 0),), dimension_numbers=('NWC', 'WIO', 'NWC'),
        feature_group_count=ch)


def _partial_rotary(x, positions):
    half = ROPE_DIM // 2
    inv_freq = jnp.power(jnp.float32(ROPE_THETA), -jnp.arange(0, ROPE_DIM, 2, dtype=jnp.float32) / ROPE_DIM)
    ang = positions.astype(jnp.float32)[:, None] * inv_freq[None, :]
    cos = jnp.cos(ang)[None, :, None, :]
    sin = jnp.sin(ang)[None, :, None, :]
    xf = x.astype(jnp.float32)
    x1, x2, rest = xf[..., :half], xf[..., half:ROPE_DIM], xf[..., ROPE_DIM:]
    out = jnp.concatenate([x1 * cos - x2 * sin, x2 * cos + x1 * sin, rest], axis=-1)
    return out.astype(x.dtype)


def _ssd_chunked(x, dt, a, b_in, c_in):
    bsz, seq, nh, hp = x.shape
    ng, ns = b_in.shape[2], b_in.shape[3]
    hg = nh // ng
    nc, L = seq // SSM_CHUNK, SSM_CHUNK
    xc = (x.astype(jnp.float32) * dt[..., None]).reshape(bsz, nc, L, ng, hg, hp)
    adt = (dt * a).reshape(bsz, nc, L, ng, hg).transpose(0, 1, 3, 4, 2)
    a_cs = jnp.cumsum(adt, axis=-1)
    bc = b_in.astype(jnp.float32).reshape(bsz, nc, L, ng, ns)
    cc = c_in.astype(jnp.float32).reshape(bsz, nc, L, ng, ns)
    tril = jnp.tril(jnp.ones((L, L), dtype=bool))
    seg = a_cs[..., :, None] - a_cs[..., None, :]
    decay = jnp.exp(jnp.where(tril, seg, -jnp.inf))
    cb = jnp.einsum('bclgn,bcsgn->bcgls', cc, bc)
    y_diag = jnp.einsum('bcgls,bcghls,bcsghp->bclghp', cb, decay, xc)
    decay_states = jnp.exp(a_cs[..., -1:] - a_cs)
    states = jnp.einsum('bclgn,bcghl,bclghp->bcghpn', bc, decay_states, xc)
    chunk_decay = jnp.exp(a_cs[..., -1])

    def step(h, inp):
        st, dec = inp
        return h * dec[..., None, None] + st, h

    h0 = jnp.zeros((bsz, ng, hg, hp, ns), jnp.float32)
    _, prev = lax.scan(step, h0, (jnp.moveaxis(states, 1, 0), jnp.moveaxis(chunk_decay, 1, 0)))
    prev = jnp.moveaxis(prev, 0, 1)
    y_off = jnp.einsum('bclgn,bcghpn,bcghl->bclghp', cc, prev, jnp.exp(a_cs))
    return (y_diag + y_off).reshape(bsz, seq, nh, hp)


def _mamba2_mixer(h, w_in, conv_w, conv_b, dt_bias, a_log, d_skip, norm_w, w_out):
    bsz, seq, _ = h.shape
    zxbcdt = h @ w_in.astype(h.dtype)
    z = zxbcdt[..., :SSM_D_INNER]
    xbc = zxbcdt[..., SSM_D_INNER:SSM_D_INNER + SSM_CONV_DIM]
    dt_raw = zxbcdt[..., SSM_D_INNER + SSM_CONV_DIM:]
    xbc = jax.nn.silu(_causal_depthwise_conv(xbc, conv_w) + conv_b.astype(h.dtype))
    gn = SSM_N_GROUPS * SSM_D_STATE
    xs = xbc[..., :SSM_D_INNER].reshape(bsz, seq, SSM_N_HEADS, SSM_HEAD_DIM)
    b_in = xbc[..., SSM_D_INNER:SSM_D_INNER + gn].reshape(bsz, seq, SSM_N_GROUPS, SSM_D_STATE)
    c_in = xbc[..., SSM_D_INNER + gn:].reshape(bsz, seq, SSM_N_GROUPS, SSM_D_STATE)
    dt = jax.nn.softplus(dt_raw.astype(jnp.float32) + dt_bias.astype(jnp.float32))
    a = -jnp.exp(a_log.astype(jnp.float32))
    y = _ssd_chunked(xs, dt, a, b_in, c_in)
    y = y + xs.astype(jnp.float32) * d_skip.astype(jnp.float32)[:, None]
    y = y.reshape(bsz, seq, SSM_D_INNER) * jax.nn.silu(z.astype(jnp.float32))
    yg = y.reshape(bsz, seq, SSM_N_GROUPS, SSM_D_INNER // SSM_N_GROUPS)
    yg = yg * lax.rsqrt(jnp.mean(yg * yg, axis=-1, keepdims=True) + GATED_NORM_EPS)
    y = yg.reshape(bsz, seq, SSM_D_INNER) * norm_w.astype(jnp.float32)
    return y.astype(h.dtype) @ w_out.astype(h.dtype)


def _shared_kv(x, kv_norm, w_kv, positions):
    bsz, seq, _ = x.shape
    kv = (_rms_norm(x, kv_norm) @ w_kv.astype(x.dtype)).reshape(
        bsz, seq, 2, ATT_N_GROUPS * ATT_KV_HEADS_PER_GROUP, ATT_HEAD_DIM)
    k = _partial_rotary(kv[:, :, 0], positions).reshape(bsz, seq, ATT_N_GROUPS, ATT_KV_HEADS_PER_GROUP, ATT_HEAD_DIM)
    v = kv[:, :, 1].reshape(bsz, seq, ATT_N_GROUPS, ATT_KV_HEADS_PER_GROUP, ATT_HEAD_DIM)
    return k, v


def _dilated_group_attention(q, k, v, window, dilation):
    bsz, seq, n_heads, hd = q.shape
    n_kv = k.shape[2]
    rep = n_heads // n_kv
    blk = window // dilation
    n_sub = seq // dilation
    n_blk = -(-n_sub // blk)
    n_pad = n_blk * blk

    def to_blocks(t):
        t = jnp.moveaxis(t.reshape(bsz, n_sub, dilation, *t.shape[2:]), 2, 1)
        pad = [(0, 0)] * t.ndim
        pad[2] = (0, n_pad - n_sub)
        t = jnp.pad(t, pad)
        return t.reshape(bsz, dilation, n_blk, blk, *t.shape[3:])

    def with_prev(t):
        prev = jnp.pad(t[:, :, :-1], ((0, 0), (0, 0), (1, 0), (0, 0), (0, 0), (0, 0)))
        return jnp.concatenate([prev, t], axis=3)

    qb = to_blocks(q.astype(jnp.float32)).reshape(bsz, dilation, n_blk, blk, n_kv, rep, hd)
    kb = with_prev(to_blocks(k.astype(jnp.float32)))
    vb = with_prev(to_blocks(v.astype(jnp.float32)))
    scores = jnp.einsum('brnqhgd,brnshd->brnhgqs', qb, kb) * (hd ** -0.5)
    q_idx = jnp.arange(blk)[:, None]
    s_idx = jnp.arange(2 * blk)[None, :]
    rel = q_idx + blk - s_idx
    band = (rel >= 0) & (rel <= blk)
    has_prev = (jnp.arange(n_blk) > 0)[:, None, None] | (s_idx >= blk)[None]
    mask = band[None] & has_prev
    scores = jnp.where(mask[:, None, None], scores, -jnp.inf)
    m = jnp.max(scores, axis=-1, keepdims=True)
    p = jnp.exp(scores - m)
    denom = jnp.sum(p, axis=-1)
    lse = jnp.transpose(m[..., 0] + jnp.log(denom), (0, 1, 2, 5, 3, 4))
    out = jnp.einsum('brnhgqs,brnshd->brnqhgd', p, vb) / jnp.transpose(denom, (0, 1, 2, 5, 3, 4))[..., None]

    def from_blocks(t):
        t = t.reshape(bsz, dilation, n_pad, n_heads, *t.shape[6:])[:, :, :n_sub]
        return jnp.moveaxis(t, 1, 2).reshape(bsz, seq, n_heads, *t.shape[4:])

    return from_blocks(out), from_blocks(lse)


def _dilated_mixer(h, k_sh, v_sh, w_q, w_o, positions):
    bsz, seq, _ = h.shape
    q = (h @ w_q.astype(h.dtype)).reshape(bsz, seq, ATT_N_GROUPS * ATT_HEADS_PER_GROUP, ATT_HEAD_DIM)
    q = _partial_rotary(q, positions).reshape(bsz, seq, ATT_N_GROUPS, ATT_HEADS_PER_GROUP, ATT_HEAD_DIM)
    outs, lses = [], []
    for g, (window, dilation) in enumerate(ATT_PATTERNS):
        o_g, lse_g = _dilated_group_attention(q[:, :, g], k_sh[:, :, g], v_sh[:, :, g], window, dilation)
        outs.append(o_g)
        lses.append(lse_g)
    wts = jax.nn.softmax(jnp.stack(lses, axis=0), axis=0)
    o = jnp.einsum('gbsh,gbshd->bshd', wts, jnp.stack(outs, axis=0))
    return o.reshape(bsz, seq, ATT_OUT_DIM).astype(h.dtype) @ w_o.astype(h.dtype)


def _conv_ffn(h, w_up, conv_w, w_down):
    u = _causal_depthwise_conv(h @ w_up.astype(h.dtype), conv_w)
    gate, val = u[..., :FFN_DIM], u[..., FFN_DIM:]
    return (jax.nn.silu(gate) * val) @ w_down.astype(h.dtype)


def setup_inputs(seed: int = 0) -> dict:
    key = jax.random.key(seed)
    ks = jax.random.split(key, 24)
    f32 = jnp.float32
    out_scale = (2.0 * DEPTH) ** -0.5

    def nrm(k, shape, scale):
        return jax.random.normal(k, shape, f32) * scale

    x = nrm(ks[0], (BATCH, SEQ, D_MODEL), 1.0)
    a_norm = 1.0 + nrm(ks[1], (N_A_LAYERS, D_MODEL), 0.02)
    ssm_w_in = nrm(ks[2], (N_A_LAYERS, D_MODEL, SSM_IN_DIM), D_MODEL ** -0.5)
    ssm_conv_w = nrm(ks[3], (N_A_LAYERS, SSM_CONV, SSM_CONV_DIM), SSM_CONV ** -0.5)
    ssm_conv_b = nrm(ks[4], (N_A_LAYERS, SSM_CONV_DIM), 0.02)
    dt0 = jnp.exp(jax.random.uniform(ks[5], (N_A_LAYERS, SSM_N_HEADS), f32, math.log(1e-3), math.log(1e-1)))
    ssm_dt_bias = dt0 + jnp.log(-jnp.expm1(-dt0))
    ssm_a_log = jnp.log(jax.random.uniform(ks[6], (N_A_LAYERS, SSM_N_HEADS), f32, 1.0, 16.0))
    ssm_d = 1.0 + nrm(ks[7], (N_A_LAYERS, SSM_N_HEADS), 0.02)
    ssm_norm = 1.0 + nrm(ks[8], (N_A_LAYERS, SSM_D_INNER), 0.02)
    ssm_w_out = nrm(ks[9], (N_A_LAYERS, SSM_D_INNER, D_MODEL), SSM_D_INNER ** -0.5 * out_scale)
    kv_norm = 1.0 + nrm(ks[10], (D_MODEL,), 0.02)
    w_kv = nrm(ks[11], (D_MODEL, 2 * ATT_KV_DIM), D_MODEL ** -0.5)
    b_norm = 1.0 + nrm(ks[12], (N_B_LAYERS, D_MODEL), 0.02)
    att_w_q = nrm(ks[13], (N_B_LAYERS, D_MODEL, ATT_Q_DIM), D_MODEL ** -0.5)
    att_w_o = nrm(ks[14], (N_B_LAYERS, ATT_OUT_DIM, D_MODEL), ATT_OUT_DIM ** -0.5 * out_scale)
    ffn_norm = 1.0 + nrm(ks[15], (DEPTH, D_MODEL), 0.02)
    ffn_w_up = nrm(ks[16], (DEPTH, D_MODEL, 2 * FFN_DIM), D_MODEL ** -0.5)
    ffn_conv_w = nrm(ks[17], (DEPTH, FFN_CONV, 2 * FFN_DIM), FFN_CONV ** -0.5)
    ffn_w_down = nrm(ks[18], (DEPTH, FFN_DIM, D_MODEL), FFN_DIM ** -0.5 * out_scale)
    final_norm = 1.0 + nrm(ks[19], (D_MODEL,), 0.02)
    return {'x': x, 'a_norm': a_norm, 'ssm_w_in': ssm_w_in, 'ssm_conv_w': ssm_conv_w,
            'ssm_conv_b': ssm_conv_b, 'ssm_dt_bias': ssm_dt_bias, 'ssm_a_log': ssm_a_log,
            'ssm_d': ssm_d, 'ssm_norm': ssm_norm, 'ssm_w_out': ssm_w_out, 'kv_norm': kv_norm,
            'w_kv': w_kv, 'b_norm': b_norm, 'att_w_q': att_w_q, 'att_w_o': att_w_o,
            'ffn_norm': ffn_norm, 'ffn_w_up': ffn_w_up, 'ffn_conv_w': ffn_conv_w,
            'ffn_w_down': ffn_w_down, 'final_norm': final_norm}


def reference(x, a_norm, ssm_w_in, ssm_conv_w, ssm_conv_b, ssm_dt_bias, ssm_a_log, ssm_d,
              ssm_norm, ssm_w_out, kv_norm, w_kv, b_norm, att_w_q, att_w_o,
              ffn_norm, ffn_w_up, ffn_conv_w, ffn_w_down, final_norm):
    positions = jnp.arange(x.shape[1], dtype=jnp.int32)
    k_sh, v_sh = None, None
    for layer in range(DEPTH):
        if layer < N_A_LAYERS:
            i = layer
            x = x + _mamba2_mixer(_rms_norm(x, a_norm[i]), ssm_w_in[i], ssm_conv_w[i], ssm_conv_b[i],
                                  ssm_dt_bias[i], ssm_a_log[i], ssm_d[i], ssm_norm[i], ssm_w_out[i])
        else:
            if layer == N_A_LAYERS:
                k_sh, v_sh = _shared_kv(x, kv_norm, w_kv, positions)
            j = layer - N_A_LAYERS
            x = x + _dilated_mixer(_rms_norm(x, b_norm[j]), k_sh, v_sh, att_w_q[j], att_w_o[j], positions)
        x = x + _conv_ffn(_rms_norm(x, ffn_norm[layer]), ffn_w_up[layer], ffn_conv_w[layer], ffn_w_down[layer])
    return _rms_norm(x, final_norm)
```

```python
import contextlib
import numpy as np
import concourse.bass as bass
import concourse.mybir as mybir
from concourse.bass_utils import run_bass_kernel_spmd

F32 = mybir.dt.float32
BF16 = mybir.dt.bfloat16
AF = mybir.ActivationFunctionType
ALU = mybir.AluOpType

S_LEN = 4096
D = 1024
NCH = S_LEN // 128
FF = 2816
NF = FF // 128
D_IN = 2048
XBC = 4096
IN_DIM = 6176
NEG = -30000.0


class Buf:
    __slots__ = ("ap", "lw", "rd", "name", "chan")

    def __init__(self, ap, name=""):
        self.ap = ap
        self.lw = None
        self.rd = {}
        self.name = name
        self.chan = None


class Eng:
    def __init__(self, name, q, sem):
        self.name = name
        self.q = q
        self.sem = sem
        self.count = 0
        self.seen = {}


class Sched:
    def __init__(self, nc, es):
        self.nc = nc
        self.es = es
        mk = lambda n, q: Eng(n, q, es.enter_context(nc.semaphore("sem_" + n)))
        self.pe = mk("pe", nc.tensor)
        self.act = mk("act", nc.scalar)
        self.dve = mk("dve", nc.vector)
        self.pool = mk("pool", nc.gpsimd)
        self.sp = mk("sp", nc.sync)
        self.engs = [self.pe, self.act, self.dve, self.pool, self.sp]
        self.chans = []
        self.free_chans = []

    def new_chan(self, name):
        if self.free_chans:
            c = self.free_chans.pop()
            return c
        c = Eng("ch%d_%s" % (len(self.chans), name), None, self.es.enter_context(self.nc.semaphore("semch%d" % len(self.chans))))
        self.chans.append(c)
        return c

    def _waits(self, eng, reads, writes, pe_acc):
        waits = {}

        def need(dep):
            e, c = dep
            if eng.seen.get(e.name, 0) < c:
                if waits.get(e.name, (None, 0))[1] < c:
                    waits[e.name] = (e, c)

        for b in reads:
            if b.lw is not None:
                need(b.lw)
        for b in writes:
            if b.lw is not None and not (pe_acc and b.lw[0] is eng):
                need(b.lw)
            for dep in b.rd.values():
                need(dep)
        for e, c in waits.values():
            eng.q.wait_ge(e.sem, c)
            eng.seen[e.name] = c

    def emit(self, eng, fn, reads=(), writes=(), pe_acc=False, sig=True):
        self._waits(eng, reads, writes, pe_acc)
        ins = fn()
        if sig:
            eng.count += 1
            ins.then_inc(eng.sem, 1)
            cnt = eng.count
        else:
            cnt = eng.count + 1
        for b in reads:
            b.rd[eng.name] = (eng, cnt)
        for b in writes:
            b.lw = (eng, cnt)
            b.rd = {}
        return ins

    def dma(self, eng, out_ap, in_ap, reads=(), writes=(), chan_buf=None):
        if chan_buf.chan is None:
            chan_buf.chan = self.new_chan(chan_buf.name)
        ch = chan_buf.chan
        self._waits(eng, reads, writes, False)
        ins = eng.q.dma_start(out=out_ap, in_=in_ap)
        ch.count += 16
        ins.then_inc(ch.sem, 16)
        for b in reads:
            b.rd[ch.name] = (ch, ch.count)
        for b in writes:
            b.lw = (ch, ch.count)
            b.rd = {}
        return ins

    def barrier(self):
        allsrc = self.engs + self.chans
        for e in self.engs:
            for o in allsrc:
                if o is e or o.count == 0:
                    continue
                if e.seen.get(o.name, 0) < o.count:
                    e.q.wait_ge(o.sem, o.count)
                    e.seen[o.name] = o.count


class Ctx:
    def __init__(self, nc, es):
        self.nc = nc
        self.S = Sched(nc, es)
        self.n = 0
        self.rr = 0

    def sb(self, es, name, shape, dt):
        self.n += 1
        t = es.enter_context(self.nc.sbuf_tensor("%s_%d" % (name, self.n), shape, dt))
        return Buf(t, name)

    def ps(self, es, name, shape, dt):
        self.n += 1
        t = es.enter_context(self.nc.psum_tensor("%s_%d" % (name, self.n), shape, dt))
        return Buf(t, name)


def bc(ap, shape):
    return ap.broadcast_to(shape)


def setup_consts(C, es):
    nc, S = C.nc, C.S
    k = {}
    identf = C.sb(es, "identf", [128, 128], F32)
    ident = C.sb(es, "ident", [128, 128], BF16)
    S.emit(S.pool, lambda: nc.gpsimd.memset(identf.ap[:], 1.0), writes=[identf])
    S.emit(S.pool, lambda: nc.gpsimd.affine_select(out=identf.ap[:], in_=identf.ap[:], pattern=[[-1, 128]], compare_op=ALU.is_equal, fill=0.0, base=0, channel_multiplier=1), reads=[identf], writes=[identf])
    S.emit(S.dve, lambda: nc.vector.tensor_copy(out=ident.ap[:], in_=identf.ap[:]), reads=[identf], writes=[ident])
    mhalf = C.sb(es, "mhalf", [128, 8], F32)
    S.emit(S.pool, lambda: nc.gpsimd.memset(mhalf.ap[:], -0.5), writes=[mhalf])
    nmf = C.sb(es, "nmf", [128, 128], F32)
    nmc = C.sb(es, "nmc", [128, 512], BF16)
    nmp = C.sb(es, "nmp", [128, 512], BF16)
    S.emit(S.pool, lambda: nc.gpsimd.memset(nmf.ap[:], 0.0), writes=[nmf])
    S.emit(S.pool, lambda: nc.gpsimd.affine_select(out=nmf.ap[:], in_=nmf.ap[:], pattern=[[1, 128]], compare_op=ALU.is_ge, fill=NEG, base=0, channel_multiplier=-1), reads=[nmf], writes=[nmf])
    S.emit(S.dve, lambda: nc.vector.tensor_copy(out=nmc.ap[:].rearrange("p (h l) -> p h l", h=4), in_=bc(nmf.ap[:].unsqueeze(1), [128, 4, 128])), reads=[nmf], writes=[nmc])
    S.emit(S.pool, lambda: nc.gpsimd.memset(nmf.ap[:], 0.0), reads=[nmf], writes=[nmf])
    S.emit(S.pool, lambda: nc.gpsimd.affine_select(out=nmf.ap[:], in_=nmf.ap[:], pattern=[[-1, 128]], compare_op=ALU.is_ge, fill=NEG, base=0, channel_multiplier=1), reads=[nmf], writes=[nmf])
    S.emit(S.dve, lambda: nc.vector.tensor_copy(out=nmp.ap[:].rearrange("p (h l) -> p h l", h=4), in_=bc(nmf.ap[:].unsqueeze(1), [128, 4, 128])), reads=[nmf], writes=[nmp])
    k.update(identf=identf, ident=ident, mhalf=mhalf, nmc=nmc, nmp=nmp)
    return k


def prep_weight(C, es_stage, dst, src, nk, ncols, gain=None, col0=0):
    nc, S = C.nc, C.S
    stg = C.stg
    CW = 2048
    for kk in range(nk):
        for c0 in range(0, ncols, CW):
            cw = min(CW, ncols - c0)
            st = stg[C.rr % len(stg)]
            S.dma(S.sp, st.ap[:, 0:cw], src[kk * 128:(kk + 1) * 128, c0:c0 + cw], writes=[st], chan_buf=st)
            o = dst.ap[:, kk, col0 + c0:col0 + c0 + cw]
            which = C.rr % 3
            C.rr += 1
            rds = [st] + ([gain] if gain is not None else [])
            if gain is None:
                if which == 0:
                    S.emit(S.act, lambda: nc.scalar.copy(out=o, in_=st.ap[:, 0:cw]), reads=rds, writes=[dst])
                elif which == 1:
                    S.emit(S.dve, lambda: nc.vector.tensor_copy(out=o, in_=st.ap[:, 0:cw]), reads=rds, writes=[dst])
                else:
                    S.emit(S.pool, lambda: nc.gpsimd.tensor_copy(out=o, in_=st.ap[:, 0:cw]), reads=rds, writes=[dst])
            else:
                g = gain.ap[:, kk:kk + 1]
                if which == 0:
                    S.emit(S.act, lambda: nc.scalar.mul(out=o, in_=st.ap[:, 0:cw], mul=g), reads=rds, writes=[dst])
                elif which == 1:
                    S.emit(S.dve, lambda: nc.vector.tensor_scalar(out=o, in0=st.ap[:, 0:cw], scalar1=g, scalar2=None, op0=ALU.mult), reads=rds, writes=[dst])
                else:
                    S.emit(S.pool, lambda: nc.gpsimd.tensor_scalar(out=o, in0=st.ap[:, 0:cw], scalar1=g, scalar2=None, op0=ALU.mult), reads=rds, writes=[dst])


def load_cols(C, dst, src_vec, nk):
    S = C.S
    with C.nc.allow_non_contiguous_dma(reason="small param vector"):
        S.dma(S.sp, dst.ap[:, 0:nk], src_vec.rearrange("(k p) -> p k", p=128), writes=[dst], chan_buf=dst)


def rms_rstd(C, x_ap, xbuf, junk, ss, rstd, K, n_feat, eps):
    nc, S = C.nc, C.S
    S.emit(S.act, lambda: nc.scalar.activation(out=junk.ap[:, 0:n_feat], in_=x_ap, func=AF.Square, accum_out=ss.ap[:, 0:1]), reads=[xbuf], writes=[junk, ss])
    S.emit(S.dve, lambda: nc.vector.tensor_scalar(out=ss.ap[:, 0:1], in0=ss.ap[:, 0:1], scalar1=1.0 / n_feat, scalar2=eps, op0=ALU.mult, op1=ALU.add), reads=[ss], writes=[ss])
    S.emit(S.pool, lambda: nc.gpsimd.tensor_tensor(out=rstd.ap[:, 0:1], in0=ss.ap[:, 0:1], in1=K["mhalf"].ap[:, 0:1], op=ALU.pow), reads=[ss, K["mhalf"]], writes=[rstd])


def norm_transpose(C, K, x_ap, xbuf, junk, ss, rstd, xn, ptr, xnT_ap, xnT):
    nc, S = C.nc, C.S
    rms_rstd(C, x_ap, xbuf, junk, ss, rstd, K, D, 1e-6)
    S.emit(S.act, lambda: nc.scalar.mul(out=xn.ap[:, 0:D], in_=x_ap, mul=rstd.ap[:, 0:1]), reads=[xbuf, rstd], writes=[xn])
    for kk in range(8):
        S.emit(S.pe, lambda: nc.tensor.transpose(out=ptr.ap[:, kk * 128:(kk + 1) * 128], in_=xn.ap[:, kk * 128:(kk + 1) * 128], identity=K["ident"].ap[:]),
               reads=[xn, K["ident"]], writes=[ptr], pe_acc=True, sig=(kk == 7))
    S.emit(S.dve, lambda: nc.vector.tensor_copy(out=xnT_ap, in_=ptr.ap[:, 0:1024].rearrange("p (k t) -> p k t", k=8)), reads=[ptr], writes=[xnT])


def phase_ffn(C, K, layer, src, dst, w_up, conv_w, w_dn, gain_vec, final_gain=None):
    nc, S = C.nc, C.S
    T = 256
    NT = S_LEN // T
    with contextlib.ExitStack() as es:
        wup = C.sb(es, "wup", [128, 8, 2 * FF], BF16)
        wdn = C.sb(es, "wdn", [128, NF, D], BF16)
        gain = C.sb(es, "gain", [128, 8], F32)
        cw = C.sb(es, "cw", [128, 2 * NF, 3], F32)
        cbuf = C.sb(es, "cbuf", [128, 2 * NF, 2], F32)
        load_cols(C, gain, gain_vec, 8)
        with nc.allow_non_contiguous_dma(reason="conv weights"):
            for kk in range(3):
                S.dma(S.sp, cw.ap[:, :, kk], conv_w[kk].rearrange("(j p) -> p j", p=128), writes=[cw], chan_buf=cw)
        S.emit(S.pool, lambda: nc.gpsimd.memset(cbuf.ap[:], 0.0), writes=[cbuf])
        with contextlib.ExitStack() as es2:
            C.stg = [C.sb(es2, "stg", [128, 2048], F32) for _ in range(3)]
            prep_weight(C, es2, wup, w_up, 8, 2 * FF, gain)
            prep_weight(C, es2, wdn, w_dn, NF, D, None)
            S.barrier()
        xts = [C.sb(es, "xt", [128, 2, D], F32) for _ in range(2)]
        xn = C.sb(es, "xn", [128, D], BF16)
        junk = C.sb(es, "junk", [128, D], BF16)
        xnT = C.sb(es, "xnT", [128, 8, T], BF16)
        ss = C.sb(es, "ss", [128, 1], F32)
        rstd = C.sb(es, "rstd", [128, 1], F32)
        ubufs = [C.sb(es, "ubuf", [128, 2, T + 2], F32) for _ in range(2)]
        accg = [C.sb(es, "accg", [128, T], F32) for _ in range(2)]
        accv = [C.sb(es, "accv", [128, T], F32) for _ in range(2)]
        sg = [C.sb(es, "sg", [128, T], F32) for _ in range(2)]
        h2T = C.sb(es, "h2T", [128, NF, T], BF16)
        tmpb = [C.sb(es, "tmpb", [128, T], F32) for _ in range(2)]
        if final_gain is not None:
            gfin = C.sb(es, "gfin", [128, D], F32)
            obuf = [C.sb(es, "obuf", [128, D], F32) for _ in range(2)]
            S.dma(S.sp, gfin.ap[:], final_gain.partition_broadcast(128), writes=[gfin], chan_buf=gfin)
        ptrs = [C.ps(es, "ptr", [128, 1024], BF16) for _ in range(2)]
        pus = [C.ps(es, "pu", [128, 512], F32) for _ in range(3)]
        pds = [C.ps(es, "pd", [128, 512], F32) for _ in range(2)]

        def load(i):
            xt = xts[i % 2]
            S.dma(S.sp, xt.ap[:], src[i * T:(i + 1) * T, :].rearrange("(c p) d -> p c d", p=128), writes=[xt], chan_buf=xt)

        load(0)
        npu = 0
        npd = 0
        for i in range(NT):
            xt = xts[i % 2]
            if i + 1 < NT:
                load(i + 1)
            for c in range(2):
                norm_transpose(C, K, xt.ap[:, c, :], xt, junk, ss, rstd, xn, ptrs[c], xnT.ap[:, :, c * 128:(c + 1) * 128], xnT)
            for j in range(NF):
                pu = pus[npu % 3]
                npu += 1
                ub = ubufs[j % 2]
                ag, av, sgb = accg[j % 2], accv[j % 2], sg[j % 2]
                for half, col in ((0, j), (1, NF + j)):
                    for kk in range(8):
                        S.emit(S.pe, lambda: nc.tensor.matmul(pu.ap[:, half * T:(half + 1) * T], lhsT=wup.ap[:, kk, col * 128:(col + 1) * 128], rhs=xnT.ap[:, kk, :], start=(kk == 0), stop=(kk == 7)),
                               reads=[wup, xnT], writes=[pu], pe_acc=True, sig=(kk == 7 and half == 1))
                S.emit(S.pool, lambda: nc.gpsimd.tensor_copy(out=ub.ap[:, :, 0:2], in_=cbuf.ap[:, j:2 * NF:NF, :]), reads=[cbuf], writes=[ub])
                S.emit(S.act, lambda: nc.scalar.copy(out=ub.ap[:, :, 2:T + 2], in_=pu.ap[:, :].rearrange("p (h t) -> p h t", h=2)), reads=[pu], writes=[ub])
                S.emit(S.pool, lambda: nc.gpsimd.tensor_copy(out=cbuf.ap[:, j:2 * NF:NF, :], in_=ub.ap[:, :, T:T + 2]), reads=[ub], writes=[cbuf])
                S.emit(S.dve, lambda: nc.vector.tensor_scalar(out=ag.ap[:], in0=ub.ap[:, 0, 0:T], scalar1=cw.ap[:, j, 0:1], scalar2=None, op0=ALU.mult), reads=[ub, cw], writes=[ag])
                for tap in (1, 2):
                    S.emit(S.dve, lambda: nc.vector.scalar_tensor_tensor(out=ag.ap[:], in0=ub.ap[:, 0, tap:tap + T], scalar=cw.ap[:, j, tap:tap + 1], in1=ag.ap[:], op0=ALU.mult, op1=ALU.add), reads=[ub, cw, ag], writes=[ag])
                S.emit(S.act, lambda: nc.scalar.mul(out=av.ap[:], in_=ub.ap[:, 1, 0:T], mul=cw.ap[:, NF + j, 0:1]), reads=[ub, cw], writes=[av])
                for tap in (1, 2):
                    tb = tmpb[tap - 1]
                    S.emit(S.act, lambda: nc.scalar.mul(out=tb.ap[:], in_=ub.ap[:, 1, tap:tap + T], mul=cw.ap[:, NF + j, tap:tap + 1]), reads=[ub, cw], writes=[tb])
                    S.emit(S.pool, lambda: nc.gpsimd.tensor_tensor(out=av.ap[:], in0=av.ap[:], in1=tb.ap[:], op=ALU.add), reads=[av, tb], writes=[av])
                S.emit(S.act, lambda: nc.scalar.activation(out=sgb.ap[:], in_=ag.ap[:], func=AF.Silu), reads=[ag], writes=[sgb])
                S.emit(S.dve, lambda: nc.vector.tensor_tensor(out=h2T.ap[:, j, :], in0=sgb.ap[:], in1=av.ap[:], op=ALU.mult), reads=[sgb, av], writes=[h2T])
            for c in range(2):
                for n in range(2):
                    pd = pds[npd % 2]
                    npd += 1
                    for j in range(NF):
                        S.emit(S.pe, lambda: nc.tensor.matmul(pd.ap[:], lhsT=h2T.ap[:, j, c * 128:(c + 1) * 128], rhs=wdn.ap[:, j, n * 512:(n + 1) * 512], start=(j == 0), stop=(j == NF - 1)),
                               reads=[h2T, wdn], writes=[pd], pe_acc=True, sig=(j == NF - 1))
                    S.emit(S.dve, lambda: nc.vector.tensor_tensor(out=xt.ap[:, c, n * 512:(n + 1) * 512], in0=pd.ap[:], in1=xt.ap[:, c, n * 512:(n + 1) * 512], op=ALU.add), reads=[pd, xt], writes=[xt])
                if final_gain is not None:
                    ob = obuf[c]
                    rms_rstd(C, xt.ap[:, c, :], xt, junk, ss, rstd, K, D, 1e-6)
                    S.emit(S.dve, lambda: nc.vector.scalar_tensor_tensor(out=ob.ap[:], in0=xt.ap[:, c, :], scalar=rstd.ap[:, 0:1], in1=gfin.ap[:], op0=ALU.mult, op1=ALU.mult), reads=[xt, rstd, gfin], writes=[ob])
                    S.dma(S.pool, dst[i * T + c * 128:i * T + (c + 1) * 128, :], ob.ap[:], reads=[ob], chan_buf=ob)
            if final_gain is None:
                S.dma(S.pool, dst[i * T:(i + 1) * T, :].rearrange("(c p) d -> p c d", p=128), xt.ap[:], reads=[xt], chan_buf=xt)
        S.barrier()


def phase_mamba(C, K, I, src, dst):
    nc, S = C.nc, C.S
    ynd = nc.dram_tensor("ynd", [S_LEN, D_IN], BF16, kind="Internal").ap()
    ident, identf = K["ident"], K["identf"]
    with contextlib.ExitStack() as es:
        win = C.sb(es, "win", [128, 8, IN_DIM], BF16)
        gain = C.sb(es, "gain", [128, 8], F32)
        cw = C.sb(es, "cw", [128, 32, 4], F32)
        cb = C.sb(es, "cb", [128, 32], F32)
        cb3 = C.sb(es, "cb3", [128, 32, 3], F32)
        dcol = C.sb(es, "dcol", [128, 16], F32)
        diagD = C.sb(es, "diagD", [128, 16, 128], BF16)
        dtb = C.sb(es, "dtb", [32, 1], F32)
        acol = C.sb(es, "acol", [32, 1], F32)
        ones32 = C.sb(es, "ones32", [32, 128], F32)
        load_cols(C, gain, I["a_norm"], 8)
        load_cols(C, cb, I["ssm_conv_b"], 32)
        with nc.allow_non_contiguous_dma(reason="small params"):
            for kk in range(4):
                S.dma(S.sp, cw.ap[:, :, kk], I["ssm_conv_w"][kk].rearrange("(j p) -> p j", p=128), writes=[cw], chan_buf=cw)
            S.dma(S.sp, dtb.ap[:, 0:1], I["ssm_dt_bias"].rearrange("(h o) -> h o", o=1), writes=[dtb], chan_buf=dtb)
            S.dma(S.sp, acol.ap[:, 0:1], I["ssm_a_log"].rearrange("(h o) -> h o", o=1), writes=[acol], chan_buf=acol)
            dv = I["ssm_d"].rearrange("(k two) -> two k", two=2)
            for hh in range(2):
                S.dma(S.sp, dcol.ap[hh * 64:(hh + 1) * 64, :], dv[hh].partition_broadcast(64), writes=[dcol], chan_buf=dcol)
        S.emit(S.act, lambda: nc.scalar.activation(out=acol.ap[:], in_=acol.ap[:], func=AF.Exp), reads=[acol], writes=[acol])
        S.emit(S.dve, lambda: nc.vector.tensor_scalar(out=acol.ap[:], in0=acol.ap[:], scalar1=-1.0, scalar2=None, op0=ALU.mult), reads=[acol], writes=[acol])
        S.emit(S.pool, lambda: nc.gpsimd.memset(cb3.ap[:], 0.0), writes=[cb3])
        S.emit(S.pool, lambda: nc.gpsimd.memset(ones32.ap[:], 1.0), writes=[ones32])
        for kk in range(16):
            S.emit(S.dve, lambda: nc.vector.tensor_scalar(out=diagD.ap[:, kk, :], in0=identf.ap[:], scalar1=dcol.ap[:, kk:kk + 1], scalar2=None, op0=ALU.mult), reads=[identf, dcol], writes=[diagD])
        with contextlib.ExitStack() as es2:
            C.stg = [C.sb(es2, "stg", [128, 2048], F32) for _ in range(3)]
            prep_weight(C, es2, win, I["ssm_w_in"], 8, IN_DIM, gain)
            S.barrier()
        xts = [C.sb(es, "xt", [128, D], F32) for _ in range(2)]
        xn = C.sb(es, "xn", [128, D], BF16)
        xnT = C.sb(es, "xnT", [128, 8, 128], BF16)
        ss = C.sb(es, "ss", [128, 1], F32)
        rstd = C.sb(es, "rstd", [128, 1], F32)
        szs = [C.sb(es, "sz", [128, 512], F32) for _ in range(2)]
        xbcT = C.sb(es, "xbcT", [128, 32, 128], BF16)
        ub4s = [C.sb(es, "ub4", [128, 4, 131], F32) for _ in range(2)]
        acc4s = [C.sb(es, "acc4", [128, 4, 128], F32) for _ in range(2)]
        tmpc = [C.sb(es, "tmpc", [128, 128], F32) for _ in range(2)]
        xc = C.sb(es, "xc", [128, D_IN], BF16)
        xcdec = C.sb(es, "xcdec", [128, D_IN], BF16)
        Btok = C.sb(es, "Btok", [128, 1024], BF16)
        ADs = [C.sb(es, "AD", [32, 512], F32) for _ in range(2)]
        Egs = [C.sb(es, "Eg", [128, 512], F32) for _ in range(2)]
        EAs = [C.sb(es, "EA", [128, 512], F32) for _ in range(2)]
        MTs = [C.sb(es, "MT", [128, 512], BF16) for _ in range(2)]
        CEs = [C.sb(es, "CE", [128, 512], BF16) for _ in range(2)]
        prevT = C.sb(es, "prevT", [128, D_IN], F32)
        prevbf = C.sb(es, "prevbf", [128, D_IN], BF16)
        yzs = [C.sb(es, "yz", [128, 256], F32) for _ in range(2)]
        ssg = C.sb(es, "ssg", [128, 1], F32)
        rg = C.sb(es, "rg", [128, 1], F32)
        yns = [C.sb(es, "yn", [128, D_IN], BF16) for _ in range(2)]
        sm = {n: C.sb(es, n, [32, 128], F32) for n in ("e1", "dtT", "adtT", "acsT", "nacsT", "decT", "dtdecT")}
        ecol = C.sb(es, "ecol", [32, 1], F32)
        dg = C.sb(es, "dg", [32, 32], F32)
        tok = C.sb(es, "tok", [128, 128], F32)
        S.emit(S.pool, lambda: nc.gpsimd.memset(prevT.ap[:], 0.0), writes=[prevT])
        S.emit(S.pool, lambda: nc.gpsimd.memset(prevbf.ap[:], 0.0), writes=[prevbf])
        ptrs = [C.ps(es, "ptr", [128, 1024], BF16) for _ in range(2)]
        pmm = [C.ps(es, "pmm", [128, 512], F32) for _ in range(2)]
        pseg = [C.ps(es, "pseg", [128, 512], F32) for _ in range(2)]
        py = C.ps(es, "py", [128, 512], F32)
        psm = C.ps(es, "psm", [128, 512], F32)
        cnt = {"mm": 0, "sg": 0, "tr": 0}

        def nxt(lst, key):
            b = lst[cnt[key] % len(lst)]
            cnt[key] += 1
            return b

        def load(c):
            S.dma(S.sp, xts[c % 2].ap[:], src[c * 128:(c + 1) * 128, :], writes=[xts[c % 2]], chan_buf=xts[c % 2])

        load(0)
        for c in range(NCH):
            xt = xts[c % 2]
            yn = yns[c % 2]
            if c + 1 < NCH:
                load(c + 1)
            norm_transpose(C, K, xt.ap[:], xt, xn, ss, rstd, xn, nxt(ptrs, "tr"), xnT.ap[:], xnT)
            pdt = nxt(pmm, "mm")
            for kk in range(8):
                S.emit(S.pe, lambda: nc.tensor.matmul(pdt.ap[0:32, 0:128], lhsT=win.ap[:, kk, 6144:6176], rhs=xnT.ap[:, kk, :], start=(kk == 0), stop=(kk == 7)),
                       reads=[win, xnT], writes=[pdt], pe_acc=True, sig=(kk == 7))
            e1, dtT, adtT, acsT, nacsT, decT, dtdecT = [sm[n] for n in ("e1", "dtT", "adtT", "acsT", "nacsT", "decT", "dtdecT")]
            S.emit(S.act, lambda: nc.scalar.activation(out=e1.ap[:], in_=pdt.ap[0:32, 0:128], func=AF.Exp, bias=dtb.ap[:, 0:1]), reads=[pdt, dtb], writes=[e1])
            S.emit(S.act, lambda: nc.scalar.activation(out=dtT.ap[:], in_=e1.ap[:], func=AF.Ln, bias=1.0), reads=[e1], writes=[dtT])
            S.emit(S.dve, lambda: nc.vector.tensor_scalar(out=adtT.ap[:], in0=dtT.ap[:], scalar1=acol.ap[:, 0:1], scalar2=None, op0=ALU.mult), reads=[dtT, acol], writes=[adtT])
            S.emit(S.dve, lambda: nc.vector.tensor_tensor_scan(out=acsT.ap[:], data0=ones32.ap[:], data1=adtT.ap[:], initial=0.0, op0=ALU.mult, op1=ALU.add), reads=[ones32, adtT], writes=[acsT])
            S.emit(S.act, lambda: nc.scalar.mul(out=nacsT.ap[:], in_=acsT.ap[:], mul=-1.0), reads=[acsT], writes=[nacsT])
            S.emit(S.act, lambda: nc.scalar.activation(out=decT.ap[:], in_=acsT.ap[:], func=AF.Exp, scale=-1.0, bias=acsT.ap[:, 127:128]), reads=[acsT], writes=[decT])
            S.emit(S.dve, lambda: nc.vector.tensor_tensor(out=dtdecT.ap[:], in0=dtT.ap[:], in1=decT.ap[:], op=ALU.mult), reads=[dtT, decT], writes=[dtdecT])
            S.emit(S.act, lambda: nc.scalar.activation(out=ecol.ap[:], in_=acsT.ap[:, 127:128], func=AF.Exp), reads=[acsT], writes=[ecol])
            S.emit(S.dve, lambda: nc.vector.tensor_scalar(out=dg.ap[:], in0=identf.ap[0:32, 0:32], scalar1=ecol.ap[:, 0:1], scalar2=None, op0=ALU.mult), reads=[identf, ecol], writes=[dg])
            ptok = nxt(pmm, "mm")
            for i4, (lh, rh) in enumerate(((dtT, None), (dtdecT, None), (nacsT, None), (ones32, dg))):
                rhs_ap = identf.ap[0:32, 0:32] if rh is None else rh.ap[:]
                S.emit(S.pe, lambda: nc.tensor.matmul(ptok.ap[:, i4 * 32:(i4 + 1) * 32], lhsT=lh.ap[:], rhs=rhs_ap, start=True, stop=True),
                       reads=[lh, identf] + ([rh] if rh is not None else []), writes=[ptok], pe_acc=True, sig=(i4 == 3))
            S.emit(S.dve, lambda: nc.vector.tensor_copy(out=tok.ap[:], in_=ptok.ap[:, 0:128]), reads=[ptok], writes=[tok])
            for q4 in range(8):
                pj = nxt(pmm, "mm")
                ub4 = ub4s[q4 % 2]
                acc4 = acc4s[q4 % 2]
                for jj in range(4):
                    j = 4 * q4 + jj
                    for kk in range(8):
                        S.emit(S.pe, lambda: nc.tensor.matmul(pj.ap[:, jj * 128:(jj + 1) * 128], lhsT=win.ap[:, kk, 2048 + j * 128:2048 + (j + 1) * 128], rhs=xnT.ap[:, kk, :], start=(kk == 0), stop=(kk == 7)),
                               reads=[win, xnT], writes=[pj], pe_acc=True, sig=(kk == 7 and jj == 3))
                S.emit(S.pool, lambda: nc.gpsimd.tensor_copy(out=ub4.ap[:, :, 0:3], in_=cb3.ap[:, 4 * q4:4 * q4 + 4, :]), reads=[cb3], writes=[ub4])
                S.emit(S.act, lambda: nc.scalar.copy(out=ub4.ap[:, :, 3:131], in_=pj.ap[:, :].rearrange("p (j t) -> p j t", j=4)), reads=[pj], writes=[ub4])
                S.emit(S.pool, lambda: nc.gpsimd.tensor_copy(out=cb3.ap[:, 4 * q4:4 * q4 + 4, :], in_=ub4.ap[:, :, 128:131]), reads=[ub4], writes=[cb3])
                for jj in range(4):
                    j = 4 * q4 + jj
                    if jj % 2 == 0:
                        S.emit(S.dve, lambda: nc.vector.tensor_scalar(out=acc4.ap[:, jj, :], in0=ub4.ap[:, jj, 0:128], scalar1=cw.ap[:, j, 0:1], scalar2=cb.ap[:, j:j + 1], op0=ALU.mult, op1=ALU.add), reads=[ub4, cw, cb], writes=[acc4])
                        for tap in (1, 2, 3):
                            S.emit(S.dve, lambda: nc.vector.scalar_tensor_tensor(out=acc4.ap[:, jj, :], in0=ub4.ap[:, jj, tap:tap + 128], scalar=cw.ap[:, j, tap:tap + 1], in1=acc4.ap[:, jj, :], op0=ALU.mult, op1=ALU.add), reads=[ub4, cw, acc4], writes=[acc4])
                    else:
                        S.emit(S.act, lambda: nc.scalar.activation(out=acc4.ap[:, jj, :], in_=ub4.ap[:, jj, 0:128], func=AF.Identity, scale=cw.ap[:, j, 0:1], bias=cb.ap[:, j:j + 1]), reads=[ub4, cw, cb], writes=[acc4])
                        for tap in (1, 2, 3):
                            tb = tmpc[tap % 2]
                            S.emit(S.act, lambda: nc.scalar.mul(out=tb.ap[:], in_=ub4.ap[:, jj, tap:tap + 128], mul=cw.ap[:, j, tap:tap + 1]), reads=[ub4, cw], writes=[tb])
                            S.emit(S.pool, lambda: nc.gpsimd.tensor_tensor(out=acc4.ap[:, jj, :], in0=acc4.ap[:, jj, :], in1=tb.ap[:], op=ALU.add), reads=[acc4, tb], writes=[acc4])
                S.emit(S.act, lambda: nc.scalar.activation(out=xbcT.ap[:, 4 * q4:4 * q4 + 4, :], in_=acc4.ap[:], func=AF.Silu), reads=[acc4], writes=[xbcT])
            for half in range(2):
                ptr = nxt(ptrs, "tr")
                for jj in range(8):
                    j = 8 * half + jj
                    S.emit(S.pe, lambda: nc.tensor.transpose(out=ptr.ap[:, jj * 128:(jj + 1) * 128], in_=xbcT.ap[:, j, :], identity=ident.ap[:]),
                           reads=[xbcT, ident], writes=[ptr], pe_acc=True, sig=(jj == 7))
                pv = ptr.ap[:, 0:1024].rearrange("p (h e) -> p h e", h=16)
                S.emit(S.dve, lambda: nc.vector.tensor_tensor(out=xc.ap[:, half * 1024:(half + 1) * 1024].rearrange("p (h e) -> p h e", h=16), in0=pv, in1=bc(tok.ap[:, 16 * half:16 * half + 16].unsqueeze(2), [128, 16, 64]), op=ALU.mult), reads=[ptr, tok], writes=[xc])
                S.emit(S.dve, lambda: nc.vector.tensor_tensor(out=xcdec.ap[:, half * 1024:(half + 1) * 1024].rearrange("p (h e) -> p h e", h=16), in0=pv, in1=bc(tok.ap[:, 32 + 16 * half:32 + 16 * half + 16].unsqueeze(2), [128, 16, 64]), op=ALU.mult), reads=[ptr, tok], writes=[xcdec])
            ptr = nxt(ptrs, "tr")
            for jj in range(8):
                S.emit(S.pe, lambda: nc.tensor.transpose(out=ptr.ap[:, jj * 128:(jj + 1) * 128], in_=xbcT.ap[:, 16 + jj, :], identity=ident.ap[:]),
                       reads=[xbcT, ident], writes=[ptr], pe_acc=True, sig=(jj == 7))
            S.emit(S.act, lambda: nc.scalar.copy(out=Btok.ap[:], in_=ptr.ap[:, 0:1024]), reads=[ptr], writes=[Btok])
            for g in range(8):
                sz = szs[(g // 2) % 2]
                if g % 2 == 0:
                    pz = nxt(pmm, "mm")
                    n = g // 2
                    for kk in range(8):
                        S.emit(S.pe, lambda: nc.tensor.matmul(pz.ap[:], lhsT=xnT.ap[:, kk, :], rhs=win.ap[:, kk, n * 512:(n + 1) * 512], start=(kk == 0), stop=(kk == 7)),
                               reads=[xnT, win], writes=[pz], pe_acc=True, sig=(kk == 7))
                    S.emit(S.act, lambda: nc.scalar.activation(out=sz.ap[:], in_=pz.ap[:], func=AF.Silu), reads=[pz], writes=[sz])
                AD, Eg, EA, MT, CE = ADs[g % 2], Egs[g % 2], EAs[g % 2], MTs[g % 2], CEs[g % 2]
                pcb_ap = psm.ap[:, (g % 2) * 128:(g % 2 + 1) * 128]
                S.emit(S.pe, lambda: nc.tensor.matmul(pcb_ap, lhsT=xbcT.ap[:, 16 + g, :], rhs=xbcT.ap[:, 24 + g, :], start=True, stop=True), reads=[xbcT], writes=[psm], pe_acc=True)
                S.emit(S.pool, lambda: nc.gpsimd.affine_select(out=AD.ap[:].rearrange("p (h l) -> p h l", h=4), in_=bc(acsT.ap[:].unsqueeze(1), [32, 4, 128]), pattern=[[-1, 4], [0, 128]], compare_op=ALU.is_equal, fill=0.0, base=-4 * g, channel_multiplier=1), reads=[acsT], writes=[AD])
                pea = nxt(pseg, "sg")
                S.emit(S.pe, lambda: nc.tensor.matmul(pea.ap[:], lhsT=ones32.ap[:], rhs=AD.ap[:], start=True, stop=True), reads=[ones32, AD], writes=[pea], pe_acc=True)
                S.emit(S.act, lambda: nc.scalar.activation(out=EA.ap[:], in_=pea.ap[:], func=AF.Exp), reads=[pea], writes=[EA])
                psg = nxt(pseg, "sg")
                S.emit(S.pe, lambda: nc.tensor.matmul(psg.ap[:], lhsT=ones32.ap[:], rhs=AD.ap[:], start=True, stop=False), reads=[ones32, AD], writes=[psg], pe_acc=True, sig=False)
                S.emit(S.pe, lambda: nc.tensor.matmul(psg.ap[:], lhsT=ident.ap[:], rhs=K["nmc"].ap[:], start=False, stop=True), reads=[ident, K["nmc"]], writes=[psg], pe_acc=True)
                for h in range(4):
                    S.emit(S.act, lambda: nc.scalar.activation(out=Eg.ap[:, h * 128:(h + 1) * 128], in_=psg.ap[:, h * 128:(h + 1) * 128], func=AF.Exp, bias=tok.ap[:, 64 + 4 * g + h:64 + 4 * g + h + 1]), reads=[psg, tok], writes=[Eg])
                S.emit(S.dve, lambda: nc.vector.tensor_tensor(out=MT.ap[:].rearrange("p (h l) -> p h l", h=4), in0=Eg.ap[:].rearrange("p (h l) -> p h l", h=4), in1=bc(pcb_ap.unsqueeze(1), [128, 4, 128]), op=ALU.mult), reads=[Eg, psm], writes=[MT])
                S.emit(S.pool, lambda: nc.gpsimd.tensor_tensor(out=CE.ap[:].rearrange("p (h l) -> p h l", h=4), in0=EA.ap[:].rearrange("p (h l) -> p h l", h=4), in1=bc(xbcT.ap[:, 24 + g, :].unsqueeze(1), [128, 4, 128]), op=ALU.mult), reads=[EA, xbcT], writes=[CE])
                pyr = py.ap[:, (g % 2) * 256:(g % 2 + 1) * 256]
                for i2 in range(2):
                    S.emit(S.pe, lambda: nc.tensor.matmul(pyr[:, i2 * 128:(i2 + 1) * 128], lhsT=xbcT.ap[:, 2 * g + i2, :], rhs=diagD.ap[:, 2 * g + i2, :], start=True, stop=False),
                           reads=[xbcT, diagD], writes=[py], pe_acc=True, sig=False)
                    for hh in (2 * i2, 2 * i2 + 1):
                        hd = 4 * g + hh
                        S.emit(S.pe, lambda: nc.tensor.matmul(pyr[:, hh * 64:(hh + 1) * 64], lhsT=MT.ap[:, hh * 128:(hh + 1) * 128], rhs=xc.ap[:, hd * 64:(hd + 1) * 64], start=False, stop=False),
                               reads=[MT, xc], writes=[py], pe_acc=True, sig=False)
                        S.emit(S.pe, lambda: nc.tensor.matmul(pyr[:, hh * 64:(hh + 1) * 64], lhsT=CE.ap[:, hh * 128:(hh + 1) * 128], rhs=prevbf.ap[:, hd * 64:(hd + 1) * 64], start=False, stop=True),
                               reads=[CE, prevbf], writes=[py], pe_acc=True, sig=(hh == 3))
                pst_ap = psm.ap[:, 256:512]
                S.emit(S.pe, lambda: nc.tensor.matmul(pst_ap, lhsT=Btok.ap[:, g * 128:(g + 1) * 128], rhs=xcdec.ap[:, g * 256:(g + 1) * 256], start=True, stop=True), reads=[Btok, xcdec], writes=[psm], pe_acc=True)
                yz = yzs[g % 2]
                S.emit(S.dve, lambda: nc.vector.tensor_tensor(out=yz.ap[:], in0=pyr, in1=sz.ap[:, (g % 2) * 256:(g % 2 + 1) * 256], op=ALU.mult), reads=[py, sz], writes=[yz])
                S.emit(S.act, lambda: nc.scalar.activation(out=yn.ap[:, g * 256:(g + 1) * 256], in_=yz.ap[:], func=AF.Square, accum_out=ssg.ap[:, 0:1]), reads=[yz], writes=[yn, ssg])
                S.emit(S.dve, lambda: nc.vector.tensor_scalar(out=ssg.ap[:, 0:1], in0=ssg.ap[:, 0:1], scalar1=1.0 / 256.0, scalar2=1e-5, op0=ALU.mult, op1=ALU.add), reads=[ssg], writes=[ssg])
                S.emit(S.pool, lambda: nc.gpsimd.tensor_tensor(out=rg.ap[:, 0:1], in0=ssg.ap[:, 0:1], in1=K["mhalf"].ap[:, 0:1], op=ALU.pow), reads=[ssg, K["mhalf"]], writes=[rg])
                S.emit(S.dve, lambda: nc.vector.tensor_scalar(out=yn.ap[:, g * 256:(g + 1) * 256], in0=yz.ap[:], scalar1=rg.ap[:, 0:1], scalar2=None, op0=ALU.mult), reads=[yz, rg], writes=[yn])
                pvw = prevT.ap[:, g * 256:(g + 1) * 256]
                S.emit(S.pool, lambda: nc.gpsimd.tensor_tensor(out=pvw.rearrange("p (h e) -> p h e", h=4), in0=pvw.rearrange("p (h e) -> p h e", h=4), in1=bc(tok.ap[:, 96 + 4 * g:96 + 4 * g + 4].unsqueeze(2), [128, 4, 64]), op=ALU.mult), reads=[prevT, tok], writes=[prevT])
                S.emit(S.dve, lambda: nc.vector.tensor_tensor(out=pvw, in0=pst_ap, in1=pvw, op=ALU.add), reads=[psm, prevT], writes=[prevT])
                S.emit(S.act, lambda: nc.scalar.copy(out=prevbf.ap[:, g * 256:(g + 1) * 256], in_=pvw), reads=[prevT], writes=[prevbf])
            S.dma(S.pool, ynd[c * 128:(c + 1) * 128, :], yn.ap[:], reads=[yn], chan_buf=yn)
        S.barrier()
    with contextlib.ExitStack() as es:
        wout = C.sb(es, "wout", [128, 16, D], BF16)
        gout = C.sb(es, "gout", [128, 16], F32)
        load_cols(C, gout, I["ssm_norm"], 16)
        with contextlib.ExitStack() as es2:
            C.stg = [C.sb(es2, "stg", [128, 2048], F32) for _ in range(3)]
            prep_weight(C, es2, wout, I["ssm_w_out"], 16, D, gout)
            S.barrier()
        xts = [C.sb(es, "xt", [128, D], F32) for _ in range(2)]
        yls = [C.sb(es, "yl", [128, D_IN], BF16) for _ in range(2)]
        ynT = C.sb(es, "ynT", [128, 16, 128], BF16)
        ptrs = [C.ps(es, "ptr", [128, 1024], BF16) for _ in range(2)]
        pmm = [C.ps(es, "pmm", [128, 512], F32) for _ in range(2)]

        def load2(c):
            S.dma(S.sp, xts[c % 2].ap[:], src[c * 128:(c + 1) * 128, :], writes=[xts[c % 2]], chan_buf=xts[c % 2])
            S.dma(S.sp, yls[c % 2].ap[:], ynd[c * 128:(c + 1) * 128, :], writes=[yls[c % 2]], chan_buf=yls[c % 2])

        load2(0)
        nmm = 0
        for c in range(NCH):
            if c + 1 < NCH:
                load2(c + 1)
            xt, yl = xts[c % 2], yls[c % 2]
            for half in range(2):
                ptr = ptrs[half]
                for jj in range(8):
                    j = 8 * half + jj
                    S.emit(S.pe, lambda: nc.tensor.transpose(out=ptr.ap[:, jj * 128:(jj + 1) * 128], in_=yl.ap[:, j * 128:(j + 1) * 128], identity=ident.ap[:]),
                           reads=[yl, ident], writes=[ptr], pe_acc=True, sig=(jj == 7))
                if half == 0:
                    S.emit(S.act, lambda: nc.scalar.copy(out=ynT.ap[:, 0:8, :], in_=ptr.ap[:, 0:1024].rearrange("p (k t) -> p k t", k=8)), reads=[ptr], writes=[ynT])
                else:
                    S.emit(S.dve, lambda: nc.vector.tensor_copy(out=ynT.ap[:, 8:16, :], in_=ptr.ap[:, 0:1024].rearrange("p (k t) -> p k t", k=8)), reads=[ptr], writes=[ynT])
            for n2 in range(2):
                po = pmm[nmm % 2]
                nmm += 1
                for kk in range(16):
                    S.emit(S.pe, lambda: nc.tensor.matmul(po.ap[:], lhsT=ynT.ap[:, kk, :], rhs=wout.ap[:, kk, n2 * 512:(n2 + 1) * 512], start=(kk == 0), stop=(kk == 15)),
                           reads=[ynT, wout], writes=[po], pe_acc=True, sig=(kk == 15))
                S.emit(S.dve, lambda: nc.vector.tensor_tensor(out=xt.ap[:, n2 * 512:(n2 + 1) * 512], in0=po.ap[:], in1=xt.ap[:, n2 * 512:(n2 + 1) * 512], op=ALU.add), reads=[po, xt], writes=[xt])
            S.dma(S.pool, dst[c * 128:(c + 1) * 128, :], xt.ap[:], reads=[xt], chan_buf=xt)
        S.barrier()

ATT_PATTERNS = ((128, 1), (512, 4), (2048, 16))


def rotary(C, src, dst, rp, tmps, nh):
    nc, S = C.nc, C.S
    sv = src.ap[:, 0:nh * 128].rearrange("p (h e) -> p h e", h=nh)
    dv = dst.ap[:, 0:nh * 128].rearrange("p (h e) -> p h e", h=nh)
    cos = bc(rp.ap[:, 0:16].unsqueeze(1), [128, nh, 16])
    sin = bc(rp.ap[:, 16:32].unsqueeze(1), [128, nh, 16])
    t1, t2, t3, t4 = [t.ap[:, 0:nh * 16].rearrange("p (h e) -> p h e", h=nh) for t in tmps]
    x1, x2 = sv[:, :, 0:16], sv[:, :, 16:32]
    S.emit(S.dve, lambda: nc.vector.tensor_tensor(out=t1, in0=x1, in1=cos, op=ALU.mult), reads=[src, rp], writes=[tmps[0]])
    S.emit(S.pool, lambda: nc.gpsimd.tensor_tensor(out=t2, in0=x2, in1=sin, op=ALU.mult), reads=[src, rp], writes=[tmps[1]])
    S.emit(S.dve, lambda: nc.vector.tensor_tensor(out=t3, in0=x2, in1=cos, op=ALU.mult), reads=[src, rp], writes=[tmps[2]])
    S.emit(S.pool, lambda: nc.gpsimd.tensor_tensor(out=t4, in0=x1, in1=sin, op=ALU.mult), reads=[src, rp], writes=[tmps[3]])
    S.emit(S.dve, lambda: nc.vector.tensor_tensor(out=dv[:, :, 0:16], in0=t1, in1=t2, op=ALU.subtract), reads=[tmps[0], tmps[1]], writes=[dst])
    S.emit(S.pool, lambda: nc.gpsimd.tensor_tensor(out=dv[:, :, 16:32], in0=t3, in1=t4, op=ALU.add), reads=[tmps[2], tmps[3]], writes=[dst])
    S.emit(S.act, lambda: nc.scalar.copy(out=dv[:, :, 32:128], in_=sv[:, :, 32:128]), reads=[src], writes=[dst])


def phase_attn(C, K, I, src, dst):
    nc, S = C.nc, C.S
    nd = [nc.dram_tensor("numden%d" % g, [S_LEN, 8 * 129], F32, kind="Internal").ap() for g in range(3)]
    rope = I["rope"]
    with contextlib.ExitStack() as es:
        wkv = C.sb(es, "wkv", [128, 8, 1536], BF16)
        wq = C.sb(es, "wq", [128, 8, 3072], BF16)
        wo = C.sb(es, "wo", [128, 8, D], BF16)
        gkv = C.sb(es, "gkv", [128, 8], F32)
        gq = C.sb(es, "gq", [128, 8], F32)
        load_cols(C, gkv, I["kv_norm"], 8)
        load_cols(C, gq, I["b_norm"], 8)
        with contextlib.ExitStack() as es2:
            C.stg = [C.sb(es2, "stg", [128, 2048], F32) for _ in range(3)]
            prep_weight(C, es2, wkv, I["w_kv"], 8, 1536, gkv)
            prep_weight(C, es2, wq, I["att_w_q"], 8, 3072, gq)
            prep_weight(C, es2, wo, I["att_w_o"], 8, D, None)
            S.barrier()
        xts = [C.sb(es, "xt", [128, D], F32) for _ in range(2)]
        rps = [C.sb(es, "rp", [128, 32], F32) for _ in range(2)]
        xn = C.sb(es, "xn", [128, D], BF16)
        junk = C.sb(es, "junk", [128, D], BF16)
        xnT = C.sb(es, "xnT", [128, 8, 128], BF16)
        ss = C.sb(es, "ss", [128, 1], F32)
        rstd = C.sb(es, "rstd", [128, 1], F32)
        qf = C.sb(es, "qf", [128, 1024], F32)
        qb = C.sb(es, "qb", [128, 1024], BF16)
        kf = C.sb(es, "kf", [128, 256], F32)
        kb = C.sb(es, "kb", [128, 256], BF16)
        tq = [C.sb(es, "tq", [128, 128], F32) for _ in range(4)]
        tk = [C.sb(es, "tk", [128, 32], F32) for _ in range(4)]
        QT = C.sb(es, "QT", [128, 8, 128], BF16)
        KTs = [C.sb(es, "KT", [128, 2, 128], BF16) for _ in range(2)]
        vaugs = [C.sb(es, "vaug", [128, 2, 129], BF16) for _ in range(2)]
        PTs = [C.sb(es, "PT", [128, 512], BF16) for _ in range(4)]
        obs = [C.sb(es, "ob", [128, 8, 129], F32) for _ in range(2)]
        for v in vaugs:
            S.emit(S.pool, lambda: nc.gpsimd.memset(v.ap[:], 1.0), writes=[v])
        ptr = C.ps(es, "ptr", [128, 1024], BF16)
        pmm = [C.ps(es, "pmm", [128, 512], F32) for _ in range(2)]
        pS = [C.ps(es, "pS", [128, 512], F32) for _ in range(2)]
        po = C.ps(es, "po", [128, 3 * 512], F32)
        scale = 1.0 / np.sqrt(128.0)

        blocks = []
        for g, (win, dil) in enumerate(ATT_PATTERNS):
            for r in range(dil):
                for n in range(S_LEN // dil // 128):
                    blocks.append((g, dil, r, n))

        def rows(ap, dil, r, n):
            return ap.rearrange("(m dd) c -> dd m c", dd=dil)[r][n * 128:(n + 1) * 128, :]

        def load(bi):
            g, dil, r, n = blocks[bi]
            S.dma(S.sp, xts[bi % 2].ap[:], rows(src, dil, r, n), writes=[xts[bi % 2]], chan_buf=xts[bi % 2])
            S.dma(S.sp, rps[bi % 2].ap[:], rows(rope, dil, r, n), writes=[rps[bi % 2]], chan_buf=rps[bi % 2])

        load(0)
        nmm = 0
        nS = 0
        for bi, (g, dil, r, n) in enumerate(blocks):
            xt, rp = xts[bi % 2], rps[bi % 2]
            if bi + 1 < len(blocks):
                load(bi + 1)
            norm_transpose(C, K, xt.ap[:], xt, junk, ss, rstd, xn, ptr, xnT.ap[:], xnT)
            for n2 in range(2):
                pq = pmm[nmm % 2]
                nmm += 1
                for kk in range(8):
                    S.emit(S.pe, lambda: nc.tensor.matmul(pq.ap[:], lhsT=xnT.ap[:, kk, :], rhs=wq.ap[:, kk, g * 1024 + n2 * 512:g * 1024 + (n2 + 1) * 512], start=(kk == 0), stop=(kk == 7)),
                           reads=[xnT, wq], writes=[pq], pe_acc=True, sig=(kk == 7))
                S.emit(S.act, lambda: nc.scalar.copy(out=qf.ap[:, n2 * 512:(n2 + 1) * 512], in_=pq.ap[:]), reads=[pq], writes=[qf])
            pkv = pmm[nmm % 2]
            nmm += 1
            for half, c0 in ((0, g * 256), (1, 768 + g * 256)):
                for kk in range(8):
                    S.emit(S.pe, lambda: nc.tensor.matmul(pkv.ap[:, half * 256:(half + 1) * 256], lhsT=xnT.ap[:, kk, :], rhs=wkv.ap[:, kk, c0:c0 + 256], start=(kk == 0), stop=(kk == 7)),
                           reads=[xnT, wkv], writes=[pkv], pe_acc=True, sig=(kk == 7 and half == 1))
            KT, vaug = KTs[n % 2], vaugs[n % 2]
            S.emit(S.act, lambda: nc.scalar.copy(out=kf.ap[:], in_=pkv.ap[:, 0:256]), reads=[pkv], writes=[kf])
            S.emit(S.act, lambda: nc.scalar.copy(out=vaug.ap[:, :, 0:128], in_=pkv.ap[:, 256:512].rearrange("p (h e) -> p h e", h=2)), reads=[pkv], writes=[vaug])
            rotary(C, qf, qb, rp, tq, 8)
            rotary(C, kf, kb, rp, tk, 2)
            for h in range(8):
                S.emit(S.pe, lambda: nc.tensor.transpose(out=ptr.ap[:, h * 128:(h + 1) * 128], in_=qb.ap[:, h * 128:(h + 1) * 128], identity=K["ident"].ap[:]),
                       reads=[qb, K["ident"]], writes=[ptr], pe_acc=True, sig=(h == 7))
            S.emit(S.dve, lambda: nc.vector.tensor_copy(out=QT.ap[:], in_=ptr.ap[:, 0:1024].rearrange("p (k t) -> p k t", k=8)), reads=[ptr], writes=[QT])
            for h in range(2):
                S.emit(S.pe, lambda: nc.tensor.transpose(out=ptr.ap[:, h * 128:(h + 1) * 128], in_=kb.ap[:, h * 128:(h + 1) * 128], identity=K["ident"].ap[:]),
                       reads=[kb, K["ident"]], writes=[ptr], pe_acc=True, sig=(h == 1))
            S.emit(S.dve, lambda: nc.vector.tensor_copy(out=KT.ap[:], in_=ptr.ap[:, 0:256].rearrange("p (k t) -> p k t", k=2)), reads=[ptr], writes=[KT])
            kblocks = []
            if n > 0:
                kblocks.append((KTs[(n - 1) % 2], vaugs[(n - 1) % 2], K["nmp"]))
            kblocks.append((KT, vaug, K["nmc"]))
            ob = obs[bi % 2]
            for jk in range(2):
                pts = []
                for (kt, va, nm) in kblocks:
                    ps_ = pS[nS % 2]
                    pt_ = PTs[nS % 4]
                    nS += 1
                    S.emit(S.pe, lambda: nc.tensor.matmul(ps_.ap[:], lhsT=kt.ap[:, jk, :], rhs=QT.ap[:, 4 * jk:4 * jk + 4, :], start=True, stop=False),
                           reads=[kt, QT], writes=[ps_], pe_acc=True, sig=False)
                    S.emit(S.pe, lambda: nc.tensor.matmul(ps_.ap[:], lhsT=K["ident"].ap[:], rhs=nm.ap[:], start=False, stop=True),
                           reads=[K["ident"], nm], writes=[ps_], pe_acc=True)
                    S.emit(S.act, lambda: nc.scalar.activation(out=pt_.ap[:], in_=ps_.ap[:], func=AF.Exp, scale=float(scale)), reads=[ps_], writes=[pt_])
                    pts.append((pt_, va))
                for hl in range(4):
                    h = 4 * jk + hl
                    oslice = po.ap[:, (h // 3) * 512 + (h % 3) * 129:(h // 3) * 512 + (h % 3) * 129 + 129]
                    for i, (pt_, va) in enumerate(pts):
                        S.emit(S.pe, lambda: nc.tensor.matmul(oslice, lhsT=pt_.ap[:, hl * 128:(hl + 1) * 128], rhs=va.ap[:, jk, :], start=(i == 0), stop=(i == len(pts) - 1)),
                               reads=[pt_, va], writes=[po], pe_acc=True, sig=(i == len(pts) - 1 and hl == 3))
            for b3 in range(3):
                nh = 3 if b3 < 2 else 2
                eng, q = (S.act, None) if b3 != 1 else (S.dve, None)
                o_ap = ob.ap[:, b3 * 3:b3 * 3 + nh, :]
                i_ap = po.ap[:, b3 * 512:b3 * 512 + nh * 129].rearrange("p (h e) -> p h e", h=nh)
                if b3 != 1:
                    S.emit(S.act, lambda: nc.scalar.copy(out=o_ap, in_=i_ap), reads=[po], writes=[ob])
                else:
                    S.emit(S.dve, lambda: nc.vector.tensor_copy(out=o_ap, in_=i_ap), reads=[po], writes=[ob])
            S.dma(S.pool, rows(nd[g], dil, r, n), ob.ap[:].rearrange("p h e -> p (h e)"), reads=[ob], chan_buf=ob)
        S.barrier()
        nds = [[C.sb(es, "ndl", [128, 8, 129], F32) for _ in range(3)] for _ in range(2)]
        rden = C.sb(es, "rden", [128, 8], F32)
        o16 = C.sb(es, "o16", [128, 1024], BF16)
        oT = C.sb(es, "oT", [128, 8, 128], BF16)

        def load2(c):
            S.dma(S.sp, xts[c % 2].ap[:], src[c * 128:(c + 1) * 128, :], writes=[xts[c % 2]], chan_buf=xts[c % 2])
            for g in range(3):
                b = nds[c % 2][g]
                S.dma(S.sp, b.ap[:].rearrange("p h e -> p (h e)"), nd[g][c * 128:(c + 1) * 128, :], writes=[b], chan_buf=b)

        load2(0)
        for c in range(NCH):
            if c + 1 < NCH:
                load2(c + 1)
            xt = xts[c % 2]
            a0, a1, a2 = nds[c % 2]
            S.emit(S.pool, lambda: nc.gpsimd.tensor_tensor(out=a0.ap[:], in0=a0.ap[:], in1=a1.ap[:], op=ALU.add), reads=[a0, a1], writes=[a0])
            S.emit(S.pool, lambda: nc.gpsimd.tensor_tensor(out=a0.ap[:], in0=a0.ap[:], in1=a2.ap[:], op=ALU.add), reads=[a0, a2], writes=[a0])
            S.emit(S.dve, lambda: nc.vector.reciprocal(out=rden.ap[:].unsqueeze(2), in_=a0.ap[:, :, 128:129]), reads=[a0], writes=[rden])
            S.emit(S.dve, lambda: nc.vector.tensor_tensor(out=o16.ap[:].rearrange("p (h e) -> p h e", h=8), in0=a0.ap[:, :, 0:128], in1=bc(rden.ap[:].unsqueeze(2), [128, 8, 128]), op=ALU.mult), reads=[a0, rden], writes=[o16])
            for h in range(8):
                S.emit(S.pe, lambda: nc.tensor.transpose(out=ptr.ap[:, h * 128:(h + 1) * 128], in_=o16.ap[:, h * 128:(h + 1) * 128], identity=K["ident"].ap[:]),
                       reads=[o16, K["ident"]], writes=[ptr], pe_acc=True, sig=(h == 7))
            S.emit(S.act, lambda: nc.scalar.copy(out=oT.ap[:], in_=ptr.ap[:, 0:1024].rearrange("p (k t) -> p k t", k=8)), reads=[ptr], writes=[oT])
            for n2 in range(2):
                pq = pmm[nmm % 2]
                nmm += 1
                for kk in range(8):
                    S.emit(S.pe, lambda: nc.tensor.matmul(pq.ap[:], lhsT=oT.ap[:, kk, :], rhs=wo.ap[:, kk, n2 * 512:(n2 + 1) * 512], start=(kk == 0), stop=(kk == 7)),
                           reads=[oT, wo], writes=[pq], pe_acc=True, sig=(kk == 7))
                S.emit(S.dve, lambda: nc.vector.tensor_tensor(out=xt.ap[:, n2 * 512:(n2 + 1) * 512], in0=pq.ap[:], in1=xt.ap[:, n2 * 512:(n2 + 1) * 512], op=ALU.add), reads=[pq, xt], writes=[xt])
            S.dma(S.pool, dst[c * 128:(c + 1) * 128, :], xt.ap[:], reads=[xt], chan_buf=xt)
        S.barrier()

def build(phases=("m", "f0", "a", "f1"), debug=False):
    nc = bass.Bass("TRN2", target_bir_lowering=False)
    I = {}

    def inp(name, shape):
        I[name] = nc.dram_tensor(name, shape, F32, kind="ExternalInput").ap()

    inp("x", [S_LEN, D])
    inp("a_norm", [D]); inp("ssm_w_in", [D, IN_DIM]); inp("ssm_conv_w", [4, XBC]); inp("ssm_conv_b", [XBC])
    inp("ssm_dt_bias", [32]); inp("ssm_a_log", [32]); inp("ssm_d", [32]); inp("ssm_norm", [D_IN]); inp("ssm_w_out", [D_IN, D])
    inp("kv_norm", [D]); inp("w_kv", [D, 1536]); inp("b_norm", [D]); inp("att_w_q", [D, 3072]); inp("att_w_o", [D, D])
    inp("ffn_norm", [2, D]); inp("ffn_w_up", [2, D, 2 * FF]); inp("ffn_conv_w", [2, 3, 2 * FF]); inp("ffn_w_down", [2, FF, D])
    inp("final_norm", [D]); inp("rope", [S_LEN, 32])
    out = nc.dram_tensor("out", [S_LEN, D], F32, kind="ExternalOutput").ap()
    kind = "ExternalOutput" if debug else "Internal"
    xa = nc.dram_tensor("xa", [S_LEN, D], F32, kind=kind).ap()
    xb = nc.dram_tensor("xb", [S_LEN, D], F32, kind=kind).ap()
    xc = nc.dram_tensor("xc", [S_LEN, D], F32, kind=kind).ap()
    with contextlib.ExitStack() as es:
        C = Ctx(nc, es)
        K = setup_consts(C, es)
        C.S.barrier()
        cur = I["x"]
        if "m" in phases:
            phase_mamba(C, K, I, cur, xa)
            cur = xa
        if "f0" in phases:
            phase_ffn(C, K, 0, cur, xb, I["ffn_w_up"][0], I["ffn_conv_w"][0], I["ffn_w_down"][0], I["ffn_norm"][0])
            cur = xb
        if "a" in phases:
            phase_attn(C, K, I, cur, xc)
            cur = xc
        if "f1" in phases:
            phase_ffn(C, K, 1, cur, out, I["ffn_w_up"][1], I["ffn_conv_w"][1], I["ffn_w_down"][1], I["ffn_norm"][1], final_gain=I["final_norm"])
        C.S.barrier()
    return nc


def rope_table():
    half = 16
    inv_freq = np.power(np.float32(500000.0), -np.arange(0, 32, 2, dtype=np.float32) / np.float32(32)).astype(np.float32)
    ang = (np.arange(S_LEN, dtype=np.float32)[:, None] * inv_freq[None, :]).astype(np.float32)
    return np.concatenate([np.cos(ang.astype(np.float64)), np.sin(ang.astype(np.float64))], axis=1).astype(np.float32)


def make_in_maps(inputs, n_cores):
    f = lambda a: np.ascontiguousarray(np.asarray(a, dtype=np.float32))
    shared = {
        "a_norm": f(inputs["a_norm"][0]), "ssm_w_in": f(inputs["ssm_w_in"][0]), "ssm_conv_w": f(inputs["ssm_conv_w"][0]),
        "ssm_conv_b": f(inputs["ssm_conv_b"][0]), "ssm_dt_bias": f(inputs["ssm_dt_bias"][0]), "ssm_a_log": f(inputs["ssm_a_log"][0]),
        "ssm_d": f(inputs["ssm_d"][0]), "ssm_norm": f(inputs["ssm_norm"][0]), "ssm_w_out": f(inputs["ssm_w_out"][0]),
        "kv_norm": f(inputs["kv_norm"]), "w_kv": f(inputs["w_kv"]), "b_norm": f(inputs["b_norm"][0]),
        "att_w_q": f(inputs["att_w_q"][0]), "att_w_o": f(inputs["att_w_o"][0]), "ffn_norm": f(inputs["ffn_norm"]),
        "ffn_w_up": f(inputs["ffn_w_up"]), "ffn_conv_w": f(inputs["ffn_conv_w"]), "ffn_w_down": f(inputs["ffn_w_down"]),
        "final_norm": f(inputs["final_norm"]), "rope": rope_table(),
    }
    x = f(inputs["x"])
    maps = []
    for c in range(n_cores):
        m = dict(shared)
        m["x"] = x[c]
        maps.append(m)
    return maps


def kernel(**inputs):
    nc = build()
    maps = make_in_maps(inputs, 8)
    res = run_bass_kernel_spmd(nc, maps, core_ids=list(range(8)))
    return np.stack([np.asarray(r["out"]) for r in res.results], axis=0).astype(np.float32)
```

```python
import contextlib
import numpy as np
import concourse.bass as bass
import concourse.mybir as mybir
from concourse.bass_utils import run_bass_kernel_spmd

F32 = mybir.dt.float32
BF16 = mybir.dt.bfloat16
AF = mybir.ActivationFunctionType
ALU = mybir.AluOpType

S_LEN = 4096
D = 1024
NCH = S_LEN // 128
FF = 2816
NF = FF // 128
D_IN = 2048
XBC = 4096
IN_DIM = 6176
NEG = -30000.0


class Buf:
    __slots__ = ("ap", "lw", "rd", "name", "chan")

    def __init__(self, ap, name=""):
        self.ap = ap
        self.lw = None
        self.rd = {}
        self.name = name
        self.chan = None


class Eng:
    def __init__(self, name, q, sem):
        self.name = name
        self.q = q
        self.sem = sem
        self.count = 0
        self.seen = {}


class Sched:
    def __init__(self, nc, es):
        self.nc = nc
        self.es = es
        mk = lambda n, q: Eng(n, q, es.enter_context(nc.semaphore("sem_" + n)))
        self.pe = mk("pe", nc.tensor)
        self.act = mk("act", nc.scalar)
        self.dve = mk("dve", nc.vector)
        self.pool = mk("pool", nc.gpsimd)
        self.sp = mk("sp", nc.sync)
        self.engs = [self.pe, self.act, self.dve, self.pool, self.sp]
        self.chans = []
        self.free_chans = []

    def new_chan(self, name):
        if self.free_chans:
            c = self.free_chans.pop()
            return c
        c = Eng("ch%d_%s" % (len(self.chans), name), None, self.es.enter_context(self.nc.semaphore("semch%d" % len(self.chans))))
        self.chans.append(c)
        return c

    def _waits(self, eng, reads, writes, pe_acc):
        waits = {}

        def need(dep):
            e, c = dep
            if eng.seen.get(e.name, 0) < c:
                if waits.get(e.name, (None, 0))[1] < c:
                    waits[e.name] = (e, c)

        for b in reads:
            if b.lw is not None:
                need(b.lw)
        for b in writes:
            if b.lw is not None and not (pe_acc and b.lw[0] is eng):
                need(b.lw)
            for dep in b.rd.values():
                need(dep)
        for e, c in waits.values():
            eng.q.wait_ge(e.sem, c)
            eng.seen[e.name] = c

    def emit(self, eng, fn, reads=(), writes=(), pe_acc=False, sig=True):
        self._waits(eng, reads, writes, pe_acc)
        ins = fn()
        if sig:
            eng.count += 1
            ins.then_inc(eng.sem, 1)
            cnt = eng.count
        else:
            cnt = eng.count + 1
        for b in reads:
            b.rd[eng.name] = (eng, cnt)
        for b in writes:
            b.lw = (eng, cnt)
            b.rd = {}
        return ins

    def dma(self, eng, out_ap, in_ap, reads=(), writes=(), chan_buf=None):
        if chan_buf.chan is None:
            chan_buf.chan = self.new_chan(chan_buf.name)
        ch = chan_buf.chan
        self._waits(eng, reads, writes, False)
        ins = eng.q.dma_start(out=out_ap, in_=in_ap)
        ch.count += 16
        ins.then_inc(ch.sem, 16)
        for b in reads:
            b.rd[ch.name] = (ch, ch.count)
        for b in writes:
            b.lw = (ch, ch.count)
            b.rd = {}
        return ins

    def barrier(self):
        allsrc = self.engs + self.chans
        for e in self.engs:
            for o in allsrc:
                if o is e or o.count == 0:
                    continue
                if e.seen.get(o.name, 0) < o.count:
                    e.q.wait_ge(o.sem, o.count)
                    e.seen[o.name] = o.count


class Ctx:
    def __init__(self, nc, es):
        self.nc = nc
        self.S = Sched(nc, es)
        self.n = 0
        self.rr = 0

    def sb(self, es, name, shape, dt):
        self.n += 1
        t = es.enter_context(self.nc.sbuf_tensor("%s_%d" % (name, self.n), shape, dt))
        return Buf(t, name)

    def ps(self, es, name, shape, dt):
        self.n += 1
        t = es.enter_context(self.nc.psum_tensor("%s_%d" % (name, self.n), shape, dt))
        return Buf(t, name)


def bc(ap, shape):
    return ap.broadcast_to(shape)


def setup_consts(C, es):
    nc, S = C.nc, C.S
    k = {}
    identf = C.sb(es, "identf", [128, 128], F32)
    ident = C.sb(es, "ident", [128, 128], BF16)
    S.emit(S.pool, lambda: nc.gpsimd.memset(identf.ap[:], 1.0), writes=[identf])
    S.emit(S.pool, lambda: nc.gpsimd.affine_select(out=identf.ap[:], in_=identf.ap[:], pattern=[[-1, 128]], compare_op=ALU.is_equal, fill=0.0, base=0, channel_multiplier=1), reads=[identf], writes=[identf])
    S.emit(S.dve, lambda: nc.vector.tensor_copy(out=ident.ap[:], in_=identf.ap[:]), reads=[identf], writes=[ident])
    mhalf = C.sb(es, "mhalf", [128, 8], F32)
    S.emit(S.pool, lambda: nc.gpsimd.memset(mhalf.ap[:], -0.5), writes=[mhalf])
    nmf = C.sb(es, "nmf", [128, 128], F32)
    nmc = C.sb(es, "nmc", [128, 512], BF16)
    nmp = C.sb(es, "nmp", [128, 512], BF16)
    S.emit(S.pool, lambda: nc.gpsimd.memset(nmf.ap[:], 0.0), writes=[nmf])
    S.emit(S.pool, lambda: nc.gpsimd.affine_select(out=nmf.ap[:], in_=nmf.ap[:], pattern=[[1, 128]], compare_op=ALU.is_ge, fill=NEG, base=0, channel_multiplier=-1), reads=[nmf], writes=[nmf])
    S.emit(S.dve, lambda: nc.vector.tensor_copy(out=nmc.ap[:].rearrange("p (h l) -> p h l", h=4), in_=bc(nmf.ap[:].unsqueeze(1), [128, 4, 128])), reads=[nmf], writes=[nmc])
    S.emit(S.pool, lambda: nc.gpsimd.memset(nmf.ap[:], 0.0), reads=[nmf], writes=[nmf])
    S.emit(S.pool, lambda: nc.gpsimd.affine_select(out=nmf.ap[:], in_=nmf.ap[:], pattern=[[-1, 128]], compare_op=ALU.is_ge, fill=NEG, base=0, channel_multiplier=1), reads=[nmf], writes=[nmf])
    S.emit(S.dve, lambda: nc.vector.tensor_copy(out=nmp.ap[:].rearrange("p (h l) -> p h l", h=4), in_=bc(nmf.ap[:].unsqueeze(1), [128, 4, 128])), reads=[nmf], writes=[nmp])
    k.update(identf=identf, ident=ident, mhalf=mhalf, nmc=nmc, nmp=nmp)
    return k


def prep_weight(C, es_stage, dst, src, nk, ncols, gain=None, col0=0):
    nc, S = C.nc, C.S
    stg = C.stg
    CW = 2048
    for kk in range(nk):
        for c0 in range(0, ncols, CW):
            cw = min(CW, ncols - c0)
            st = stg[C.rr % len(stg)]
            S.dma(S.sp, st.ap[:, 0:cw], src[kk * 128:(kk + 1) * 128, c0:c0 + cw], writes=[st], chan_buf=st)
            o = dst.ap[:, kk, col0 + c0:col0 + c0 + cw]
            which = C.rr % 3
            C.rr += 1
            rds = [st] + ([gain] if gain is not None else [])
            if gain is None:
                if which == 0:
                    S.emit(S.act, lambda: nc.scalar.copy(out=o, in_=st.ap[:, 0:cw]), reads=rds, writes=[dst])
                elif which == 1:
                    S.emit(S.dve, lambda: nc.vector.tensor_copy(out=o, in_=st.ap[:, 0:cw]), reads=rds, writes=[dst])
                else:
                    S.emit(S.pool, lambda: nc.gpsimd.tensor_copy(out=o, in_=st.ap[:, 0:cw]), reads=rds, writes=[dst])
            else:
                g = gain.ap[:, kk:kk + 1]
                if which == 0:
                    S.emit(S.act, lambda: nc.scalar.mul(out=o, in_=st.ap[:, 0:cw], mul=g), reads=rds, writes=[dst])
                elif which == 1:
                    S.emit(S.dve, lambda: nc.vector.tensor_scalar(out=o, in0=st.ap[:, 0:cw], scalar1=g, scalar2=None, op0=ALU.mult), reads=rds, writes=[dst])
                else:
                    S.emit(S.pool, lambda: nc.gpsimd.tensor_scalar(out=o, in0=st.ap[:, 0:cw], scalar1=g, scalar2=None, op0=ALU.mult), reads=rds, writes=[dst])


def load_cols(C, dst, src_vec, nk):
    S = C.S
    with C.nc.allow_non_contiguous_dma(reason="small param vector"):
        S.dma(S.sp, dst.ap[:, 0:nk], src_vec.rearrange("(k p) -> p k", p=128), writes=[dst], chan_buf=dst)


def rms_rstd(C, x_ap, xbuf, junk, ss, rstd, K, n_feat, eps):
    nc, S = C.nc, C.S
    S.emit(S.act, lambda: nc.scalar.activation(out=junk.ap[:, 0:n_feat], in_=x_ap, func=AF.Square, accum_out=ss.ap[:, 0:1]), reads=[xbuf], writes=[junk, ss])
    S.emit(S.dve, lambda: nc.vector.tensor_scalar(out=ss.ap[:, 0:1], in0=ss.ap[:, 0:1], scalar1=1.0 / n_feat, scalar2=eps, op0=ALU.mult, op1=ALU.add), reads=[ss], writes=[ss])
    S.emit(S.pool, lambda: nc.gpsimd.tensor_tensor(out=rstd.ap[:, 0:1], in0=ss.ap[:, 0:1], in1=K["mhalf"].ap[:, 0:1], op=ALU.pow), reads=[ss, K["mhalf"]], writes=[rstd])


def norm_transpose(C, K, x_ap, xbuf, junk, ss, rstd, xn, ptr, xnT_ap, xnT):
    nc, S = C.nc, C.S
    norm_only(C, K, x_ap, xbuf, junk, ss, rstd, xn)
    transpose_only(C, K, xn, ptr, xnT_ap, xnT)


def norm_only(C, K, x_ap, xbuf, junk, ss, rstd, xn):
    nc, S = C.nc, C.S
    rms_rstd(C, x_ap, xbuf, junk, ss, rstd, K, D, 1e-6)
    S.emit(S.act, lambda: nc.scalar.mul(out=xn.ap[:, 0:D], in_=x_ap, mul=rstd.ap[:, 0:1]), reads=[xbuf, rstd], writes=[xn])


def transpose_only(C, K, xn, ptr, xnT_ap, xnT):
    nc, S = C.nc, C.S
    for kk in range(8):
        S.emit(S.pe, lambda: nc.tensor.transpose(out=ptr.ap[:, kk * 128:(kk + 1) * 128], in_=xn.ap[:, kk * 128:(kk + 1) * 128], identity=K["ident"].ap[:]),
               reads=[xn, K["ident"]], writes=[ptr], pe_acc=True, sig=(kk == 7))
    S.emit(S.dve, lambda: nc.vector.tensor_copy(out=xnT_ap, in_=ptr.ap[:, 0:1024].rearrange("p (k t) -> p k t", k=8)), reads=[ptr], writes=[xnT])


def phase_ffn(C, K, layer, src, dst, w_up, conv_w, w_dn, gain_vec, final_gain=None):
    nc, S = C.nc, C.S
    T = 256
    NT = S_LEN // T
    with contextlib.ExitStack() as es:
        wup = C.sb(es, "wup", [128, 8, 2 * FF], BF16)
        wdn = C.sb(es, "wdn", [128, NF, D], BF16)
        gain = C.sb(es, "gain", [128, 8], F32)
        cw = C.sb(es, "cw", [128, 2 * NF, 3], F32)
        load_cols(C, gain, gain_vec, 8)
        with nc.allow_non_contiguous_dma(reason="conv weights"):
            for kk in range(3):
                S.dma(S.sp, cw.ap[:, :, kk], conv_w[kk].rearrange("(j p) -> p j", p=128), writes=[cw], chan_buf=cw)
        with contextlib.ExitStack() as es2:
            C.stg = [C.sb(es2, "stg", [128, 2048], F32) for _ in range(3)]
            prep_weight(C, es2, wup, w_up, 8, 2 * FF, gain)
            prep_weight(C, es2, wdn, w_dn, NF, D, None)
            S.barrier()
        xts = [C.sb(es, "xt", [128, 2, D], F32) for _ in range(2)]
        xn = C.sb(es, "xn", [128, D], BF16)
        xnTs = [C.sb(es, "xnT", [128, 8, T + 2], BF16) for _ in range(2)]
        ss = C.sb(es, "ss", [128, 1], F32)
        rstd = C.sb(es, "rstd", [128, 1], F32)
        ags = [C.sb(es, "ag", [128, T], F32) for _ in range(3)]
        avs = [C.sb(es, "av", [128, T], F32) for _ in range(3)]
        sgs = [C.sb(es, "sg", [128, T], F32) for _ in range(3)]
        h2_t = es.enter_context(nc.sbuf_tensor("h2T_all%d" % layer, [128, NF, T], BF16))
        h2T = [Buf(h2_t[:, j, :], "h2T%d" % j) for j in range(NF)]
        if final_gain is not None:
            gfin = C.sb(es, "gfin", [128, D], F32)
            obuf = [C.sb(es, "obuf", [128, D], F32) for _ in range(2)]
            S.dma(S.sp, gfin.ap[:], final_gain.partition_broadcast(128), writes=[gfin], chan_buf=gfin)
        S.emit(S.pool, lambda: nc.gpsimd.memset(xnTs[0].ap[:, :, 0:2], 0.0), writes=[xnTs[0]])
        ptrs = [C.ps(es, "ptr", [128, 1024], BF16) for _ in range(2)]
        pus = [C.ps(es, "pu", [128, 512], F32) for _ in range(4)]
        pds = [C.ps(es, "pd", [128, 512], F32) for _ in range(2)]

        def load(i):
            xt = xts[i % 2]
            S.dma(S.sp, xt.ap[:], src[i * T:(i + 1) * T, :].rearrange("(c p) d -> p c d", p=128), writes=[xt], chan_buf=xt)

        xn2 = [xn, C.sb(es, "xnb", [128, D], BF16)]
        ss2 = [ss, C.sb(es, "ssb", [128, 1], F32)]
        rstd2 = [rstd, C.sb(es, "rstdb", [128, 1], F32)]

        def head_norm(i):
            xt = xts[i % 2]
            for c in range(2):
                norm_only(C, K, xt.ap[:, c, :], xt, xn2[c], ss2[c], rstd2[c], xn2[c])

        def head_tr(i):
            xnT = xnTs[i % 2]
            if i > 0:
                S.emit(S.pool, lambda: nc.gpsimd.tensor_copy(out=xnT.ap[:, :, 0:2], in_=xnTs[(i - 1) % 2].ap[:, :, T:T + 2]), reads=[xnTs[(i - 1) % 2]], writes=[xnT])
            for c in range(2):
                transpose_only(C, K, xn2[c], ptrs[c], xnT.ap[:, :, 2 + c * 128:2 + (c + 1) * 128], xnT)

        def head(i):
            head_norm(i)
            head_tr(i)

        load(0)
        head(0)
        npu = 0
        npd = 0
        nb = 0
        for i in range(NT):
            xt = xts[i % 2]
            xnT = xnTs[i % 2]
            if i + 1 < NT:
                load(i + 1)
            for j in range(NF):
                pg, pv = pus[npu % 4], pus[(npu + 1) % 4]
                npu += 2
                ag, av, sgb = ags[nb % 3], avs[nb % 3], sgs[nb % 3]
                nb += 1
                for (pp, col) in ((pg, j), (pv, NF + j)):
                    for kk in range(8):
                        S.emit(S.pe, lambda: nc.tensor.matmul(pp.ap[:, 0:T + 2], lhsT=wup.ap[:, kk, col * 128:(col + 1) * 128], rhs=xnT.ap[:, kk, :], start=(kk == 0), stop=(kk == 7)),
                               reads=[wup, xnT], writes=[pp], pe_acc=True, sig=(kk == 7))
                S.emit(S.act, lambda: nc.scalar.mul(out=ag.ap[:], in_=pg.ap[:, 2:T + 2], mul=cw.ap[:, j, 2:3]), reads=[pg, cw], writes=[ag])
                S.emit(S.act, lambda: nc.scalar.mul(out=av.ap[:], in_=pv.ap[:, 2:T + 2], mul=cw.ap[:, NF + j, 2:3]), reads=[pv, cw], writes=[av])
                for tap in (1, 0):
                    S.emit(S.dve, lambda: nc.vector.scalar_tensor_tensor(out=ag.ap[:], in0=pg.ap[:, tap:tap + T], scalar=cw.ap[:, j, tap:tap + 1], in1=ag.ap[:], op0=ALU.mult, op1=ALU.add), reads=[pg, cw, ag], writes=[ag])
                for tap in (1, 0):
                    S.emit(S.dve, lambda: nc.vector.scalar_tensor_tensor(out=av.ap[:], in0=pv.ap[:, tap:tap + T], scalar=cw.ap[:, NF + j, tap:tap + 1], in1=av.ap[:], op0=ALU.mult, op1=ALU.add), reads=[pv, cw, av], writes=[av])
                S.emit(S.act, lambda: nc.scalar.activation(out=sgb.ap[:], in_=ag.ap[:], func=AF.Silu), reads=[ag], writes=[sgb])
                S.emit(S.pool, lambda: nc.gpsimd.tensor_tensor(out=h2T[j].ap[:], in0=sgb.ap[:], in1=av.ap[:], op=ALU.mult), reads=[sgb, av], writes=[h2T[j]])
                if j == 3 and i + 1 < NT:
                    head_norm(i + 1)
                if j == 12 and i + 1 < NT:
                    head_tr(i + 1)
            for c in range(2):
                for n in range(2):
                    pd = pds[npd % 2]
                    npd += 1
                    for j in range(NF):
                        S.emit(S.pe, lambda: nc.tensor.matmul(pd.ap[:], lhsT=h2T[j].ap[:, c * 128:(c + 1) * 128], rhs=wdn.ap[:, j, n * 512:(n + 1) * 512], start=(j == 0), stop=(j == NF - 1)),
                               reads=[h2T[j], wdn], writes=[pd], pe_acc=True, sig=(j == NF - 1))
                    S.emit(S.dve, lambda: nc.vector.tensor_tensor(out=xt.ap[:, c, n * 512:(n + 1) * 512], in0=pd.ap[:], in1=xt.ap[:, c, n * 512:(n + 1) * 512], op=ALU.add), reads=[pd, xt], writes=[xt])
                if final_gain is not None:
                    ob = obuf[c]
                    rms_rstd(C, xt.ap[:, c, :], xt, xn, ss, rstd, K, D, 1e-6)
                    S.emit(S.dve, lambda: nc.vector.scalar_tensor_tensor(out=ob.ap[:], in0=xt.ap[:, c, :], scalar=rstd.ap[:, 0:1], in1=gfin.ap[:], op0=ALU.mult, op1=ALU.mult), reads=[xt, rstd, gfin], writes=[ob])
                    S.dma(S.pool, dst[i * T + c * 128:i * T + (c + 1) * 128, :], ob.ap[:], reads=[ob], chan_buf=ob)
            if final_gain is None:
                S.dma(S.pool, dst[i * T:(i + 1) * T, :].rearrange("(c p) d -> p c d", p=128), xt.ap[:], reads=[xt], chan_buf=xt)
        S.barrier()


def phase_mamba(C, K, I, src, dst):
    nc, S = C.nc, C.S
    ynd = nc.dram_tensor("ynd", [S_LEN, D_IN], BF16, kind="Internal").ap()
    ident, identf = K["ident"], K["identf"]
    T = 256
    NT = S_LEN // T
    with contextlib.ExitStack() as es:
        win = C.sb(es, "win", [128, 8, IN_DIM], BF16)
        gain = C.sb(es, "gain", [128, 8], F32)
        cw = C.sb(es, "cw", [128, 32, 4], F32)
        cb = C.sb(es, "cb", [128, 32], F32)
        dcol = C.sb(es, "dcol", [128, 16], F32)
        diagD = C.sb(es, "diagD", [128, 16, 128], BF16)
        dtb = C.sb(es, "dtb", [32, 1], F32)
        acol = C.sb(es, "acol", [32, 1], F32)
        ones32 = C.sb(es, "ones32", [32, 128], F32)
        load_cols(C, gain, I["a_norm"], 8)
        load_cols(C, cb, I["ssm_conv_b"], 32)
        with nc.allow_non_contiguous_dma(reason="small params"):
            for kk in range(4):
                S.dma(S.sp, cw.ap[:, :, kk], I["ssm_conv_w"][kk].rearrange("(j p) -> p j", p=128), writes=[cw], chan_buf=cw)
            S.dma(S.sp, dtb.ap[:, 0:1], I["ssm_dt_bias"].rearrange("(h o) -> h o", o=1), writes=[dtb], chan_buf=dtb)
            S.dma(S.sp, acol.ap[:, 0:1], I["ssm_a_log"].rearrange("(h o) -> h o", o=1), writes=[acol], chan_buf=acol)
            dv = I["ssm_d"].rearrange("(k two) -> two k", two=2)
            for hh in range(2):
                S.dma(S.sp, dcol.ap[hh * 64:(hh + 1) * 64, :], dv[hh].partition_broadcast(64), writes=[dcol], chan_buf=dcol)
        S.emit(S.act, lambda: nc.scalar.activation(out=acol.ap[:], in_=acol.ap[:], func=AF.Exp), reads=[acol], writes=[acol])
        S.emit(S.dve, lambda: nc.vector.tensor_scalar(out=acol.ap[:], in0=acol.ap[:], scalar1=-1.0, scalar2=None, op0=ALU.mult), reads=[acol], writes=[acol])
        S.emit(S.pool, lambda: nc.gpsimd.memset(ones32.ap[:], 1.0), writes=[ones32])
        for kk in range(16):
            S.emit(S.dve, lambda: nc.vector.tensor_scalar(out=diagD.ap[:, kk, :], in0=identf.ap[:], scalar1=dcol.ap[:, kk:kk + 1], scalar2=None, op0=ALU.mult), reads=[identf, dcol], writes=[diagD])
        with contextlib.ExitStack() as es2:
            C.stg = [C.sb(es2, "stg", [128, 2048], F32) for _ in range(3)]
            prep_weight(C, es2, win, I["ssm_w_in"], 8, IN_DIM, gain)
            S.barrier()
        xts = [C.sb(es, "xt", [128, 2, D], F32) for _ in range(2)]
        xn = C.sb(es, "xn", [128, D], BF16)
        xnT2s = [C.sb(es, "xnT2", [128, 8, T + 4], BF16) for _ in range(2)]
        ss = C.sb(es, "ss", [128, 1], F32)
        rstd = C.sb(es, "rstd", [128, 1], F32)
        szs = [C.sb(es, "sz", [128, 512], F32) for _ in range(2)]
        szeas = [C.sb(es, "szea", [128, 512], F32) for _ in range(2)]
        xbcT_t = es.enter_context(nc.sbuf_tensor("xbcT_all", [128, 32, T], BF16))
        xbcT = [Buf(xbcT_t[:, j, :], "xbcT%d" % j) for j in range(32)]
        accs = [C.sb(es, "acc", [128, T], F32) for _ in range(3)]
        xc_t = es.enter_context(nc.sbuf_tensor("xc_all", [128, D_IN], BF16))
        xcd_t = es.enter_context(nc.sbuf_tensor("xcd_all", [128, D_IN], BF16))
        xc = [Buf(xc_t[:, h * 1024:(h + 1) * 1024], "xc%d" % h) for h in range(2)]
        xcdec = [Buf(xcd_t[:, h * 1024:(h + 1) * 1024], "xcd%d" % h) for h in range(2)]
        Btok = C.sb(es, "Btok", [128, 1024], BF16)
        ADs = [C.sb(es, "AD", [32, 512], F32) for _ in range(1)]
        Egs = [C.sb(es, "Eg", [128, 512], F32) for _ in range(2)]
        MTs = [C.sb(es, "MT", [128, 512], BF16) for _ in range(2)]
        prevT_t = es.enter_context(nc.sbuf_tensor("prevT_all", [128, D_IN], F32))
        prevbf_t = es.enter_context(nc.sbuf_tensor("prevbf_all", [128, D_IN], BF16))
        prevT = [Buf(prevT_t[:, g * 256:(g + 1) * 256], "prevT%d" % g) for g in range(8)]
        prevbf = [Buf(prevbf_t[:, g * 256:(g + 1) * 256], "prevbf%d" % g) for g in range(8)]
        yts = [C.sb(es, "yt", [128, 256], F32) for _ in range(2)]
        yus = [C.sb(es, "yu", [128, 256], F32) for _ in range(2)]
        yzs = [C.sb(es, "yz", [128, 256], F32) for _ in range(2)]
        junkg = C.sb(es, "junkg", [128, 256], BF16)
        ssgs = [C.sb(es, "ssg", [128, 1], F32) for _ in range(2)]
        rgs = [C.sb(es, "rg", [128, 1], F32) for _ in range(2)]
        yns = [C.sb(es, "yn", [128, D_IN], BF16) for _ in range(1)]
        sm = {n: C.sb(es, n, [32, T], F32) for n in ("dtT", "adtT", "acsT", "nacsT", "decT")}
        sm["e1"] = sm["dtT"]
        sm["dtdecT"] = sm["decT"]
        ecol = C.sb(es, "ecol", [32, 2], F32)
        dgs = [C.sb(es, "dg", [32, 32], F32) for _ in range(2)]
        toks = [C.sb(es, "tok", [128, 160], F32) for _ in range(2)]
        for g in range(8):
            S.emit(S.pool, lambda: nc.gpsimd.memset(prevT[g].ap[:], 0.0), writes=[prevT[g]])
            S.emit(S.pool, lambda: nc.gpsimd.memset(prevbf[g].ap[:], 0.0), writes=[prevbf[g]])
        S.emit(S.pool, lambda: nc.gpsimd.memset(xnT2s[0].ap[:, :, 0:4], 0.0), writes=[xnT2s[0]])
        ptrs = [C.ps(es, "ptr", [128, 1024], BF16) for _ in range(2)]
        pgen = [C.ps(es, "pgen", [128, 512], F32) for _ in range(3)]
        pys = [C.ps(es, "py", [128, 512], F32) for _ in range(2)]
        psm = C.ps(es, "psm", [128, 512], F32)
        cnt = {"g": 0, "tr": 0, "acc": 0}

        def nxt(lst, key):
            b = lst[cnt[key] % len(lst)]
            cnt[key] += 1
            return b

        def load(i):
            xt = xts[i % 2]
            S.dma(S.sp, xt.ap[:], src[i * T:(i + 1) * T, :].rearrange("(c p) d -> p c d", p=128), writes=[xt], chan_buf=xt)

        e1, dtT, adtT, acsT, nacsT, decT, dtdecT = [sm[n] for n in ("e1", "dtT", "adtT", "acsT", "nacsT", "decT", "dtdecT")]
        load(0)
        for i in range(NT):
            xt = xts[i % 2]
            xnT2 = xnT2s[i % 2]
            if i + 1 < NT:
                load(i + 1)
            for c in range(2):
                norm_transpose(C, K, xt.ap[:, c, :], xt, xn, ss, rstd, xn, nxt(ptrs, "tr"), xnT2.ap[:, :, 4 + c * 128:4 + (c + 1) * 128], xnT2)
            S.emit(S.pool, lambda: nc.gpsimd.tensor_copy(out=xnT2s[(i + 1) % 2].ap[:, :, 0:4], in_=xnT2.ap[:, :, T:T + 4]), reads=[xnT2], writes=[xnT2s[(i + 1) % 2]])
            pdt = nxt(pgen, "g")
            for kk in range(8):
                S.emit(S.pe, lambda: nc.tensor.matmul(pdt.ap[0:32, 0:T], lhsT=win.ap[:, kk, 6144:6176], rhs=xnT2.ap[:, kk, 4:T + 4], start=(kk == 0), stop=(kk == 7)),
                       reads=[win, xnT2], writes=[pdt], pe_acc=True, sig=(kk == 7))
            S.emit(S.act, lambda: nc.scalar.activation(out=e1.ap[:], in_=pdt.ap[0:32, 0:T], func=AF.Exp, bias=dtb.ap[:, 0:1]), reads=[pdt, dtb], writes=[e1])
            S.emit(S.act, lambda: nc.scalar.activation(out=dtT.ap[:], in_=e1.ap[:], func=AF.Ln, bias=1.0), reads=[e1], writes=[dtT])
            S.emit(S.dve, lambda: nc.vector.tensor_scalar(out=adtT.ap[:], in0=dtT.ap[:], scalar1=acol.ap[:, 0:1], scalar2=None, op0=ALU.mult), reads=[dtT, acol], writes=[adtT])
            for c in range(2):
                cs = slice(c * 128, (c + 1) * 128)
                S.emit(S.dve, lambda: nc.vector.tensor_tensor_scan(out=acsT.ap[:, cs], data0=ones32.ap[:], data1=adtT.ap[:, cs], initial=0.0, op0=ALU.mult, op1=ALU.add), reads=[ones32, adtT], writes=[acsT])
            S.emit(S.act, lambda: nc.scalar.mul(out=nacsT.ap[:], in_=acsT.ap[:], mul=-1.0), reads=[acsT], writes=[nacsT])
            for c in range(2):
                cs = slice(c * 128, (c + 1) * 128)
                last = acsT.ap[:, c * 128 + 127:c * 128 + 128]
                S.emit(S.act, lambda: nc.scalar.activation(out=decT.ap[:, cs], in_=acsT.ap[:, cs], func=AF.Exp, scale=-1.0, bias=last), reads=[acsT], writes=[decT])
                S.emit(S.act, lambda: nc.scalar.activation(out=ecol.ap[:, c:c + 1], in_=last, func=AF.Exp), reads=[acsT], writes=[ecol])
                S.emit(S.dve, lambda: nc.vector.tensor_scalar(out=dgs[c].ap[:], in0=identf.ap[0:32, 0:32], scalar1=ecol.ap[:, c:c + 1], scalar2=None, op0=ALU.mult), reads=[identf, ecol], writes=[dgs[c]])
            S.emit(S.dve, lambda: nc.vector.tensor_tensor(out=dtdecT.ap[:], in0=dtT.ap[:], in1=decT.ap[:], op=ALU.mult), reads=[dtT, decT], writes=[dtdecT])
            for c in range(2):
                cs = slice(c * 128, (c + 1) * 128)
                tok = toks[c]
                ptok = nxt(pgen, "g")
                for i4, (lh, rh) in enumerate(((dtT, None), (dtdecT, None), (nacsT, None), (ones32, dgs[c]))):
                    rhs_ap = identf.ap[0:32, 0:32] if rh is None else rh.ap[:]
                    lh_ap = lh.ap[:, cs] if rh is None else lh.ap[:]
                    S.emit(S.pe, lambda: nc.tensor.matmul(ptok.ap[:, i4 * 32:(i4 + 1) * 32], lhsT=lh_ap, rhs=rhs_ap, start=True, stop=True),
                           reads=[lh, identf] + ([rh] if rh is not None else []), writes=[ptok], pe_acc=True, sig=(i4 == 3))
                S.emit(S.dve, lambda: nc.vector.tensor_copy(out=tok.ap[:, 0:128], in_=ptok.ap[:, 0:128]), reads=[ptok], writes=[tok])
                S.emit(S.act, lambda: nc.scalar.activation(out=tok.ap[:, 128:160], in_=ptok.ap[:, 64:96], func=AF.Exp, scale=-1.0), reads=[ptok], writes=[tok])
            for j in range(32):
                pj = nxt(pgen, "g")
                acc = nxt(accs, "acc")
                for kk in range(8):
                    S.emit(S.pe, lambda: nc.tensor.matmul(pj.ap[:, 0:T + 4], lhsT=win.ap[:, kk, 2048 + j * 128:2048 + (j + 1) * 128], rhs=xnT2.ap[:, kk, :], start=(kk == 0), stop=(kk == 7)),
                           reads=[win, xnT2], writes=[pj], pe_acc=True, sig=(kk == 7))
                S.emit(S.act, lambda: nc.scalar.activation(out=acc.ap[:], in_=pj.ap[:, 4:T + 4], func=AF.Identity, scale=cw.ap[:, j, 3:4], bias=cb.ap[:, j:j + 1]), reads=[pj, cw, cb], writes=[acc])
                for tap in (2, 1, 0):
                    S.emit(S.dve, lambda: nc.vector.scalar_tensor_tensor(out=acc.ap[:], in0=pj.ap[:, 1 + tap:1 + tap + T], scalar=cw.ap[:, j, tap:tap + 1], in1=acc.ap[:], op0=ALU.mult, op1=ALU.add), reads=[pj, cw, acc], writes=[acc])
                S.emit(S.act, lambda: nc.scalar.activation(out=xbcT[j].ap[:], in_=acc.ap[:], func=AF.Silu), reads=[acc], writes=[xbcT[j]])
            for c in range(2):
                cs = slice(c * 128, (c + 1) * 128)
                tok = toks[c]
                yn = yns[0]
                for half in range(2):
                    ptr = nxt(ptrs, "tr")
                    for jj in range(8):
                        j = 8 * half + jj
                        S.emit(S.pe, lambda: nc.tensor.transpose(out=ptr.ap[:, jj * 128:(jj + 1) * 128], in_=xbcT[j].ap[:, cs], identity=ident.ap[:]),
                               reads=[xbcT[j], ident], writes=[ptr], pe_acc=True, sig=(jj == 7))
                    pv = ptr.ap[:, 0:1024].rearrange("p (h e) -> p h e", h=16)
                    S.emit(S.dve, lambda: nc.vector.tensor_tensor(out=xc[half].ap[:].rearrange("p (h e) -> p h e", h=16), in0=pv, in1=bc(tok.ap[:, 16 * half:16 * half + 16].unsqueeze(2), [128, 16, 64]), op=ALU.mult), reads=[ptr, tok], writes=[xc[half]])
                    S.emit(S.dve, lambda: nc.vector.tensor_tensor(out=xcdec[half].ap[:].rearrange("p (h e) -> p h e", h=16), in0=pv, in1=bc(tok.ap[:, 32 + 16 * half:32 + 16 * half + 16].unsqueeze(2), [128, 16, 64]), op=ALU.mult), reads=[ptr, tok], writes=[xcdec[half]])
                ptr = nxt(ptrs, "tr")
                for jj in range(8):
                    S.emit(S.pe, lambda: nc.tensor.transpose(out=ptr.ap[:, jj * 128:(jj + 1) * 128], in_=xbcT[16 + jj].ap[:, cs], identity=ident.ap[:]),
                           reads=[xbcT[16 + jj], ident], writes=[ptr], pe_acc=True, sig=(jj == 7))
                S.emit(S.act, lambda: nc.scalar.copy(out=Btok.ap[:], in_=ptr.ap[:, 0:1024]), reads=[ptr], writes=[Btok])

                def front(g):
                    sz, szea = szs[(g // 2) % 2], szeas[(g // 2) % 2]
                    if g % 2 == 0:
                        pz = nxt(pgen, "g")
                        n = g // 2
                        for kk in range(8):
                            S.emit(S.pe, lambda: nc.tensor.matmul(pz.ap[:], lhsT=xnT2.ap[:, kk, 4 + c * 128:4 + (c + 1) * 128], rhs=win.ap[:, kk, n * 512:(n + 1) * 512], start=(kk == 0), stop=(kk == 7)),
                                   reads=[xnT2, win], writes=[pz], pe_acc=True, sig=(kk == 7))
                        S.emit(S.act, lambda: nc.scalar.activation(out=sz.ap[:], in_=pz.ap[:], func=AF.Silu), reads=[pz], writes=[sz])
                        S.emit(S.pool, lambda: nc.gpsimd.tensor_tensor(out=szea.ap[:].rearrange("p (h e) -> p h e", h=8), in0=sz.ap[:].rearrange("p (h e) -> p h e", h=8), in1=bc(tok.ap[:, 128 + 8 * n:128 + 8 * n + 8].unsqueeze(2), [128, 8, 64]), op=ALU.mult), reads=[sz, tok], writes=[szea])
                    AD, Eg, MT = ADs[0], Egs[g % 2], MTs[g % 2]
                    pcb_ap = psm.ap[:, (g % 2) * 128:(g % 2 + 1) * 128]
                    S.emit(S.pe, lambda: nc.tensor.matmul(pcb_ap, lhsT=xbcT[16 + g].ap[:, cs], rhs=xbcT[24 + g].ap[:, cs], start=True, stop=True), reads=[xbcT[16 + g], xbcT[24 + g]], writes=[psm], pe_acc=True)
                    S.emit(S.pool, lambda: nc.gpsimd.affine_select(out=AD.ap[:].rearrange("p (h l) -> p h l", h=4), in_=bc(acsT.ap[:, cs].unsqueeze(1), [32, 4, 128]), pattern=[[-1, 4], [0, 128]], compare_op=ALU.is_equal, fill=0.0, base=-4 * g, channel_multiplier=1), reads=[acsT], writes=[AD])
                    psg = nxt(pgen, "g")
                    S.emit(S.pe, lambda: nc.tensor.matmul(psg.ap[:], lhsT=ones32.ap[:], rhs=AD.ap[:], start=True, stop=False), reads=[ones32, AD], writes=[psg], pe_acc=True, sig=False)
                    S.emit(S.pe, lambda: nc.tensor.matmul(psg.ap[:], lhsT=ident.ap[:], rhs=K["nmc"].ap[:], start=False, stop=True), reads=[ident, K["nmc"]], writes=[psg], pe_acc=True)
                    for h in range(4):
                        S.emit(S.act, lambda: nc.scalar.activation(out=Eg.ap[:, h * 128:(h + 1) * 128], in_=psg.ap[:, h * 128:(h + 1) * 128], func=AF.Exp, bias=tok.ap[:, 64 + 4 * g + h:64 + 4 * g + h + 1]), reads=[psg, tok], writes=[Eg])
                    S.emit(S.dve, lambda: nc.vector.tensor_tensor(out=MT.ap[:].rearrange("p (h l) -> p h l", h=4), in0=Eg.ap[:].rearrange("p (h l) -> p h l", h=4), in1=bc(pcb_ap.unsqueeze(1), [128, 4, 128]), op=ALU.mult), reads=[Eg, psm], writes=[MT])

                def back(g):
                    sz, szea = szs[(g // 2) % 2], szeas[(g // 2) % 2]
                    MT = MTs[g % 2]
                    py = pys[g % 2]
                    pyr = py.ap[:, 0:256]
                    pyo = py.ap[:, 256:512]
                    xcg = xc[g // 4]
                    for i2 in range(2):
                        S.emit(S.pe, lambda: nc.tensor.matmul(pyr[:, i2 * 128:(i2 + 1) * 128], lhsT=xbcT[2 * g + i2].ap[:, cs], rhs=diagD.ap[:, 2 * g + i2, :], start=True, stop=False),
                               reads=[xbcT[2 * g + i2], diagD], writes=[py], pe_acc=True, sig=False)
                        for hh in (2 * i2, 2 * i2 + 1):
                            hd = 4 * (g % 4) + hh
                            S.emit(S.pe, lambda: nc.tensor.matmul(pyr[:, hh * 64:(hh + 1) * 64], lhsT=MT.ap[:, hh * 128:(hh + 1) * 128], rhs=xcg.ap[:, hd * 64:(hd + 1) * 64], start=False, stop=(hh % 2 == 1)),
                                   reads=[MT, xcg], writes=[py], pe_acc=True, sig=False)
                    S.emit(S.pe, lambda: nc.tensor.matmul(pyo, lhsT=xbcT[24 + g].ap[:, cs], rhs=prevbf[g].ap[:], start=True, stop=True), reads=[xbcT[24 + g], prevbf[g]], writes=[py], pe_acc=True)
                    pst_ap = psm.ap[:, 256:512]
                    S.emit(S.pe, lambda: nc.tensor.matmul(pst_ap, lhsT=Btok.ap[:, g * 128:(g + 1) * 128], rhs=xcdec[g // 4].ap[:, (g % 4) * 256:(g % 4 + 1) * 256], start=True, stop=True), reads=[Btok, xcdec[g // 4]], writes=[psm], pe_acc=True)
                    yt, yu, yz, ssg, rg = yts[g % 2], yus[g % 2], yzs[g % 2], ssgs[g % 2], rgs[g % 2]
                    gs = slice((g % 2) * 256, (g % 2 + 1) * 256)
                    S.emit(S.dve, lambda: nc.vector.tensor_tensor(out=yt.ap[:], in0=pyo, in1=szea.ap[:, gs], op=ALU.mult), reads=[py, szea], writes=[yt])
                    S.emit(S.dve, lambda: nc.vector.tensor_tensor(out=yu.ap[:], in0=pyr, in1=sz.ap[:, gs], op=ALU.mult), reads=[py, sz], writes=[yu])
                    S.emit(S.pool, lambda: nc.gpsimd.tensor_tensor(out=yz.ap[:], in0=yt.ap[:], in1=yu.ap[:], op=ALU.add), reads=[yt, yu], writes=[yz])
                    S.emit(S.act, lambda: nc.scalar.activation(out=junkg.ap[:], in_=yz.ap[:], func=AF.Square, accum_out=ssg.ap[:, 0:1]), reads=[yz], writes=[junkg, ssg])
                    S.emit(S.dve, lambda: nc.vector.tensor_scalar(out=ssg.ap[:, 0:1], in0=ssg.ap[:, 0:1], scalar1=1.0 / 256.0, scalar2=1e-5, op0=ALU.mult, op1=ALU.add), reads=[ssg], writes=[ssg])
                    S.emit(S.pool, lambda: nc.gpsimd.tensor_tensor(out=rg.ap[:, 0:1], in0=ssg.ap[:, 0:1], in1=K["mhalf"].ap[:, 0:1], op=ALU.pow), reads=[ssg, K["mhalf"]], writes=[rg])
                    S.emit(S.act, lambda: nc.scalar.mul(out=yn.ap[:, g * 256:(g + 1) * 256], in_=yz.ap[:], mul=rg.ap[:, 0:1]), reads=[yz, rg], writes=[yn])
                    pvw = prevT[g].ap[:]
                    S.emit(S.pool, lambda: nc.gpsimd.tensor_tensor(out=pvw.rearrange("p (h e) -> p h e", h=4), in0=pvw.rearrange("p (h e) -> p h e", h=4), in1=bc(tok.ap[:, 96 + 4 * g:96 + 4 * g + 4].unsqueeze(2), [128, 4, 64]), op=ALU.mult), reads=[prevT[g], tok], writes=[prevT[g]])
                    S.emit(S.dve, lambda: nc.vector.tensor_tensor(out=pvw, in0=pst_ap, in1=pvw, op=ALU.add), reads=[psm, prevT[g]], writes=[prevT[g]])
                    S.emit(S.act, lambda: nc.scalar.copy(out=prevbf[g].ap[:], in_=pvw), reads=[prevT[g]], writes=[prevbf[g]])

                front(0)
                for g in range(8):
                    if g + 1 < 8:
                        front(g + 1)
                    back(g)
                S.dma(S.pool, ynd[i * T + c * 128:i * T + (c + 1) * 128, :], yn.ap[:], reads=[yn], chan_buf=yn)
        S.barrier()
    with contextlib.ExitStack() as es:
        wout = C.sb(es, "wout", [128, 16, D], BF16)
        gout = C.sb(es, "gout", [128, 16], F32)
        load_cols(C, gout, I["ssm_norm"], 16)
        with contextlib.ExitStack() as es2:
            C.stg = [C.sb(es2, "stg", [128, 2048], F32) for _ in range(3)]
            prep_weight(C, es2, wout, I["ssm_w_out"], 16, D, gout)
            S.barrier()
        xts = [C.sb(es, "xt", [128, D], F32) for _ in range(2)]
        yls = [C.sb(es, "yl", [128, D_IN], BF16) for _ in range(2)]
        ynT = C.sb(es, "ynT", [128, 16, 128], BF16)
        ptrs = [C.ps(es, "ptr", [128, 1024], BF16) for _ in range(2)]
        pmm = [C.ps(es, "pmm", [128, 512], F32) for _ in range(2)]

        def load2(c):
            S.dma(S.sp, xts[c % 2].ap[:], src[c * 128:(c + 1) * 128, :], writes=[xts[c % 2]], chan_buf=xts[c % 2])
            S.dma(S.sp, yls[c % 2].ap[:], ynd[c * 128:(c + 1) * 128, :], writes=[yls[c % 2]], chan_buf=yls[c % 2])

        load2(0)
        nmm = 0
        for c in range(NCH):
            if c + 1 < NCH:
                load2(c + 1)
            xt, yl = xts[c % 2], yls[c % 2]
            for half in range(2):
                ptr = ptrs[half]
                for jj in range(8):
                    j = 8 * half + jj
                    S.emit(S.pe, lambda: nc.tensor.transpose(out=ptr.ap[:, jj * 128:(jj + 1) * 128], in_=yl.ap[:, j * 128:(j + 1) * 128], identity=ident.ap[:]),
                           reads=[yl, ident], writes=[ptr], pe_acc=True, sig=(jj == 7))
                if half == 0:
                    S.emit(S.act, lambda: nc.scalar.copy(out=ynT.ap[:, 0:8, :], in_=ptr.ap[:, 0:1024].rearrange("p (k t) -> p k t", k=8)), reads=[ptr], writes=[ynT])
                else:
                    S.emit(S.dve, lambda: nc.vector.tensor_copy(out=ynT.ap[:, 8:16, :], in_=ptr.ap[:, 0:1024].rearrange("p (k t) -> p k t", k=8)), reads=[ptr], writes=[ynT])
            for n2 in range(2):
                po = pmm[nmm % 2]
                nmm += 1
                for kk in range(16):
                    S.emit(S.pe, lambda: nc.tensor.matmul(po.ap[:], lhsT=ynT.ap[:, kk, :], rhs=wout.ap[:, kk, n2 * 512:(n2 + 1) * 512], start=(kk == 0), stop=(kk == 15)),
                           reads=[ynT, wout], writes=[po], pe_acc=True, sig=(kk == 15))
                S.emit(S.dve, lambda: nc.vector.tensor_tensor(out=xt.ap[:, n2 * 512:(n2 + 1) * 512], in0=po.ap[:], in1=xt.ap[:, n2 * 512:(n2 + 1) * 512], op=ALU.add), reads=[po, xt], writes=[xt])
            S.dma(S.pool, dst[c * 128:(c + 1) * 128, :], xt.ap[:], reads=[xt], chan_buf=xt)
        S.barrier()

ATT_PATTERNS = ((128, 1), (512, 4), (2048, 16))


def rotary(C, src, dst, rp, tmps, nh):
    nc, S = C.nc, C.S
    sv = src.ap[:, 0:nh * 128].rearrange("p (h e) -> p h e", h=nh)
    dv = dst.ap[:, 0:nh * 128].rearrange("p (h e) -> p h e", h=nh)
    cos = bc(rp.ap[:, 0:16].unsqueeze(1), [128, nh, 16])
    sin = bc(rp.ap[:, 16:32].unsqueeze(1), [128, nh, 16])
    t1, t2, t3, t4 = [t.ap[:, 0:nh * 16].rearrange("p (h e) -> p h e", h=nh) for t in tmps]
    x1, x2 = sv[:, :, 0:16], sv[:, :, 16:32]
    S.emit(S.dve, lambda: nc.vector.tensor_tensor(out=t1, in0=x1, in1=cos, op=ALU.mult), reads=[src, rp], writes=[tmps[0]])
    S.emit(S.pool, lambda: nc.gpsimd.tensor_tensor(out=t2, in0=x2, in1=sin, op=ALU.mult), reads=[src, rp], writes=[tmps[1]])
    S.emit(S.dve, lambda: nc.vector.tensor_tensor(out=t3, in0=x2, in1=cos, op=ALU.mult), reads=[src, rp], writes=[tmps[2]])
    S.emit(S.pool, lambda: nc.gpsimd.tensor_tensor(out=t4, in0=x1, in1=sin, op=ALU.mult), reads=[src, rp], writes=[tmps[3]])
    S.emit(S.dve, lambda: nc.vector.tensor_tensor(out=dv[:, :, 0:16], in0=t1, in1=t2, op=ALU.subtract), reads=[tmps[0], tmps[1]], writes=[dst])
    S.emit(S.pool, lambda: nc.gpsimd.tensor_tensor(out=dv[:, :, 16:32], in0=t3, in1=t4, op=ALU.add), reads=[tmps[2], tmps[3]], writes=[dst])
    S.emit(S.act, lambda: nc.scalar.copy(out=dv[:, :, 32:128], in_=sv[:, :, 32:128]), reads=[src], writes=[dst])


def phase_attn(C, K, I, src, dst):
    nc, S = C.nc, C.S
    nd = [nc.dram_tensor("numden%d" % g, [S_LEN, 8 * 129], F32, kind="Internal").ap() for g in range(3)]
    rope = I["rope"]
    with contextlib.ExitStack() as es:
        wkv = C.sb(es, "wkv", [128, 8, 1536], BF16)
        wq = C.sb(es, "wq", [128, 8, 3072], BF16)
        wo = C.sb(es, "wo", [128, 8, D], BF16)
        gkv = C.sb(es, "gkv", [128, 8], F32)
        gq = C.sb(es, "gq", [128, 8], F32)
        load_cols(C, gkv, I["kv_norm"], 8)
        load_cols(C, gq, I["b_norm"], 8)
        with contextlib.ExitStack() as es2:
            C.stg = [C.sb(es2, "stg", [128, 2048], F32) for _ in range(3)]
            prep_weight(C, es2, wkv, I["w_kv"], 8, 1536, gkv)
            prep_weight(C, es2, wq, I["att_w_q"], 8, 3072, gq)
            prep_weight(C, es2, wo, I["att_w_o"], 8, D, None)
            S.barrier()
        xts = [C.sb(es, "xt", [128, D], F32) for _ in range(2)]
        rps = [C.sb(es, "rp", [128, 32], F32) for _ in range(2)]
        xn = C.sb(es, "xn", [128, D], BF16)
        junk = C.sb(es, "junk", [128, D], BF16)
        xnT = C.sb(es, "xnT", [128, 8, 128], BF16)
        ss = C.sb(es, "ss", [128, 1], F32)
        rstd = C.sb(es, "rstd", [128, 1], F32)
        qf = C.sb(es, "qf", [128, 1024], F32)
        qb = C.sb(es, "qb", [128, 1024], BF16)
        kf = C.sb(es, "kf", [128, 256], F32)
        kb = C.sb(es, "kb", [128, 256], BF16)
        tq = [C.sb(es, "tq", [128, 128], F32) for _ in range(4)]
        tk = [C.sb(es, "tk", [128, 32], F32) for _ in range(4)]
        QT = C.sb(es, "QT", [128, 8, 128], BF16)
        KTs = [C.sb(es, "KT", [128, 2, 128], BF16) for _ in range(2)]
        vaugs = [C.sb(es, "vaug", [128, 2, 129], BF16) for _ in range(2)]
        PTs = [C.sb(es, "PT", [128, 512], BF16) for _ in range(4)]
        obs = [C.sb(es, "ob", [128, 8, 129], F32) for _ in range(2)]
        for v in vaugs:
            S.emit(S.pool, lambda: nc.gpsimd.memset(v.ap[:], 1.0), writes=[v])
        ptr = C.ps(es, "ptr", [128, 1024], BF16)
        pmm = [C.ps(es, "pmm", [128, 512], F32) for _ in range(2)]
        pS = [C.ps(es, "pS", [128, 512], F32) for _ in range(2)]
        po = C.ps(es, "po", [128, 3 * 512], F32)
        scale = 1.0 / np.sqrt(128.0)

        blocks = []
        for g, (win, dil) in enumerate(ATT_PATTERNS):
            for r in range(dil):
                for n in range(S_LEN // dil // 128):
                    blocks.append((g, dil, r, n))

        def rows(ap, dil, r, n):
            return ap.rearrange("(m dd) c -> dd m c", dd=dil)[r][n * 128:(n + 1) * 128, :]

        def load(bi):
            g, dil, r, n = blocks[bi]
            S.dma(S.sp, xts[bi % 2].ap[:], rows(src, dil, r, n), writes=[xts[bi % 2]], chan_buf=xts[bi % 2])
            S.dma(S.sp, rps[bi % 2].ap[:], rows(rope, dil, r, n), writes=[rps[bi % 2]], chan_buf=rps[bi % 2])

        load(0)
        nmm = 0
        nS = 0
        for bi, (g, dil, r, n) in enumerate(blocks):
            xt, rp = xts[bi % 2], rps[bi % 2]
            if bi + 1 < len(blocks):
                load(bi + 1)
            norm_transpose(C, K, xt.ap[:], xt, junk, ss, rstd, xn, ptr, xnT.ap[:], xnT)
            for n2 in range(2):
                pq = pmm[nmm % 2]
                nmm += 1
                for kk in range(8):
                    S.emit(S.pe, lambda: nc.tensor.matmul(pq.ap[:], lhsT=xnT.ap[:, kk, :], rhs=wq.ap[:, kk, g * 1024 + n2 * 512:g * 1024 + (n2 + 1) * 512], start=(kk == 0), stop=(kk == 7)),
                           reads=[xnT, wq], writes=[pq], pe_acc=True, sig=(kk == 7))
                S.emit(S.act, lambda: nc.scalar.copy(out=qf.ap[:, n2 * 512:(n2 + 1) * 512], in_=pq.ap[:]), reads=[pq], writes=[qf])
            pkv = pmm[nmm % 2]
            nmm += 1
            for half, c0 in ((0, g * 256), (1, 768 + g * 256)):
                for kk in range(8):
                    S.emit(S.pe, lambda: nc.tensor.matmul(pkv.ap[:, half * 256:(half + 1) * 256], lhsT=xnT.ap[:, kk, :], rhs=wkv.ap[:, kk, c0:c0 + 256], start=(kk == 0), stop=(kk == 7)),
                           reads=[xnT, wkv], writes=[pkv], pe_acc=True, sig=(kk == 7 and half == 1))
            KT, vaug = KTs[n % 2], vaugs[n % 2]
            S.emit(S.act, lambda: nc.scalar.copy(out=kf.ap[:], in_=pkv.ap[:, 0:256]), reads=[pkv], writes=[kf])
            S.emit(S.act, lambda: nc.scalar.copy(out=vaug.ap[:, :, 0:128], in_=pkv.ap[:, 256:512].rearrange("p (h e) -> p h e", h=2)), reads=[pkv], writes=[vaug])
            rotary(C, qf, qb, rp, tq, 8)
            rotary(C, kf, kb, rp, tk, 2)
            for h in range(8):
                S.emit(S.pe, lambda: nc.tensor.transpose(out=ptr.ap[:, h * 128:(h + 1) * 128], in_=qb.ap[:, h * 128:(h + 1) * 128], identity=K["ident"].ap[:]),
                       reads=[qb, K["ident"]], writes=[ptr], pe_acc=True, sig=(h == 7))
            S.emit(S.dve, lambda: nc.vector.tensor_copy(out=QT.ap[:], in_=ptr.ap[:, 0:1024].rearrange("p (k t) -> p k t", k=8)), reads=[ptr], writes=[QT])
            for h in range(2):
                S.emit(S.pe, lambda: nc.tensor.transpose(out=ptr.ap[:, h * 128:(h + 1) * 128], in_=kb.ap[:, h * 128:(h + 1) * 128], identity=K["ident"].ap[:]),
                       reads=[kb, K["ident"]], writes=[ptr], pe_acc=True, sig=(h == 1))
            S.emit(S.dve, lambda: nc.vector.tensor_copy(out=KT.ap[:], in_=ptr.ap[:, 0:256].rearrange("p (k t) -> p k t", k=2)), reads=[ptr], writes=[KT])
            kblocks = []
            if n > 0:
                kblocks.append((KTs[(n - 1) % 2], vaugs[(n - 1) % 2], K["nmp"]))
            kblocks.append((KT, vaug, K["nmc"]))
            ob = obs[bi % 2]
            for jk in range(2):
                pts = []
                for (kt, va, nm) in kblocks:
                    ps_ = pS[nS % 2]
                    pt_ = PTs[nS % 4]
                    nS += 1
                    S.emit(S.pe, lambda: nc.tensor.matmul(ps_.ap[:], lhsT=kt.ap[:, jk, :], rhs=QT.ap[:, 4 * jk:4 * jk + 4, :], start=True, stop=False),
                           reads=[kt, QT], writes=[ps_], pe_acc=True, sig=False)
                    S.emit(S.pe, lambda: nc.tensor.matmul(ps_.ap[:], lhsT=K["ident"].ap[:], rhs=nm.ap[:], start=False, stop=True),
                           reads=[K["ident"], nm], writes=[ps_], pe_acc=True)
                    S.emit(S.act, lambda: nc.scalar.activation(out=pt_.ap[:], in_=ps_.ap[:], func=AF.Exp, scale=float(scale)), reads=[ps_], writes=[pt_])
                    pts.append((pt_, va))
                for hl in range(4):
                    h = 4 * jk + hl
                    oslice = po.ap[:, (h // 3) * 512 + (h % 3) * 129:(h // 3) * 512 + (h % 3) * 129 + 129]
                    for i, (pt_, va) in enumerate(pts):
                        S.emit(S.pe, lambda: nc.tensor.matmul(oslice, lhsT=pt_.ap[:, hl * 128:(hl + 1) * 128], rhs=va.ap[:, jk, :], start=(i == 0), stop=(i == len(pts) - 1)),
                               reads=[pt_, va], writes=[po], pe_acc=True, sig=(i == len(pts) - 1 and hl == 3))
            for b3 in range(3):
                nh = 3 if b3 < 2 else 2
                eng, q = (S.act, None) if b3 != 1 else (S.dve, None)
                o_ap = ob.ap[:, b3 * 3:b3 * 3 + nh, :]
                i_ap = po.ap[:, b3 * 512:b3 * 512 + nh * 129].rearrange("p (h e) -> p h e", h=nh)
                if b3 != 1:
                    S.emit(S.act, lambda: nc.scalar.copy(out=o_ap, in_=i_ap), reads=[po], writes=[ob])
                else:
                    S.emit(S.dve, lambda: nc.vector.tensor_copy(out=o_ap, in_=i_ap), reads=[po], writes=[ob])
            S.dma(S.pool, rows(nd[g], dil, r, n), ob.ap[:].rearrange("p h e -> p (h e)"), reads=[ob], chan_buf=ob)
        S.barrier()
        nds = [[C.sb(es, "ndl", [128, 8, 129], F32) for _ in range(3)] for _ in range(2)]
        rden = C.sb(es, "rden", [128, 8], F32)
        o16 = C.sb(es, "o16", [128, 1024], BF16)
        oT = C.sb(es, "oT", [128, 8, 128], BF16)

        def load2(c):
            S.dma(S.sp, xts[c % 2].ap[:], src[c * 128:(c + 1) * 128, :], writes=[xts[c % 2]], chan_buf=xts[c % 2])
            for g in range(3):
                b = nds[c % 2][g]
                S.dma(S.sp, b.ap[:].rearrange("p h e -> p (h e)"), nd[g][c * 128:(c + 1) * 128, :], writes=[b], chan_buf=b)

        load2(0)
        for c in range(NCH):
            if c + 1 < NCH:
                load2(c + 1)
            xt = xts[c % 2]
            a0, a1, a2 = nds[c % 2]
            S.emit(S.pool, lambda: nc.gpsimd.tensor_tensor(out=a0.ap[:], in0=a0.ap[:], in1=a1.ap[:], op=ALU.add), reads=[a0, a1], writes=[a0])
            S.emit(S.pool, lambda: nc.gpsimd.tensor_tensor(out=a0.ap[:], in0=a0.ap[:], in1=a2.ap[:], op=ALU.add), reads=[a0, a2], writes=[a0])
            S.emit(S.dve, lambda: nc.vector.reciprocal(out=rden.ap[:].unsqueeze(2), in_=a0.ap[:, :, 128:129]), reads=[a0], writes=[rden])
            S.emit(S.dve, lambda: nc.vector.tensor_tensor(out=o16.ap[:].rearrange("p (h e) -> p h e", h=8), in0=a0.ap[:, :, 0:128], in1=bc(rden.ap[:].unsqueeze(2), [128, 8, 128]), op=ALU.mult), reads=[a0, rden], writes=[o16])
            for h in range(8):
                S.emit(S.pe, lambda: nc.tensor.transpose(out=ptr.ap[:, h * 128:(h + 1) * 128], in_=o16.ap[:, h * 128:(h + 1) * 128], identity=K["ident"].ap[:]),
                       reads=[o16, K["ident"]], writes=[ptr], pe_acc=True, sig=(h == 7))
            S.emit(S.act, lambda: nc.scalar.copy(out=oT.ap[:], in_=ptr.ap[:, 0:1024].rearrange("p (k t) -> p k t", k=8)), reads=[ptr], writes=[oT])
            for n2 in range(2):
                pq = pmm[nmm % 2]
                nmm += 1
                for kk in range(8):
                    S.emit(S.pe, lambda: nc.tensor.matmul(pq.ap[:], lhsT=oT.ap[:, kk, :], rhs=wo.ap[:, kk, n2 * 512:(n2 + 1) * 512], start=(kk == 0), stop=(kk == 7)),
                           reads=[oT, wo], writes=[pq], pe_acc=True, sig=(kk == 7))
                S.emit(S.dve, lambda: nc.vector.tensor_tensor(out=xt.ap[:, n2 * 512:(n2 + 1) * 512], in0=pq.ap[:], in1=xt.ap[:, n2 * 512:(n2 + 1) * 512], op=ALU.add), reads=[pq, xt], writes=[xt])
            S.dma(S.pool, dst[c * 128:(c + 1) * 128, :], xt.ap[:], reads=[xt], chan_buf=xt)
        S.barrier()

def build(phases=("m", "f0", "a", "f1"), debug=False):
    nc = bass.Bass("TRN2", target_bir_lowering=False)
    I = {}

    def inp(name, shape):
        I[name] = nc.dram_tensor(name, shape, F32, kind="ExternalInput").ap()

    inp("x", [S_LEN, D])
    inp("a_norm", [D]); inp("ssm_w_in", [D, IN_DIM]); inp("ssm_conv_w", [4, XBC]); inp("ssm_conv_b", [XBC])
    inp("ssm_dt_bias", [32]); inp("ssm_a_log", [32]); inp("ssm_d", [32]); inp("ssm_norm", [D_IN]); inp("ssm_w_out", [D_IN, D])
    inp("kv_norm", [D]); inp("w_kv", [D, 1536]); inp("b_norm", [D]); inp("att_w_q", [D, 3072]); inp("att_w_o", [D, D])
    inp("ffn_norm", [2, D]); inp("ffn_w_up", [2, D, 2 * FF]); inp("ffn_conv_w", [2, 3, 2 * FF]); inp("ffn_w_down", [2, FF, D])
    inp("final_norm", [D]); inp("rope", [S_LEN, 32])
    out = nc.dram_tensor("out", [S_LEN, D], F32, kind="ExternalOutput").ap()
    kind = "ExternalOutput" if debug else "Internal"
    xa = nc.dram_tensor("xa", [S_LEN, D], F32, kind=kind).ap()
    xb = nc.dram_tensor("xb", [S_LEN, D], F32, kind=kind).ap()
    xc = nc.dram_tensor("xc", [S_LEN, D], F32, kind=kind).ap()
    with contextlib.ExitStack() as es:
        C = Ctx(nc, es)
        K = setup_consts(C, es)
        C.S.barrier()
        cur = I["x"]
        if "m" in phases:
            phase_mamba(C, K, I, cur, xa)
            cur = xa
        if "f0" in phases:
            phase_ffn(C, K, 0, cur, xb, I["ffn_w_up"][0], I["ffn_conv_w"][0], I["ffn_w_down"][0], I["ffn_norm"][0])
            cur = xb
        if "a" in phases:
            phase_attn(C, K, I, cur, xc)
            cur = xc
        if "f1" in phases:
            phase_ffn(C, K, 1, cur, out, I["ffn_w_up"][1], I["ffn_conv_w"][1], I["ffn_w_down"][1], I["ffn_norm"][1], final_gain=I["final_norm"])
        C.S.barrier()
    return nc


def rope_table():
    half = 16
    inv_freq = np.power(np.float32(500000.0), -np.arange(0, 32, 2, dtype=np.float32) / np.float32(32)).astype(np.float32)
    ang = (np.arange(S_LEN, dtype=np.float32)[:, None] * inv_freq[None, :]).astype(np.float32)
    return np.concatenate([np.cos(ang.astype(np.float64)), np.sin(ang.astype(np.float64))], axis=1).astype(np.float32)


def make_in_maps(inputs, n_cores):
    f = lambda a: np.ascontiguousarray(np.asarray(a, dtype=np.float32))
    shared = {
        "a_norm": f(inputs["a_norm"][0]), "ssm_w_in": f(inputs["ssm_w_in"][0]), "ssm_conv_w": f(inputs["ssm_conv_w"][0]),
        "ssm_conv_b": f(inputs["ssm_conv_b"][0]), "ssm_dt_bias": f(inputs["ssm_dt_bias"][0]), "ssm_a_log": f(inputs["ssm_a_log"][0]),
        "ssm_d": f(inputs["ssm_d"][0]), "ssm_norm": f(inputs["ssm_norm"][0]), "ssm_w_out": f(inputs["ssm_w_out"][0]),
        "kv_norm": f(inputs["kv_norm"]), "w_kv": f(inputs["w_kv"]), "b_norm": f(inputs["b_norm"][0]),
        "att_w_q": f(inputs["att_w_q"][0]), "att_w_o": f(inputs["att_w_o"][0]), "ffn_norm": f(inputs["ffn_norm"]),
        "ffn_w_up": f(inputs["ffn_w_up"]), "ffn_conv_w": f(inputs["ffn_conv_w"]), "ffn_w_down": f(inputs["ffn_w_down"]),
        "final_norm": f(inputs["final_norm"]), "rope": rope_table(),
    }
    x = f(inputs["x"])
    maps = []
    for c in range(n_cores):
        m = dict(shared)
        m["x"] = x[c]
        maps.append(m)
    return maps


def kernel(**inputs):
    nc = build()
    maps = make_in_maps(inputs, 8)
    res = run_bass_kernel_spmd(nc, maps, core_ids=list(range(8)))
    return np.stack([np.asarray(r["out"]) for r in res.results], axis=0).astype(np.float32)
```

```python
import contextlib
import numpy as np
import concourse.bass as bass
import concourse.mybir as mybir
from concourse.bass_utils import run_bass_kernel_spmd

F32 = mybir.dt.float32
BF16 = mybir.dt.bfloat16
AF = mybir.ActivationFunctionType
ALU = mybir.AluOpType

S_LEN = 4096
D = 1024
NCH = S_LEN // 128
FF = 2816
NF = FF // 128
D_IN = 2048
XBC = 4096
IN_DIM = 6176
NEG = -30000.0


class Buf:
    __slots__ = ("ap", "lw", "rd", "name", "chan", "chan_sw")

    def __init__(self, ap, name=""):
        self.ap = ap
        self.lw = None
        self.rd = {}
        self.name = name
        self.chan = None
        self.chan_sw = None


class Eng:
    def __init__(self, name, q, sem):
        self.name = name
        self.q = q
        self.sem = sem
        self.count = 0
        self.seen = {}


class Sched:
    def __init__(self, nc, es):
        self.nc = nc
        self.es = es
        mk = lambda n, q: Eng(n, q, es.enter_context(nc.semaphore("sem_" + n)))
        self.pe = mk("pe", nc.tensor)
        self.act = mk("act", nc.scalar)
        self.dve = mk("dve", nc.vector)
        self.pool = mk("pool", nc.gpsimd)
        self.sp = mk("sp", nc.sync)
        self.engs = [self.pe, self.act, self.dve, self.pool, self.sp]
        self.chans = []
        self.free_chans = []

    def new_chan(self, name):
        if self.free_chans:
            c = self.free_chans.pop()
            return c
        c = Eng("ch%d_%s" % (len(self.chans), name), None, self.es.enter_context(self.nc.semaphore("semch%d" % len(self.chans))))
        self.chans.append(c)
        return c

    def _waits(self, eng, reads, writes, pe_acc):
        waits = {}

        def need(dep):
            e, c = dep
            if eng.seen.get(e.name, 0) < c:
                if waits.get(e.name, (None, 0))[1] < c:
                    waits[e.name] = (e, c)

        for b in reads:
            if b.lw is not None:
                need(b.lw)
        for b in writes:
            if b.lw is not None and not (pe_acc and b.lw[0] is eng):
                need(b.lw)
            for dep in b.rd.values():
                need(dep)
        for e, c in waits.values():
            eng.q.wait_ge(e.sem, c)
            eng.seen[e.name] = c

    def emit(self, eng, fn, reads=(), writes=(), pe_acc=False, sig=True):
        self._waits(eng, reads, writes, pe_acc)
        ins = fn()
        if sig:
            eng.count += 1
            ins.then_inc(eng.sem, 1)
            cnt = eng.count
        else:
            cnt = eng.count + 1
        for b in reads:
            b.rd[eng.name] = (eng, cnt)
        for b in writes:
            b.lw = (eng, cnt)
            b.rd = {}
        return ins

    def dma(self, eng, out_ap, in_ap, reads=(), writes=(), chan_buf=None):
        if eng is self.pool:
            if chan_buf.chan_sw is None:
                chan_buf.chan_sw = self.new_chan(chan_buf.name + "_sw")
            ch = chan_buf.chan_sw
        else:
            if chan_buf.chan is None:
                chan_buf.chan = self.new_chan(chan_buf.name)
            ch = chan_buf.chan
        self._waits(eng, reads, writes, False)
        ins = eng.q.dma_start(out=out_ap, in_=in_ap)
        ch.count += 16
        ins.then_inc(ch.sem, 16)
        for b in reads:
            b.rd[ch.name] = (ch, ch.count)
        for b in writes:
            b.lw = (ch, ch.count)
            b.rd = {}
        return ins

    def barrier(self):
        allsrc = self.engs + self.chans
        for e in self.engs:
            for o in allsrc:
                if o is e or o.count == 0:
                    continue
                if e.seen.get(o.name, 0) < o.count:
                    e.q.wait_ge(o.sem, o.count)
                    e.seen[o.name] = o.count


class Ctx:
    def __init__(self, nc, es):
        self.nc = nc
        self.S = Sched(nc, es)
        self.n = 0
        self.rr = 0

    def sb(self, es, name, shape, dt):
        self.n += 1
        t = es.enter_context(self.nc.sbuf_tensor("%s_%d" % (name, self.n), shape, dt))
        return Buf(t, name)

    def ps(self, es, name, shape, dt):
        self.n += 1
        t = es.enter_context(self.nc.psum_tensor("%s_%d" % (name, self.n), shape, dt))
        return Buf(t, name)


def bc(ap, shape):
    return ap.broadcast_to(shape)


def setup_consts(C, es):
    nc, S = C.nc, C.S
    k = {}
    identf = C.sb(es, "identf", [128, 128], F32)
    ident = C.sb(es, "ident", [128, 128], BF16)
    S.emit(S.pool, lambda: nc.gpsimd.memset(identf.ap[:], 1.0), writes=[identf])
    S.emit(S.pool, lambda: nc.gpsimd.affine_select(out=identf.ap[:], in_=identf.ap[:], pattern=[[-1, 128]], compare_op=ALU.is_equal, fill=0.0, base=0, channel_multiplier=1), reads=[identf], writes=[identf])
    S.emit(S.dve, lambda: nc.vector.tensor_copy(out=ident.ap[:], in_=identf.ap[:]), reads=[identf], writes=[ident])
    mhalf = C.sb(es, "mhalf", [128, 8], F32)
    S.emit(S.pool, lambda: nc.gpsimd.memset(mhalf.ap[:], -0.5), writes=[mhalf])
    nmf = C.sb(es, "nmf", [128, 128], F32)
    nmc = C.sb(es, "nmc", [128, 512], BF16)
    nmp = C.sb(es, "nmp", [128, 512], BF16)
    S.emit(S.pool, lambda: nc.gpsimd.memset(nmf.ap[:], 0.0), writes=[nmf])
    S.emit(S.pool, lambda: nc.gpsimd.affine_select(out=nmf.ap[:], in_=nmf.ap[:], pattern=[[1, 128]], compare_op=ALU.is_ge, fill=NEG, base=0, channel_multiplier=-1), reads=[nmf], writes=[nmf])
    S.emit(S.dve, lambda: nc.vector.tensor_copy(out=nmc.ap[:].rearrange("p (h l) -> p h l", h=4), in_=bc(nmf.ap[:].unsqueeze(1), [128, 4, 128])), reads=[nmf], writes=[nmc])
    S.emit(S.pool, lambda: nc.gpsimd.memset(nmf.ap[:], 0.0), reads=[nmf], writes=[nmf])
    S.emit(S.pool, lambda: nc.gpsimd.affine_select(out=nmf.ap[:], in_=nmf.ap[:], pattern=[[-1, 128]], compare_op=ALU.is_ge, fill=NEG, base=0, channel_multiplier=1), reads=[nmf], writes=[nmf])
    S.emit(S.dve, lambda: nc.vector.tensor_copy(out=nmp.ap[:].rearrange("p (h l) -> p h l", h=4), in_=bc(nmf.ap[:].unsqueeze(1), [128, 4, 128])), reads=[nmf], writes=[nmp])
    k.update(identf=identf, ident=ident, mhalf=mhalf, nmc=nmc, nmp=nmp)
    return k


def prep_weight(C, es_stage, dst, src, nk, ncols, gain=None, col0=0):
    nc, S = C.nc, C.S
    stg = C.stg
    CW = 2048
    for kk in range(nk):
        for c0 in range(0, ncols, CW):
            cw = min(CW, ncols - c0)
            st = stg[C.rr % len(stg)]
            S.dma(S.sp, st.ap[:, 0:cw], src[kk * 128:(kk + 1) * 128, c0:c0 + cw], writes=[st], chan_buf=st)
            o = dst.ap[:, kk, col0 + c0:col0 + c0 + cw]
            which = C.rr % 3
            C.rr += 1
            rds = [st] + ([gain] if gain is not None else [])
            if gain is None:
                if which == 0:
                    S.emit(S.act, lambda: nc.scalar.copy(out=o, in_=st.ap[:, 0:cw]), reads=rds, writes=[dst])
                elif which == 1:
                    S.emit(S.dve, lambda: nc.vector.tensor_copy(out=o, in_=st.ap[:, 0:cw]), reads=rds, writes=[dst])
                else:
                    S.emit(S.pool, lambda: nc.gpsimd.tensor_copy(out=o, in_=st.ap[:, 0:cw]), reads=rds, writes=[dst])
            else:
                g = gain.ap[:, kk:kk + 1]
                if which == 0:
                    S.emit(S.act, lambda: nc.scalar.mul(out=o, in_=st.ap[:, 0:cw], mul=g), reads=rds, writes=[dst])
                elif which == 1:
                    S.emit(S.dve, lambda: nc.vector.tensor_scalar(out=o, in0=st.ap[:, 0:cw], scalar1=g, scalar2=None, op0=ALU.mult), reads=rds, writes=[dst])
                else:
                    S.emit(S.pool, lambda: nc.gpsimd.tensor_scalar(out=o, in0=st.ap[:, 0:cw], scalar1=g, scalar2=None, op0=ALU.mult), reads=rds, writes=[dst])


def load_cols(C, dst, src_vec, nk):
    S = C.S
    with C.nc.allow_non_contiguous_dma(reason="small param vector"):
        S.dma(S.sp, dst.ap[:, 0:nk], src_vec.rearrange("(k p) -> p k", p=128), writes=[dst], chan_buf=dst)


def rms_rstd(C, x_ap, xbuf, junk, ss, rstd, K, n_feat, eps):
    nc, S = C.nc, C.S
    S.emit(S.act, lambda: nc.scalar.activation(out=junk.ap[:, 0:n_feat], in_=x_ap, func=AF.Square, accum_out=ss.ap[:, 0:1]), reads=[xbuf], writes=[junk, ss])
    S.emit(S.dve, lambda: nc.vector.tensor_scalar(out=ss.ap[:, 0:1], in0=ss.ap[:, 0:1], scalar1=1.0 / n_feat, scalar2=eps, op0=ALU.mult, op1=ALU.add), reads=[ss], writes=[ss])
    S.emit(S.pool, lambda: nc.gpsimd.tensor_tensor(out=rstd.ap[:, 0:1], in0=ss.ap[:, 0:1], in1=K["mhalf"].ap[:, 0:1], op=ALU.pow), reads=[ss, K["mhalf"]], writes=[rstd])


def norm_transpose(C, K, x_ap, xbuf, junk, ss, rstd, xn, ptr, xnT_ap, xnT, extra=()):
    nc, S = C.nc, C.S
    norm_only(C, K, x_ap, xbuf, junk, ss, rstd, xn)
    transpose_only(C, K, xn, ptr, xnT_ap, xnT, extra)


def norm_only(C, K, x_ap, xbuf, junk, ss, rstd, xn):
    nc, S = C.nc, C.S
    rms_rstd(C, x_ap, xbuf, junk, ss, rstd, K, D, 1e-6)
    S.emit(S.act, lambda: nc.scalar.mul(out=xn.ap[:, 0:D], in_=x_ap, mul=rstd.ap[:, 0:1]), reads=[xbuf, rstd], writes=[xn])


def transpose_only(C, K, xn, ptr, xnT_ap, xnT, extra=()):
    nc, S = C.nc, C.S
    for kk in range(8):
        S.emit(S.pe, lambda: nc.tensor.transpose(out=ptr.ap[:, kk * 128:(kk + 1) * 128], in_=xn.ap[:, kk * 128:(kk + 1) * 128], identity=K["ident"].ap[:]),
               reads=[xn, K["ident"]], writes=[ptr] + list(extra), pe_acc=True, sig=(kk == 7))
    S.emit(S.dve, lambda: nc.vector.tensor_copy(out=xnT_ap, in_=ptr.ap[:, 0:1024].rearrange("p (k t) -> p k t", k=8)), reads=[ptr], writes=[xnT])


def phase_ffn(C, K, layer, src, dst, w_up, conv_w, w_dn, gain_vec, final_gain=None):
    nc, S = C.nc, C.S
    T = 256
    NT = S_LEN // T
    with contextlib.ExitStack() as es:
        wup = C.sb(es, "wup", [128, 8, 2 * FF], BF16)
        wdn = C.sb(es, "wdn", [128, NF, D], BF16)
        gain = C.sb(es, "gain", [128, 8], F32)
        cw = C.sb(es, "cw", [128, 2 * NF, 3], F32)
        load_cols(C, gain, gain_vec, 8)
        with nc.allow_non_contiguous_dma(reason="conv weights"):
            for kk in range(3):
                S.dma(S.sp, cw.ap[:, :, kk], conv_w[kk].rearrange("(j p) -> p j", p=128), writes=[cw], chan_buf=cw)
        with contextlib.ExitStack() as es2:
            C.stg = [C.sb(es2, "stg", [128, 2048], F32) for _ in range(3)]
            prep_weight(C, es2, wup, w_up, 8, 2 * FF, gain)
            prep_weight(C, es2, wdn, w_dn, NF, D, None)
            S.barrier()
        xts = [C.sb(es, "xt", [128, 2, D], F32) for _ in range(2)]
        xn = C.sb(es, "xn", [128, D], BF16)
        xnTs = [C.sb(es, "xnT", [128, 8, T + 2], BF16) for _ in range(2)]
        ss = C.sb(es, "ss", [128, 1], F32)
        rstd = C.sb(es, "rstd", [128, 1], F32)
        ags = [C.sb(es, "ag", [128, T], F32) for _ in range(3)]
        avs = [C.sb(es, "av", [128, T], F32) for _ in range(3)]
        sgs = [C.sb(es, "sg", [128, T], F32) for _ in range(3)]
        h2_t = es.enter_context(nc.sbuf_tensor("h2T_all%d" % layer, [128, NF, T], BF16))
        h2T = [Buf(h2_t[:, j, :], "h2T%d" % j) for j in range(NF)]
        if final_gain is not None:
            gfin = C.sb(es, "gfin", [128, D], F32)
            obuf = [C.sb(es, "obuf", [128, D], F32) for _ in range(2)]
            S.dma(S.sp, gfin.ap[:], final_gain.partition_broadcast(128), writes=[gfin], chan_buf=gfin)
        S.emit(S.pool, lambda: nc.gpsimd.memset(xnTs[0].ap[:, :, 0:2], 0.0), writes=[xnTs[0]])
        ptrs = [C.ps(es, "ptr", [128, 1024], BF16) for _ in range(2)]
        pus = [C.ps(es, "pu", [128, 512], F32) for _ in range(4)]
        pds = [C.ps(es, "pd", [128, 512], F32) for _ in range(2)]

        def load(i):
            xt = xts[i % 2]
            S.dma(S.sp, xt.ap[:], src[i * T:(i + 1) * T, :].rearrange("(c p) d -> p c d", p=128), writes=[xt], chan_buf=xt)

        xn2 = [xn, C.sb(es, "xnb", [128, D], BF16)]
        ss2 = [ss, C.sb(es, "ssb", [128, 1], F32)]
        rstd2 = [rstd, C.sb(es, "rstdb", [128, 1], F32)]

        def head_norm(i):
            xt = xts[i % 2]
            for c in range(2):
                norm_only(C, K, xt.ap[:, c, :], xt, xn2[c], ss2[c], rstd2[c], xn2[c])

        def head_tr(i):
            xnT = xnTs[i % 2]
            if i > 0:
                S.emit(S.pool, lambda: nc.gpsimd.tensor_copy(out=xnT.ap[:, :, 0:2], in_=xnTs[(i - 1) % 2].ap[:, :, T:T + 2]), reads=[xnTs[(i - 1) % 2]], writes=[xnT])
            for c in range(2):
                transpose_only(C, K, xn2[c], ptrs[c], xnT.ap[:, :, 2 + c * 128:2 + (c + 1) * 128], xnT)

        def head(i):
            head_norm(i)
            head_tr(i)

        load(0)
        head(0)
        npu = 0
        npd = 0
        nb = 0
        for i in range(NT):
            xt = xts[i % 2]
            xnT = xnTs[i % 2]
            if i + 1 < NT:
                load(i + 1)
            for j in range(NF):
                pg, pv = pus[npu % 4], pus[(npu + 1) % 4]
                npu += 2
                ag, av, sgb = ags[nb % 3], avs[nb % 3], sgs[nb % 3]
                nb += 1
                for (pp, col) in ((pg, j), (pv, NF + j)):
                    for kk in range(8):
                        S.emit(S.pe, lambda: nc.tensor.matmul(pp.ap[:, 0:T + 2], lhsT=wup.ap[:, kk, col * 128:(col + 1) * 128], rhs=xnT.ap[:, kk, :], start=(kk == 0), stop=(kk == 7)),
                               reads=[wup, xnT], writes=[pp], pe_acc=True, sig=(kk == 7))
                S.emit(S.act, lambda: nc.scalar.mul(out=ag.ap[:], in_=pg.ap[:, 2:T + 2], mul=cw.ap[:, j, 2:3]), reads=[pg, cw], writes=[ag])
                S.emit(S.act, lambda: nc.scalar.mul(out=av.ap[:], in_=pv.ap[:, 2:T + 2], mul=cw.ap[:, NF + j, 2:3]), reads=[pv, cw], writes=[av])
                for tap in (1, 0):
                    S.emit(S.dve, lambda: nc.vector.scalar_tensor_tensor(out=ag.ap[:], in0=pg.ap[:, tap:tap + T], scalar=cw.ap[:, j, tap:tap + 1], in1=ag.ap[:], op0=ALU.mult, op1=ALU.add), reads=[pg, cw, ag], writes=[ag])
                for tap in (1, 0):
                    S.emit(S.dve, lambda: nc.vector.scalar_tensor_tensor(out=av.ap[:], in0=pv.ap[:, tap:tap + T], scalar=cw.ap[:, NF + j, tap:tap + 1], in1=av.ap[:], op0=ALU.mult, op1=ALU.add), reads=[pv, cw, av], writes=[av])
                S.emit(S.act, lambda: nc.scalar.activation(out=sgb.ap[:], in_=ag.ap[:], func=AF.Silu), reads=[ag], writes=[sgb])
                S.emit(S.pool, lambda: nc.gpsimd.tensor_tensor(out=h2T[j].ap[:], in0=sgb.ap[:], in1=av.ap[:], op=ALU.mult), reads=[sgb, av], writes=[h2T[j]])
                if j == 3 and i + 1 < NT:
                    head_norm(i + 1)
                if j == 12 and i + 1 < NT:
                    head_tr(i + 1)
            for c in range(2):
                for n in range(2):
                    pd = pds[npd % 2]
                    npd += 1
                    for j in range(NF):
                        S.emit(S.pe, lambda: nc.tensor.matmul(pd.ap[:], lhsT=h2T[j].ap[:, c * 128:(c + 1) * 128], rhs=wdn.ap[:, j, n * 512:(n + 1) * 512], start=(j == 0), stop=(j == NF - 1)),
                               reads=[h2T[j], wdn], writes=[pd], pe_acc=True, sig=(j == NF - 1))
                    S.emit(S.dve, lambda: nc.vector.tensor_tensor(out=xt.ap[:, c, n * 512:(n + 1) * 512], in0=pd.ap[:], in1=xt.ap[:, c, n * 512:(n + 1) * 512], op=ALU.add), reads=[pd, xt], writes=[xt])
                if final_gain is not None:
                    ob = obuf[c]
                    rms_rstd(C, xt.ap[:, c, :], xt, xn, ss, rstd, K, D, 1e-6)
                    S.emit(S.dve, lambda: nc.vector.scalar_tensor_tensor(out=ob.ap[:], in0=xt.ap[:, c, :], scalar=rstd.ap[:, 0:1], in1=gfin.ap[:], op0=ALU.mult, op1=ALU.mult), reads=[xt, rstd, gfin], writes=[ob])
                    S.dma(S.pool, dst[i * T + c * 128:i * T + (c + 1) * 128, :], ob.ap[:], reads=[ob], chan_buf=ob)
            if final_gain is None:
                S.dma(S.pool, dst[i * T:(i + 1) * T, :].rearrange("(c p) d -> p c d", p=128), xt.ap[:], reads=[xt], chan_buf=xt)
        S.barrier()


def phase_mamba(C, K, I, src, dst):
    nc, S = C.nc, C.S
    ynd = nc.dram_tensor("ynd", [S_LEN, D_IN], BF16, kind="Internal").ap()
    ident, identf = K["ident"], K["identf"]
    T = 256
    NT = S_LEN // T
    with contextlib.ExitStack() as es:
        win = C.sb(es, "win", [128, 8, IN_DIM], BF16)
        gain = C.sb(es, "gain", [128, 8], F32)
        cw = C.sb(es, "cw", [128, 32, 4], F32)
        cb = C.sb(es, "cb", [128, 32], F32)
        dcol = C.sb(es, "dcol", [128, 16], F32)
        diagD = C.sb(es, "diagD", [128, 16, 128], BF16)
        dtb = C.sb(es, "dtb", [32, 1], F32)
        acol = C.sb(es, "acol", [32, 1], F32)
        ones32 = C.sb(es, "ones32", [32, 128], F32)
        load_cols(C, gain, I["a_norm"], 8)
        load_cols(C, cb, I["ssm_conv_b"], 32)
        with nc.allow_non_contiguous_dma(reason="small params"):
            for kk in range(4):
                S.dma(S.sp, cw.ap[:, :, kk], I["ssm_conv_w"][kk].rearrange("(j p) -> p j", p=128), writes=[cw], chan_buf=cw)
            S.dma(S.sp, dtb.ap[:, 0:1], I["ssm_dt_bias"].rearrange("(h o) -> h o", o=1), writes=[dtb], chan_buf=dtb)
            S.dma(S.sp, acol.ap[:, 0:1], I["ssm_a_log"].rearrange("(h o) -> h o", o=1), writes=[acol], chan_buf=acol)
            dv = I["ssm_d"].rearrange("(k two) -> two k", two=2)
            for hh in range(2):
                S.dma(S.sp, dcol.ap[hh * 64:(hh + 1) * 64, :], dv[hh].partition_broadcast(64), writes=[dcol], chan_buf=dcol)
        S.emit(S.act, lambda: nc.scalar.activation(out=acol.ap[:], in_=acol.ap[:], func=AF.Exp), reads=[acol], writes=[acol])
        S.emit(S.dve, lambda: nc.vector.tensor_scalar(out=acol.ap[:], in0=acol.ap[:], scalar1=-1.0, scalar2=None, op0=ALU.mult), reads=[acol], writes=[acol])
        S.emit(S.pool, lambda: nc.gpsimd.memset(ones32.ap[:], 1.0), writes=[ones32])
        for kk in range(16):
            S.emit(S.dve, lambda: nc.vector.tensor_scalar(out=diagD.ap[:, kk, :], in0=identf.ap[:], scalar1=dcol.ap[:, kk:kk + 1], scalar2=None, op0=ALU.mult), reads=[identf, dcol], writes=[diagD])
        with contextlib.ExitStack() as es2:
            C.stg = [C.sb(es2, "stg", [128, 2048], F32) for _ in range(3)]
            prep_weight(C, es2, win, I["ssm_w_in"], 8, IN_DIM, gain)
            S.barrier()
        xts = [C.sb(es, "xt", [128, 2, D], F32) for _ in range(2)]
        xn = C.sb(es, "xn", [128, D], BF16)
        xnT2s = [C.sb(es, "xnT2", [128, 8, T + 4], BF16) for _ in range(2)]
        ss = C.sb(es, "ss", [128, 1], F32)
        rstd = C.sb(es, "rstd", [128, 1], F32)
        szs = [C.sb(es, "sz", [128, 512], F32) for _ in range(2)]
        szeas = [C.sb(es, "szea", [128, 512], F32) for _ in range(2)]
        xbcT_t = es.enter_context(nc.sbuf_tensor("xbcT_all", [128, 32, T], BF16))
        xbcT = [Buf(xbcT_t[:, j, :], "xbcT%d" % j) for j in range(32)]
        accs = [C.sb(es, "acc", [128, T], F32) for _ in range(3)]
        xcgs = [C.sb(es, "xcg", [128, 256], BF16) for _ in range(3)]
        xcdgs = [C.sb(es, "xcdg", [128, 256], BF16) for _ in range(3)]
        Btgs = [C.sb(es, "Btg", [128, 128], BF16) for _ in range(3)]
        ADs = [C.sb(es, "AD", [32, 512], F32) for _ in range(2)]
        Egs = [C.sb(es, "Eg", [128, 512], F32) for _ in range(2)]
        MTs = [C.sb(es, "MT", [128, 512], BF16) for _ in range(2)]
        prevT_t = es.enter_context(nc.sbuf_tensor("prevT_all", [128, D_IN], F32))
        prevbf_t = es.enter_context(nc.sbuf_tensor("prevbf_all", [128, D_IN], BF16))
        prevT = [Buf(prevT_t[:, g * 256:(g + 1) * 256], "prevT%d" % g) for g in range(8)]
        prevbf = [Buf(prevbf_t[:, g * 256:(g + 1) * 256], "prevbf%d" % g) for g in range(8)]
        yts = [C.sb(es, "yt", [128, 256], F32) for _ in range(2)]
        yus = [C.sb(es, "yu", [128, 256], F32) for _ in range(2)]
        yzs = [C.sb(es, "yz", [128, 256], F32) for _ in range(4)]
        junkg = C.sb(es, "junkg", [128, 256], BF16)
        ssgs = [C.sb(es, "ssg", [128, 1], F32) for _ in range(4)]
        rgs = [C.sb(es, "rg", [128, 1], F32) for _ in range(4)]
        yns = [C.sb(es, "yn", [128, D_IN], BF16) for _ in range(2)]
        sm = {n: C.sb(es, n, [32, T], F32) for n in ("dtT", "adtT", "acsT", "nacsT", "decT")}
        sm["e1"] = sm["dtT"]
        sm["dtdecT"] = sm["decT"]
        ecol = C.sb(es, "ecol", [32, 2], F32)
        dgs = [C.sb(es, "dg", [32, 32], F32) for _ in range(2)]
        toks = [C.sb(es, "tok", [128, 160], F32) for _ in range(2)]
        for g in range(8):
            S.emit(S.pool, lambda: nc.gpsimd.memset(prevT[g].ap[:], 0.0), writes=[prevT[g]])
            S.emit(S.pool, lambda: nc.gpsimd.memset(prevbf[g].ap[:], 0.0), writes=[prevbf[g]])
        S.emit(S.pool, lambda: nc.gpsimd.memset(xnT2s[0].ap[:, :, 0:4], 0.0), writes=[xnT2s[0]])
        bkA = [C.ps(es, "bkA", [128, 1024], BF16) for _ in range(2)]
        pgen2 = [C.ps(es, "pgen", [128, 512], F32) for _ in range(2)]
        pzb1 = C.ps(es, "pzb", [128, 512], F32)
        pgen = pgen2 + [pzb1]
        pzb = [pzb1, pzb1]
        pys = [C.ps(es, "py", [128, 512], F32) for _ in range(2)]
        pst1 = C.ps(es, "pst", [128, 512], F32)
        psts = [pst1, pst1]
        psgs = [None] * 4
        cnt = {"g": 0, "tr": 0, "acc": 0}

        def nxt(lst, key):
            b = lst[cnt[key] % len(lst)]
            cnt[key] += 1
            return b

        def load(i):
            xt = xts[i % 2]
            S.dma(S.sp, xt.ap[:], src[i * T:(i + 1) * T, :].rearrange("(c p) d -> p c d", p=128), writes=[xt], chan_buf=xt)

        e1, dtT, adtT, acsT, nacsT, decT, dtdecT = [sm[n] for n in ("e1", "dtT", "adtT", "acsT", "nacsT", "decT", "dtdecT")]
        load(0)
        for i in range(NT):
            xt = xts[i % 2]
            xnT2 = xnT2s[i % 2]
            if i + 1 < NT:
                load(i + 1)
            for c in range(2):
                norm_transpose(C, K, xt.ap[:, c, :], xt, xn, ss, rstd, xn, bkA[c], xnT2.ap[:, :, 4 + c * 128:4 + (c + 1) * 128], xnT2)
            S.emit(S.pool, lambda: nc.gpsimd.tensor_copy(out=xnT2s[(i + 1) % 2].ap[:, :, 0:4], in_=xnT2.ap[:, :, T:T + 4]), reads=[xnT2], writes=[xnT2s[(i + 1) % 2]])
            pdt = nxt(pgen, "g")
            for kk in range(8):
                S.emit(S.pe, lambda: nc.tensor.matmul(pdt.ap[0:32, 0:T], lhsT=win.ap[:, kk, 6144:6176], rhs=xnT2.ap[:, kk, 4:T + 4], start=(kk == 0), stop=(kk == 7)),
                       reads=[win, xnT2], writes=[pdt], pe_acc=True, sig=(kk == 7))
            S.emit(S.act, lambda: nc.scalar.activation(out=e1.ap[:], in_=pdt.ap[0:32, 0:T], func=AF.Exp, bias=dtb.ap[:, 0:1]), reads=[pdt, dtb], writes=[e1])
            S.emit(S.act, lambda: nc.scalar.activation(out=dtT.ap[:], in_=e1.ap[:], func=AF.Ln, bias=1.0), reads=[e1], writes=[dtT])
            S.emit(S.dve, lambda: nc.vector.tensor_scalar(out=adtT.ap[:], in0=dtT.ap[:], scalar1=acol.ap[:, 0:1], scalar2=None, op0=ALU.mult), reads=[dtT, acol], writes=[adtT])
            for c in range(2):
                cs = slice(c * 128, (c + 1) * 128)
                S.emit(S.dve, lambda: nc.vector.tensor_tensor_scan(out=acsT.ap[:, cs], data0=ones32.ap[:], data1=adtT.ap[:, cs], initial=0.0, op0=ALU.mult, op1=ALU.add), reads=[ones32, adtT], writes=[acsT])
            S.emit(S.act, lambda: nc.scalar.mul(out=nacsT.ap[:], in_=acsT.ap[:], mul=-1.0), reads=[acsT], writes=[nacsT])
            for c in range(2):
                cs = slice(c * 128, (c + 1) * 128)
                last = acsT.ap[:, c * 128 + 127:c * 128 + 128]
                S.emit(S.act, lambda: nc.scalar.activation(out=decT.ap[:, cs], in_=acsT.ap[:, cs], func=AF.Exp, scale=-1.0, bias=last), reads=[acsT], writes=[decT])
                S.emit(S.act, lambda: nc.scalar.activation(out=ecol.ap[:, c:c + 1], in_=last, func=AF.Exp), reads=[acsT], writes=[ecol])
                S.emit(S.dve, lambda: nc.vector.tensor_scalar(out=dgs[c].ap[:], in0=identf.ap[0:32, 0:32], scalar1=ecol.ap[:, c:c + 1], scalar2=None, op0=ALU.mult), reads=[identf, ecol], writes=[dgs[c]])
            S.emit(S.dve, lambda: nc.vector.tensor_tensor(out=dtdecT.ap[:], in0=dtT.ap[:], in1=decT.ap[:], op=ALU.mult), reads=[dtT, decT], writes=[dtdecT])
            for c in range(2):
                cs = slice(c * 128, (c + 1) * 128)
                tok = toks[c]
                ptok = nxt(pgen, "g")
                for i4, (lh, rh) in enumerate(((dtT, None), (dtdecT, None), (nacsT, None), (ones32, dgs[c]))):
                    rhs_ap = identf.ap[0:32, 0:32] if rh is None else rh.ap[:]
                    lh_ap = lh.ap[:, cs] if rh is None else lh.ap[:]
                    S.emit(S.pe, lambda: nc.tensor.matmul(ptok.ap[:, i4 * 32:(i4 + 1) * 32], lhsT=lh_ap, rhs=rhs_ap, start=True, stop=True),
                           reads=[lh, identf] + ([rh] if rh is not None else []), writes=[ptok], pe_acc=True, sig=(i4 == 3))
                S.emit(S.dve, lambda: nc.vector.tensor_copy(out=tok.ap[:, 0:128], in_=ptok.ap[:, 0:128]), reads=[ptok], writes=[tok])
                S.emit(S.act, lambda: nc.scalar.activation(out=tok.ap[:, 128:160], in_=ptok.ap[:, 64:96], func=AF.Exp, scale=-1.0), reads=[ptok], writes=[tok])
            for j in range(32):
                pj = nxt(pgen, "g")
                acc = nxt(accs, "acc")
                for kk in range(8):
                    S.emit(S.pe, lambda: nc.tensor.matmul(pj.ap[:, 0:T + 4], lhsT=win.ap[:, kk, 2048 + j * 128:2048 + (j + 1) * 128], rhs=xnT2.ap[:, kk, :], start=(kk == 0), stop=(kk == 7)),
                           reads=[win, xnT2], writes=[pj], pe_acc=True, sig=(kk == 7))
                S.emit(S.act, lambda: nc.scalar.activation(out=acc.ap[:], in_=pj.ap[:, 4:T + 4], func=AF.Identity, scale=cw.ap[:, j, 3:4], bias=cb.ap[:, j:j + 1]), reads=[pj, cw, cb], writes=[acc])
                for tap in (2, 1, 0):
                    S.emit(S.dve, lambda: nc.vector.scalar_tensor_tensor(out=acc.ap[:], in0=pj.ap[:, 1 + tap:1 + tap + T], scalar=cw.ap[:, j, tap:tap + 1], in1=acc.ap[:], op0=ALU.mult, op1=ALU.add), reads=[pj, cw, acc], writes=[acc])
                S.emit(S.act, lambda: nc.scalar.activation(out=xbcT[j].ap[:], in_=acc.ap[:], func=AF.Silu), reads=[acc], writes=[xbcT[j]])
            def info(gi):
                c, g = gi // 8, gi % 8
                return c, g, slice(c * 128, (c + 1) * 128), toks[c]

            def st_ad(gi):
                c, g, cs, tok = info(gi)
                AD = ADs[gi % 2]
                S.emit(S.pool, lambda: nc.gpsimd.affine_select(out=AD.ap[:].rearrange("p (h l) -> p h l", h=4), in_=bc(acsT.ap[:, cs].unsqueeze(1), [32, 4, 128]), pattern=[[-1, 4], [0, 128]], compare_op=ALU.is_equal, fill=0.0, base=-4 * g, channel_multiplier=1), reads=[acsT], writes=[AD])

            def st0(gi):
                c, g, cs, tok = info(gi)
                bk = bkA[gi % 2]
                for jj, j in enumerate((2 * g, 2 * g + 1, 16 + g)):
                    S.emit(S.pe, lambda: nc.tensor.transpose(out=bk.ap[:, jj * 128:(jj + 1) * 128], in_=xbcT[j].ap[:, cs], identity=ident.ap[:]),
                           reads=[xbcT[j], ident], writes=[bk], pe_acc=True, sig=False)
                S.emit(S.pe, lambda: nc.tensor.matmul(bk.ap[:, 512:768].bitcast(F32), lhsT=xbcT[16 + g].ap[:, cs], rhs=xbcT[24 + g].ap[:, cs], start=True, stop=True), reads=[xbcT[16 + g], xbcT[24 + g]], writes=[bk], pe_acc=True)
                psg = pgen2[gi % 2]
                psgs[gi % 4] = psg
                AD = ADs[gi % 2]
                S.emit(S.pe, lambda: nc.tensor.matmul(psg.ap[:], lhsT=ones32.ap[:], rhs=AD.ap[:], start=True, stop=False), reads=[ones32, AD], writes=[psg], pe_acc=True, sig=False)
                S.emit(S.pe, lambda: nc.tensor.matmul(psg.ap[:], lhsT=ident.ap[:], rhs=K["nmc"].ap[:], start=False, stop=True), reads=[ident, K["nmc"]], writes=[psg], pe_acc=True)
                if g % 2 == 0:
                    pz = pzb[(gi // 2) % 2]
                    n = g // 2
                    for kk in range(8):
                        S.emit(S.pe, lambda: nc.tensor.matmul(pz.ap[:], lhsT=xnT2.ap[:, kk, 4 + c * 128:4 + (c + 1) * 128], rhs=win.ap[:, kk, n * 512:(n + 1) * 512], start=(kk == 0), stop=(kk == 7)),
                               reads=[xnT2, win], writes=[pz], pe_acc=True, sig=(kk == 7))

            def st1(gi):
                c, g, cs, tok = info(gi)
                ptr = bkA[gi % 2]
                psg = psgs[gi % 4]
                Eg = Egs[gi % 2]
                for h in range(4):
                    S.emit(S.act, lambda: nc.scalar.activation(out=Eg.ap[:, h * 128:(h + 1) * 128], in_=psg.ap[:, h * 128:(h + 1) * 128], func=AF.Exp, bias=tok.ap[:, 64 + 4 * g + h:64 + 4 * g + h + 1]), reads=[psg, tok], writes=[Eg])
                MT = MTs[gi % 2]
                S.emit(S.dve, lambda: nc.vector.tensor_tensor(out=MT.ap[:].rearrange("p (h l) -> p h l", h=4), in0=Eg.ap[:].rearrange("p (h l) -> p h l", h=4), in1=bc(ptr.ap[:, 512:768].bitcast(F32).unsqueeze(1), [128, 4, 128]), op=ALU.mult), reads=[Eg, ptr], writes=[MT])
                xcg, xcdg, Btg = xcgs[gi % 3], xcdgs[gi % 3], Btgs[gi % 3]
                pv = ptr.ap[:, 0:256].rearrange("p (h e) -> p h e", h=4)
                S.emit(S.dve, lambda: nc.vector.tensor_tensor(out=xcg.ap[:].rearrange("p (h e) -> p h e", h=4), in0=pv, in1=bc(tok.ap[:, 4 * g:4 * g + 4].unsqueeze(2), [128, 4, 64]), op=ALU.mult), reads=[ptr, tok], writes=[xcg])
                S.emit(S.dve, lambda: nc.vector.tensor_tensor(out=xcdg.ap[:].rearrange("p (h e) -> p h e", h=4), in0=pv, in1=bc(tok.ap[:, 32 + 4 * g:32 + 4 * g + 4].unsqueeze(2), [128, 4, 64]), op=ALU.mult), reads=[ptr, tok], writes=[xcdg])
                S.emit(S.act, lambda: nc.scalar.copy(out=Btg.ap[:], in_=ptr.ap[:, 256:384]), reads=[ptr], writes=[Btg])
                if g % 2 == 0:
                    pz = pzb[(gi // 2) % 2]
                    sz, szea = szs[(gi // 2) % 2], szeas[(gi // 2) % 2]
                    n = g // 2
                    S.emit(S.act, lambda: nc.scalar.activation(out=sz.ap[:], in_=pz.ap[:], func=AF.Tanh, scale=0.5), reads=[pz], writes=[sz])
                    S.emit(S.dve, lambda: nc.vector.scalar_tensor_tensor(out=sz.ap[:], in0=sz.ap[:], scalar=1.0, in1=pz.ap[:], op0=ALU.add, op1=ALU.mult), reads=[sz, pz], writes=[sz])
                    S.emit(S.pool, lambda: nc.gpsimd.tensor_tensor(out=szea.ap[:].rearrange("p (h e) -> p h e", h=8), in0=sz.ap[:].rearrange("p (h e) -> p h e", h=8), in1=bc(tok.ap[:, 128 + 8 * n:128 + 8 * n + 8].unsqueeze(2), [128, 8, 64]), op=ALU.mult), reads=[sz, tok], writes=[szea])
                pvw = prevT[g].ap[:]
                S.emit(S.pool, lambda: nc.gpsimd.tensor_tensor(out=pvw.rearrange("p (h e) -> p h e", h=4), in0=pvw.rearrange("p (h e) -> p h e", h=4), in1=bc(tok.ap[:, 96 + 4 * g:96 + 4 * g + 4].unsqueeze(2), [128, 4, 64]), op=ALU.mult), reads=[prevT[g], tok], writes=[prevT[g]])

            def st2(gi):
                c, g, cs, tok = info(gi)
                MT = MTs[gi % 2]
                xcg, xcdg, Btg = xcgs[gi % 3], xcdgs[gi % 3], Btgs[gi % 3]
                py = pys[gi % 2]
                pyr = py.ap[:, 0:256]
                pyo = py.ap[:, 256:512]
                for i2 in range(2):
                    S.emit(S.pe, lambda: nc.tensor.matmul(pyr[:, i2 * 128:(i2 + 1) * 128], lhsT=xbcT[2 * g + i2].ap[:, cs], rhs=diagD.ap[:, 2 * g + i2, :], start=True, stop=False),
                           reads=[xbcT[2 * g + i2], diagD], writes=[py], pe_acc=True, sig=False)
                    for hh in (2 * i2, 2 * i2 + 1):
                        S.emit(S.pe, lambda: nc.tensor.matmul(pyr[:, hh * 64:(hh + 1) * 64], lhsT=MT.ap[:, hh * 128:(hh + 1) * 128], rhs=xcg.ap[:, hh * 64:(hh + 1) * 64], start=False, stop=(hh % 2 == 1)),
                               reads=[MT, xcg], writes=[py], pe_acc=True, sig=False)
                S.emit(S.pe, lambda: nc.tensor.matmul(pyo, lhsT=xbcT[24 + g].ap[:, cs], rhs=prevbf[g].ap[:], start=True, stop=True), reads=[xbcT[24 + g], prevbf[g]], writes=[py], pe_acc=True)
                pst = psts[gi % 2]
                S.emit(S.pe, lambda: nc.tensor.matmul(pst.ap[:, 0:256], lhsT=Btg.ap[:], rhs=xcdg.ap[:], start=True, stop=True), reads=[Btg, xcdg], writes=[pst], pe_acc=True)

            def st3(gi):
                c, g, cs, tok = info(gi)
                sz, szea = szs[(gi // 2) % 2], szeas[(gi // 2) % 2]
                py = pys[gi % 2]
                yt, yu = yts[gi % 2], yus[gi % 2]
                gs = slice((g % 2) * 256, (g % 2 + 1) * 256)
                S.emit(S.dve, lambda: nc.vector.tensor_tensor(out=yt.ap[:], in0=py.ap[:, 256:512], in1=szea.ap[:, gs], op=ALU.mult), reads=[py, szea], writes=[yt])
                S.emit(S.dve, lambda: nc.vector.tensor_tensor(out=yu.ap[:], in0=py.ap[:, 0:256], in1=sz.ap[:, gs], op=ALU.mult), reads=[py, sz], writes=[yu])
                pst = psts[gi % 2]
                S.emit(S.dve, lambda: nc.vector.tensor_tensor(out=prevT[g].ap[:], in0=pst.ap[:, 0:256], in1=prevT[g].ap[:], op=ALU.add), reads=[pst, prevT[g]], writes=[prevT[g]])

            def st4(gi):
                c, g, cs, tok = info(gi)
                yt, yu, yz = yts[gi % 2], yus[gi % 2], yzs[gi % 4]
                S.emit(S.pool, lambda: nc.gpsimd.tensor_tensor(out=yz.ap[:], in0=yt.ap[:], in1=yu.ap[:], op=ALU.add), reads=[yt, yu], writes=[yz])
                S.emit(S.act, lambda: nc.scalar.copy(out=prevbf[g].ap[:], in_=prevT[g].ap[:]), reads=[prevT[g]], writes=[prevbf[g]])

            def st5(gi):
                yz, ssg = yzs[gi % 4], ssgs[gi % 4]
                S.emit(S.act, lambda: nc.scalar.activation(out=junkg.ap[:], in_=yz.ap[:], func=AF.Square, accum_out=ssg.ap[:, 0:1]), reads=[yz], writes=[junkg, ssg])

            def st6(gi):
                ssg, rg = ssgs[gi % 4], rgs[gi % 4]
                S.emit(S.pool, lambda: nc.gpsimd.tensor_scalar(out=ssg.ap[:, 0:1], in0=ssg.ap[:, 0:1], scalar1=1.0 / 256.0, scalar2=4e-5, op0=ALU.mult, op1=ALU.add), reads=[ssg], writes=[ssg])
                S.emit(S.pool, lambda: nc.gpsimd.tensor_tensor(out=rg.ap[:, 0:1], in0=ssg.ap[:, 0:1], in1=K["mhalf"].ap[:, 0:1], op=ALU.pow), reads=[ssg, K["mhalf"]], writes=[rg])

            def st7(gi):
                c, g, cs, tok = info(gi)
                yz, rg = yzs[gi % 4], rgs[gi % 4]
                yn = yns[c]
                S.emit(S.act, lambda: nc.scalar.mul(out=yn.ap[:, g * 256:(g + 1) * 256], in_=yz.ap[:], mul=rg.ap[:, 0:1]), reads=[yz, rg], writes=[yn])
                if g == 7:
                    S.dma(S.pool, ynd[i * T + c * 128:i * T + (c + 1) * 128, :], yn.ap[:], reads=[yn], chan_buf=yn)

            stages = [st_ad, st0, st1, st2, st3, st4, st5, st6, st7]
            NG = 16
            for t in range(NG + len(stages) - 1):
                for k in reversed(range(len(stages))):
                    gi = t - k
                    if 0 <= gi < NG:
                        stages[k](gi)
        S.barrier()
    with contextlib.ExitStack() as es:
        wout = C.sb(es, "wout", [128, 16, D], BF16)
        gout = C.sb(es, "gout", [128, 16], F32)
        load_cols(C, gout, I["ssm_norm"], 16)
        with contextlib.ExitStack() as es2:
            C.stg = [C.sb(es2, "stg", [128, 2048], F32) for _ in range(3)]
            prep_weight(C, es2, wout, I["ssm_w_out"], 16, D, gout)
            S.barrier()
        xts = [C.sb(es, "xt", [128, D], F32) for _ in range(2)]
        yls = [C.sb(es, "yl", [128, D_IN], BF16) for _ in range(2)]
        ynT = C.sb(es, "ynT", [128, 16, 128], BF16)
        ptrs = [C.ps(es, "ptr", [128, 1024], BF16) for _ in range(2)]
        pmm = [C.ps(es, "pmm", [128, 512], F32) for _ in range(2)]

        def load2(c):
            S.dma(S.sp, xts[c % 2].ap[:], src[c * 128:(c + 1) * 128, :], writes=[xts[c % 2]], chan_buf=xts[c % 2])
            S.dma(S.sp, yls[c % 2].ap[:], ynd[c * 128:(c + 1) * 128, :], writes=[yls[c % 2]], chan_buf=yls[c % 2])

        load2(0)
        nmm = 0
        for c in range(NCH):
            if c + 1 < NCH:
                load2(c + 1)
            xt, yl = xts[c % 2], yls[c % 2]
            for half in range(2):
                ptr = ptrs[half]
                for jj in range(8):
                    j = 8 * half + jj
                    S.emit(S.pe, lambda: nc.tensor.transpose(out=ptr.ap[:, jj * 128:(jj + 1) * 128], in_=yl.ap[:, j * 128:(j + 1) * 128], identity=ident.ap[:]),
                           reads=[yl, ident], writes=[ptr], pe_acc=True, sig=(jj == 7))
                if half == 0:
                    S.emit(S.act, lambda: nc.scalar.copy(out=ynT.ap[:, 0:8, :], in_=ptr.ap[:, 0:1024].rearrange("p (k t) -> p k t", k=8)), reads=[ptr], writes=[ynT])
                else:
                    S.emit(S.dve, lambda: nc.vector.tensor_copy(out=ynT.ap[:, 8:16, :], in_=ptr.ap[:, 0:1024].rearrange("p (k t) -> p k t", k=8)), reads=[ptr], writes=[ynT])
            for n2 in range(2):
                po = pmm[nmm % 2]
                nmm += 1
                for kk in range(16):
                    S.emit(S.pe, lambda: nc.tensor.matmul(po.ap[:], lhsT=ynT.ap[:, kk, :], rhs=wout.ap[:, kk, n2 * 512:(n2 + 1) * 512], start=(kk == 0), stop=(kk == 15)),
                           reads=[ynT, wout], writes=[po], pe_acc=True, sig=(kk == 15))
                S.emit(S.dve, lambda: nc.vector.tensor_tensor(out=xt.ap[:, n2 * 512:(n2 + 1) * 512], in0=po.ap[:], in1=xt.ap[:, n2 * 512:(n2 + 1) * 512], op=ALU.add), reads=[po, xt], writes=[xt])
            S.dma(S.pool, dst[c * 128:(c + 1) * 128, :], xt.ap[:], reads=[xt], chan_buf=xt)
        S.barrier()

ATT_PATTERNS = ((128, 1), (512, 4), (2048, 16))


def rotary(C, src, dst, rp, tmps, nh):
    nc, S = C.nc, C.S
    sv = src.ap[:, 0:nh * 128].rearrange("p (h e) -> p h e", h=nh)
    dv = dst.ap[:, 0:nh * 128].rearrange("p (h e) -> p h e", h=nh)
    cos = bc(rp.ap[:, 0:16].unsqueeze(1), [128, nh, 16])
    sin = bc(rp.ap[:, 16:32].unsqueeze(1), [128, nh, 16])
    t1, t2, t3, t4 = [t.ap[:, 0:nh * 16].rearrange("p (h e) -> p h e", h=nh) for t in tmps]
    x1, x2 = sv[:, :, 0:16], sv[:, :, 16:32]
    S.emit(S.dve, lambda: nc.vector.tensor_tensor(out=t1, in0=x1, in1=cos, op=ALU.mult), reads=[src, rp], writes=[tmps[0]])
    S.emit(S.pool, lambda: nc.gpsimd.tensor_tensor(out=t2, in0=x2, in1=sin, op=ALU.mult), reads=[src, rp], writes=[tmps[1]])
    S.emit(S.dve, lambda: nc.vector.tensor_tensor(out=t3, in0=x2, in1=cos, op=ALU.mult), reads=[src, rp], writes=[tmps[2]])
    S.emit(S.pool, lambda: nc.gpsimd.tensor_tensor(out=t4, in0=x1, in1=sin, op=ALU.mult), reads=[src, rp], writes=[tmps[3]])
    S.emit(S.dve, lambda: nc.vector.tensor_tensor(out=dv[:, :, 0:16], in0=t1, in1=t2, op=ALU.subtract), reads=[tmps[0], tmps[1]], writes=[dst])
    S.emit(S.pool, lambda: nc.gpsimd.tensor_tensor(out=dv[:, :, 16:32], in0=t3, in1=t4, op=ALU.add), reads=[tmps[2], tmps[3]], writes=[dst])
    S.emit(S.act, lambda: nc.scalar.copy(out=dv[:, :, 32:128], in_=sv[:, :, 32:128]), reads=[src], writes=[dst])


def phase_attn(C, K, I, src, dst):
    nc, S = C.nc, C.S
    nd = [nc.dram_tensor("numden%d" % g, [S_LEN, 8 * 129], F32, kind="Internal").ap() for g in range(3)]
    rope = I["rope"]
    with contextlib.ExitStack() as es:
        wkv = C.sb(es, "wkv", [128, 8, 1536], BF16)
        wq = C.sb(es, "wq", [128, 8, 3072], BF16)
        wo = C.sb(es, "wo", [128, 8, D], BF16)
        gkv = C.sb(es, "gkv", [128, 8], F32)
        gq = C.sb(es, "gq", [128, 8], F32)
        load_cols(C, gkv, I["kv_norm"], 8)
        load_cols(C, gq, I["b_norm"], 8)
        with contextlib.ExitStack() as es2:
            C.stg = [C.sb(es2, "stg", [128, 2048], F32) for _ in range(3)]
            prep_weight(C, es2, wkv, I["w_kv"], 8, 1536, gkv)
            prep_weight(C, es2, wq, I["att_w_q"], 8, 3072, gq)
            prep_weight(C, es2, wo, I["att_w_o"], 8, D, None)
            S.barrier()
        xts = [C.sb(es, "xt", [128, D], F32) for _ in range(2)]
        rps = [C.sb(es, "rp", [128, 32], F32) for _ in range(2)]
        xn = C.sb(es, "xn", [128, D], BF16)
        junk = C.sb(es, "junk", [128, D], BF16)
        xnT = C.sb(es, "xnT", [128, 8, 128], BF16)
        ss = C.sb(es, "ss", [128, 1], F32)
        rstd = C.sb(es, "rstd", [128, 1], F32)
        qf = C.sb(es, "qf", [128, 1024], F32)
        qb = C.sb(es, "qb", [128, 1024], BF16)
        kf = C.sb(es, "kf", [128, 256], F32)
        kb = C.sb(es, "kb", [128, 256], BF16)
        tq = [C.sb(es, "tq", [128, 128], F32) for _ in range(4)]
        tk = [C.sb(es, "tk", [128, 32], F32) for _ in range(4)]
        QT = C.sb(es, "QT", [128, 8, 128], BF16)
        KTs = [C.sb(es, "KT", [128, 2, 128], BF16) for _ in range(2)]
        vaugs = [C.sb(es, "vaug", [128, 2, 129], BF16) for _ in range(2)]
        PTs = [C.sb(es, "PT", [128, 512], BF16) for _ in range(4)]
        obs = [C.sb(es, "ob", [128, 8, 129], F32) for _ in range(2)]
        for v in vaugs:
            S.emit(S.pool, lambda: nc.gpsimd.memset(v.ap[:], 1.0), writes=[v])
        ptr = C.ps(es, "ptr", [128, 1024], BF16)
        pmm = [C.ps(es, "pmm", [128, 512], F32) for _ in range(2)]
        pS = [C.ps(es, "pS", [128, 512], F32) for _ in range(2)]
        po = C.ps(es, "po", [128, 3 * 512], F32)
        scale = 1.0 / np.sqrt(128.0)

        blocks = []
        for g, (win, dil) in enumerate(ATT_PATTERNS):
            for r in range(dil):
                for n in range(S_LEN // dil // 128):
                    blocks.append((g, dil, r, n))

        def rows(ap, dil, r, n):
            return ap.rearrange("(m dd) c -> dd m c", dd=dil)[r][n * 128:(n + 1) * 128, :]

        def load(bi):
            g, dil, r, n = blocks[bi]
            S.dma(S.sp, xts[bi % 2].ap[:], rows(src, dil, r, n), writes=[xts[bi % 2]], chan_buf=xts[bi % 2])
            S.dma(S.sp, rps[bi % 2].ap[:], rows(rope, dil, r, n), writes=[rps[bi % 2]], chan_buf=rps[bi % 2])

        load(0)
        nmm = 0
        nS = 0
        for bi, (g, dil, r, n) in enumerate(blocks):
            xt, rp = xts[bi % 2], rps[bi % 2]
            if bi + 1 < len(blocks):
                load(bi + 1)
            norm_transpose(C, K, xt.ap[:], xt, junk, ss, rstd, xn, ptr, xnT.ap[:], xnT)
            for n2 in range(2):
                pq = pmm[nmm % 2]
                nmm += 1
                for kk in range(8):
                    S.emit(S.pe, lambda: nc.tensor.matmul(pq.ap[:], lhsT=xnT.ap[:, kk, :], rhs=wq.ap[:, kk, g * 1024 + n2 * 512:g * 1024 + (n2 + 1) * 512], start=(kk == 0), stop=(kk == 7)),
                           reads=[xnT, wq], writes=[pq], pe_acc=True, sig=(kk == 7))
                S.emit(S.act, lambda: nc.scalar.copy(out=qf.ap[:, n2 * 512:(n2 + 1) * 512], in_=pq.ap[:]), reads=[pq], writes=[qf])
            pkv = pmm[nmm % 2]
            nmm += 1
            for half, c0 in ((0, g * 256), (1, 768 + g * 256)):
                for kk in range(8):
                    S.emit(S.pe, lambda: nc.tensor.matmul(pkv.ap[:, half * 256:(half + 1) * 256], lhsT=xnT.ap[:, kk, :], rhs=wkv.ap[:, kk, c0:c0 + 256], start=(kk == 0), stop=(kk == 7)),
                           reads=[xnT, wkv], writes=[pkv], pe_acc=True, sig=(kk == 7 and half == 1))
            KT, vaug = KTs[n % 2], vaugs[n % 2]
            S.emit(S.act, lambda: nc.scalar.copy(out=kf.ap[:], in_=pkv.ap[:, 0:256]), reads=[pkv], writes=[kf])
            S.emit(S.act, lambda: nc.scalar.copy(out=vaug.ap[:, :, 0:128], in_=pkv.ap[:, 256:512].rearrange("p (h e) -> p h e", h=2)), reads=[pkv], writes=[vaug])
            rotary(C, qf, qb, rp, tq, 8)
            rotary(C, kf, kb, rp, tk, 2)
            for h in range(8):
                S.emit(S.pe, lambda: nc.tensor.transpose(out=ptr.ap[:, h * 128:(h + 1) * 128], in_=qb.ap[:, h * 128:(h + 1) * 128], identity=K["ident"].ap[:]),
                       reads=[qb, K["ident"]], writes=[ptr], pe_acc=True, sig=(h == 7))
            S.emit(S.dve, lambda: nc.vector.tensor_copy(out=QT.ap[:], in_=ptr.ap[:, 0:1024].rearrange("p (k t) -> p k t", k=8)), reads=[ptr], writes=[QT])
            for h in range(2):
                S.emit(S.pe, lambda: nc.tensor.transpose(out=ptr.ap[:, h * 128:(h + 1) * 128], in_=kb.ap[:, h * 128:(h + 1) * 128], identity=K["ident"].ap[:]),
                       reads=[kb, K["ident"]], writes=[ptr], pe_acc=True, sig=(h == 1))
            S.emit(S.dve, lambda: nc.vector.tensor_copy(out=KT.ap[:], in_=ptr.ap[:, 0:256].rearrange("p (k t) -> p k t", k=2)), reads=[ptr], writes=[KT])
            kblocks = []
            if n > 0:
                kblocks.append((KTs[(n - 1) % 2], vaugs[(n - 1) % 2], K["nmp"]))
            kblocks.append((KT, vaug, K["nmc"]))
            ob = obs[bi % 2]
            for jk in range(2):
                pts = []
                for (kt, va, nm) in kblocks:
                    ps_ = pS[nS % 2]
                    pt_ = PTs[nS % 4]
                    nS += 1
                    S.emit(S.pe, lambda: nc.tensor.matmul(ps_.ap[:], lhsT=kt.ap[:, jk, :], rhs=QT.ap[:, 4 * jk:4 * jk + 4, :], start=True, stop=False),
                           reads=[kt, QT], writes=[ps_], pe_acc=True, sig=False)
                    S.emit(S.pe, lambda: nc.tensor.matmul(ps_.ap[:], lhsT=K["ident"].ap[:], rhs=nm.ap[:], start=False, stop=True),
                           reads=[K["ident"], nm], writes=[ps_], pe_acc=True)
                    S.emit(S.act, lambda: nc.scalar.activation(out=pt_.ap[:], in_=ps_.ap[:], func=AF.Exp, scale=float(scale)), reads=[ps_], writes=[pt_])
                    pts.append((pt_, va))
                for hl in range(4):
                    h = 4 * jk + hl
                    oslice = po.ap[:, (h // 3) * 512 + (h % 3) * 129:(h // 3) * 512 + (h % 3) * 129 + 129]
                    for i, (pt_, va) in enumerate(pts):
                        S.emit(S.pe, lambda: nc.tensor.matmul(oslice, lhsT=pt_.ap[:, hl * 128:(hl + 1) * 128], rhs=va.ap[:, jk, :], start=(i == 0), stop=(i == len(pts) - 1)),
                               reads=[pt_, va], writes=[po], pe_acc=True, sig=(i == len(pts) - 1 and hl == 3))
            for b3 in range(3):
                nh = 3 if b3 < 2 else 2
                eng, q = (S.act, None) if b3 != 1 else (S.dve, None)
                o_ap = ob.ap[:, b3 * 3:b3 * 3 + nh, :]
                i_ap = po.ap[:, b3 * 512:b3 * 512 + nh * 129].rearrange("p (h e) -> p h e", h=nh)
                if b3 != 1:
                    S.emit(S.act, lambda: nc.scalar.copy(out=o_ap, in_=i_ap), reads=[po], writes=[ob])
                else:
                    S.emit(S.dve, lambda: nc.vector.tensor_copy(out=o_ap, in_=i_ap), reads=[po], writes=[ob])
            S.dma(S.pool, rows(nd[g], dil, r, n), ob.ap[:].rearrange("p h e -> p (h e)"), reads=[ob], chan_buf=ob)
        S.barrier()
        nds = [[C.sb(es, "ndl", [128, 8, 129], F32) for _ in range(3)] for _ in range(2)]
        rden = C.sb(es, "rden", [128, 8], F32)
        o16 = C.sb(es, "o16", [128, 1024], BF16)
        oT = C.sb(es, "oT", [128, 8, 128], BF16)

        def load2(c):
            S.dma(S.sp, xts[c % 2].ap[:], src[c * 128:(c + 1) * 128, :], writes=[xts[c % 2]], chan_buf=xts[c % 2])
            for g in range(3):
                b = nds[c % 2][g]
                S.dma(S.sp, b.ap[:].rearrange("p h e -> p (h e)"), nd[g][c * 128:(c + 1) * 128, :], writes=[b], chan_buf=b)

        load2(0)
        for c in range(NCH):
            if c + 1 < NCH:
                load2(c + 1)
            xt = xts[c % 2]
            a0, a1, a2 = nds[c % 2]
            S.emit(S.pool, lambda: nc.gpsimd.tensor_tensor(out=a0.ap[:], in0=a0.ap[:], in1=a1.ap[:], op=ALU.add), reads=[a0, a1], writes=[a0])
            S.emit(S.pool, lambda: nc.gpsimd.tensor_tensor(out=a0.ap[:], in0=a0.ap[:], in1=a2.ap[:], op=ALU.add), reads=[a0, a2], writes=[a0])
            S.emit(S.dve, lambda: nc.vector.reciprocal(out=rden.ap[:].unsqueeze(2), in_=a0.ap[:, :, 128:129]), reads=[a0], writes=[rden])
            S.emit(S.dve, lambda: nc.vector.tensor_tensor(out=o16.ap[:].rearrange("p (h e) -> p h e", h=8), in0=a0.ap[:, :, 0:128], in1=bc(rden.ap[:].unsqueeze(2), [128, 8, 128]), op=ALU.mult), reads=[a0, rden], writes=[o16])
            for h in range(8):
                S.emit(S.pe, lambda: nc.tensor.transpose(out=ptr.ap[:, h * 128:(h + 1) * 128], in_=o16.ap[:, h * 128:(h + 1) * 128], identity=K["ident"].ap[:]),
                       reads=[o16, K["ident"]], writes=[ptr], pe_acc=True, sig=(h == 7))
            S.emit(S.act, lambda: nc.scalar.copy(out=oT.ap[:], in_=ptr.ap[:, 0:1024].rearrange("p (k t) -> p k t", k=8)), reads=[ptr], writes=[oT])
            for n2 in range(2):
                pq = pmm[nmm % 2]
                nmm += 1
                for kk in range(8):
                    S.emit(S.pe, lambda: nc.tensor.matmul(pq.ap[:], lhsT=oT.ap[:, kk, :], rhs=wo.ap[:, kk, n2 * 512:(n2 + 1) * 512], start=(kk == 0), stop=(kk == 7)),
                           reads=[oT, wo], writes=[pq], pe_acc=True, sig=(kk == 7))
                S.emit(S.dve, lambda: nc.vector.tensor_tensor(out=xt.ap[:, n2 * 512:(n2 + 1) * 512], in0=pq.ap[:], in1=xt.ap[:, n2 * 512:(n2 + 1) * 512], op=ALU.add), reads=[pq, xt], writes=[xt])
            S.dma(S.pool, dst[c * 128:(c + 1) * 128, :], xt.ap[:], reads=[xt], chan_buf=xt)
        S.barrier()

def build(phases=("m", "f0", "a", "f1"), debug=False):
    nc = bass.Bass("TRN2", target_bir_lowering=False)
    I = {}

    def inp(name, shape):
        I[name] = nc.dram_tensor(name, shape, F32, kind="ExternalInput").ap()

    inp("x", [S_LEN, D])
    inp("a_norm", [D]); inp("ssm_w_in", [D, IN_DIM]); inp("ssm_conv_w", [4, XBC]); inp("ssm_conv_b", [XBC])
    inp("ssm_dt_bias", [32]); inp("ssm_a_log", [32]); inp("ssm_d", [32]); inp("ssm_norm", [D_IN]); inp("ssm_w_out", [D_IN, D])
    inp("kv_norm", [D]); inp("w_kv", [D, 1536]); inp("b_norm", [D]); inp("att_w_q", [D, 3072]); inp("att_w_o", [D, D])
    inp("ffn_norm", [2, D]); inp("ffn_w_up", [2, D, 2 * FF]); inp("ffn_conv_w", [2, 3, 2 * FF]); inp("ffn_w_down", [2, FF, D])
    inp("final_norm", [D]); inp("rope", [S_LEN, 32])
    out = nc.dram_tensor("out", [S_LEN, D], F32, kind="ExternalOutput").ap()
    kind = "ExternalOutput" if debug else "Internal"
    xa = nc.dram_tensor("xa", [S_LEN, D], F32, kind=kind).ap()
    xb = nc.dram_tensor("xb", [S_LEN, D], F32, kind=kind).ap()
    xc = nc.dram_tensor("xc", [S_LEN, D], F32, kind=kind).ap()
    with contextlib.ExitStack() as es:
        C = Ctx(nc, es)
        K = setup_consts(C, es)
        C.S.barrier()
        cur = I["x"]
        if "m" in phases:
            phase_mamba(C, K, I, cur, xa)
            cur = xa
        if "f0" in phases:
            phase_ffn(C, K, 0, cur, xb, I["ffn_w_up"][0], I["ffn_conv_w"][0], I["ffn_w_down"][0], I["ffn_norm"][0])
            cur = xb
        if "a" in phases:
            phase_attn(C, K, I, cur, xc)
            cur = xc
        if "f1" in phases:
            phase_ffn(C, K, 1, cur, out, I["ffn_w_up"][1], I["ffn_conv_w"][1], I["ffn_w_down"][1], I["ffn_norm"][1], final_gain=I["final_norm"])
        C.S.barrier()
    return nc


def rope_table():
    half = 16
    inv_freq = np.power(np.float32(500000.0), -np.arange(0, 32, 2, dtype=np.float32) / np.float32(32)).astype(np.float32)
    ang = (np.arange(S_LEN, dtype=np.float32)[:, None] * inv_freq[None, :]).astype(np.float32)
    return np.concatenate([np.cos(ang.astype(np.float64)), np.sin(ang.astype(np.float64))], axis=1).astype(np.float32)


def make_in_maps(inputs, n_cores):
    f = lambda a: np.ascontiguousarray(np.asarray(a, dtype=np.float32))
    shared = {
        "a_norm": f(inputs["a_norm"][0]), "ssm_w_in": f(inputs["ssm_w_in"][0]), "ssm_conv_w": f(inputs["ssm_conv_w"][0]),
        "ssm_conv_b": f(inputs["ssm_conv_b"][0]), "ssm_dt_bias": f(inputs["ssm_dt_bias"][0]), "ssm_a_log": f(inputs["ssm_a_log"][0]),
        "ssm_d": f(inputs["ssm_d"][0]), "ssm_norm": f(inputs["ssm_norm"][0]), "ssm_w_out": f(inputs["ssm_w_out"][0]),
        "kv_norm": f(inputs["kv_norm"]), "w_kv": f(inputs["w_kv"]), "b_norm": f(inputs["b_norm"][0]),
        "att_w_q": f(inputs["att_w_q"][0]), "att_w_o": f(inputs["att_w_o"][0]), "ffn_norm": f(inputs["ffn_norm"]),
        "ffn_w_up": f(inputs["ffn_w_up"]), "ffn_conv_w": f(inputs["ffn_conv_w"]), "ffn_w_down": f(inputs["ffn_w_down"]),
        "final_norm": f(inputs["final_norm"]), "rope": rope_table(),
    }
    x = f(inputs["x"])
    maps = []
    for c in range(n_cores):
        m = dict(shared)
        m["x"] = x[c]
        maps.append(m)
    return maps


def kernel(**inputs):
    nc = build()
    maps = make_in_maps(inputs, 8)
    res = run_bass_kernel_spmd(nc, maps, core_ids=list(range(8)))
    return np.stack([np.asarray(r["out"]) for r in res.results], axis=0).astype(np.float32)
```

```python
import contextlib
import numpy as np
import concourse.bass as bass
import concourse.mybir as mybir
from concourse.bass_utils import run_bass_kernel_spmd

F32 = mybir.dt.float32
BF16 = mybir.dt.bfloat16
AF = mybir.ActivationFunctionType
ALU = mybir.AluOpType

S_LEN = 4096
D = 1024
NCH = S_LEN // 128
FF = 2816
NF = FF // 128
D_IN = 2048
XBC = 4096
IN_DIM = 6176
NEG = -30000.0


class Buf:
    __slots__ = ("ap", "lw", "rd", "name", "chan", "chan_sw")

    def __init__(self, ap, name=""):
        self.ap = ap
        self.lw = None
        self.rd = {}
        self.name = name
        self.chan = None
        self.chan_sw = None


class Eng:
    def __init__(self, name, q, sem):
        self.name = name
        self.q = q
        self.sem = sem
        self.count = 0
        self.seen = {}


class Sched:
    def __init__(self, nc, es):
        self.nc = nc
        self.es = es
        mk = lambda n, q: Eng(n, q, es.enter_context(nc.semaphore("sem_" + n)))
        self.pe = mk("pe", nc.tensor)
        self.act = mk("act", nc.scalar)
        self.dve = mk("dve", nc.vector)
        self.pool = mk("pool", nc.gpsimd)
        self.sp = mk("sp", nc.sync)
        self.engs = [self.pe, self.act, self.dve, self.pool, self.sp]
        self.chans = []
        self.free_chans = []

    def new_chan(self, name):
        if self.free_chans:
            c = self.free_chans.pop()
            return c
        c = Eng("ch%d_%s" % (len(self.chans), name), None, self.es.enter_context(self.nc.semaphore("semch%d" % len(self.chans))))
        self.chans.append(c)
        return c

    def _waits(self, eng, reads, writes, pe_acc):
        waits = {}

        def need(dep):
            e, c = dep
            if eng.seen.get(e.name, 0) < c:
                if waits.get(e.name, (None, 0))[1] < c:
                    waits[e.name] = (e, c)

        for b in reads:
            if b.lw is not None:
                need(b.lw)
        for b in writes:
            if b.lw is not None and not (pe_acc and b.lw[0] is eng):
                need(b.lw)
            for dep in b.rd.values():
                need(dep)
        for e, c in waits.values():
            eng.q.wait_ge(e.sem, c)
            eng.seen[e.name] = c

    def emit(self, eng, fn, reads=(), writes=(), pe_acc=False, sig=True):
        self._waits(eng, reads, writes, pe_acc)
        ins = fn()
        if sig:
            eng.count += 1
            ins.then_inc(eng.sem, 1)
            cnt = eng.count
        else:
            cnt = eng.count + 1
        for b in reads:
            b.rd[eng.name] = (eng, cnt)
        for b in writes:
            b.lw = (eng, cnt)
            b.rd = {}
        return ins

    def dma(self, eng, out_ap, in_ap, reads=(), writes=(), chan_buf=None):
        if eng is self.pool:
            if chan_buf.chan_sw is None:
                chan_buf.chan_sw = self.new_chan(chan_buf.name + "_sw")
            ch = chan_buf.chan_sw
        else:
            if chan_buf.chan is None:
                chan_buf.chan = self.new_chan(chan_buf.name)
            ch = chan_buf.chan
        self._waits(eng, reads, writes, False)
        ins = eng.q.dma_start(out=out_ap, in_=in_ap)
        ch.count += 16
        ins.then_inc(ch.sem, 16)
        for b in reads:
            b.rd[ch.name] = (ch, ch.count)
        for b in writes:
            b.lw = (ch, ch.count)
            b.rd = {}
        return ins

    def barrier(self):
        allsrc = self.engs + self.chans
        for e in self.engs:
            for o in allsrc:
                if o is e or o.count == 0:
                    continue
                if e.seen.get(o.name, 0) < o.count:
                    e.q.wait_ge(o.sem, o.count)
                    e.seen[o.name] = o.count


class Ctx:
    def __init__(self, nc, es):
        self.nc = nc
        self.S = Sched(nc, es)
        self.n = 0
        self.rr = 0

    def sb(self, es, name, shape, dt):
        self.n += 1
        t = es.enter_context(self.nc.sbuf_tensor("%s_%d" % (name, self.n), shape, dt))
        return Buf(t, name)

    def ps(self, es, name, shape, dt):
        self.n += 1
        t = es.enter_context(self.nc.psum_tensor("%s_%d" % (name, self.n), shape, dt))
        return Buf(t, name)


def bc(ap, shape):
    return ap.broadcast_to(shape)


def setup_consts(C, es):
    nc, S = C.nc, C.S
    k = {}
    identf = C.sb(es, "identf", [128, 128], F32)
    ident = C.sb(es, "ident", [128, 128], BF16)
    S.emit(S.pool, lambda: nc.gpsimd.memset(identf.ap[:], 1.0), writes=[identf])
    S.emit(S.pool, lambda: nc.gpsimd.affine_select(out=identf.ap[:], in_=identf.ap[:], pattern=[[-1, 128]], compare_op=ALU.is_equal, fill=0.0, base=0, channel_multiplier=1), reads=[identf], writes=[identf])
    S.emit(S.dve, lambda: nc.vector.tensor_copy(out=ident.ap[:], in_=identf.ap[:]), reads=[identf], writes=[ident])
    mhalf = C.sb(es, "mhalf", [128, 8], F32)
    S.emit(S.pool, lambda: nc.gpsimd.memset(mhalf.ap[:], -0.5), writes=[mhalf])
    nmf = C.sb(es, "nmf", [128, 128], F32)
    nmc = C.sb(es, "nmc", [128, 512], BF16)
    nmp = C.sb(es, "nmp", [128, 512], BF16)
    S.emit(S.pool, lambda: nc.gpsimd.memset(nmf.ap[:], 0.0), writes=[nmf])
    S.emit(S.pool, lambda: nc.gpsimd.affine_select(out=nmf.ap[:], in_=nmf.ap[:], pattern=[[1, 128]], compare_op=ALU.is_ge, fill=NEG, base=0, channel_multiplier=-1), reads=[nmf], writes=[nmf])
    S.emit(S.dve, lambda: nc.vector.tensor_copy(out=nmc.ap[:].rearrange("p (h l) -> p h l", h=4), in_=bc(nmf.ap[:].unsqueeze(1), [128, 4, 128])), reads=[nmf], writes=[nmc])
    S.emit(S.pool, lambda: nc.gpsimd.memset(nmf.ap[:], 0.0), reads=[nmf], writes=[nmf])
    S.emit(S.pool, lambda: nc.gpsimd.affine_select(out=nmf.ap[:], in_=nmf.ap[:], pattern=[[-1, 128]], compare_op=ALU.is_ge, fill=NEG, base=0, channel_multiplier=1), reads=[nmf], writes=[nmf])
    S.emit(S.dve, lambda: nc.vector.tensor_copy(out=nmp.ap[:].rearrange("p (h l) -> p h l", h=4), in_=bc(nmf.ap[:].unsqueeze(1), [128, 4, 128])), reads=[nmf], writes=[nmp])
    k.update(identf=identf, ident=ident, mhalf=mhalf, nmc=nmc, nmp=nmp)
    return k


def prep_weight(C, es_stage, dst, src, nk, ncols, gain=None, col0=0):
    nc, S = C.nc, C.S
    stg = C.stg
    CW = 2048
    for kk in range(nk):
        for c0 in range(0, ncols, CW):
            cw = min(CW, ncols - c0)
            st = stg[C.rr % len(stg)]
            S.dma(S.sp, st.ap[:, 0:cw], src[kk * 128:(kk + 1) * 128, c0:c0 + cw], writes=[st], chan_buf=st)
            o = dst.ap[:, kk, col0 + c0:col0 + c0 + cw]
            which = C.rr % 3
            C.rr += 1
            rds = [st] + ([gain] if gain is not None else [])
            if gain is None:
                if which == 0:
                    S.emit(S.act, lambda: nc.scalar.copy(out=o, in_=st.ap[:, 0:cw]), reads=rds, writes=[dst])
                elif which == 1:
                    S.emit(S.dve, lambda: nc.vector.tensor_copy(out=o, in_=st.ap[:, 0:cw]), reads=rds, writes=[dst])
                else:
                    S.emit(S.pool, lambda: nc.gpsimd.tensor_copy(out=o, in_=st.ap[:, 0:cw]), reads=rds, writes=[dst])
            else:
                g = gain.ap[:, kk:kk + 1]
                if which == 0:
                    S.emit(S.act, lambda: nc.scalar.mul(out=o, in_=st.ap[:, 0:cw], mul=g), reads=rds, writes=[dst])
                elif which == 1:
                    S.emit(S.dve, lambda: nc.vector.tensor_scalar(out=o, in0=st.ap[:, 0:cw], scalar1=g, scalar2=None, op0=ALU.mult), reads=rds, writes=[dst])
                else:
                    S.emit(S.pool, lambda: nc.gpsimd.tensor_scalar(out=o, in0=st.ap[:, 0:cw], scalar1=g, scalar2=None, op0=ALU.mult), reads=rds, writes=[dst])


def load_cols(C, dst, src_vec, nk):
    S = C.S
    with C.nc.allow_non_contiguous_dma(reason="small param vector"):
        S.dma(S.sp, dst.ap[:, 0:nk], src_vec.rearrange("(k p) -> p k", p=128), writes=[dst], chan_buf=dst)


def rms_rstd(C, x_ap, xbuf, junk, ss, rstd, K, n_feat, eps):
    nc, S = C.nc, C.S
    S.emit(S.act, lambda: nc.scalar.activation(out=junk.ap[:, 0:n_feat], in_=x_ap, func=AF.Square, accum_out=ss.ap[:, 0:1]), reads=[xbuf], writes=[junk, ss])
    S.emit(S.dve, lambda: nc.vector.tensor_scalar(out=ss.ap[:, 0:1], in0=ss.ap[:, 0:1], scalar1=1.0 / n_feat, scalar2=eps, op0=ALU.mult, op1=ALU.add), reads=[ss], writes=[ss])
    S.emit(S.pool, lambda: nc.gpsimd.tensor_tensor(out=rstd.ap[:, 0:1], in0=ss.ap[:, 0:1], in1=K["mhalf"].ap[:, 0:1], op=ALU.pow), reads=[ss, K["mhalf"]], writes=[rstd])


def norm_transpose(C, K, x_ap, xbuf, junk, ss, rstd, xn, ptr, xnT_ap, xnT, extra=()):
    nc, S = C.nc, C.S
    norm_only(C, K, x_ap, xbuf, junk, ss, rstd, xn)
    transpose_only(C, K, xn, ptr, xnT_ap, xnT, extra)


def norm_only(C, K, x_ap, xbuf, junk, ss, rstd, xn):
    nc, S = C.nc, C.S
    rms_rstd(C, x_ap, xbuf, junk, ss, rstd, K, D, 1e-6)
    S.emit(S.act, lambda: nc.scalar.mul(out=xn.ap[:, 0:D], in_=x_ap, mul=rstd.ap[:, 0:1]), reads=[xbuf, rstd], writes=[xn])


def transpose_only(C, K, xn, ptr, xnT_ap, xnT, extra=()):
    nc, S = C.nc, C.S
    for kk in range(8):
        S.emit(S.pe, lambda: nc.tensor.transpose(out=ptr.ap[:, kk * 128:(kk + 1) * 128], in_=xn.ap[:, kk * 128:(kk + 1) * 128], identity=K["ident"].ap[:]),
               reads=[xn, K["ident"]], writes=[ptr] + list(extra), pe_acc=True, sig=(kk == 7))
    S.emit(S.dve, lambda: nc.vector.tensor_copy(out=xnT_ap, in_=ptr.ap[:, 0:1024].rearrange("p (k t) -> p k t", k=8)), reads=[ptr], writes=[xnT])


def phase_ffn(C, K, layer, src, dst, w_up, conv_w, w_dn, gain_vec, final_gain=None):
    nc, S = C.nc, C.S
    T = 256
    NT = S_LEN // T
    with contextlib.ExitStack() as es:
        wup = C.sb(es, "wup", [128, 8, 2 * FF], BF16)
        wdn = C.sb(es, "wdn", [128, NF, D], BF16)
        gain = C.sb(es, "gain", [128, 8], F32)
        cw = C.sb(es, "cw", [128, 2 * NF, 3], F32)
        load_cols(C, gain, gain_vec, 8)
        with nc.allow_non_contiguous_dma(reason="conv weights"):
            for kk in range(3):
                S.dma(S.sp, cw.ap[:, :, kk], conv_w[kk].rearrange("(j p) -> p j", p=128), writes=[cw], chan_buf=cw)
        with contextlib.ExitStack() as es2:
            C.stg = [C.sb(es2, "stg", [128, 2048], F32) for _ in range(3)]
            prep_weight(C, es2, wup, w_up, 8, 2 * FF, gain)
            prep_weight(C, es2, wdn, w_dn, NF, D, None)
            S.barrier()
        xts = [C.sb(es, "xt", [128, 2, D], F32) for _ in range(2)]
        xn = C.sb(es, "xn", [128, D], BF16)
        xnTs = [C.sb(es, "xnT", [128, 8, T + 2], BF16) for _ in range(2)]
        ss = C.sb(es, "ss", [128, 1], F32)
        rstd = C.sb(es, "rstd", [128, 1], F32)
        ags = [C.sb(es, "ag", [128, T], F32) for _ in range(3)]
        avs = [C.sb(es, "av", [128, T], F32) for _ in range(3)]
        sgs = [C.sb(es, "sg", [128, T], F32) for _ in range(3)]
        h2_t = es.enter_context(nc.sbuf_tensor("h2T_all%d" % layer, [128, NF, T], BF16))
        h2T = [Buf(h2_t[:, j, :], "h2T%d" % j) for j in range(NF)]
        if final_gain is not None:
            gfin = C.sb(es, "gfin", [128, D], F32)
            obuf = [C.sb(es, "obuf", [128, D], F32) for _ in range(2)]
            S.dma(S.sp, gfin.ap[:], final_gain.partition_broadcast(128), writes=[gfin], chan_buf=gfin)
        S.emit(S.pool, lambda: nc.gpsimd.memset(xnTs[0].ap[:, :, 0:2], 0.0), writes=[xnTs[0]])
        ptrs = [C.ps(es, "ptr", [128, 1024], BF16) for _ in range(2)]
        pus = [C.ps(es, "pu", [128, 512], F32) for _ in range(4)]
        pds = [C.ps(es, "pd", [128, 512], F32) for _ in range(2)]

        def load(i):
            xt = xts[i % 2]
            S.dma(S.sp, xt.ap[:], src[i * T:(i + 1) * T, :].rearrange("(c p) d -> p c d", p=128), writes=[xt], chan_buf=xt)

        xn2 = [xn, C.sb(es, "xnb", [128, D], BF16)]
        ss2 = [ss, C.sb(es, "ssb", [128, 1], F32)]
        rstd2 = [rstd, C.sb(es, "rstdb", [128, 1], F32)]

        def head_norm(i):
            xt = xts[i % 2]
            for c in range(2):
                norm_only(C, K, xt.ap[:, c, :], xt, xn2[c], ss2[c], rstd2[c], xn2[c])

        def head_tr(i):
            xnT = xnTs[i % 2]
            if i > 0:
                S.emit(S.pool, lambda: nc.gpsimd.tensor_copy(out=xnT.ap[:, :, 0:2], in_=xnTs[(i - 1) % 2].ap[:, :, T:T + 2]), reads=[xnTs[(i - 1) % 2]], writes=[xnT])
            for c in range(2):
                transpose_only(C, K, xn2[c], ptrs[c], xnT.ap[:, :, 2 + c * 128:2 + (c + 1) * 128], xnT)

        def head(i):
            head_norm(i)
            head_tr(i)

        load(0)
        head(0)
        npu = 0
        npd = 0
        nb = 0
        for i in range(NT):
            xt = xts[i % 2]
            xnT = xnTs[i % 2]
            if i + 1 < NT:
                load(i + 1)
            for j in range(NF):
                pg, pv = pus[npu % 4], pus[(npu + 1) % 4]
                npu += 2
                ag, av, sgb = ags[nb % 3], avs[nb % 3], sgs[nb % 3]
                nb += 1
                for (pp, col) in ((pg, j), (pv, NF + j)):
                    for kk in range(8):
                        S.emit(S.pe, lambda: nc.tensor.matmul(pp.ap[:, 0:T + 2], lhsT=wup.ap[:, kk, col * 128:(col + 1) * 128], rhs=xnT.ap[:, kk, :], start=(kk == 0), stop=(kk == 7)),
                               reads=[wup, xnT], writes=[pp], pe_acc=True, sig=(kk == 7))
                S.emit(S.act, lambda: nc.scalar.mul(out=ag.ap[:], in_=pg.ap[:, 2:T + 2], mul=cw.ap[:, j, 2:3]), reads=[pg, cw], writes=[ag])
                S.emit(S.act, lambda: nc.scalar.mul(out=av.ap[:], in_=pv.ap[:, 2:T + 2], mul=cw.ap[:, NF + j, 2:3]), reads=[pv, cw], writes=[av])
                for tap in (1, 0):
                    S.emit(S.dve, lambda: nc.vector.scalar_tensor_tensor(out=ag.ap[:], in0=pg.ap[:, tap:tap + T], scalar=cw.ap[:, j, tap:tap + 1], in1=ag.ap[:], op0=ALU.mult, op1=ALU.add), reads=[pg, cw, ag], writes=[ag])
                for tap in (1, 0):
                    S.emit(S.dve, lambda: nc.vector.scalar_tensor_tensor(out=av.ap[:], in0=pv.ap[:, tap:tap + T], scalar=cw.ap[:, NF + j, tap:tap + 1], in1=av.ap[:], op0=ALU.mult, op1=ALU.add), reads=[pv, cw, av], writes=[av])
                S.emit(S.act, lambda: nc.scalar.activation(out=sgb.ap[:], in_=ag.ap[:], func=AF.Silu), reads=[ag], writes=[sgb])
                S.emit(S.pool, lambda: nc.gpsimd.tensor_tensor(out=h2T[j].ap[:], in0=sgb.ap[:], in1=av.ap[:], op=ALU.mult), reads=[sgb, av], writes=[h2T[j]])
                if j == 3 and i + 1 < NT:
                    head_norm(i + 1)
                if j == 12 and i + 1 < NT:
                    head_tr(i + 1)
            for c in range(2):
                for n in range(2):
                    pd = pds[npd % 2]
                    npd += 1
                    for j in range(NF):
                        S.emit(S.pe, lambda: nc.tensor.matmul(pd.ap[:], lhsT=h2T[j].ap[:, c * 128:(c + 1) * 128], rhs=wdn.ap[:, j, n * 512:(n + 1) * 512], start=(j == 0), stop=(j == NF - 1)),
                               reads=[h2T[j], wdn], writes=[pd], pe_acc=True, sig=(j == NF - 1))
                    S.emit(S.dve, lambda: nc.vector.tensor_tensor(out=xt.ap[:, c, n * 512:(n + 1) * 512], in0=pd.ap[:], in1=xt.ap[:, c, n * 512:(n + 1) * 512], op=ALU.add), reads=[pd, xt], writes=[xt])
                if final_gain is not None:
                    ob = obuf[c]
                    rms_rstd(C, xt.ap[:, c, :], xt, xn, ss, rstd, K, D, 1e-6)
                    S.emit(S.dve, lambda: nc.vector.scalar_tensor_tensor(out=ob.ap[:], in0=xt.ap[:, c, :], scalar=rstd.ap[:, 0:1], in1=gfin.ap[:], op0=ALU.mult, op1=ALU.mult), reads=[xt, rstd, gfin], writes=[ob])
                    S.dma(S.pool, dst[i * T + c * 128:i * T + (c + 1) * 128, :], ob.ap[:], reads=[ob], chan_buf=ob)
            if final_gain is None:
                S.dma(S.pool, dst[i * T:(i + 1) * T, :].rearrange("(c p) d -> p c d", p=128), xt.ap[:], reads=[xt], chan_buf=xt)
        S.barrier()


def phase_mamba(C, K, I, src, dst):
    nc, S = C.nc, C.S
    ynd = nc.dram_tensor("ynd", [S_LEN, D_IN], BF16, kind="Internal").ap()
    ident, identf = K["ident"], K["identf"]
    T = 256
    NT = S_LEN // T
    with contextlib.ExitStack() as es:
        win = C.sb(es, "win", [128, 8, IN_DIM], BF16)
        gain = C.sb(es, "gain", [128, 8], F32)
        cw = C.sb(es, "cw", [128, 32, 4], F32)
        cb = C.sb(es, "cb", [128, 32], F32)
        dcol = C.sb(es, "dcol", [128, 16], F32)
        diagD = C.sb(es, "diagD", [128, 16, 128], BF16)
        dtb = C.sb(es, "dtb", [32, 1], F32)
        acol = C.sb(es, "acol", [32, 1], F32)
        ones32 = C.sb(es, "ones32", [32, 128], F32)
        load_cols(C, gain, I["a_norm"], 8)
        load_cols(C, cb, I["ssm_conv_b"], 32)
        with nc.allow_non_contiguous_dma(reason="small params"):
            for kk in range(4):
                S.dma(S.sp, cw.ap[:, :, kk], I["ssm_conv_w"][kk].rearrange("(j p) -> p j", p=128), writes=[cw], chan_buf=cw)
            S.dma(S.sp, dtb.ap[:, 0:1], I["ssm_dt_bias"].rearrange("(h o) -> h o", o=1), writes=[dtb], chan_buf=dtb)
            S.dma(S.sp, acol.ap[:, 0:1], I["ssm_a_log"].rearrange("(h o) -> h o", o=1), writes=[acol], chan_buf=acol)
            dv = I["ssm_d"].rearrange("(k two) -> two k", two=2)
            for hh in range(2):
                S.dma(S.sp, dcol.ap[hh * 64:(hh + 1) * 64, :], dv[hh].partition_broadcast(64), writes=[dcol], chan_buf=dcol)
        S.emit(S.act, lambda: nc.scalar.activation(out=acol.ap[:], in_=acol.ap[:], func=AF.Exp), reads=[acol], writes=[acol])
        S.emit(S.dve, lambda: nc.vector.tensor_scalar(out=acol.ap[:], in0=acol.ap[:], scalar1=-1.0, scalar2=None, op0=ALU.mult), reads=[acol], writes=[acol])
        S.emit(S.pool, lambda: nc.gpsimd.memset(ones32.ap[:], 1.0), writes=[ones32])
        for kk in range(16):
            S.emit(S.dve, lambda: nc.vector.tensor_scalar(out=diagD.ap[:, kk, :], in0=identf.ap[:], scalar1=dcol.ap[:, kk:kk + 1], scalar2=None, op0=ALU.mult), reads=[identf, dcol], writes=[diagD])
        with contextlib.ExitStack() as es2:
            C.stg = [C.sb(es2, "stg", [128, 2048], F32) for _ in range(3)]
            prep_weight(C, es2, win, I["ssm_w_in"], 8, IN_DIM, gain)
            S.barrier()
        xts = [C.sb(es, "xt", [128, 2, D], F32) for _ in range(2)]
        xn = C.sb(es, "xn", [128, D], BF16)
        xnT2s = [C.sb(es, "xnT2", [128, 8, T + 4], BF16) for _ in range(2)]
        ss = C.sb(es, "ss", [128, 1], F32)
        rstd = C.sb(es, "rstd", [128, 1], F32)
        szs = [C.sb(es, "sz", [128, 512], F32) for _ in range(2)]
        szeas = [C.sb(es, "szea", [128, 512], F32) for _ in range(2)]
        xbcT_t = es.enter_context(nc.sbuf_tensor("xbcT_all", [128, 32, T], BF16))
        xbcT = [Buf(xbcT_t[:, j, :], "xbcT%d" % j) for j in range(32)]
        accs = [C.sb(es, "acc", [128, T], F32) for _ in range(3)]
        xcgs = [C.sb(es, "xcg", [128, 256], BF16) for _ in range(3)]
        xcdgs = [C.sb(es, "xcdg", [128, 256], BF16) for _ in range(3)]
        Btgs = [C.sb(es, "Btg", [128, 128], BF16) for _ in range(3)]
        ADs = [C.sb(es, "AD", [32, 512], F32) for _ in range(2)]
        Egs = [C.sb(es, "Eg", [128, 512], F32) for _ in range(2)]
        MTs = [C.sb(es, "MT", [128, 512], BF16) for _ in range(2)]
        prevT_t = es.enter_context(nc.sbuf_tensor("prevT_all", [128, D_IN], F32))
        prevbf_t = es.enter_context(nc.sbuf_tensor("prevbf_all", [128, D_IN], BF16))
        prevT = [Buf(prevT_t[:, g * 256:(g + 1) * 256], "prevT%d" % g) for g in range(8)]
        prevbf = [Buf(prevbf_t[:, g * 256:(g + 1) * 256], "prevbf%d" % g) for g in range(8)]
        yts = [C.sb(es, "yt", [128, 256], F32) for _ in range(2)]
        yus = [C.sb(es, "yu", [128, 256], F32) for _ in range(2)]
        yzs = [C.sb(es, "yz", [128, 256], F32) for _ in range(4)]
        junkg = C.sb(es, "junkg", [128, 256], BF16)
        ssgs = [C.sb(es, "ssg", [128, 1], F32) for _ in range(4)]
        rgs = [C.sb(es, "rg", [128, 1], F32) for _ in range(4)]
        yns = [C.sb(es, "yn", [128, D_IN], BF16) for _ in range(2)]
        sm = {n: C.sb(es, n, [32, T], F32) for n in ("dtT", "adtT", "acsT", "nacsT", "decT")}
        sm["e1"] = sm["dtT"]
        sm["dtdecT"] = sm["decT"]
        ecol = C.sb(es, "ecol", [32, 2], F32)
        dgs = [C.sb(es, "dg", [32, 32], F32) for _ in range(2)]
        toks = [C.sb(es, "tok", [128, 160], F32) for _ in range(2)]
        for g in range(8):
            S.emit(S.pool, lambda: nc.gpsimd.memset(prevT[g].ap[:], 0.0), writes=[prevT[g]])
            S.emit(S.pool, lambda: nc.gpsimd.memset(prevbf[g].ap[:], 0.0), writes=[prevbf[g]])
        S.emit(S.pool, lambda: nc.gpsimd.memset(xnT2s[0].ap[:, :, 0:4], 0.0), writes=[xnT2s[0]])
        bkA = [C.ps(es, "bkA", [128, 1024], BF16) for _ in range(2)]
        pgen2 = [C.ps(es, "pgen", [128, 512], F32) for _ in range(2)]
        pzb1 = C.ps(es, "pzb", [128, 512], F32)
        pgen = pgen2 + [pzb1]
        pzb = [pzb1, pzb1]
        pys = [C.ps(es, "py", [128, 512], F32) for _ in range(2)]
        pst1 = C.ps(es, "pst", [128, 512], F32)
        psts = [pst1, pst1]
        psgs = [None] * 4
        cnt = {"g": 0, "tr": 0, "acc": 0}

        def nxt(lst, key):
            b = lst[cnt[key] % len(lst)]
            cnt[key] += 1
            return b

        def load(i):
            xt = xts[i % 2]
            S.dma(S.sp, xt.ap[:], src[i * T:(i + 1) * T, :].rearrange("(c p) d -> p c d", p=128), writes=[xt], chan_buf=xt)

        e1, dtT, adtT, acsT, nacsT, decT, dtdecT = [sm[n] for n in ("e1", "dtT", "adtT", "acsT", "nacsT", "decT", "dtdecT")]
        load(0)
        for i in range(NT):
            xt = xts[i % 2]
            xnT2 = xnT2s[i % 2]
            if i + 1 < NT:
                load(i + 1)
            for c in range(2):
                norm_transpose(C, K, xt.ap[:, c, :], xt, xn, ss, rstd, xn, bkA[c], xnT2.ap[:, :, 4 + c * 128:4 + (c + 1) * 128], xnT2)
            S.emit(S.pool, lambda: nc.gpsimd.tensor_copy(out=xnT2s[(i + 1) % 2].ap[:, :, 0:4], in_=xnT2.ap[:, :, T:T + 4]), reads=[xnT2], writes=[xnT2s[(i + 1) % 2]])
            pdt = nxt(pgen, "g")
            for kk in range(8):
                S.emit(S.pe, lambda: nc.tensor.matmul(pdt.ap[0:32, 0:T], lhsT=win.ap[:, kk, 6144:6176], rhs=xnT2.ap[:, kk, 4:T + 4], start=(kk == 0), stop=(kk == 7)),
                       reads=[win, xnT2], writes=[pdt], pe_acc=True, sig=(kk == 7))
            S.emit(S.act, lambda: nc.scalar.activation(out=e1.ap[:], in_=pdt.ap[0:32, 0:T], func=AF.Exp, bias=dtb.ap[:, 0:1]), reads=[pdt, dtb], writes=[e1])
            S.emit(S.act, lambda: nc.scalar.activation(out=dtT.ap[:], in_=e1.ap[:], func=AF.Ln, bias=1.0), reads=[e1], writes=[dtT])
            S.emit(S.dve, lambda: nc.vector.tensor_scalar(out=adtT.ap[:], in0=dtT.ap[:], scalar1=acol.ap[:, 0:1], scalar2=None, op0=ALU.mult), reads=[dtT, acol], writes=[adtT])
            for c in range(2):
                cs = slice(c * 128, (c + 1) * 128)
                S.emit(S.dve, lambda: nc.vector.tensor_tensor_scan(out=acsT.ap[:, cs], data0=ones32.ap[:], data1=adtT.ap[:, cs], initial=0.0, op0=ALU.mult, op1=ALU.add), reads=[ones32, adtT], writes=[acsT])
            S.emit(S.act, lambda: nc.scalar.mul(out=nacsT.ap[:], in_=acsT.ap[:], mul=-1.0), reads=[acsT], writes=[nacsT])
            for c in range(2):
                cs = slice(c * 128, (c + 1) * 128)
                last = acsT.ap[:, c * 128 + 127:c * 128 + 128]
                S.emit(S.act, lambda: nc.scalar.activation(out=decT.ap[:, cs], in_=acsT.ap[:, cs], func=AF.Exp, scale=-1.0, bias=last), reads=[acsT], writes=[decT])
                S.emit(S.act, lambda: nc.scalar.activation(out=ecol.ap[:, c:c + 1], in_=last, func=AF.Exp), reads=[acsT], writes=[ecol])
                S.emit(S.dve, lambda: nc.vector.tensor_scalar(out=dgs[c].ap[:], in0=identf.ap[0:32, 0:32], scalar1=ecol.ap[:, c:c + 1], scalar2=None, op0=ALU.mult), reads=[identf, ecol], writes=[dgs[c]])
            S.emit(S.dve, lambda: nc.vector.tensor_tensor(out=dtdecT.ap[:], in0=dtT.ap[:], in1=decT.ap[:], op=ALU.mult), reads=[dtT, decT], writes=[dtdecT])
            for c in range(2):
                cs = slice(c * 128, (c + 1) * 128)
                tok = toks[c]
                ptok = nxt(pgen, "g")
                for i4, (lh, rh) in enumerate(((dtT, None), (dtdecT, None), (nacsT, None), (ones32, dgs[c]))):
                    rhs_ap = identf.ap[0:32, 0:32] if rh is None else rh.ap[:]
                    lh_ap = lh.ap[:, cs] if rh is None else lh.ap[:]
                    S.emit(S.pe, lambda: nc.tensor.matmul(ptok.ap[:, i4 * 32:(i4 + 1) * 32], lhsT=lh_ap, rhs=rhs_ap, start=True, stop=True),
                           reads=[lh, identf] + ([rh] if rh is not None else []), writes=[ptok], pe_acc=True, sig=(i4 == 3))
                S.emit(S.dve, lambda: nc.vector.tensor_copy(out=tok.ap[:, 0:128], in_=ptok.ap[:, 0:128]), reads=[ptok], writes=[tok])
                S.emit(S.act, lambda: nc.scalar.activation(out=tok.ap[:, 128:160], in_=ptok.ap[:, 64:96], func=AF.Exp, scale=-1.0), reads=[ptok], writes=[tok])
            for j in range(32):
                pj = nxt(pgen, "g")
                acc = nxt(accs, "acc")
                for kk in range(8):
                    S.emit(S.pe, lambda: nc.tensor.matmul(pj.ap[:, 0:T + 4], lhsT=win.ap[:, kk, 2048 + j * 128:2048 + (j + 1) * 128], rhs=xnT2.ap[:, kk, :], start=(kk == 0), stop=(kk == 7)),
                           reads=[win, xnT2], writes=[pj], pe_acc=True, sig=(kk == 7))
                S.emit(S.act, lambda: nc.scalar.activation(out=acc.ap[:], in_=pj.ap[:, 4:T + 4], func=AF.Identity, scale=cw.ap[:, j, 3:4], bias=cb.ap[:, j:j + 1]), reads=[pj, cw, cb], writes=[acc])
                for tap in (2, 1, 0):
                    S.emit(S.dve, lambda: nc.vector.scalar_tensor_tensor(out=acc.ap[:], in0=pj.ap[:, 1 + tap:1 + tap + T], scalar=cw.ap[:, j, tap:tap + 1], in1=acc.ap[:], op0=ALU.mult, op1=ALU.add), reads=[pj, cw, acc], writes=[acc])
                S.emit(S.act, lambda: nc.scalar.activation(out=xbcT[j].ap[:], in_=acc.ap[:], func=AF.Silu), reads=[acc], writes=[xbcT[j]])
            def info(gi):
                c, g = gi // 8, gi % 8
                return c, g, slice(c * 128, (c + 1) * 128), toks[c]

            def st_ad(gi):
                c, g, cs, tok = info(gi)
                AD = ADs[gi % 2]
                S.emit(S.pool, lambda: nc.gpsimd.affine_select(out=AD.ap[:].rearrange("p (h l) -> p h l", h=4), in_=bc(acsT.ap[:, cs].unsqueeze(1), [32, 4, 128]), pattern=[[-1, 4], [0, 128]], compare_op=ALU.is_equal, fill=0.0, base=-4 * g, channel_multiplier=1), reads=[acsT], writes=[AD])

            def st0(gi):
                c, g, cs, tok = info(gi)
                bk = bkA[gi % 2]
                for jj, j in enumerate((2 * g, 2 * g + 1, 16 + g)):
                    S.emit(S.pe, lambda: nc.tensor.transpose(out=bk.ap[:, jj * 128:(jj + 1) * 128], in_=xbcT[j].ap[:, cs], identity=ident.ap[:]),
                           reads=[xbcT[j], ident], writes=[bk], pe_acc=True, sig=False)
                S.emit(S.pe, lambda: nc.tensor.matmul(bk.ap[:, 512:768].bitcast(F32), lhsT=xbcT[16 + g].ap[:, cs], rhs=xbcT[24 + g].ap[:, cs], start=True, stop=True), reads=[xbcT[16 + g], xbcT[24 + g]], writes=[bk], pe_acc=True)
                psg = pgen2[gi % 2]
                psgs[gi % 4] = psg
                AD = ADs[gi % 2]
                S.emit(S.pe, lambda: nc.tensor.matmul(psg.ap[:], lhsT=ones32.ap[:], rhs=AD.ap[:], start=True, stop=False), reads=[ones32, AD], writes=[psg], pe_acc=True, sig=False)
                S.emit(S.pe, lambda: nc.tensor.matmul(psg.ap[:], lhsT=ident.ap[:], rhs=K["nmc"].ap[:], start=False, stop=True), reads=[ident, K["nmc"]], writes=[psg], pe_acc=True)
                if g % 2 == 0:
                    pz = pzb[(gi // 2) % 2]
                    n = g // 2
                    for kk in range(8):
                        S.emit(S.pe, lambda: nc.tensor.matmul(pz.ap[:], lhsT=xnT2.ap[:, kk, 4 + c * 128:4 + (c + 1) * 128], rhs=win.ap[:, kk, n * 512:(n + 1) * 512], start=(kk == 0), stop=(kk == 7)),
                               reads=[xnT2, win], writes=[pz], pe_acc=True, sig=(kk == 7))

            def st1(gi):
                c, g, cs, tok = info(gi)
                ptr = bkA[gi % 2]
                psg = psgs[gi % 4]
                Eg = Egs[gi % 2]
                for h in range(4):
                    S.emit(S.act, lambda: nc.scalar.activation(out=Eg.ap[:, h * 128:(h + 1) * 128], in_=psg.ap[:, h * 128:(h + 1) * 128], func=AF.Exp, bias=tok.ap[:, 64 + 4 * g + h:64 + 4 * g + h + 1]), reads=[psg, tok], writes=[Eg])
                MT = MTs[gi % 2]
                S.emit(S.dve, lambda: nc.vector.tensor_tensor(out=MT.ap[:].rearrange("p (h l) -> p h l", h=4), in0=Eg.ap[:].rearrange("p (h l) -> p h l", h=4), in1=bc(ptr.ap[:, 512:768].bitcast(F32).unsqueeze(1), [128, 4, 128]), op=ALU.mult), reads=[Eg, ptr], writes=[MT])
                xcg, xcdg, Btg = xcgs[gi % 3], xcdgs[gi % 3], Btgs[gi % 3]
                pv = ptr.ap[:, 0:256].rearrange("p (h e) -> p h e", h=4)
                S.emit(S.dve, lambda: nc.vector.tensor_tensor(out=xcg.ap[:].rearrange("p (h e) -> p h e", h=4), in0=pv, in1=bc(tok.ap[:, 4 * g:4 * g + 4].unsqueeze(2), [128, 4, 64]), op=ALU.mult), reads=[ptr, tok], writes=[xcg])
                S.emit(S.dve, lambda: nc.vector.tensor_tensor(out=xcdg.ap[:].rearrange("p (h e) -> p h e", h=4), in0=pv, in1=bc(tok.ap[:, 32 + 4 * g:32 + 4 * g + 4].unsqueeze(2), [128, 4, 64]), op=ALU.mult), reads=[ptr, tok], writes=[xcdg])
                S.emit(S.act, lambda: nc.scalar.copy(out=Btg.ap[:], in_=ptr.ap[:, 256:384]), reads=[ptr], writes=[Btg])
                if g % 2 == 0:
                    pz = pzb[(gi // 2) % 2]
                    sz, szea = szs[(gi // 2) % 2], szeas[(gi // 2) % 2]
                    n = g // 2
                    S.emit(S.act, lambda: nc.scalar.activation(out=sz.ap[:], in_=pz.ap[:], func=AF.Tanh, scale=0.5), reads=[pz], writes=[sz])
                    S.emit(S.dve, lambda: nc.vector.scalar_tensor_tensor(out=sz.ap[:], in0=sz.ap[:], scalar=1.0, in1=pz.ap[:], op0=ALU.add, op1=ALU.mult), reads=[sz, pz], writes=[sz])
                    S.emit(S.pool, lambda: nc.gpsimd.tensor_tensor(out=szea.ap[:].rearrange("p (h e) -> p h e", h=8), in0=sz.ap[:].rearrange("p (h e) -> p h e", h=8), in1=bc(tok.ap[:, 128 + 8 * n:128 + 8 * n + 8].unsqueeze(2), [128, 8, 64]), op=ALU.mult), reads=[sz, tok], writes=[szea])
                pvw = prevT[g].ap[:]
                S.emit(S.pool, lambda: nc.gpsimd.tensor_tensor(out=pvw.rearrange("p (h e) -> p h e", h=4), in0=pvw.rearrange("p (h e) -> p h e", h=4), in1=bc(tok.ap[:, 96 + 4 * g:96 + 4 * g + 4].unsqueeze(2), [128, 4, 64]), op=ALU.mult), reads=[prevT[g], tok], writes=[prevT[g]])

            def st2(gi):
                c, g, cs, tok = info(gi)
                MT = MTs[gi % 2]
                xcg, xcdg, Btg = xcgs[gi % 3], xcdgs[gi % 3], Btgs[gi % 3]
                py = pys[gi % 2]
                pyr = py.ap[:, 0:256]
                pyo = py.ap[:, 256:512]
                for i2 in range(2):
                    S.emit(S.pe, lambda: nc.tensor.matmul(pyr[:, i2 * 128:(i2 + 1) * 128], lhsT=xbcT[2 * g + i2].ap[:, cs], rhs=diagD.ap[:, 2 * g + i2, :], start=True, stop=False),
                           reads=[xbcT[2 * g + i2], diagD], writes=[py], pe_acc=True, sig=False)
                    for hh in (2 * i2, 2 * i2 + 1):
                        S.emit(S.pe, lambda: nc.tensor.matmul(pyr[:, hh * 64:(hh + 1) * 64], lhsT=MT.ap[:, hh * 128:(hh + 1) * 128], rhs=xcg.ap[:, hh * 64:(hh + 1) * 64], start=False, stop=(hh % 2 == 1)),
                               reads=[MT, xcg], writes=[py], pe_acc=True, sig=False)
                S.emit(S.pe, lambda: nc.tensor.matmul(pyo, lhsT=xbcT[24 + g].ap[:, cs], rhs=prevbf[g].ap[:], start=True, stop=True), reads=[xbcT[24 + g], prevbf[g]], writes=[py], pe_acc=True)
                pst = psts[gi % 2]
                S.emit(S.pe, lambda: nc.tensor.matmul(pst.ap[:, 0:256], lhsT=Btg.ap[:], rhs=xcdg.ap[:], start=True, stop=True), reads=[Btg, xcdg], writes=[pst], pe_acc=True)

            def st3(gi):
                c, g, cs, tok = info(gi)
                sz, szea = szs[(gi // 2) % 2], szeas[(gi // 2) % 2]
                py = pys[gi % 2]
                yt, yu = yts[gi % 2], yus[gi % 2]
                gs = slice((g % 2) * 256, (g % 2 + 1) * 256)
                S.emit(S.dve, lambda: nc.vector.tensor_tensor(out=yt.ap[:], in0=py.ap[:, 256:512], in1=szea.ap[:, gs], op=ALU.mult), reads=[py, szea], writes=[yt])
                S.emit(S.dve, lambda: nc.vector.tensor_tensor(out=yu.ap[:], in0=py.ap[:, 0:256], in1=sz.ap[:, gs], op=ALU.mult), reads=[py, sz], writes=[yu])
                pst = psts[gi % 2]
                S.emit(S.dve, lambda: nc.vector.tensor_tensor(out=prevT[g].ap[:], in0=pst.ap[:, 0:256], in1=prevT[g].ap[:], op=ALU.add), reads=[pst, prevT[g]], writes=[prevT[g]])

            def st4(gi):
                c, g, cs, tok = info(gi)
                yt, yu, yz = yts[gi % 2], yus[gi % 2], yzs[gi % 4]
                S.emit(S.pool, lambda: nc.gpsimd.tensor_tensor(out=yz.ap[:], in0=yt.ap[:], in1=yu.ap[:], op=ALU.add), reads=[yt, yu], writes=[yz])
                S.emit(S.act, lambda: nc.scalar.copy(out=prevbf[g].ap[:], in_=prevT[g].ap[:]), reads=[prevT[g]], writes=[prevbf[g]])

            def st5(gi):
                yz, ssg = yzs[gi % 4], ssgs[gi % 4]
                S.emit(S.act, lambda: nc.scalar.activation(out=junkg.ap[:], in_=yz.ap[:], func=AF.Square, accum_out=ssg.ap[:, 0:1]), reads=[yz], writes=[junkg, ssg])

            def st6(gi):
                ssg, rg = ssgs[gi % 4], rgs[gi % 4]
                S.emit(S.pool, lambda: nc.gpsimd.tensor_scalar(out=ssg.ap[:, 0:1], in0=ssg.ap[:, 0:1], scalar1=1.0 / 256.0, scalar2=4e-5, op0=ALU.mult, op1=ALU.add), reads=[ssg], writes=[ssg])
                S.emit(S.pool, lambda: nc.gpsimd.tensor_tensor(out=rg.ap[:, 0:1], in0=ssg.ap[:, 0:1], in1=K["mhalf"].ap[:, 0:1], op=ALU.pow), reads=[ssg, K["mhalf"]], writes=[rg])

            def st7(gi):
                c, g, cs, tok = info(gi)
                yz, rg = yzs[gi % 4], rgs[gi % 4]
                yn = yns[c]
                S.emit(S.act, lambda: nc.scalar.mul(out=yn.ap[:, g * 256:(g + 1) * 256], in_=yz.ap[:], mul=rg.ap[:, 0:1]), reads=[yz, rg], writes=[yn])
                if g == 7:
                    S.dma(S.pool, ynd[i * T + c * 128:i * T + (c + 1) * 128, :], yn.ap[:], reads=[yn], chan_buf=yn)

            stages = [st_ad, st0, st1, st2, st3, st4, st5, st6, st7]
            NG = 16
            for t in range(NG + len(stages) - 1):
                for k in reversed(range(len(stages))):
                    gi = t - k
                    if 0 <= gi < NG:
                        stages[k](gi)
        S.barrier()
    with contextlib.ExitStack() as es:
        wout = C.sb(es, "wout", [128, 16, D], BF16)
        gout = C.sb(es, "gout", [128, 16], F32)
        load_cols(C, gout, I["ssm_norm"], 16)
        with contextlib.ExitStack() as es2:
            C.stg = [C.sb(es2, "stg", [128, 2048], F32) for _ in range(3)]
            prep_weight(C, es2, wout, I["ssm_w_out"], 16, D, gout)
            S.barrier()
        xts = [C.sb(es, "xt", [128, D], F32) for _ in range(2)]
        yls = [C.sb(es, "yl", [128, D_IN], BF16) for _ in range(2)]
        ynT = C.sb(es, "ynT", [128, 16, 128], BF16)
        ptrs = [C.ps(es, "ptr", [128, 1024], BF16) for _ in range(2)]
        pmm = [C.ps(es, "pmm", [128, 512], F32) for _ in range(2)]

        def load2(c):
            S.dma(S.sp, xts[c % 2].ap[:], src[c * 128:(c + 1) * 128, :], writes=[xts[c % 2]], chan_buf=xts[c % 2])
            S.dma(S.sp, yls[c % 2].ap[:], ynd[c * 128:(c + 1) * 128, :], writes=[yls[c % 2]], chan_buf=yls[c % 2])

        load2(0)
        nmm = 0
        for c in range(NCH):
            if c + 1 < NCH:
                load2(c + 1)
            xt, yl = xts[c % 2], yls[c % 2]
            for half in range(2):
                ptr = ptrs[half]
                for jj in range(8):
                    j = 8 * half + jj
                    S.emit(S.pe, lambda: nc.tensor.transpose(out=ptr.ap[:, jj * 128:(jj + 1) * 128], in_=yl.ap[:, j * 128:(j + 1) * 128], identity=ident.ap[:]),
                           reads=[yl, ident], writes=[ptr], pe_acc=True, sig=(jj == 7))
                if half == 0:
                    S.emit(S.act, lambda: nc.scalar.copy(out=ynT.ap[:, 0:8, :], in_=ptr.ap[:, 0:1024].rearrange("p (k t) -> p k t", k=8)), reads=[ptr], writes=[ynT])
                else:
                    S.emit(S.dve, lambda: nc.vector.tensor_copy(out=ynT.ap[:, 8:16, :], in_=ptr.ap[:, 0:1024].rearrange("p (k t) -> p k t", k=8)), reads=[ptr], writes=[ynT])
            for n2 in range(2):
                po = pmm[nmm % 2]
                nmm += 1
                for kk in range(16):
                    S.emit(S.pe, lambda: nc.tensor.matmul(po.ap[:], lhsT=ynT.ap[:, kk, :], rhs=wout.ap[:, kk, n2 * 512:(n2 + 1) * 512], start=(kk == 0), stop=(kk == 15)),
                           reads=[ynT, wout], writes=[po], pe_acc=True, sig=(kk == 15))
                S.emit(S.dve, lambda: nc.vector.tensor_tensor(out=xt.ap[:, n2 * 512:(n2 + 1) * 512], in0=po.ap[:], in1=xt.ap[:, n2 * 512:(n2 + 1) * 512], op=ALU.add), reads=[po, xt], writes=[xt])
            S.dma(S.pool, dst[c * 128:(c + 1) * 128, :], xt.ap[:], reads=[xt], chan_buf=xt)
        S.barrier()

ATT_PATTERNS = ((128, 1), (512, 4), (2048, 16))
ATT_MAXSTAGE = 7
ATT_DBG = 0


def rotary(C, src, dst, rp, tmps, nh):
    nc, S = C.nc, C.S
    sv = src.ap[:, 0:nh * 128].rearrange("p (h e) -> p h e", h=nh)
    dv = dst.ap[:, 0:nh * 128].rearrange("p (h e) -> p h e", h=nh)
    cos = bc(rp.ap[:, 0:16].unsqueeze(1), [128, nh, 16])
    sin = bc(rp.ap[:, 16:32].unsqueeze(1), [128, nh, 16])
    t1, t2, t3, t4 = [t.ap[:, 0:nh * 16].rearrange("p (h e) -> p h e", h=nh) for t in tmps]
    x1, x2 = sv[:, :, 0:16], sv[:, :, 16:32]
    S.emit(S.dve, lambda: nc.vector.tensor_tensor(out=t1, in0=x1, in1=cos, op=ALU.mult), reads=[src, rp], writes=[tmps[0]])
    S.emit(S.pool, lambda: nc.gpsimd.tensor_tensor(out=t2, in0=x2, in1=sin, op=ALU.mult), reads=[src, rp], writes=[tmps[1]])
    S.emit(S.dve, lambda: nc.vector.tensor_tensor(out=t3, in0=x2, in1=cos, op=ALU.mult), reads=[src, rp], writes=[tmps[2]])
    S.emit(S.pool, lambda: nc.gpsimd.tensor_tensor(out=t4, in0=x1, in1=sin, op=ALU.mult), reads=[src, rp], writes=[tmps[3]])
    S.emit(S.dve, lambda: nc.vector.tensor_tensor(out=dv[:, :, 0:16], in0=t1, in1=t2, op=ALU.subtract), reads=[tmps[0], tmps[1]], writes=[dst])
    S.emit(S.pool, lambda: nc.gpsimd.tensor_tensor(out=dv[:, :, 16:32], in0=t3, in1=t4, op=ALU.add), reads=[tmps[2], tmps[3]], writes=[dst])
    S.emit(S.act, lambda: nc.scalar.copy(out=dv[:, :, 32:128], in_=sv[:, :, 32:128]), reads=[src], writes=[dst])


def phase_attn(C, K, I, src, dst):
    nc, S = C.nc, C.S
    nd = [nc.dram_tensor("numden%d" % g, [S_LEN, 8 * 129], F32, kind="Internal").ap() for g in range(3)]
    rope = I["rope"]
    with contextlib.ExitStack() as es:
        wkv = C.sb(es, "wkv", [128, 8, 1536], BF16)
        wq = C.sb(es, "wq", [128, 8, 3072], BF16)
        wo = C.sb(es, "wo", [128, 8, D], BF16)
        gkv = C.sb(es, "gkv", [128, 8], F32)
        gq = C.sb(es, "gq", [128, 8], F32)
        load_cols(C, gkv, I["kv_norm"], 8)
        load_cols(C, gq, I["b_norm"], 8)
        with contextlib.ExitStack() as es2:
            C.stg = [C.sb(es2, "stg", [128, 2048], F32) for _ in range(3)]
            prep_weight(C, es2, wkv, I["w_kv"], 8, 1536, gkv)
            prep_weight(C, es2, wq, I["att_w_q"], 8, 3072, gq)
            prep_weight(C, es2, wo, I["att_w_o"], 8, D, None)
            S.barrier()
        xts = [C.sb(es, "xt", [128, D], F32) for _ in range(3)]
        rps = [C.sb(es, "rp", [128, 32], F32) for _ in range(8)]
        xns = [C.sb(es, "xn", [128, D], BF16) for _ in range(2)]
        xnTs = [C.sb(es, "xnT", [128, 8, 128], BF16) for _ in range(2)]
        sss = [C.sb(es, "ss", [128, 1], F32) for _ in range(2)]
        rstds = [C.sb(es, "rstd", [128, 1], F32) for _ in range(2)]
        qfs = [C.sb(es, "qf", [128, 1024], F32) for _ in range(2)]
        qbs = [C.sb(es, "qb", [128, 1024], BF16) for _ in range(2)]
        kfs = [C.sb(es, "kf", [128, 256], F32) for _ in range(2)]
        kbs = [C.sb(es, "kb", [128, 256], BF16) for _ in range(2)]
        tq = [C.sb(es, "tq", [128, 128], F32) for _ in range(4)]
        tk = [C.sb(es, "tk", [128, 32], F32) for _ in range(4)]
        QTs = [C.sb(es, "QT", [128, 8, 128], BF16) for _ in range(3)]
        KTs = [C.sb(es, "KT", [128, 2, 128], BF16) for _ in range(4)]
        vaugs = [C.sb(es, "vaug", [128, 2, 130], BF16) for _ in range(6)]
        PTs = [C.sb(es, "PT", [128, 512], BF16) for _ in range(8)]
        obs = [C.sb(es, "ob", [128, 8, 129], F32) for _ in range(2)]
        for v in vaugs:
            S.emit(S.pool, lambda: nc.gpsimd.memset(v.ap[:], 1.0), writes=[v])
        ptr = C.ps(es, "ptr", [128, 1024], BF16)
        pmm = [C.ps(es, "pmm", [128, 512], F32) for _ in range(2)]
        pS = [C.ps(es, "pS", [128, 512], F32) for _ in range(2)]
        po = C.ps(es, "po", [128, 3 * 512], F32)
        scale = 1.0 / np.sqrt(128.0)

        blocks = []
        for g, (win, dil) in enumerate(ATT_PATTERNS):
            for r in range(dil):
                for n in range(S_LEN // dil // 128):
                    blocks.append((g, dil, r, n))
        NB = len(blocks)

        def rows(ap, dil, r, n):
            return ap.rearrange("(m dd) c -> dd m c", dd=dil)[r][n * 128:(n + 1) * 128, :]

        def load(bi):
            g, dil, r, n = blocks[bi]
            S.dma(S.sp, xts[bi % 3].ap[:], rows(src, dil, r, n), writes=[xts[bi % 3]], chan_buf=xts[bi % 3])
            S.dma(S.sp, rps[bi % 8].ap[:], rows(rope, dil, r, n), writes=[rps[bi % 8]], chan_buf=rps[bi % 8])

        cnt = {"mm": 0, "S": 0}

        def A1(bi):
            xt = xts[bi % 3]
            norm_only(C, K, xt.ap[:], xt, xns[bi % 2], sss[bi % 2], rstds[bi % 2], xns[bi % 2])

        def A2(bi):
            transpose_only(C, K, xns[bi % 2], ptr, xnTs[bi % 2].ap[:], xnTs[bi % 2])

        def A3(bi):
            g, dil, r, n = blocks[bi]
            xnT, qf, kf, vaug = xnTs[bi % 2], qfs[bi % 2], kfs[bi % 2], vaugs[bi % 6]
            for n2 in range(2):
                pq = pmm[cnt["mm"] % 2]
                cnt["mm"] += 1
                for kk in range(8):
                    S.emit(S.pe, lambda: nc.tensor.matmul(pq.ap[:], lhsT=xnT.ap[:, kk, :], rhs=wq.ap[:, kk, g * 1024 + n2 * 512:g * 1024 + (n2 + 1) * 512], start=(kk == 0), stop=(kk == 7)),
                           reads=[xnT, wq], writes=[pq], pe_acc=True, sig=(kk == 7))
                S.emit(S.act, lambda: nc.scalar.copy(out=qf.ap[:, n2 * 512:(n2 + 1) * 512], in_=pq.ap[:]), reads=[pq], writes=[qf])
            if ATT_DBG == 1:
                return
            pkv = pmm[cnt["mm"] % 2]
            cnt["mm"] += 1
            for half, c0 in ((0, g * 256), (1, 768 + g * 256)):
                for kk in range(8):
                    S.emit(S.pe, lambda: nc.tensor.matmul(pkv.ap[:, half * 256:(half + 1) * 256], lhsT=xnT.ap[:, kk, :], rhs=wkv.ap[:, kk, c0:c0 + 256], start=(kk == 0), stop=(kk == 7)),
                           reads=[xnT, wkv], writes=[pkv], pe_acc=True, sig=(kk == 7 and half == 1))
            S.emit(S.act, lambda: nc.scalar.copy(out=kf.ap[:], in_=pkv.ap[:, 0:256]), reads=[pkv], writes=[kf])
            if ATT_DBG == 2:
                return
            for hv in range(2):
                S.emit(S.act, lambda: nc.scalar.copy(out=vaug.ap[:, hv, 0:128], in_=pkv.ap[:, 256 + hv * 128:256 + (hv + 1) * 128]), reads=[pkv], writes=[vaug])

        def A4(bi):
            rp = rps[bi % 8]
            rotary(C, qfs[bi % 2], qbs[bi % 2], rp, tq, 8)
            rotary(C, kfs[bi % 2], kbs[bi % 2], rp, tk, 2)

        def A5(bi):
            qb, kb, QT, KT = qbs[bi % 2], kbs[bi % 2], QTs[bi % 3], KTs[bi % 4]
            for h in range(8):
                S.emit(S.pe, lambda: nc.tensor.transpose(out=ptr.ap[:, h * 128:(h + 1) * 128], in_=qb.ap[:, h * 128:(h + 1) * 128], identity=K["ident"].ap[:]),
                       reads=[qb, K["ident"]], writes=[ptr], pe_acc=True, sig=(h == 7))
            S.emit(S.dve, lambda: nc.vector.tensor_copy(out=QT.ap[:], in_=ptr.ap[:, 0:1024].rearrange("p (k t) -> p k t", k=8)), reads=[ptr], writes=[QT])
            for h in range(2):
                S.emit(S.pe, lambda: nc.tensor.transpose(out=ptr.ap[:, h * 128:(h + 1) * 128], in_=kb.ap[:, h * 128:(h + 1) * 128], identity=K["ident"].ap[:]),
                       reads=[kb, K["ident"]], writes=[ptr], pe_acc=True, sig=(h == 1))
            S.emit(S.act, lambda: nc.scalar.copy(out=KT.ap[:], in_=ptr.ap[:, 0:256].rearrange("p (k t) -> p k t", k=2)), reads=[ptr], writes=[KT])

        def kb_list(bi):
            g, dil, r, n = blocks[bi]
            kbl = []
            if n > 0:
                kbl.append((KTs[(bi - 1) % 4], vaugs[(bi - 1) % 6], K["nmp"]))
            kbl.append((KTs[bi % 4], vaugs[bi % 6], K["nmc"]))
            return kbl

        def B1(bi):
            QT = QTs[bi % 3]
            kbl = kb_list(bi)
            for jk in range(2):
                for i2, (kt, va, nm) in enumerate(kbl):
                    ps_ = pS[cnt["S"] % 2]
                    cnt["S"] += 1
                    pt_ = PTs[(4 * bi + 2 * jk + i2) % 8]
                    S.emit(S.pe, lambda: nc.tensor.matmul(ps_.ap[:], lhsT=kt.ap[:, jk, :], rhs=QT.ap[:, 4 * jk:4 * jk + 4, :], start=True, stop=False),
                           reads=[kt, QT], writes=[ps_], pe_acc=True, sig=False)
                    S.emit(S.pe, lambda: nc.tensor.matmul(ps_.ap[:], lhsT=K["ident"].ap[:], rhs=nm.ap[:], start=False, stop=True),
                           reads=[K["ident"], nm], writes=[ps_], pe_acc=True)
                    S.emit(S.act, lambda: nc.scalar.activation(out=pt_.ap[:], in_=ps_.ap[:], func=AF.Exp, scale=float(scale)), reads=[ps_], writes=[pt_])

        def B2(bi):
            g, dil, r, n = blocks[bi]
            kbl = kb_list(bi)
            ob = obs[bi % 2]
            for jk in range(2):
                for hl in range(4):
                    h = 4 * jk + hl
                    oslice = po.ap[:, (h // 3) * 512 + (h % 3) * 129:(h // 3) * 512 + (h % 3) * 129 + 129]
                    for i2, (kt, va, nm) in enumerate(kbl):
                        pt_ = PTs[(4 * bi + 2 * jk + i2) % 8]
                        S.emit(S.pe, lambda: nc.tensor.matmul(oslice, lhsT=pt_.ap[:, hl * 128:(hl + 1) * 128], rhs=va.ap[:, jk, 0:129], start=(i2 == 0), stop=(i2 == len(kbl) - 1)),
                               reads=[pt_, va], writes=[po], pe_acc=True, sig=(i2 == len(kbl) - 1 and hl == 3))
            for b3 in range(3):
                nh = 3 if b3 < 2 else 2
                o_ap = ob.ap[:, b3 * 3:b3 * 3 + nh, :]
                i_ap = po.ap[:, b3 * 512:b3 * 512 + nh * 129].rearrange("p (h e) -> p h e", h=nh)
                if b3 != 1:
                    S.emit(S.act, lambda: nc.scalar.copy(out=o_ap, in_=i_ap), reads=[po], writes=[ob])
                else:
                    S.emit(S.dve, lambda: nc.vector.tensor_copy(out=o_ap, in_=i_ap), reads=[po], writes=[ob])
            S.dma(S.pool, rows(nd[g], dil, r, n), ob.ap[:].rearrange("p h e -> p (h e)"), reads=[ob], chan_buf=ob)

        stages = [A1, A2, A3, A4, A5, B1, B2][:ATT_MAXSTAGE]
        load(0)
        load(1)
        for t in range(NB + len(stages) - 1):
            if t + 2 < NB:
                load(t + 2)
            for k in reversed(range(len(stages))):
                bi = t - k
                if 0 <= bi < NB:
                    stages[k](bi)
        S.barrier()
        nmm = cnt["mm"]
        nds = [[C.sb(es, "ndl", [128, 8, 129], F32) for _ in range(3)] for _ in range(2)]
        rden = C.sb(es, "rden", [128, 8], F32)
        o16 = C.sb(es, "o16", [128, 1024], BF16)
        oT = C.sb(es, "oT", [128, 8, 128], BF16)

        def load2(c):
            S.dma(S.sp, xts[c % 2].ap[:], src[c * 128:(c + 1) * 128, :], writes=[xts[c % 2]], chan_buf=xts[c % 2])
            for g in range(3):
                b = nds[c % 2][g]
                S.dma(S.sp, b.ap[:].rearrange("p h e -> p (h e)"), nd[g][c * 128:(c + 1) * 128, :], writes=[b], chan_buf=b)

        load2(0)
        for c in range(NCH):
            if c + 1 < NCH:
                load2(c + 1)
            xt = xts[c % 2]
            a0, a1, a2 = nds[c % 2]
            S.emit(S.pool, lambda: nc.gpsimd.tensor_tensor(out=a0.ap[:], in0=a0.ap[:], in1=a1.ap[:], op=ALU.add), reads=[a0, a1], writes=[a0])
            S.emit(S.pool, lambda: nc.gpsimd.tensor_tensor(out=a0.ap[:], in0=a0.ap[:], in1=a2.ap[:], op=ALU.add), reads=[a0, a2], writes=[a0])
            S.emit(S.dve, lambda: nc.vector.reciprocal(out=rden.ap[:].unsqueeze(2), in_=a0.ap[:, :, 128:129]), reads=[a0], writes=[rden])
            S.emit(S.dve, lambda: nc.vector.tensor_tensor(out=o16.ap[:].rearrange("p (h e) -> p h e", h=8), in0=a0.ap[:, :, 0:128], in1=bc(rden.ap[:].unsqueeze(2), [128, 8, 128]), op=ALU.mult), reads=[a0, rden], writes=[o16])
            for h in range(8):
                S.emit(S.pe, lambda: nc.tensor.transpose(out=ptr.ap[:, h * 128:(h + 1) * 128], in_=o16.ap[:, h * 128:(h + 1) * 128], identity=K["ident"].ap[:]),
                       reads=[o16, K["ident"]], writes=[ptr], pe_acc=True, sig=(h == 7))
            S.emit(S.act, lambda: nc.scalar.copy(out=oT.ap[:], in_=ptr.ap[:, 0:1024].rearrange("p (k t) -> p k t", k=8)), reads=[ptr], writes=[oT])
            for n2 in range(2):
                pq = pmm[nmm % 2]
                nmm += 1
                for kk in range(8):
                    S.emit(S.pe, lambda: nc.tensor.matmul(pq.ap[:], lhsT=oT.ap[:, kk, :], rhs=wo.ap[:, kk, n2 * 512:(n2 + 1) * 512], start=(kk == 0), stop=(kk == 7)),
                           reads=[oT, wo], writes=[pq], pe_acc=True, sig=(kk == 7))
                S.emit(S.dve, lambda: nc.vector.tensor_tensor(out=xt.ap[:, n2 * 512:(n2 + 1) * 512], in0=pq.ap[:], in1=xt.ap[:, n2 * 512:(n2 + 1) * 512], op=ALU.add), reads=[pq, xt], writes=[xt])
            S.dma(S.pool, dst[c * 128:(c + 1) * 128, :], xt.ap[:], reads=[xt], chan_buf=xt)
        S.barrier()

def build(phases=("m", "f0", "a", "f1"), debug=False):
    nc = bass.Bass("TRN2", target_bir_lowering=False)
    I = {}

    def inp(name, shape):
        I[name] = nc.dram_tensor(name, shape, F32, kind="ExternalInput").ap()

    inp("x", [S_LEN, D])
    inp("a_norm", [D]); inp("ssm_w_in", [D, IN_DIM]); inp("ssm_conv_w", [4, XBC]); inp("ssm_conv_b", [XBC])
    inp("ssm_dt_bias", [32]); inp("ssm_a_log", [32]); inp("ssm_d", [32]); inp("ssm_norm", [D_IN]); inp("ssm_w_out", [D_IN, D])
    inp("kv_norm", [D]); inp("w_kv", [D, 1536]); inp("b_norm", [D]); inp("att_w_q", [D, 3072]); inp("att_w_o", [D, D])
    inp("ffn_norm", [2, D]); inp("ffn_w_up", [2, D, 2 * FF]); inp("ffn_conv_w", [2, 3, 2 * FF]); inp("ffn_w_down", [2, FF, D])
    inp("final_norm", [D]); inp("rope", [S_LEN, 32])
    out = nc.dram_tensor("out", [S_LEN, D], F32, kind="ExternalOutput").ap()
    kind = "ExternalOutput" if debug else "Internal"
    xa = nc.dram_tensor("xa", [S_LEN, D], F32, kind=kind).ap()
    xb = nc.dram_tensor("xb", [S_LEN, D], F32, kind=kind).ap()
    xc = nc.dram_tensor("xc", [S_LEN, D], F32, kind=kind).ap()
    with contextlib.ExitStack() as es:
        C = Ctx(nc, es)
        K = setup_consts(C, es)
        C.S.barrier()
        cur = I["x"]
        if "m" in phases:
            phase_mamba(C, K, I, cur, xa)
            cur = xa
        if "f0" in phases:
            phase_ffn(C, K, 0, cur, xb, I["ffn_w_up"][0], I["ffn_conv_w"][0], I["ffn_w_down"][0], I["ffn_norm"][0])
            cur = xb
        if "a" in phases:
            phase_attn(C, K, I, cur, xc)
            cur = xc
        if "f1" in phases:
            phase_ffn(C, K, 1, cur, out, I["ffn_w_up"][1], I["ffn_conv_w"][1], I["ffn_w_down"][1], I["ffn_norm"][1], final_gain=I["final_norm"])
        C.S.barrier()
    return nc


def rope_table():
    half = 16
    inv_freq = np.power(np.float32(500000.0), -np.arange(0, 32, 2, dtype=np.float32) / np.float32(32)).astype(np.float32)
    ang = (np.arange(S_LEN, dtype=np.float32)[:, None] * inv_freq[None, :]).astype(np.float32)
    return np.concatenate([np.cos(ang.astype(np.float64)), np.sin(ang.astype(np.float64))], axis=1).astype(np.float32)


def make_in_maps(inputs, n_cores):
    f = lambda a: np.ascontiguousarray(np.asarray(a, dtype=np.float32))
    shared = {
        "a_norm": f(inputs["a_norm"][0]), "ssm_w_in": f(inputs["ssm_w_in"][0]), "ssm_conv_w": f(inputs["ssm_conv_w"][0]),
        "ssm_conv_b": f(inputs["ssm_conv_b"][0]), "ssm_dt_bias": f(inputs["ssm_dt_bias"][0]), "ssm_a_log": f(inputs["ssm_a_log"][0]),
        "ssm_d": f(inputs["ssm_d"][0]), "ssm_norm": f(inputs["ssm_norm"][0]), "ssm_w_out": f(inputs["ssm_w_out"][0]),
        "kv_norm": f(inputs["kv_norm"]), "w_kv": f(inputs["w_kv"]), "b_norm": f(inputs["b_norm"][0]),
        "att_w_q": f(inputs["att_w_q"][0]), "att_w_o": f(inputs["att_w_o"][0]), "ffn_norm": f(inputs["ffn_norm"]),
        "ffn_w_up": f(inputs["ffn_w_up"]), "ffn_conv_w": f(inputs["ffn_conv_w"]), "ffn_w_down": f(inputs["ffn_w_down"]),
        "final_norm": f(inputs["final_norm"]), "rope": rope_table(),
    }
    x = f(inputs["x"])
    maps = []
    for c in range(n_cores):
        m = dict(shared)
        m["x"] = x[c]
        maps.append(m)
    return maps


def kernel(**inputs):
    nc = build()
    maps = make_in_maps(inputs, 8)
    res = run_bass_kernel_spmd(nc, maps, core_ids=list(range(8)))
    return np.stack([np.asarray(r["out"]) for r in res.results], axis=0).astype(np.float32)
```

```python
import contextlib
import numpy as np
import concourse.bass as bass
import concourse.mybir as mybir
from concourse.bass_utils import run_bass_kernel_spmd

F32 = mybir.dt.float32
BF16 = mybir.dt.bfloat16
AF = mybir.ActivationFunctionType
ALU = mybir.AluOpType

S_LEN = 4096
D = 1024
NCH = S_LEN // 128
FF = 2816
NF = FF // 128
D_IN = 2048
XBC = 4096
IN_DIM = 6176
NEG = -30000.0


class Buf:
    __slots__ = ("ap", "lw", "rd", "name", "chan", "chan_sw")

    def __init__(self, ap, name=""):
        self.ap = ap
        self.lw = None
        self.rd = {}
        self.name = name
        self.chan = None
        self.chan_sw = None


class Eng:
    def __init__(self, name, q, sem):
        self.name = name
        self.q = q
        self.sem = sem
        self.count = 0
        self.seen = {}


class Sched:
    def __init__(self, nc, es):
        self.nc = nc
        self.es = es
        mk = lambda n, q: Eng(n, q, es.enter_context(nc.semaphore("sem_" + n)))
        self.pe = mk("pe", nc.tensor)
        self.act = mk("act", nc.scalar)
        self.dve = mk("dve", nc.vector)
        self.pool = mk("pool", nc.gpsimd)
        self.sp = mk("sp", nc.sync)
        self.engs = [self.pe, self.act, self.dve, self.pool, self.sp]
        self.chans = []
        self.free_chans = []

    def new_chan(self, name):
        if self.free_chans:
            c = self.free_chans.pop()
            return c
        c = Eng("ch%d_%s" % (len(self.chans), name), None, self.es.enter_context(self.nc.semaphore("semch%d" % len(self.chans))))
        self.chans.append(c)
        return c

    def _waits(self, eng, reads, writes, pe_acc):
        waits = {}

        def need(dep):
            e, c = dep
            if eng.seen.get(e.name, 0) < c:
                if waits.get(e.name, (None, 0))[1] < c:
                    waits[e.name] = (e, c)

        for b in reads:
            if b.lw is not None:
                need(b.lw)
        for b in writes:
            if b.lw is not None and not (pe_acc and b.lw[0] is eng):
                need(b.lw)
            for dep in b.rd.values():
                need(dep)
        for e, c in waits.values():
            eng.q.wait_ge(e.sem, c)
            eng.seen[e.name] = c

    def emit(self, eng, fn, reads=(), writes=(), pe_acc=False, sig=True):
        self._waits(eng, reads, writes, pe_acc)
        ins = fn()
        if sig:
            eng.count += 1
            ins.then_inc(eng.sem, 1)
            cnt = eng.count
        else:
            cnt = eng.count + 1
        for b in reads:
            b.rd[eng.name] = (eng, cnt)
        for b in writes:
            b.lw = (eng, cnt)
            b.rd = {}
        return ins

    def dma(self, eng, out_ap, in_ap, reads=(), writes=(), chan_buf=None):
        if eng is self.pool:
            if chan_buf.chan_sw is None:
                chan_buf.chan_sw = self.new_chan(chan_buf.name + "_sw")
            ch = chan_buf.chan_sw
        else:
            if chan_buf.chan is None:
                chan_buf.chan = self.new_chan(chan_buf.name)
            ch = chan_buf.chan
        self._waits(eng, reads, writes, False)
        ins = eng.q.dma_start(out=out_ap, in_=in_ap)
        ch.count += 16
        ins.then_inc(ch.sem, 16)
        for b in reads:
            b.rd[ch.name] = (ch, ch.count)
        for b in writes:
            b.lw = (ch, ch.count)
            b.rd = {}
        return ins

    def barrier(self):
        allsrc = self.engs + self.chans
        for e in self.engs:
            for o in allsrc:
                if o is e or o.count == 0:
                    continue
                if e.seen.get(o.name, 0) < o.count:
                    e.q.wait_ge(o.sem, o.count)
                    e.seen[o.name] = o.count


class Ctx:
    def __init__(self, nc, es):
        self.nc = nc
        self.S = Sched(nc, es)
        self.n = 0
        self.rr = 0

    def sb(self, es, name, shape, dt):
        self.n += 1
        t = es.enter_context(self.nc.sbuf_tensor("%s_%d" % (name, self.n), shape, dt))
        return Buf(t, name)

    def ps(self, es, name, shape, dt):
        self.n += 1
        t = es.enter_context(self.nc.psum_tensor("%s_%d" % (name, self.n), shape, dt))
        return Buf(t, name)


def bc(ap, shape):
    return ap.broadcast_to(shape)


def setup_consts(C, es):
    nc, S = C.nc, C.S
    k = {}
    identf = C.sb(es, "identf", [128, 128], F32)
    ident = C.sb(es, "ident", [128, 128], BF16)
    S.emit(S.pool, lambda: nc.gpsimd.memset(identf.ap[:], 1.0), writes=[identf])
    S.emit(S.pool, lambda: nc.gpsimd.affine_select(out=identf.ap[:], in_=identf.ap[:], pattern=[[-1, 128]], compare_op=ALU.is_equal, fill=0.0, base=0, channel_multiplier=1), reads=[identf], writes=[identf])
    S.emit(S.dve, lambda: nc.vector.tensor_copy(out=ident.ap[:], in_=identf.ap[:]), reads=[identf], writes=[ident])
    mhalf = C.sb(es, "mhalf", [128, 8], F32)
    S.emit(S.pool, lambda: nc.gpsimd.memset(mhalf.ap[:], -0.5), writes=[mhalf])
    nmf = C.sb(es, "nmf", [128, 128], F32)
    nmc = C.sb(es, "nmc", [128, 512], BF16)
    nmp = C.sb(es, "nmp", [128, 512], BF16)
    S.emit(S.pool, lambda: nc.gpsimd.memset(nmf.ap[:], 0.0), writes=[nmf])
    S.emit(S.pool, lambda: nc.gpsimd.affine_select(out=nmf.ap[:], in_=nmf.ap[:], pattern=[[1, 128]], compare_op=ALU.is_ge, fill=NEG, base=0, channel_multiplier=-1), reads=[nmf], writes=[nmf])
    S.emit(S.dve, lambda: nc.vector.tensor_copy(out=nmc.ap[:].rearrange("p (h l) -> p h l", h=4), in_=bc(nmf.ap[:].unsqueeze(1), [128, 4, 128])), reads=[nmf], writes=[nmc])
    S.emit(S.pool, lambda: nc.gpsimd.memset(nmf.ap[:], 0.0), reads=[nmf], writes=[nmf])
    S.emit(S.pool, lambda: nc.gpsimd.affine_select(out=nmf.ap[:], in_=nmf.ap[:], pattern=[[-1, 128]], compare_op=ALU.is_ge, fill=NEG, base=0, channel_multiplier=1), reads=[nmf], writes=[nmf])
    S.emit(S.dve, lambda: nc.vector.tensor_copy(out=nmp.ap[:].rearrange("p (h l) -> p h l", h=4), in_=bc(nmf.ap[:].unsqueeze(1), [128, 4, 128])), reads=[nmf], writes=[nmp])
    k.update(identf=identf, ident=ident, mhalf=mhalf, nmc=nmc, nmp=nmp)
    return k


def prep_weight(C, es_stage, dst, src, nk, ncols, gain=None, col0=0):
    nc, S = C.nc, C.S
    stg = C.stg
    CW = 2048
    for kk in range(nk):
        for c0 in range(0, ncols, CW):
            cw = min(CW, ncols - c0)
            st = stg[C.rr % len(stg)]
            S.dma(S.sp, st.ap[:, 0:cw], src[kk * 128:(kk + 1) * 128, c0:c0 + cw], writes=[st], chan_buf=st)
            o = dst.ap[:, kk, col0 + c0:col0 + c0 + cw]
            which = (C.rr % 3) if gain is None else (C.rr % 2)
            C.rr += 1
            rds = [st] + ([gain] if gain is not None else [])
            if gain is None:
                if which == 0:
                    S.emit(S.act, lambda: nc.scalar.copy(out=o, in_=st.ap[:, 0:cw]), reads=rds, writes=[dst])
                elif which == 1:
                    S.emit(S.dve, lambda: nc.vector.tensor_copy(out=o, in_=st.ap[:, 0:cw]), reads=rds, writes=[dst])
                else:
                    S.emit(S.pool, lambda: nc.gpsimd.tensor_copy(out=o, in_=st.ap[:, 0:cw]), reads=rds, writes=[dst])
            else:
                g = gain.ap[:, kk:kk + 1]
                if which == 0:
                    S.emit(S.act, lambda: nc.scalar.mul(out=o, in_=st.ap[:, 0:cw], mul=g), reads=rds, writes=[dst])
                elif which == 1:
                    S.emit(S.dve, lambda: nc.vector.tensor_scalar(out=o, in0=st.ap[:, 0:cw], scalar1=g, scalar2=None, op0=ALU.mult), reads=rds, writes=[dst])
                else:
                    S.emit(S.pool, lambda: nc.gpsimd.tensor_scalar(out=o, in0=st.ap[:, 0:cw], scalar1=g, scalar2=None, op0=ALU.mult), reads=rds, writes=[dst])


def load_cols(C, dst, src_vec, nk):
    S = C.S
    with C.nc.allow_non_contiguous_dma(reason="small param vector"):
        S.dma(S.sp, dst.ap[:, 0:nk], src_vec.rearrange("(k p) -> p k", p=128), writes=[dst], chan_buf=dst)


def rms_rstd(C, x_ap, xbuf, junk, ss, rstd, K, n_feat, eps):
    nc, S = C.nc, C.S
    S.emit(S.act, lambda: nc.scalar.activation(out=junk.ap[:, 0:n_feat], in_=x_ap, func=AF.Square, accum_out=ss.ap[:, 0:1]), reads=[xbuf], writes=[junk, ss])
    S.emit(S.dve, lambda: nc.vector.tensor_scalar(out=ss.ap[:, 0:1], in0=ss.ap[:, 0:1], scalar1=1.0 / n_feat, scalar2=eps, op0=ALU.mult, op1=ALU.add), reads=[ss], writes=[ss])
    S.emit(S.pool, lambda: nc.gpsimd.tensor_tensor(out=rstd.ap[:, 0:1], in0=ss.ap[:, 0:1], in1=K["mhalf"].ap[:, 0:1], op=ALU.pow), reads=[ss, K["mhalf"]], writes=[rstd])


def norm_transpose(C, K, x_ap, xbuf, junk, ss, rstd, xn, ptr, xnT_ap, xnT, extra=()):
    nc, S = C.nc, C.S
    norm_only(C, K, x_ap, xbuf, junk, ss, rstd, xn)
    transpose_only(C, K, xn, ptr, xnT_ap, xnT, extra)


def norm_only(C, K, x_ap, xbuf, junk, ss, rstd, xn):
    nc, S = C.nc, C.S
    rms_rstd(C, x_ap, xbuf, junk, ss, rstd, K, D, 1e-6)
    S.emit(S.act, lambda: nc.scalar.mul(out=xn.ap[:, 0:D], in_=x_ap, mul=rstd.ap[:, 0:1]), reads=[xbuf, rstd], writes=[xn])


def transpose_only(C, K, xn, ptr, xnT_ap, xnT, extra=()):
    nc, S = C.nc, C.S
    for kk in range(8):
        S.emit(S.pe, lambda: nc.tensor.transpose(out=ptr.ap[:, kk * 128:(kk + 1) * 128], in_=xn.ap[:, kk * 128:(kk + 1) * 128], identity=K["ident"].ap[:]),
               reads=[xn, K["ident"]], writes=[ptr] + list(extra), pe_acc=True, sig=(kk == 7))
    S.emit(S.dve, lambda: nc.vector.tensor_copy(out=xnT_ap, in_=ptr.ap[:, 0:1024].rearrange("p (k t) -> p k t", k=8)), reads=[ptr], writes=[xnT])


def phase_ffn(C, K, layer, src, dst, w_up, conv_w, w_dn, gain_vec, final_gain=None):
    nc, S = C.nc, C.S
    T = 256
    NT = S_LEN // T
    with contextlib.ExitStack() as es:
        wup = C.sb(es, "wup", [128, 8, 2 * FF], BF16)
        wdn = C.sb(es, "wdn", [128, NF, D], BF16)
        gain = C.sb(es, "gain", [128, 8], F32)
        cw = C.sb(es, "cw", [128, 2 * NF, 3], F32)
        load_cols(C, gain, gain_vec, 8)
        with nc.allow_non_contiguous_dma(reason="conv weights"):
            for kk in range(3):
                S.dma(S.sp, cw.ap[:, :, kk], conv_w[kk].rearrange("(j p) -> p j", p=128), writes=[cw], chan_buf=cw)
        with contextlib.ExitStack() as es2:
            C.stg = [C.sb(es2, "stg", [128, 2048], F32) for _ in range(3)]
            prep_weight(C, es2, wup, w_up, 8, 2 * FF, gain)
            prep_weight(C, es2, wdn, w_dn, NF, D, None)
            S.barrier()
        xts = [C.sb(es, "xt", [128, 2, D], F32) for _ in range(2)]
        xn = C.sb(es, "xn", [128, D], BF16)
        xnTs = [C.sb(es, "xnT", [128, 8, T + 2], BF16) for _ in range(2)]
        ss = C.sb(es, "ss", [128, 1], F32)
        rstd = C.sb(es, "rstd", [128, 1], F32)
        ags = [C.sb(es, "ag", [128, T], F32) for _ in range(3)]
        avs = [C.sb(es, "av", [128, T], F32) for _ in range(3)]
        sgs = [C.sb(es, "sg", [128, T], F32) for _ in range(3)]
        h2_t = es.enter_context(nc.sbuf_tensor("h2T_all%d" % layer, [128, NF, T], BF16))
        h2T = [Buf(h2_t[:, j, :], "h2T%d" % j) for j in range(NF)]
        if final_gain is not None:
            gfin = C.sb(es, "gfin", [128, D], F32)
            obuf = [C.sb(es, "obuf", [128, D], F32) for _ in range(2)]
            S.dma(S.sp, gfin.ap[:], final_gain.partition_broadcast(128), writes=[gfin], chan_buf=gfin)
        S.emit(S.pool, lambda: nc.gpsimd.memset(xnTs[0].ap[:, :, 0:2], 0.0), writes=[xnTs[0]])
        ptrs = [C.ps(es, "ptr", [128, 1024], BF16) for _ in range(2)]
        pus = [C.ps(es, "pu", [128, 512], F32) for _ in range(4)]
        pds = [C.ps(es, "pd", [128, 512], F32) for _ in range(2)]

        def load(i):
            xt = xts[i % 2]
            S.dma(S.sp, xt.ap[:], src[i * T:(i + 1) * T, :].rearrange("(c p) d -> p c d", p=128), writes=[xt], chan_buf=xt)

        xn2 = [xn, C.sb(es, "xnb", [128, D], BF16)]
        ss2 = [ss, C.sb(es, "ssb", [128, 1], F32)]
        rstd2 = [rstd, C.sb(es, "rstdb", [128, 1], F32)]

        def head_norm(i):
            xt = xts[i % 2]
            for c in range(2):
                norm_only(C, K, xt.ap[:, c, :], xt, xn2[c], ss2[c], rstd2[c], xn2[c])

        def head_tr(i):
            xnT = xnTs[i % 2]
            if i > 0:
                S.emit(S.pool, lambda: nc.gpsimd.tensor_copy(out=xnT.ap[:, :, 0:2], in_=xnTs[(i - 1) % 2].ap[:, :, T:T + 2]), reads=[xnTs[(i - 1) % 2]], writes=[xnT])
            for c in range(2):
                transpose_only(C, K, xn2[c], ptrs[c], xnT.ap[:, :, 2 + c * 128:2 + (c + 1) * 128], xnT)

        def head(i):
            head_norm(i)
            head_tr(i)

        load(0)
        head(0)
        npu = 0
        npd = 0
        nb = 0
        for i in range(NT):
            xt = xts[i % 2]
            xnT = xnTs[i % 2]
            if i + 1 < NT:
                load(i + 1)
            for j in range(NF):
                pg, pv = pus[npu % 4], pus[(npu + 1) % 4]
                npu += 2
                ag, av, sgb = ags[nb % 3], avs[nb % 3], sgs[nb % 3]
                nb += 1
                for (pp, col) in ((pg, j), (pv, NF + j)):
                    for kk in range(8):
                        S.emit(S.pe, lambda: nc.tensor.matmul(pp.ap[:, 0:T + 2], lhsT=wup.ap[:, kk, col * 128:(col + 1) * 128], rhs=xnT.ap[:, kk, :], start=(kk == 0), stop=(kk == 7)),
                               reads=[wup, xnT], writes=[pp], pe_acc=True, sig=(kk == 7))
                S.emit(S.act, lambda: nc.scalar.mul(out=ag.ap[:], in_=pg.ap[:, 2:T + 2], mul=cw.ap[:, j, 2:3]), reads=[pg, cw], writes=[ag])
                S.emit(S.act, lambda: nc.scalar.mul(out=av.ap[:], in_=pv.ap[:, 2:T + 2], mul=cw.ap[:, NF + j, 2:3]), reads=[pv, cw], writes=[av])
                for tap in (1, 0):
                    S.emit(S.dve, lambda: nc.vector.scalar_tensor_tensor(out=ag.ap[:], in0=pg.ap[:, tap:tap + T], scalar=cw.ap[:, j, tap:tap + 1], in1=ag.ap[:], op0=ALU.mult, op1=ALU.add), reads=[pg, cw, ag], writes=[ag])
                for tap in (1, 0):
                    S.emit(S.dve, lambda: nc.vector.scalar_tensor_tensor(out=av.ap[:], in0=pv.ap[:, tap:tap + T], scalar=cw.ap[:, NF + j, tap:tap + 1], in1=av.ap[:], op0=ALU.mult, op1=ALU.add), reads=[pv, cw, av], writes=[av])
                S.emit(S.act, lambda: nc.scalar.activation(out=sgb.ap[:], in_=ag.ap[:], func=AF.Silu), reads=[ag], writes=[sgb])
                S.emit(S.pool, lambda: nc.gpsimd.tensor_tensor(out=h2T[j].ap[:], in0=sgb.ap[:], in1=av.ap[:], op=ALU.mult), reads=[sgb, av], writes=[h2T[j]])
                if j == 3 and i + 1 < NT:
                    head_norm(i + 1)
                if j == 12 and i + 1 < NT:
                    head_tr(i + 1)
            for c in range(2):
                for n in range(2):
                    pd = pds[npd % 2]
                    npd += 1
                    for j in range(NF):
                        S.emit(S.pe, lambda: nc.tensor.matmul(pd.ap[:], lhsT=h2T[j].ap[:, c * 128:(c + 1) * 128], rhs=wdn.ap[:, j, n * 512:(n + 1) * 512], start=(j == 0), stop=(j == NF - 1)),
                               reads=[h2T[j], wdn], writes=[pd], pe_acc=True, sig=(j == NF - 1))
                    S.emit(S.dve, lambda: nc.vector.tensor_tensor(out=xt.ap[:, c, n * 512:(n + 1) * 512], in0=pd.ap[:], in1=xt.ap[:, c, n * 512:(n + 1) * 512], op=ALU.add), reads=[pd, xt], writes=[xt])
                if final_gain is not None:
                    ob = obuf[c]
                    rms_rstd(C, xt.ap[:, c, :], xt, xn, ss, rstd, K, D, 1e-6)
                    S.emit(S.dve, lambda: nc.vector.scalar_tensor_tensor(out=ob.ap[:], in0=xt.ap[:, c, :], scalar=rstd.ap[:, 0:1], in1=gfin.ap[:], op0=ALU.mult, op1=ALU.mult), reads=[xt, rstd, gfin], writes=[ob])
                    S.dma(S.pool, dst[i * T + c * 128:i * T + (c + 1) * 128, :], ob.ap[:], reads=[ob], chan_buf=ob)
            if final_gain is None:
                S.dma(S.pool, dst[i * T:(i + 1) * T, :].rearrange("(c p) d -> p c d", p=128), xt.ap[:], reads=[xt], chan_buf=xt)
        S.barrier()


def phase_mamba(C, K, I, src, dst):
    nc, S = C.nc, C.S
    ynd = nc.dram_tensor("ynd", [S_LEN, D_IN], BF16, kind="Internal").ap()
    ident, identf = K["ident"], K["identf"]
    T = 256
    NT = S_LEN // T
    with contextlib.ExitStack() as es:
        win = C.sb(es, "win", [128, 8, IN_DIM], BF16)
        gain = C.sb(es, "gain", [128, 8], F32)
        cw = C.sb(es, "cw", [128, 32, 4], F32)
        cb = C.sb(es, "cb", [128, 32], F32)
        dcol = C.sb(es, "dcol", [128, 16], F32)
        diagD = C.sb(es, "diagD", [128, 16, 128], BF16)
        dtb = C.sb(es, "dtb", [32, 1], F32)
        acol = C.sb(es, "acol", [32, 1], F32)
        ones32 = C.sb(es, "ones32", [32, 128], F32)
        load_cols(C, gain, I["a_norm"], 8)
        load_cols(C, cb, I["ssm_conv_b"], 32)
        with nc.allow_non_contiguous_dma(reason="small params"):
            for kk in range(4):
                S.dma(S.sp, cw.ap[:, :, kk], I["ssm_conv_w"][kk].rearrange("(j p) -> p j", p=128), writes=[cw], chan_buf=cw)
            S.dma(S.sp, dtb.ap[:, 0:1], I["ssm_dt_bias"].rearrange("(h o) -> h o", o=1), writes=[dtb], chan_buf=dtb)
            S.dma(S.sp, acol.ap[:, 0:1], I["ssm_a_log"].rearrange("(h o) -> h o", o=1), writes=[acol], chan_buf=acol)
            dv = I["ssm_d"].rearrange("(k two) -> two k", two=2)
            for hh in range(2):
                S.dma(S.sp, dcol.ap[hh * 64:(hh + 1) * 64, :], dv[hh].partition_broadcast(64), writes=[dcol], chan_buf=dcol)
        S.emit(S.act, lambda: nc.scalar.activation(out=acol.ap[:], in_=acol.ap[:], func=AF.Exp), reads=[acol], writes=[acol])
        S.emit(S.dve, lambda: nc.vector.tensor_scalar(out=acol.ap[:], in0=acol.ap[:], scalar1=-1.0, scalar2=None, op0=ALU.mult), reads=[acol], writes=[acol])
        S.emit(S.pool, lambda: nc.gpsimd.memset(ones32.ap[:], 1.0), writes=[ones32])
        for kk in range(16):
            S.emit(S.dve, lambda: nc.vector.tensor_scalar(out=diagD.ap[:, kk, :], in0=identf.ap[:], scalar1=dcol.ap[:, kk:kk + 1], scalar2=None, op0=ALU.mult), reads=[identf, dcol], writes=[diagD])
        with contextlib.ExitStack() as es2:
            C.stg = [C.sb(es2, "stg", [128, 2048], F32) for _ in range(3)]
            prep_weight(C, es2, win, I["ssm_w_in"], 8, IN_DIM, gain)
            S.barrier()
        xts = [C.sb(es, "xt", [128, 2, D], F32) for _ in range(2)]
        xn = C.sb(es, "xn", [128, D], BF16)
        xnT2s = [C.sb(es, "xnT2", [128, 8, T + 4], BF16) for _ in range(2)]
        ss = C.sb(es, "ss", [128, 1], F32)
        rstd = C.sb(es, "rstd", [128, 1], F32)
        szs = [C.sb(es, "sz", [128, 512], F32) for _ in range(2)]
        szeas = [C.sb(es, "szea", [128, 512], F32) for _ in range(2)]
        xbcT_t = es.enter_context(nc.sbuf_tensor("xbcT_all", [128, 32, T], BF16))
        xbcT = [Buf(xbcT_t[:, j, :], "xbcT%d" % j) for j in range(32)]
        accs = [C.sb(es, "acc", [128, T], F32) for _ in range(3)]
        xcgs = [C.sb(es, "xcg", [128, 256], BF16) for _ in range(3)]
        xcdgs = [C.sb(es, "xcdg", [128, 256], BF16) for _ in range(3)]
        Btgs = [C.sb(es, "Btg", [128, 128], BF16) for _ in range(3)]
        ADs = [C.sb(es, "AD", [32, 512], F32) for _ in range(2)]
        Egs = [C.sb(es, "Eg", [128, 512], F32) for _ in range(2)]
        MTs = [C.sb(es, "MT", [128, 512], BF16) for _ in range(2)]
        prevT_t = es.enter_context(nc.sbuf_tensor("prevT_all", [128, D_IN], F32))
        prevbf_t = es.enter_context(nc.sbuf_tensor("prevbf_all", [128, D_IN], BF16))
        prevT = [Buf(prevT_t[:, g * 256:(g + 1) * 256], "prevT%d" % g) for g in range(8)]
        prevbf = [Buf(prevbf_t[:, g * 256:(g + 1) * 256], "prevbf%d" % g) for g in range(8)]
        yts = [C.sb(es, "yt", [128, 256], F32) for _ in range(2)]
        yus = [C.sb(es, "yu", [128, 256], F32) for _ in range(2)]
        yzs = [C.sb(es, "yz", [128, 256], F32) for _ in range(4)]
        junkg = C.sb(es, "junkg", [128, 256], BF16)
        ssgs = [C.sb(es, "ssg", [128, 1], F32) for _ in range(4)]
        rgs = [C.sb(es, "rg", [128, 1], F32) for _ in range(4)]
        yns = [C.sb(es, "yn", [128, D_IN], BF16) for _ in range(2)]
        sm = {n: C.sb(es, n, [32, T], F32) for n in ("dtT", "adtT", "acsT", "nacsT", "decT")}
        sm["e1"] = sm["dtT"]
        sm["dtdecT"] = sm["decT"]
        ecol = C.sb(es, "ecol", [32, 2], F32)
        dgs = [C.sb(es, "dg", [32, 32], F32) for _ in range(2)]
        toks = [C.sb(es, "tok", [128, 160], F32) for _ in range(2)]
        for g in range(8):
            S.emit(S.pool, lambda: nc.gpsimd.memset(prevT[g].ap[:], 0.0), writes=[prevT[g]])
            S.emit(S.pool, lambda: nc.gpsimd.memset(prevbf[g].ap[:], 0.0), writes=[prevbf[g]])
        S.emit(S.pool, lambda: nc.gpsimd.memset(xnT2s[0].ap[:, :, 0:4], 0.0), writes=[xnT2s[0]])
        bkA = [C.ps(es, "bkA", [128, 1024], BF16) for _ in range(2)]
        pgen2 = [C.ps(es, "pgen", [128, 512], F32) for _ in range(2)]
        pzb1 = C.ps(es, "pzb", [128, 512], F32)
        pgen = pgen2 + [pzb1]
        pzb = [pzb1, pzb1]
        pys = [C.ps(es, "py", [128, 512], F32) for _ in range(2)]
        pst1 = C.ps(es, "pst", [128, 512], F32)
        psts = [pst1, pst1]
        psgs = [None] * 4
        cnt = {"g": 0, "tr": 0, "acc": 0}

        def nxt(lst, key):
            b = lst[cnt[key] % len(lst)]
            cnt[key] += 1
            return b

        def load(i):
            xt = xts[i % 2]
            S.dma(S.sp, xt.ap[:], src[i * T:(i + 1) * T, :].rearrange("(c p) d -> p c d", p=128), writes=[xt], chan_buf=xt)

        e1, dtT, adtT, acsT, nacsT, decT, dtdecT = [sm[n] for n in ("e1", "dtT", "adtT", "acsT", "nacsT", "decT", "dtdecT")]
        load(0)
        for i in range(NT):
            xt = xts[i % 2]
            xnT2 = xnT2s[i % 2]
            if i + 1 < NT:
                load(i + 1)
            for c in range(2):
                norm_transpose(C, K, xt.ap[:, c, :], xt, xn, ss, rstd, xn, bkA[c], xnT2.ap[:, :, 4 + c * 128:4 + (c + 1) * 128], xnT2)
            S.emit(S.pool, lambda: nc.gpsimd.tensor_copy(out=xnT2s[(i + 1) % 2].ap[:, :, 0:4], in_=xnT2.ap[:, :, T:T + 4]), reads=[xnT2], writes=[xnT2s[(i + 1) % 2]])
            pdt = nxt(pgen, "g")
            for kk in range(8):
                S.emit(S.pe, lambda: nc.tensor.matmul(pdt.ap[0:32, 0:T], lhsT=win.ap[:, kk, 6144:6176], rhs=xnT2.ap[:, kk, 4:T + 4], start=(kk == 0), stop=(kk == 7)),
                       reads=[win, xnT2], writes=[pdt], pe_acc=True, sig=(kk == 7))
            S.emit(S.act, lambda: nc.scalar.activation(out=e1.ap[:], in_=pdt.ap[0:32, 0:T], func=AF.Exp, bias=dtb.ap[:, 0:1]), reads=[pdt, dtb], writes=[e1])
            S.emit(S.act, lambda: nc.scalar.activation(out=dtT.ap[:], in_=e1.ap[:], func=AF.Ln, bias=1.0), reads=[e1], writes=[dtT])
            S.emit(S.dve, lambda: nc.vector.tensor_scalar(out=adtT.ap[:], in0=dtT.ap[:], scalar1=acol.ap[:, 0:1], scalar2=None, op0=ALU.mult), reads=[dtT, acol], writes=[adtT])
            for c in range(2):
                cs = slice(c * 128, (c + 1) * 128)
                S.emit(S.dve, lambda: nc.vector.tensor_tensor_scan(out=acsT.ap[:, cs], data0=ones32.ap[:], data1=adtT.ap[:, cs], initial=0.0, op0=ALU.mult, op1=ALU.add), reads=[ones32, adtT], writes=[acsT])
            S.emit(S.act, lambda: nc.scalar.mul(out=nacsT.ap[:], in_=acsT.ap[:], mul=-1.0), reads=[acsT], writes=[nacsT])
            for c in range(2):
                cs = slice(c * 128, (c + 1) * 128)
                last = acsT.ap[:, c * 128 + 127:c * 128 + 128]
                S.emit(S.act, lambda: nc.scalar.activation(out=decT.ap[:, cs], in_=acsT.ap[:, cs], func=AF.Exp, scale=-1.0, bias=last), reads=[acsT], writes=[decT])
                S.emit(S.act, lambda: nc.scalar.activation(out=ecol.ap[:, c:c + 1], in_=last, func=AF.Exp), reads=[acsT], writes=[ecol])
                S.emit(S.dve, lambda: nc.vector.tensor_scalar(out=dgs[c].ap[:], in0=identf.ap[0:32, 0:32], scalar1=ecol.ap[:, c:c + 1], scalar2=None, op0=ALU.mult), reads=[identf, ecol], writes=[dgs[c]])
            S.emit(S.dve, lambda: nc.vector.tensor_tensor(out=dtdecT.ap[:], in0=dtT.ap[:], in1=decT.ap[:], op=ALU.mult), reads=[dtT, decT], writes=[dtdecT])
            for c in range(2):
                cs = slice(c * 128, (c + 1) * 128)
                tok = toks[c]
                ptok = nxt(pgen, "g")
                for i4, (lh, rh) in enumerate(((dtT, None), (dtdecT, None), (nacsT, None), (ones32, dgs[c]))):
                    rhs_ap = identf.ap[0:32, 0:32] if rh is None else rh.ap[:]
                    lh_ap = lh.ap[:, cs] if rh is None else lh.ap[:]
                    S.emit(S.pe, lambda: nc.tensor.matmul(ptok.ap[:, i4 * 32:(i4 + 1) * 32], lhsT=lh_ap, rhs=rhs_ap, start=True, stop=True),
                           reads=[lh, identf] + ([rh] if rh is not None else []), writes=[ptok], pe_acc=True, sig=(i4 == 3))
                S.emit(S.dve, lambda: nc.vector.tensor_copy(out=tok.ap[:, 0:128], in_=ptok.ap[:, 0:128]), reads=[ptok], writes=[tok])
                S.emit(S.act, lambda: nc.scalar.activation(out=tok.ap[:, 128:160], in_=ptok.ap[:, 64:96], func=AF.Exp, scale=-1.0), reads=[ptok], writes=[tok])
            cring = [pgen2[0], pgen2[1], pzb1, pys[0], pys[1], pst1]
            ctmp = yts

            def c0(j):
                pj = cring[j % 6]
                for kk in range(8):
                    S.emit(S.pe, lambda: nc.tensor.matmul(pj.ap[:, 0:T + 4], lhsT=win.ap[:, kk, 2048 + j * 128:2048 + (j + 1) * 128], rhs=xnT2.ap[:, kk, :], start=(kk == 0), stop=(kk == 7)),
                           reads=[win, xnT2], writes=[pj], pe_acc=True, sig=(kk == 7))

            def c1(j):
                pj = cring[j % 6]
                acc, tm = accs[j % 3], ctmp[j % 2]
                S.emit(S.act, lambda: nc.scalar.activation(out=acc.ap[:], in_=pj.ap[:, 4:T + 4], func=AF.Identity, scale=cw.ap[:, j, 3:4], bias=cb.ap[:, j:j + 1]), reads=[pj, cw, cb], writes=[acc])
                S.emit(S.act, lambda: nc.scalar.mul(out=tm.ap[:], in_=pj.ap[:, 3:T + 3], mul=cw.ap[:, j, 2:3]), reads=[pj, cw], writes=[tm])

            def c2(j):
                pj = cring[j % 6]
                acc, tm = accs[j % 3], ctmp[j % 2]
                S.emit(S.pool, lambda: nc.gpsimd.tensor_tensor(out=acc.ap[:], in0=acc.ap[:], in1=tm.ap[:], op=ALU.add), reads=[acc, tm], writes=[acc])
                for tap in (1, 0):
                    S.emit(S.dve, lambda: nc.vector.scalar_tensor_tensor(out=acc.ap[:], in0=pj.ap[:, 1 + tap:1 + tap + T], scalar=cw.ap[:, j, tap:tap + 1], in1=acc.ap[:], op0=ALU.mult, op1=ALU.add), reads=[pj, cw, acc], writes=[acc])

            def c3(j):
                acc = accs[j % 3]
                S.emit(S.act, lambda: nc.scalar.activation(out=xbcT[j].ap[:], in_=acc.ap[:], func=AF.Silu), reads=[acc], writes=[xbcT[j]])

            cst = [c0, c1, c2, c3]
            for t in range(32 + len(cst) - 1):
                for k in reversed(range(len(cst))):
                    j = t - k
                    if 0 <= j < 32:
                        cst[k](j)
            def info(gi):
                c, g = gi // 8, gi % 8
                return c, g, slice(c * 128, (c + 1) * 128), toks[c]

            def st_ad(gi):
                c, g, cs, tok = info(gi)
                AD = ADs[gi % 2]
                S.emit(S.pool, lambda: nc.gpsimd.affine_select(out=AD.ap[:].rearrange("p (h l) -> p h l", h=4), in_=bc(acsT.ap[:, cs].unsqueeze(1), [32, 4, 128]), pattern=[[-1, 4], [0, 128]], compare_op=ALU.is_equal, fill=0.0, base=-4 * g, channel_multiplier=1), reads=[acsT], writes=[AD])

            def st0(gi):
                c, g, cs, tok = info(gi)
                bk = bkA[gi % 2]
                for jj, j in enumerate((2 * g, 2 * g + 1, 16 + g)):
                    S.emit(S.pe, lambda: nc.tensor.transpose(out=bk.ap[:, jj * 128:(jj + 1) * 128], in_=xbcT[j].ap[:, cs], identity=ident.ap[:]),
                           reads=[xbcT[j], ident], writes=[bk], pe_acc=True, sig=False)
                S.emit(S.pe, lambda: nc.tensor.matmul(bk.ap[:, 512:768].bitcast(F32), lhsT=xbcT[16 + g].ap[:, cs], rhs=xbcT[24 + g].ap[:, cs], start=True, stop=True), reads=[xbcT[16 + g], xbcT[24 + g]], writes=[bk], pe_acc=True)
                psg = pgen2[gi % 2]
                psgs[gi % 4] = psg
                AD = ADs[gi % 2]
                S.emit(S.pe, lambda: nc.tensor.matmul(psg.ap[:], lhsT=ones32.ap[:], rhs=AD.ap[:], start=True, stop=False), reads=[ones32, AD], writes=[psg], pe_acc=True, sig=False)
                S.emit(S.pe, lambda: nc.tensor.matmul(psg.ap[:], lhsT=ident.ap[:], rhs=K["nmc"].ap[:], start=False, stop=True), reads=[ident, K["nmc"]], writes=[psg], pe_acc=True)
                if g % 2 == 0:
                    pz = pzb[(gi // 2) % 2]
                    n = g // 2
                    for kk in range(8):
                        S.emit(S.pe, lambda: nc.tensor.matmul(pz.ap[:], lhsT=xnT2.ap[:, kk, 4 + c * 128:4 + (c + 1) * 128], rhs=win.ap[:, kk, n * 512:(n + 1) * 512], start=(kk == 0), stop=(kk == 7)),
                               reads=[xnT2, win], writes=[pz], pe_acc=True, sig=(kk == 7))

            def st1(gi):
                c, g, cs, tok = info(gi)
                ptr = bkA[gi % 2]
                psg = psgs[gi % 4]
                Eg = Egs[gi % 2]
                for h in range(4):
                    S.emit(S.act, lambda: nc.scalar.activation(out=Eg.ap[:, h * 128:(h + 1) * 128], in_=psg.ap[:, h * 128:(h + 1) * 128], func=AF.Exp, bias=tok.ap[:, 64 + 4 * g + h:64 + 4 * g + h + 1]), reads=[psg, tok], writes=[Eg])
                MT = MTs[gi % 2]
                S.emit(S.dve, lambda: nc.vector.tensor_tensor(out=MT.ap[:].rearrange("p (h l) -> p h l", h=4), in0=Eg.ap[:].rearrange("p (h l) -> p h l", h=4), in1=bc(ptr.ap[:, 512:768].bitcast(F32).unsqueeze(1), [128, 4, 128]), op=ALU.mult), reads=[Eg, ptr], writes=[MT])
                xcg, xcdg, Btg = xcgs[gi % 3], xcdgs[gi % 3], Btgs[gi % 3]
                pv = ptr.ap[:, 0:256].rearrange("p (h e) -> p h e", h=4)
                S.emit(S.dve, lambda: nc.vector.tensor_tensor(out=xcg.ap[:].rearrange("p (h e) -> p h e", h=4), in0=pv, in1=bc(tok.ap[:, 4 * g:4 * g + 4].unsqueeze(2), [128, 4, 64]), op=ALU.mult), reads=[ptr, tok], writes=[xcg])
                S.emit(S.dve, lambda: nc.vector.tensor_tensor(out=xcdg.ap[:].rearrange("p (h e) -> p h e", h=4), in0=pv, in1=bc(tok.ap[:, 32 + 4 * g:32 + 4 * g + 4].unsqueeze(2), [128, 4, 64]), op=ALU.mult), reads=[ptr, tok], writes=[xcdg])
                S.emit(S.act, lambda: nc.scalar.copy(out=Btg.ap[:], in_=ptr.ap[:, 256:384]), reads=[ptr], writes=[Btg])
                if g % 2 == 0:
                    pz = pzb[(gi // 2) % 2]
                    sz, szea = szs[(gi // 2) % 2], szeas[(gi // 2) % 2]
                    n = g // 2
                    S.emit(S.act, lambda: nc.scalar.activation(out=sz.ap[:], in_=pz.ap[:], func=AF.Tanh, scale=0.5), reads=[pz], writes=[sz])
                    S.emit(S.dve, lambda: nc.vector.scalar_tensor_tensor(out=sz.ap[:], in0=sz.ap[:], scalar=1.0, in1=pz.ap[:], op0=ALU.add, op1=ALU.mult), reads=[sz, pz], writes=[sz])
                    S.emit(S.pool, lambda: nc.gpsimd.tensor_tensor(out=szea.ap[:].rearrange("p (h e) -> p h e", h=8), in0=sz.ap[:].rearrange("p (h e) -> p h e", h=8), in1=bc(tok.ap[:, 128 + 8 * n:128 + 8 * n + 8].unsqueeze(2), [128, 8, 64]), op=ALU.mult), reads=[sz, tok], writes=[szea])
                pvw = prevT[g].ap[:]
                S.emit(S.pool, lambda: nc.gpsimd.tensor_tensor(out=pvw.rearrange("p (h e) -> p h e", h=4), in0=pvw.rearrange("p (h e) -> p h e", h=4), in1=bc(tok.ap[:, 96 + 4 * g:96 + 4 * g + 4].unsqueeze(2), [128, 4, 64]), op=ALU.mult), reads=[prevT[g], tok], writes=[prevT[g]])

            def st2(gi):
                c, g, cs, tok = info(gi)
                MT = MTs[gi % 2]
                xcg, xcdg, Btg = xcgs[gi % 3], xcdgs[gi % 3], Btgs[gi % 3]
                py = pys[gi % 2]
                pyr = py.ap[:, 0:256]
                pyo = py.ap[:, 256:512]
                for i2 in range(2):
                    S.emit(S.pe, lambda: nc.tensor.matmul(pyr[:, i2 * 128:(i2 + 1) * 128], lhsT=xbcT[2 * g + i2].ap[:, cs], rhs=diagD.ap[:, 2 * g + i2, :], start=True, stop=False),
                           reads=[xbcT[2 * g + i2], diagD], writes=[py], pe_acc=True, sig=False)
                    for hh in (2 * i2, 2 * i2 + 1):
                        S.emit(S.pe, lambda: nc.tensor.matmul(pyr[:, hh * 64:(hh + 1) * 64], lhsT=MT.ap[:, hh * 128:(hh + 1) * 128], rhs=xcg.ap[:, hh * 64:(hh + 1) * 64], start=False, stop=(hh % 2 == 1)),
                               reads=[MT, xcg], writes=[py], pe_acc=True, sig=False)
                S.emit(S.pe, lambda: nc.tensor.matmul(pyo, lhsT=xbcT[24 + g].ap[:, cs], rhs=prevbf[g].ap[:], start=True, stop=True), reads=[xbcT[24 + g], prevbf[g]], writes=[py], pe_acc=True)
                pst = psts[gi % 2]
                S.emit(S.pe, lambda: nc.tensor.matmul(pst.ap[:, 0:256], lhsT=Btg.ap[:], rhs=xcdg.ap[:], start=True, stop=True), reads=[Btg, xcdg], writes=[pst], pe_acc=True)

            def st3(gi):
                c, g, cs, tok = info(gi)
                sz, szea = szs[(gi // 2) % 2], szeas[(gi // 2) % 2]
                py = pys[gi % 2]
                yt, yu = yts[gi % 2], yus[gi % 2]
                gs = slice((g % 2) * 256, (g % 2 + 1) * 256)
                S.emit(S.dve, lambda: nc.vector.tensor_tensor(out=yt.ap[:], in0=py.ap[:, 256:512], in1=szea.ap[:, gs], op=ALU.mult), reads=[py, szea], writes=[yt])
                S.emit(S.dve, lambda: nc.vector.tensor_tensor(out=yu.ap[:], in0=py.ap[:, 0:256], in1=sz.ap[:, gs], op=ALU.mult), reads=[py, sz], writes=[yu])
                pst = psts[gi % 2]
                S.emit(S.dve, lambda: nc.vector.tensor_tensor(out=prevT[g].ap[:], in0=pst.ap[:, 0:256], in1=prevT[g].ap[:], op=ALU.add), reads=[pst, prevT[g]], writes=[prevT[g]])

            def st4(gi):
                c, g, cs, tok = info(gi)
                yt, yu, yz = yts[gi % 2], yus[gi % 2], yzs[gi % 4]
                S.emit(S.pool, lambda: nc.gpsimd.tensor_tensor(out=yz.ap[:], in0=yt.ap[:], in1=yu.ap[:], op=ALU.add), reads=[yt, yu], writes=[yz])
                S.emit(S.act, lambda: nc.scalar.copy(out=prevbf[g].ap[:], in_=prevT[g].ap[:]), reads=[prevT[g]], writes=[prevbf[g]])

            def st5(gi):
                yz, ssg = yzs[gi % 4], ssgs[gi % 4]
                S.emit(S.act, lambda: nc.scalar.activation(out=junkg.ap[:], in_=yz.ap[:], func=AF.Square, accum_out=ssg.ap[:, 0:1]), reads=[yz], writes=[junkg, ssg])

            def st6(gi):
                ssg, rg = ssgs[gi % 4], rgs[gi % 4]
                S.emit(S.pool, lambda: nc.gpsimd.tensor_scalar(out=ssg.ap[:, 0:1], in0=ssg.ap[:, 0:1], scalar1=1.0 / 256.0, scalar2=4e-5, op0=ALU.mult, op1=ALU.add), reads=[ssg], writes=[ssg])
                S.emit(S.pool, lambda: nc.gpsimd.tensor_tensor(out=rg.ap[:, 0:1], in0=ssg.ap[:, 0:1], in1=K["mhalf"].ap[:, 0:1], op=ALU.pow), reads=[ssg, K["mhalf"]], writes=[rg])

            def st7(gi):
                c, g, cs, tok = info(gi)
                yz, rg = yzs[gi % 4], rgs[gi % 4]
                yn = yns[c]
                S.emit(S.act, lambda: nc.scalar.mul(out=yn.ap[:, g * 256:(g + 1) * 256], in_=yz.ap[:], mul=rg.ap[:, 0:1]), reads=[yz, rg], writes=[yn])
                if g == 7:
                    S.dma(S.pool, ynd[i * T + c * 128:i * T + (c + 1) * 128, :], yn.ap[:], reads=[yn], chan_buf=yn)

            stages = [st_ad, st0, st1, st2, st3, st4, st5, st6, st7]
            NG = 16
            for t in range(NG + len(stages) - 1):
                for k in reversed(range(len(stages))):
                    gi = t - k
                    if 0 <= gi < NG:
                        stages[k](gi)
        S.barrier()
    with contextlib.ExitStack() as es:
        wout = C.sb(es, "wout", [128, 16, D], BF16)
        gout = C.sb(es, "gout", [128, 16], F32)
        load_cols(C, gout, I["ssm_norm"], 16)
        with contextlib.ExitStack() as es2:
            C.stg = [C.sb(es2, "stg", [128, 2048], F32) for _ in range(3)]
            prep_weight(C, es2, wout, I["ssm_w_out"], 16, D, gout)
            S.barrier()
        xts = [C.sb(es, "xt", [128, D], F32) for _ in range(2)]
        yls = [C.sb(es, "yl", [128, D_IN], BF16) for _ in range(2)]
        ynT = C.sb(es, "ynT", [128, 16, 128], BF16)
        ptrs = [C.ps(es, "ptr", [128, 1024], BF16) for _ in range(2)]
        pmm = [C.ps(es, "pmm", [128, 512], F32) for _ in range(2)]

        def load2(c):
            S.dma(S.sp, xts[c % 2].ap[:], src[c * 128:(c + 1) * 128, :], writes=[xts[c % 2]], chan_buf=xts[c % 2])
            S.dma(S.sp, yls[c % 2].ap[:], ynd[c * 128:(c + 1) * 128, :], writes=[yls[c % 2]], chan_buf=yls[c % 2])

        load2(0)
        nmm = 0
        for c in range(NCH):
            if c + 1 < NCH:
                load2(c + 1)
            xt, yl = xts[c % 2], yls[c % 2]
            for half in range(2):
                ptr = ptrs[half]
                for jj in range(8):
                    j = 8 * half + jj
                    S.emit(S.pe, lambda: nc.tensor.transpose(out=ptr.ap[:, jj * 128:(jj + 1) * 128], in_=yl.ap[:, j * 128:(j + 1) * 128], identity=ident.ap[:]),
                           reads=[yl, ident], writes=[ptr], pe_acc=True, sig=(jj == 7))
                if half == 0:
                    S.emit(S.act, lambda: nc.scalar.copy(out=ynT.ap[:, 0:8, :], in_=ptr.ap[:, 0:1024].rearrange("p (k t) -> p k t", k=8)), reads=[ptr], writes=[ynT])
                else:
                    S.emit(S.dve, lambda: nc.vector.tensor_copy(out=ynT.ap[:, 8:16, :], in_=ptr.ap[:, 0:1024].rearrange("p (k t) -> p k t", k=8)), reads=[ptr], writes=[ynT])
            for n2 in range(2):
                po = pmm[nmm % 2]
                nmm += 1
                for kk in range(16):
                    S.emit(S.pe, lambda: nc.tensor.matmul(po.ap[:], lhsT=ynT.ap[:, kk, :], rhs=wout.ap[:, kk, n2 * 512:(n2 + 1) * 512], start=(kk == 0), stop=(kk == 15)),
                           reads=[ynT, wout], writes=[po], pe_acc=True, sig=(kk == 15))
                S.emit(S.dve, lambda: nc.vector.tensor_tensor(out=xt.ap[:, n2 * 512:(n2 + 1) * 512], in0=po.ap[:], in1=xt.ap[:, n2 * 512:(n2 + 1) * 512], op=ALU.add), reads=[po, xt], writes=[xt])
            S.dma(S.pool, dst[c * 128:(c + 1) * 128, :], xt.ap[:], reads=[xt], chan_buf=xt)
        S.barrier()

ATT_PATTERNS = ((128, 1), (512, 4), (2048, 16))
ATT_MAXSTAGE = 7
ATT_DBG = 0


def rotary(C, src, dst, rp, tmps, nh):
    nc, S = C.nc, C.S
    sv = src.ap[:, 0:nh * 128].rearrange("p (h e) -> p h e", h=nh)
    dv = dst.ap[:, 0:nh * 128].rearrange("p (h e) -> p h e", h=nh)
    cos = bc(rp.ap[:, 0:16].unsqueeze(1), [128, nh, 16])
    sin = bc(rp.ap[:, 16:32].unsqueeze(1), [128, nh, 16])
    t1, t2, t3, t4 = [t.ap[:, 0:nh * 16].rearrange("p (h e) -> p h e", h=nh) for t in tmps]
    x1, x2 = sv[:, :, 0:16], sv[:, :, 16:32]
    S.emit(S.dve, lambda: nc.vector.tensor_tensor(out=t1, in0=x1, in1=cos, op=ALU.mult), reads=[src, rp], writes=[tmps[0]])
    S.emit(S.pool, lambda: nc.gpsimd.tensor_tensor(out=t2, in0=x2, in1=sin, op=ALU.mult), reads=[src, rp], writes=[tmps[1]])
    S.emit(S.dve, lambda: nc.vector.tensor_tensor(out=t3, in0=x2, in1=cos, op=ALU.mult), reads=[src, rp], writes=[tmps[2]])
    S.emit(S.pool, lambda: nc.gpsimd.tensor_tensor(out=t4, in0=x1, in1=sin, op=ALU.mult), reads=[src, rp], writes=[tmps[3]])
    S.emit(S.dve, lambda: nc.vector.tensor_tensor(out=dv[:, :, 0:16], in0=t1, in1=t2, op=ALU.subtract), reads=[tmps[0], tmps[1]], writes=[dst])
    S.emit(S.pool, lambda: nc.gpsimd.tensor_tensor(out=dv[:, :, 16:32], in0=t3, in1=t4, op=ALU.add), reads=[tmps[2], tmps[3]], writes=[dst])
    S.emit(S.act, lambda: nc.scalar.copy(out=dv[:, :, 32:128], in_=sv[:, :, 32:128]), reads=[src], writes=[dst])


def phase_attn(C, K, I, src, dst):
    nc, S = C.nc, C.S
    nd = [nc.dram_tensor("numden%d" % g, [S_LEN, 8 * 129], F32, kind="Internal").ap() for g in range(3)]
    rope = I["rope"]
    with contextlib.ExitStack() as es:
        wkv = C.sb(es, "wkv", [128, 8, 1536], BF16)
        wq = C.sb(es, "wq", [128, 8, 3072], BF16)
        wo = C.sb(es, "wo", [128, 8, D], BF16)
        gkv = C.sb(es, "gkv", [128, 8], F32)
        gq = C.sb(es, "gq", [128, 8], F32)
        load_cols(C, gkv, I["kv_norm"], 8)
        load_cols(C, gq, I["b_norm"], 8)
        with contextlib.ExitStack() as es2:
            C.stg = [C.sb(es2, "stg", [128, 2048], F32) for _ in range(3)]
            prep_weight(C, es2, wkv, I["w_kv"], 8, 1536, gkv)
            prep_weight(C, es2, wq, I["att_w_q"], 8, 3072, gq)
            prep_weight(C, es2, wo, I["att_w_o"], 8, D, None)
            S.barrier()
        xts = [C.sb(es, "xt", [128, D], F32) for _ in range(3)]
        rps = [C.sb(es, "rp", [128, 32], F32) for _ in range(8)]
        xns = [C.sb(es, "xn", [128, D], BF16) for _ in range(2)]
        xnTs = [C.sb(es, "xnT", [128, 8, 128], BF16) for _ in range(2)]
        sss = [C.sb(es, "ss", [128, 1], F32) for _ in range(2)]
        rstds = [C.sb(es, "rstd", [128, 1], F32) for _ in range(2)]
        qfs = [C.sb(es, "qf", [128, 1024], F32) for _ in range(2)]
        qbs = [C.sb(es, "qb", [128, 1024], BF16) for _ in range(2)]
        kfs = [C.sb(es, "kf", [128, 256], F32) for _ in range(2)]
        kbs = [C.sb(es, "kb", [128, 256], BF16) for _ in range(2)]
        tq = [C.sb(es, "tq", [128, 128], F32) for _ in range(4)]
        tk = [C.sb(es, "tk", [128, 32], F32) for _ in range(4)]
        QTs = [C.sb(es, "QT", [128, 8, 128], BF16) for _ in range(3)]
        KTs = [C.sb(es, "KT", [128, 2, 128], BF16) for _ in range(4)]
        vaugs = [C.sb(es, "vaug", [128, 2, 130], BF16) for _ in range(6)]
        PTs = [C.sb(es, "PT", [128, 512], BF16) for _ in range(8)]
        obs = [C.sb(es, "ob", [128, 8, 129], F32) for _ in range(2)]
        for v in vaugs:
            S.emit(S.pool, lambda: nc.gpsimd.memset(v.ap[:], 1.0), writes=[v])
        ptr = C.ps(es, "ptr", [128, 1024], BF16)
        pmm = [C.ps(es, "pmm", [128, 512], F32) for _ in range(2)]
        pS = [C.ps(es, "pS", [128, 512], F32) for _ in range(2)]
        po = C.ps(es, "po", [128, 3 * 512], F32)
        scale = 1.0 / np.sqrt(128.0)

        blocks = []
        for g, (win, dil) in enumerate(ATT_PATTERNS):
            for r in range(dil):
                for n in range(S_LEN // dil // 128):
                    blocks.append((g, dil, r, n))
        NB = len(blocks)

        def rows(ap, dil, r, n):
            return ap.rearrange("(m dd) c -> dd m c", dd=dil)[r][n * 128:(n + 1) * 128, :]

        def load(bi):
            g, dil, r, n = blocks[bi]
            S.dma(S.sp, xts[bi % 3].ap[:], rows(src, dil, r, n), writes=[xts[bi % 3]], chan_buf=xts[bi % 3])
            S.dma(S.sp, rps[bi % 8].ap[:], rows(rope, dil, r, n), writes=[rps[bi % 8]], chan_buf=rps[bi % 8])

        cnt = {"mm": 0, "S": 0}

        def A1(bi):
            xt = xts[bi % 3]
            norm_only(C, K, xt.ap[:], xt, xns[bi % 2], sss[bi % 2], rstds[bi % 2], xns[bi % 2])

        def A2(bi):
            transpose_only(C, K, xns[bi % 2], ptr, xnTs[bi % 2].ap[:], xnTs[bi % 2])

        def A3(bi):
            g, dil, r, n = blocks[bi]
            xnT, qf, kf, vaug = xnTs[bi % 2], qfs[bi % 2], kfs[bi % 2], vaugs[bi % 6]
            for n2 in range(2):
                pq = pmm[cnt["mm"] % 2]
                cnt["mm"] += 1
                for kk in range(8):
                    S.emit(S.pe, lambda: nc.tensor.matmul(pq.ap[:], lhsT=xnT.ap[:, kk, :], rhs=wq.ap[:, kk, g * 1024 + n2 * 512:g * 1024 + (n2 + 1) * 512], start=(kk == 0), stop=(kk == 7)),
                           reads=[xnT, wq], writes=[pq], pe_acc=True, sig=(kk == 7))
                S.emit(S.act, lambda: nc.scalar.copy(out=qf.ap[:, n2 * 512:(n2 + 1) * 512], in_=pq.ap[:]), reads=[pq], writes=[qf])
            if ATT_DBG == 1:
                return
            pkv = pmm[cnt["mm"] % 2]
            cnt["mm"] += 1
            for half, c0 in ((0, g * 256), (1, 768 + g * 256)):
                for kk in range(8):
                    S.emit(S.pe, lambda: nc.tensor.matmul(pkv.ap[:, half * 256:(half + 1) * 256], lhsT=xnT.ap[:, kk, :], rhs=wkv.ap[:, kk, c0:c0 + 256], start=(kk == 0), stop=(kk == 7)),
                           reads=[xnT, wkv], writes=[pkv], pe_acc=True, sig=(kk == 7 and half == 1))
            S.emit(S.act, lambda: nc.scalar.copy(out=kf.ap[:], in_=pkv.ap[:, 0:256]), reads=[pkv], writes=[kf])
            if ATT_DBG == 2:
                return
            for hv in range(2):
                S.emit(S.act, lambda: nc.scalar.copy(out=vaug.ap[:, hv, 0:128], in_=pkv.ap[:, 256 + hv * 128:256 + (hv + 1) * 128]), reads=[pkv], writes=[vaug])

        def A4(bi):
            rp = rps[bi % 8]
            rotary(C, qfs[bi % 2], qbs[bi % 2], rp, tq, 8)
            rotary(C, kfs[bi % 2], kbs[bi % 2], rp, tk, 2)

        def A5(bi):
            qb, kb, QT, KT = qbs[bi % 2], kbs[bi % 2], QTs[bi % 3], KTs[bi % 4]
            for h in range(8):
                S.emit(S.pe, lambda: nc.tensor.transpose(out=ptr.ap[:, h * 128:(h + 1) * 128], in_=qb.ap[:, h * 128:(h + 1) * 128], identity=K["ident"].ap[:]),
                       reads=[qb, K["ident"]], writes=[ptr], pe_acc=True, sig=(h == 7))
            S.emit(S.dve, lambda: nc.vector.tensor_copy(out=QT.ap[:], in_=ptr.ap[:, 0:1024].rearrange("p (k t) -> p k t", k=8)), reads=[ptr], writes=[QT])
            for h in range(2):
                S.emit(S.pe, lambda: nc.tensor.transpose(out=ptr.ap[:, h * 128:(h + 1) * 128], in_=kb.ap[:, h * 128:(h + 1) * 128], identity=K["ident"].ap[:]),
                       reads=[kb, K["ident"]], writes=[ptr], pe_acc=True, sig=(h == 1))
            S.emit(S.act, lambda: nc.scalar.copy(out=KT.ap[:], in_=ptr.ap[:, 0:256].rearrange("p (k t) -> p k t", k=2)), reads=[ptr], writes=[KT])

        def kb_list(bi):
            g, dil, r, n = blocks[bi]
            kbl = []
            if n > 0:
                kbl.append((KTs[(bi - 1) % 4], vaugs[(bi - 1) % 6], K["nmp"]))
            kbl.append((KTs[bi % 4], vaugs[bi % 6], K["nmc"]))
            return kbl

        def B1(bi):
            QT = QTs[bi % 3]
            kbl = kb_list(bi)
            for jk in range(2):
                for i2, (kt, va, nm) in enumerate(kbl):
                    ps_ = pS[cnt["S"] % 2]
                    cnt["S"] += 1
                    pt_ = PTs[(4 * bi + 2 * jk + i2) % 8]
                    S.emit(S.pe, lambda: nc.tensor.matmul(ps_.ap[:], lhsT=kt.ap[:, jk, :], rhs=QT.ap[:, 4 * jk:4 * jk + 4, :], start=True, stop=False),
                           reads=[kt, QT], writes=[ps_], pe_acc=True, sig=False)
                    S.emit(S.pe, lambda: nc.tensor.matmul(ps_.ap[:], lhsT=K["ident"].ap[:], rhs=nm.ap[:], start=False, stop=True),
                           reads=[K["ident"], nm], writes=[ps_], pe_acc=True)
                    S.emit(S.act, lambda: nc.scalar.activation(out=pt_.ap[:], in_=ps_.ap[:], func=AF.Exp, scale=float(scale)), reads=[ps_], writes=[pt_])

        def B2(bi):
            g, dil, r, n = blocks[bi]
            kbl = kb_list(bi)
            ob = obs[bi % 2]
            for jk in range(2):
                for hl in range(4):
                    h = 4 * jk + hl
                    oslice = po.ap[:, (h // 3) * 512 + (h % 3) * 129:(h // 3) * 512 + (h % 3) * 129 + 129]
                    for i2, (kt, va, nm) in enumerate(kbl):
                        pt_ = PTs[(4 * bi + 2 * jk + i2) % 8]
                        S.emit(S.pe, lambda: nc.tensor.matmul(oslice, lhsT=pt_.ap[:, hl * 128:(hl + 1) * 128], rhs=va.ap[:, jk, 0:129], start=(i2 == 0), stop=(i2 == len(kbl) - 1)),
                               reads=[pt_, va], writes=[po], pe_acc=True, sig=(i2 == len(kbl) - 1 and hl == 3))
            for b3 in range(3):
                nh = 3 if b3 < 2 else 2
                o_ap = ob.ap[:, b3 * 3:b3 * 3 + nh, :]
                i_ap = po.ap[:, b3 * 512:b3 * 512 + nh * 129].rearrange("p (h e) -> p h e", h=nh)
                if b3 != 1:
                    S.emit(S.act, lambda: nc.scalar.copy(out=o_ap, in_=i_ap), reads=[po], writes=[ob])
                else:
                    S.emit(S.dve, lambda: nc.vector.tensor_copy(out=o_ap, in_=i_ap), reads=[po], writes=[ob])
            S.dma(S.pool, rows(nd[g], dil, r, n), ob.ap[:].rearrange("p h e -> p (h e)"), reads=[ob], chan_buf=ob)

        stages = [A1, A2, A3, A4, A5, B1, B2][:ATT_MAXSTAGE]
        load(0)
        load(1)
        for t in range(NB + len(stages) - 1):
            if t + 2 < NB:
                load(t + 2)
            for k in reversed(range(len(stages))):
                bi = t - k
                if 0 <= bi < NB:
                    stages[k](bi)
        S.barrier()
        nmm = cnt["mm"]
        nds = [[C.sb(es, "ndl", [128, 8, 129], F32) for _ in range(3)] for _ in range(2)]
        rden = C.sb(es, "rden", [128, 8], F32)
        o16 = C.sb(es, "o16", [128, 1024], BF16)
        oT = C.sb(es, "oT", [128, 8, 128], BF16)

        def load2(c):
            S.dma(S.sp, xts[c % 2].ap[:], src[c * 128:(c + 1) * 128, :], writes=[xts[c % 2]], chan_buf=xts[c % 2])
            for g in range(3):
                b = nds[c % 2][g]
                S.dma(S.sp, b.ap[:].rearrange("p h e -> p (h e)"), nd[g][c * 128:(c + 1) * 128, :], writes=[b], chan_buf=b)

        load2(0)
        for c in range(NCH):
            if c + 1 < NCH:
                load2(c + 1)
            xt = xts[c % 2]
            a0, a1, a2 = nds[c % 2]
            S.emit(S.pool, lambda: nc.gpsimd.tensor_tensor(out=a0.ap[:], in0=a0.ap[:], in1=a1.ap[:], op=ALU.add), reads=[a0, a1], writes=[a0])
            S.emit(S.pool, lambda: nc.gpsimd.tensor_tensor(out=a0.ap[:], in0=a0.ap[:], in1=a2.ap[:], op=ALU.add), reads=[a0, a2], writes=[a0])
            S.emit(S.dve, lambda: nc.vector.reciprocal(out=rden.ap[:].unsqueeze(2), in_=a0.ap[:, :, 128:129]), reads=[a0], writes=[rden])
            S.emit(S.dve, lambda: nc.vector.tensor_tensor(out=o16.ap[:].rearrange("p (h e) -> p h e", h=8), in0=a0.ap[:, :, 0:128], in1=bc(rden.ap[:].unsqueeze(2), [128, 8, 128]), op=ALU.mult), reads=[a0, rden], writes=[o16])
            for h in range(8):
                S.emit(S.pe, lambda: nc.tensor.transpose(out=ptr.ap[:, h * 128:(h + 1) * 128], in_=o16.ap[:, h * 128:(h + 1) * 128], identity=K["ident"].ap[:]),
                       reads=[o16, K["ident"]], writes=[ptr], pe_acc=True, sig=(h == 7))
            S.emit(S.act, lambda: nc.scalar.copy(out=oT.ap[:], in_=ptr.ap[:, 0:1024].rearrange("p (k t) -> p k t", k=8)), reads=[ptr], writes=[oT])
            for n2 in range(2):
                pq = pmm[nmm % 2]
                nmm += 1
                for kk in range(8):
                    S.emit(S.pe, lambda: nc.tensor.matmul(pq.ap[:], lhsT=oT.ap[:, kk, :], rhs=wo.ap[:, kk, n2 * 512:(n2 + 1) * 512], start=(kk == 0), stop=(kk == 7)),
                           reads=[oT, wo], writes=[pq], pe_acc=True, sig=(kk == 7))
                S.emit(S.dve, lambda: nc.vector.tensor_tensor(out=xt.ap[:, n2 * 512:(n2 + 1) * 512], in0=pq.ap[:], in1=xt.ap[:, n2 * 512:(n2 + 1) * 512], op=ALU.add), reads=[pq, xt], writes=[xt])
            S.dma(S.pool, dst[c * 128:(c + 1) * 128, :], xt.ap[:], reads=[xt], chan_buf=xt)
        S.barrier()

def build(phases=("m", "f0", "a", "f1"), debug=False):
    nc = bass.Bass("TRN2", target_bir_lowering=False)
    I = {}

    def inp(name, shape):
        I[name] = nc.dram_tensor(name, shape, F32, kind="ExternalInput").ap()

    inp("x", [S_LEN, D])
    inp("a_norm", [D]); inp("ssm_w_in", [D, IN_DIM]); inp("ssm_conv_w", [4, XBC]); inp("ssm_conv_b", [XBC])
    inp("ssm_dt_bias", [32]); inp("ssm_a_log", [32]); inp("ssm_d", [32]); inp("ssm_norm", [D_IN]); inp("ssm_w_out", [D_IN, D])
    inp("kv_norm", [D]); inp("w_kv", [D, 1536]); inp("b_norm", [D]); inp("att_w_q", [D, 3072]); inp("att_w_o", [D, D])
    inp("ffn_norm", [2, D]); inp("ffn_w_up", [2, D, 2 * FF]); inp("ffn_conv_w", [2, 3, 2 * FF]); inp("ffn_w_down", [2, FF, D])
    inp("final_norm", [D]); inp("rope", [S_LEN, 32])
    out = nc.dram_tensor("out", [S_LEN, D], F32, kind="ExternalOutput").ap()
    kind = "ExternalOutput" if debug else "Internal"
    xa = nc.dram_tensor("xa", [S_LEN, D], F32, kind=kind).ap()
    xb = nc.dram_tensor("xb", [S_LEN, D], F32, kind=kind).ap()
    xc = nc.dram_tensor("xc", [S_LEN, D], F32, kind=kind).ap()
    with contextlib.ExitStack() as es:
        C = Ctx(nc, es)
        K = setup_consts(C, es)
        C.S.barrier()
        cur = I["x"]
        if "m" in phases:
            phase_mamba(C, K, I, cur, xa)
            cur = xa
        if "f0" in phases:
            phase_ffn(C, K, 0, cur, xb, I["ffn_w_up"][0], I["ffn_conv_w"][0], I["ffn_w_down"][0], I["ffn_norm"][0])
            cur = xb
        if "a" in phases:
            phase_attn(C, K, I, cur, xc)
            cur = xc
        if "f1" in phases:
            phase_ffn(C, K, 1, cur, out, I["ffn_w_up"][1], I["ffn_conv_w"][1], I["ffn_w_down"][1], I["ffn_norm"][1], final_gain=I["final_norm"])
        C.S.barrier()
    return nc


def rope_table():
    half = 16
    inv_freq = np.power(np.float32(500000.0), -np.arange(0, 32, 2, dtype=np.float32) / np.float32(32)).astype(np.float32)
    ang = (np.arange(S_LEN, dtype=np.float32)[:, None] * inv_freq[None, :]).astype(np.float32)
    return np.concatenate([np.cos(ang.astype(np.float64)), np.sin(ang.astype(np.float64))], axis=1).astype(np.float32)


def make_in_maps(inputs, n_cores):
    f = lambda a: np.ascontiguousarray(np.asarray(a, dtype=np.float32))
    shared = {
        "a_norm": f(inputs["a_norm"][0]), "ssm_w_in": f(inputs["ssm_w_in"][0]), "ssm_conv_w": f(inputs["ssm_conv_w"][0]),
        "ssm_conv_b": f(inputs["ssm_conv_b"][0]), "ssm_dt_bias": f(inputs["ssm_dt_bias"][0]), "ssm_a_log": f(inputs["ssm_a_log"][0]),
        "ssm_d": f(inputs["ssm_d"][0]), "ssm_norm": f(inputs["ssm_norm"][0]), "ssm_w_out": f(inputs["ssm_w_out"][0]),
        "kv_norm": f(inputs["kv_norm"]), "w_kv": f(inputs["w_kv"]), "b_norm": f(inputs["b_norm"][0]),
        "att_w_q": f(inputs["att_w_q"][0]), "att_w_o": f(inputs["att_w_o"][0]), "ffn_norm": f(inputs["ffn_norm"]),
        "ffn_w_up": f(inputs["ffn_w_up"]), "ffn_conv_w": f(inputs["ffn_conv_w"]), "ffn_w_down": f(inputs["ffn_w_down"]),
        "final_norm": f(inputs["final_norm"]), "rope": rope_table(),
    }
    x = f(inputs["x"])
    maps = []
    for c in range(n_cores):
        m = dict(shared)
        m["x"] = x[c]
        maps.append(m)
    return maps


def kernel(**inputs):
    nc = build()
    maps = make_in_maps(inputs, 8)
    res = run_bass_kernel_spmd(nc, maps, core_ids=list(range(8)))
    return np.stack([np.asarray(r["out"]) for r in res.results], axis=0).astype(np.float32)
```

```python
import contextlib
import numpy as np
import concourse.bass as bass
import concourse.mybir as mybir
from concourse.bass_utils import run_bass_kernel_spmd

F32 = mybir.dt.float32
BF16 = mybir.dt.bfloat16
AF = mybir.ActivationFunctionType
ALU = mybir.AluOpType

S_LEN = 4096
D = 1024
NCH = S_LEN // 128
FF = 2816
NF = FF // 128
D_IN = 2048
XBC = 4096
IN_DIM = 6176
NEG = -30000.0


class Buf:
    __slots__ = ("ap", "lw", "rd", "name", "chan", "chan_sw")

    def __init__(self, ap, name=""):
        self.ap = ap
        self.lw = None
        self.rd = {}
        self.name = name
        self.chan = None
        self.chan_sw = None


class Eng:
    def __init__(self, name, q, sem):
        self.name = name
        self.q = q
        self.sem = sem
        self.count = 0
        self.seen = {}


class Sched:
    def __init__(self, nc, es):
        self.nc = nc
        self.es = es
        mk = lambda n, q: Eng(n, q, es.enter_context(nc.semaphore("sem_" + n)))
        self.pe = mk("pe", nc.tensor)
        self.act = mk("act", nc.scalar)
        self.dve = mk("dve", nc.vector)
        self.pool = mk("pool", nc.gpsimd)
        self.sp = mk("sp", nc.sync)
        self.engs = [self.pe, self.act, self.dve, self.pool, self.sp]
        self.chans = []
        self.free_chans = []

    def new_chan(self, name):
        if self.free_chans:
            c = self.free_chans.pop()
            return c
        c = Eng("ch%d_%s" % (len(self.chans), name), None, self.es.enter_context(self.nc.semaphore("semch%d" % len(self.chans))))
        self.chans.append(c)
        return c

    def _waits(self, eng, reads, writes, pe_acc):
        waits = {}

        def need(dep):
            e, c = dep
            if eng.seen.get(e.name, 0) < c:
                if waits.get(e.name, (None, 0))[1] < c:
                    waits[e.name] = (e, c)

        for b in reads:
            if b.lw is not None:
                need(b.lw)
        for b in writes:
            if b.lw is not None and not (pe_acc and b.lw[0] is eng):
                need(b.lw)
            for dep in b.rd.values():
                need(dep)
        for e, c in waits.values():
            eng.q.wait_ge(e.sem, c)
            eng.seen[e.name] = c

    def emit(self, eng, fn, reads=(), writes=(), pe_acc=False, sig=True):
        self._waits(eng, reads, writes, pe_acc)
        ins = fn()
        if sig:
            eng.count += 1
            ins.then_inc(eng.sem, 1)
            cnt = eng.count
        else:
            cnt = eng.count + 1
        for b in reads:
            b.rd[eng.name] = (eng, cnt)
        for b in writes:
            b.lw = (eng, cnt)
            b.rd = {}
        return ins

    def dma(self, eng, out_ap, in_ap, reads=(), writes=(), chan_buf=None):
        if eng is self.pool:
            if chan_buf.chan_sw is None:
                chan_buf.chan_sw = self.new_chan(chan_buf.name + "_sw")
            ch = chan_buf.chan_sw
        else:
            if chan_buf.chan is None:
                chan_buf.chan = self.new_chan(chan_buf.name)
            ch = chan_buf.chan
        self._waits(eng, reads, writes, False)
        ins = eng.q.dma_start(out=out_ap, in_=in_ap)
        ch.count += 16
        ins.then_inc(ch.sem, 16)
        for b in reads:
            b.rd[ch.name] = (ch, ch.count)
        for b in writes:
            b.lw = (ch, ch.count)
            b.rd = {}
        return ins

    def barrier(self):
        allsrc = self.engs + self.chans
        for e in self.engs:
            for o in allsrc:
                if o is e or o.count == 0:
                    continue
                if e.seen.get(o.name, 0) < o.count:
                    e.q.wait_ge(o.sem, o.count)
                    e.seen[o.name] = o.count


class Ctx:
    def __init__(self, nc, es):
        self.nc = nc
        self.S = Sched(nc, es)
        self.n = 0
        self.rr = 0

    def sb(self, es, name, shape, dt):
        self.n += 1
        t = es.enter_context(self.nc.sbuf_tensor("%s_%d" % (name, self.n), shape, dt))
        return Buf(t, name)

    def ps(self, es, name, shape, dt):
        self.n += 1
        t = es.enter_context(self.nc.psum_tensor("%s_%d" % (name, self.n), shape, dt))
        return Buf(t, name)


def bc(ap, shape):
    return ap.broadcast_to(shape)


def setup_consts(C, es):
    nc, S = C.nc, C.S
    k = {}
    identf = C.sb(es, "identf", [128, 128], F32)
    ident = C.sb(es, "ident", [128, 128], BF16)
    S.emit(S.pool, lambda: nc.gpsimd.memset(identf.ap[:], 1.0), writes=[identf])
    S.emit(S.pool, lambda: nc.gpsimd.affine_select(out=identf.ap[:], in_=identf.ap[:], pattern=[[-1, 128]], compare_op=ALU.is_equal, fill=0.0, base=0, channel_multiplier=1), reads=[identf], writes=[identf])
    S.emit(S.dve, lambda: nc.vector.tensor_copy(out=ident.ap[:], in_=identf.ap[:]), reads=[identf], writes=[ident])
    mhalf = C.sb(es, "mhalf", [128, 8], F32)
    S.emit(S.pool, lambda: nc.gpsimd.memset(mhalf.ap[:], -0.5), writes=[mhalf])
    nmf = C.sb(es, "nmf", [128, 128], F32)
    nmc = C.sb(es, "nmc", [128, 512], BF16)
    nmp = C.sb(es, "nmp", [128, 512], BF16)
    S.emit(S.pool, lambda: nc.gpsimd.memset(nmf.ap[:], 0.0), writes=[nmf])
    S.emit(S.pool, lambda: nc.gpsimd.affine_select(out=nmf.ap[:], in_=nmf.ap[:], pattern=[[1, 128]], compare_op=ALU.is_ge, fill=NEG, base=0, channel_multiplier=-1), reads=[nmf], writes=[nmf])
    S.emit(S.dve, lambda: nc.vector.tensor_copy(out=nmc.ap[:].rearrange("p (h l) -> p h l", h=4), in_=bc(nmf.ap[:].unsqueeze(1), [128, 4, 128])), reads=[nmf], writes=[nmc])
    S.emit(S.pool, lambda: nc.gpsimd.memset(nmf.ap[:], 0.0), reads=[nmf], writes=[nmf])
    S.emit(S.pool, lambda: nc.gpsimd.affine_select(out=nmf.ap[:], in_=nmf.ap[:], pattern=[[-1, 128]], compare_op=ALU.is_ge, fill=NEG, base=0, channel_multiplier=1), reads=[nmf], writes=[nmf])
    S.emit(S.dve, lambda: nc.vector.tensor_copy(out=nmp.ap[:].rearrange("p (h l) -> p h l", h=4), in_=bc(nmf.ap[:].unsqueeze(1), [128, 4, 128])), reads=[nmf], writes=[nmp])
    k.update(identf=identf, ident=ident, mhalf=mhalf, nmc=nmc, nmp=nmp)
    return k


def prep_weight(C, es_stage, dst, src, nk, ncols, gain=None, col0=0):
    nc, S = C.nc, C.S
    stg = C.stg
    CW = 2048
    for kk in range(nk):
        for c0 in range(0, ncols, CW):
            cw = min(CW, ncols - c0)
            st = stg[C.rr % len(stg)]
            S.dma(S.sp, st.ap[:, 0:cw], src[kk * 128:(kk + 1) * 128, c0:c0 + cw], writes=[st], chan_buf=st)
            o = dst.ap[:, kk, col0 + c0:col0 + c0 + cw]
            which = (C.rr % 3) if gain is None else (C.rr % 2)
            C.rr += 1
            rds = [st] + ([gain] if gain is not None else [])
            if gain is None:
                if which == 0:
                    S.emit(S.act, lambda: nc.scalar.copy(out=o, in_=st.ap[:, 0:cw]), reads=rds, writes=[dst])
                elif which == 1:
                    S.emit(S.dve, lambda: nc.vector.tensor_copy(out=o, in_=st.ap[:, 0:cw]), reads=rds, writes=[dst])
                else:
                    S.emit(S.pool, lambda: nc.gpsimd.tensor_copy(out=o, in_=st.ap[:, 0:cw]), reads=rds, writes=[dst])
            else:
                g = gain.ap[:, kk:kk + 1]
                if which == 0:
                    S.emit(S.act, lambda: nc.scalar.mul(out=o, in_=st.ap[:, 0:cw], mul=g), reads=rds, writes=[dst])
                elif which == 1:
                    S.emit(S.dve, lambda: nc.vector.tensor_scalar(out=o, in0=st.ap[:, 0:cw], scalar1=g, scalar2=None, op0=ALU.mult), reads=rds, writes=[dst])
                else:
                    S.emit(S.pool, lambda: nc.gpsimd.tensor_scalar(out=o, in0=st.ap[:, 0:cw], scalar1=g, scalar2=None, op0=ALU.mult), reads=rds, writes=[dst])


def load_cols(C, dst, src_vec, nk):
    S = C.S
    with C.nc.allow_non_contiguous_dma(reason="small param vector"):
        S.dma(S.sp, dst.ap[:, 0:nk], src_vec.rearrange("(k p) -> p k", p=128), writes=[dst], chan_buf=dst)


def rms_rstd(C, x_ap, xbuf, junk, ss, rstd, K, n_feat, eps):
    nc, S = C.nc, C.S
    S.emit(S.act, lambda: nc.scalar.activation(out=junk.ap[:, 0:n_feat], in_=x_ap, func=AF.Square, accum_out=ss.ap[:, 0:1]), reads=[xbuf], writes=[junk, ss])
    S.emit(S.dve, lambda: nc.vector.tensor_scalar(out=ss.ap[:, 0:1], in0=ss.ap[:, 0:1], scalar1=1.0 / n_feat, scalar2=eps, op0=ALU.mult, op1=ALU.add), reads=[ss], writes=[ss])
    S.emit(S.pool, lambda: nc.gpsimd.tensor_tensor(out=rstd.ap[:, 0:1], in0=ss.ap[:, 0:1], in1=K["mhalf"].ap[:, 0:1], op=ALU.pow), reads=[ss, K["mhalf"]], writes=[rstd])


def norm_transpose(C, K, x_ap, xbuf, junk, ss, rstd, xn, ptr, xnT_ap, xnT, extra=()):
    nc, S = C.nc, C.S
    norm_only(C, K, x_ap, xbuf, junk, ss, rstd, xn)
    transpose_only(C, K, xn, ptr, xnT_ap, xnT, extra)


def norm_only(C, K, x_ap, xbuf, junk, ss, rstd, xn):
    nc, S = C.nc, C.S
    rms_rstd(C, x_ap, xbuf, junk, ss, rstd, K, D, 1e-6)
    S.emit(S.act, lambda: nc.scalar.mul(out=xn.ap[:, 0:D], in_=x_ap, mul=rstd.ap[:, 0:1]), reads=[xbuf, rstd], writes=[xn])


def transpose_only(C, K, xn, ptr, xnT_ap, xnT, extra=()):
    nc, S = C.nc, C.S
    for kk in range(8):
        S.emit(S.pe, lambda: nc.tensor.transpose(out=ptr.ap[:, kk * 128:(kk + 1) * 128], in_=xn.ap[:, kk * 128:(kk + 1) * 128], identity=K["ident"].ap[:]),
               reads=[xn, K["ident"]], writes=[ptr] + list(extra), pe_acc=True, sig=(kk == 7))
    S.emit(S.dve, lambda: nc.vector.tensor_copy(out=xnT_ap, in_=ptr.ap[:, 0:1024].rearrange("p (k t) -> p k t", k=8)), reads=[ptr], writes=[xnT])


def phase_ffn(C, K, layer, src, dst, w_up, conv_w, w_dn, gain_vec, final_gain=None):
    nc, S = C.nc, C.S
    T = 256
    NT = S_LEN // T
    with contextlib.ExitStack() as es:
        wup = C.sb(es, "wup", [128, 8, 2 * FF], BF16)
        wdn = C.sb(es, "wdn", [128, NF, D], BF16)
        gain = C.sb(es, "gain", [128, 8], F32)
        cw = C.sb(es, "cw", [128, 2 * NF, 3], F32)
        load_cols(C, gain, gain_vec, 8)
        with nc.allow_non_contiguous_dma(reason="conv weights"):
            for kk in range(3):
                S.dma(S.sp, cw.ap[:, :, kk], conv_w[kk].rearrange("(j p) -> p j", p=128), writes=[cw], chan_buf=cw)
        with contextlib.ExitStack() as es2:
            C.stg = [C.sb(es2, "stg", [128, 2048], F32) for _ in range(6)]
            prep_weight(C, es2, wup, w_up, 8, 2 * FF, gain)
            prep_weight(C, es2, wdn, w_dn, NF, D, None)
            S.barrier()
        xts = [C.sb(es, "xt", [128, 2, D], F32) for _ in range(2)]
        xn = C.sb(es, "xn", [128, D], BF16)
        xnTs = [C.sb(es, "xnT", [128, 8, T + 2], BF16) for _ in range(2)]
        ss = C.sb(es, "ss", [128, 1], F32)
        rstd = C.sb(es, "rstd", [128, 1], F32)
        ags = [C.sb(es, "ag", [128, T], F32) for _ in range(3)]
        avs = [C.sb(es, "av", [128, T], F32) for _ in range(3)]
        sgs = [C.sb(es, "sg", [128, T], F32) for _ in range(3)]
        h2_t = es.enter_context(nc.sbuf_tensor("h2T_all%d" % layer, [128, NF, T], BF16))
        h2T = [Buf(h2_t[:, j, :], "h2T%d" % j) for j in range(NF)]
        if final_gain is not None:
            gfin = C.sb(es, "gfin", [128, D], F32)
            obuf = [C.sb(es, "obuf", [128, D], F32) for _ in range(2)]
            S.dma(S.sp, gfin.ap[:], final_gain.partition_broadcast(128), writes=[gfin], chan_buf=gfin)
        S.emit(S.pool, lambda: nc.gpsimd.memset(xnTs[0].ap[:, :, 0:2], 0.0), writes=[xnTs[0]])
        ptrs = [C.ps(es, "ptr", [128, 1024], BF16) for _ in range(2)]
        pus = [C.ps(es, "pu", [128, 512], F32) for _ in range(4)]
        pds = [C.ps(es, "pd", [128, 512], F32) for _ in range(2)]

        def load(i):
            xt = xts[i % 2]
            S.dma(S.sp, xt.ap[:], src[i * T:(i + 1) * T, :].rearrange("(c p) d -> p c d", p=128), writes=[xt], chan_buf=xt)

        xn2 = [xn, C.sb(es, "xnb", [128, D], BF16)]
        ss2 = [ss, C.sb(es, "ssb", [128, 1], F32)]
        rstd2 = [rstd, C.sb(es, "rstdb", [128, 1], F32)]

        def head_norm(i):
            xt = xts[i % 2]
            for c in range(2):
                norm_only(C, K, xt.ap[:, c, :], xt, xn2[c], ss2[c], rstd2[c], xn2[c])

        def head_tr(i):
            xnT = xnTs[i % 2]
            if i > 0:
                S.emit(S.pool, lambda: nc.gpsimd.tensor_copy(out=xnT.ap[:, :, 0:2], in_=xnTs[(i - 1) % 2].ap[:, :, T:T + 2]), reads=[xnTs[(i - 1) % 2]], writes=[xnT])
            for c in range(2):
                transpose_only(C, K, xn2[c], ptrs[c], xnT.ap[:, :, 2 + c * 128:2 + (c + 1) * 128], xnT)

        def head(i):
            head_norm(i)
            head_tr(i)

        load(0)
        head(0)
        npu = 0
        npd = 0
        nb = 0
        for i in range(NT):
            xt = xts[i % 2]
            xnT = xnTs[i % 2]
            if i + 1 < NT:
                load(i + 1)
            for j in range(NF):
                pg, pv = pus[npu % 4], pus[(npu + 1) % 4]
                npu += 2
                ag, av, sgb = ags[nb % 3], avs[nb % 3], sgs[nb % 3]
                nb += 1
                for (pp, col) in ((pg, j), (pv, NF + j)):
                    for kk in range(8):
                        S.emit(S.pe, lambda: nc.tensor.matmul(pp.ap[:, 0:T + 2], lhsT=wup.ap[:, kk, col * 128:(col + 1) * 128], rhs=xnT.ap[:, kk, :], start=(kk == 0), stop=(kk == 7)),
                               reads=[wup, xnT], writes=[pp], pe_acc=True, sig=(kk == 7))
                S.emit(S.act, lambda: nc.scalar.mul(out=ag.ap[:], in_=pg.ap[:, 2:T + 2], mul=cw.ap[:, j, 2:3]), reads=[pg, cw], writes=[ag])
                S.emit(S.act, lambda: nc.scalar.mul(out=av.ap[:], in_=pv.ap[:, 2:T + 2], mul=cw.ap[:, NF + j, 2:3]), reads=[pv, cw], writes=[av])
                for tap in (1, 0):
                    S.emit(S.dve, lambda: nc.vector.scalar_tensor_tensor(out=ag.ap[:], in0=pg.ap[:, tap:tap + T], scalar=cw.ap[:, j, tap:tap + 1], in1=ag.ap[:], op0=ALU.mult, op1=ALU.add), reads=[pg, cw, ag], writes=[ag])
                for tap in (1, 0):
                    S.emit(S.dve, lambda: nc.vector.scalar_tensor_tensor(out=av.ap[:], in0=pv.ap[:, tap:tap + T], scalar=cw.ap[:, NF + j, tap:tap + 1], in1=av.ap[:], op0=ALU.mult, op1=ALU.add), reads=[pv, cw, av], writes=[av])
                S.emit(S.act, lambda: nc.scalar.activation(out=sgb.ap[:], in_=ag.ap[:], func=AF.Silu), reads=[ag], writes=[sgb])
                S.emit(S.pool, lambda: nc.gpsimd.tensor_tensor(out=h2T[j].ap[:], in0=sgb.ap[:], in1=av.ap[:], op=ALU.mult), reads=[sgb, av], writes=[h2T[j]])
                if j == 3 and i + 1 < NT:
                    head_norm(i + 1)
                if j == 12 and i + 1 < NT:
                    head_tr(i + 1)
            for c in range(2):
                for n in range(2):
                    pd = pds[npd % 2]
                    npd += 1
                    for j in range(NF):
                        S.emit(S.pe, lambda: nc.tensor.matmul(pd.ap[:], lhsT=h2T[j].ap[:, c * 128:(c + 1) * 128], rhs=wdn.ap[:, j, n * 512:(n + 1) * 512], start=(j == 0), stop=(j == NF - 1)),
                               reads=[h2T[j], wdn], writes=[pd], pe_acc=True, sig=(j == NF - 1))
                    S.emit(S.dve, lambda: nc.vector.tensor_tensor(out=xt.ap[:, c, n * 512:(n + 1) * 512], in0=pd.ap[:], in1=xt.ap[:, c, n * 512:(n + 1) * 512], op=ALU.add), reads=[pd, xt], writes=[xt])
                if final_gain is not None:
                    ob = obuf[c]
                    rms_rstd(C, xt.ap[:, c, :], xt, xn, ss, rstd, K, D, 1e-6)
                    S.emit(S.dve, lambda: nc.vector.scalar_tensor_tensor(out=ob.ap[:], in0=xt.ap[:, c, :], scalar=rstd.ap[:, 0:1], in1=gfin.ap[:], op0=ALU.mult, op1=ALU.mult), reads=[xt, rstd, gfin], writes=[ob])
                    S.dma(S.pool, dst[i * T + c * 128:i * T + (c + 1) * 128, :], ob.ap[:], reads=[ob], chan_buf=ob)
            if final_gain is None:
                S.dma(S.pool, dst[i * T:(i + 1) * T, :].rearrange("(c p) d -> p c d", p=128), xt.ap[:], reads=[xt], chan_buf=xt)
        S.barrier()


def phase_mamba(C, K, I, src, dst):
    nc, S = C.nc, C.S
    ynd = nc.dram_tensor("ynd", [S_LEN, D_IN], BF16, kind="Internal").ap()
    ident, identf = K["ident"], K["identf"]
    T = 256
    NT = S_LEN // T
    with contextlib.ExitStack() as es:
        win = C.sb(es, "win", [128, 8, IN_DIM], BF16)
        gain = C.sb(es, "gain", [128, 8], F32)
        cw = C.sb(es, "cw", [128, 32, 4], F32)
        cb = C.sb(es, "cb", [128, 32], F32)
        dcol = C.sb(es, "dcol", [128, 16], F32)
        diagD = C.sb(es, "diagD", [128, 16, 128], BF16)
        dtb = C.sb(es, "dtb", [32, 1], F32)
        acol = C.sb(es, "acol", [32, 1], F32)
        ones32 = C.sb(es, "ones32", [32, 128], F32)
        load_cols(C, gain, I["a_norm"], 8)
        load_cols(C, cb, I["ssm_conv_b"], 32)
        with nc.allow_non_contiguous_dma(reason="small params"):
            for kk in range(4):
                S.dma(S.sp, cw.ap[:, :, kk], I["ssm_conv_w"][kk].rearrange("(j p) -> p j", p=128), writes=[cw], chan_buf=cw)
            S.dma(S.sp, dtb.ap[:, 0:1], I["ssm_dt_bias"].rearrange("(h o) -> h o", o=1), writes=[dtb], chan_buf=dtb)
            S.dma(S.sp, acol.ap[:, 0:1], I["ssm_a_log"].rearrange("(h o) -> h o", o=1), writes=[acol], chan_buf=acol)
            dv = I["ssm_d"].rearrange("(k two) -> two k", two=2)
            for hh in range(2):
                S.dma(S.sp, dcol.ap[hh * 64:(hh + 1) * 64, :], dv[hh].partition_broadcast(64), writes=[dcol], chan_buf=dcol)
        S.emit(S.act, lambda: nc.scalar.activation(out=acol.ap[:], in_=acol.ap[:], func=AF.Exp), reads=[acol], writes=[acol])
        S.emit(S.dve, lambda: nc.vector.tensor_scalar(out=acol.ap[:], in0=acol.ap[:], scalar1=-1.0, scalar2=None, op0=ALU.mult), reads=[acol], writes=[acol])
        S.emit(S.pool, lambda: nc.gpsimd.memset(ones32.ap[:], 1.0), writes=[ones32])
        for kk in range(16):
            S.emit(S.dve, lambda: nc.vector.tensor_scalar(out=diagD.ap[:, kk, :], in0=identf.ap[:], scalar1=dcol.ap[:, kk:kk + 1], scalar2=None, op0=ALU.mult), reads=[identf, dcol], writes=[diagD])
        with contextlib.ExitStack() as es2:
            C.stg = [C.sb(es2, "stg", [128, 2048], F32) for _ in range(6)]
            prep_weight(C, es2, win, I["ssm_w_in"], 8, IN_DIM, gain)
            S.barrier()
        xts = [C.sb(es, "xt", [128, 2, D], F32) for _ in range(2)]
        xn = C.sb(es, "xn", [128, D], BF16)
        xnT2s = [C.sb(es, "xnT2", [128, 8, T + 4], BF16) for _ in range(2)]
        ss = C.sb(es, "ss", [128, 1], F32)
        rstd = C.sb(es, "rstd", [128, 1], F32)
        szs = [C.sb(es, "sz", [128, 512], F32) for _ in range(2)]
        szeas = [C.sb(es, "szea", [128, 512], F32) for _ in range(2)]
        xbcT_t = es.enter_context(nc.sbuf_tensor("xbcT_all", [128, 32, T], BF16))
        xbcT = [Buf(xbcT_t[:, j, :], "xbcT%d" % j) for j in range(32)]
        accs = [C.sb(es, "acc", [128, T], F32) for _ in range(3)]
        xcgs = [C.sb(es, "xcg", [128, 256], BF16) for _ in range(3)]
        xcdgs = [C.sb(es, "xcdg", [128, 256], BF16) for _ in range(3)]
        Btgs = [C.sb(es, "Btg", [128, 128], BF16) for _ in range(3)]
        ADs = [C.sb(es, "AD", [32, 512], F32) for _ in range(2)]
        Egs = [C.sb(es, "Eg", [128, 512], F32) for _ in range(2)]
        MTs = [C.sb(es, "MT", [128, 512], BF16) for _ in range(2)]
        prevT_t = es.enter_context(nc.sbuf_tensor("prevT_all", [128, D_IN], F32))
        prevbf_t = es.enter_context(nc.sbuf_tensor("prevbf_all", [128, D_IN], BF16))
        prevT = [Buf(prevT_t[:, g * 256:(g + 1) * 256], "prevT%d" % g) for g in range(8)]
        prevbf = [Buf(prevbf_t[:, g * 256:(g + 1) * 256], "prevbf%d" % g) for g in range(8)]
        yts = [C.sb(es, "yt", [128, 256], F32) for _ in range(2)]
        yus = [C.sb(es, "yu", [128, 256], F32) for _ in range(2)]
        yzs = [C.sb(es, "yz", [128, 256], F32) for _ in range(4)]
        junkg = C.sb(es, "junkg", [128, 256], BF16)
        ssgs = [C.sb(es, "ssg", [128, 1], F32) for _ in range(4)]
        rgs = [C.sb(es, "rg", [128, 1], F32) for _ in range(4)]
        yns = [C.sb(es, "yn", [128, D_IN], BF16) for _ in range(2)]
        sm = {n: C.sb(es, n, [32, T], F32) for n in ("dtT", "adtT", "acsT", "nacsT", "decT")}
        sm["e1"] = sm["dtT"]
        sm["dtdecT"] = sm["decT"]
        ecol = C.sb(es, "ecol", [32, 2], F32)
        dgs = [C.sb(es, "dg", [32, 32], F32) for _ in range(2)]
        toks = [C.sb(es, "tok", [128, 160], F32) for _ in range(2)]
        for g in range(8):
            S.emit(S.pool, lambda: nc.gpsimd.memset(prevT[g].ap[:], 0.0), writes=[prevT[g]])
            S.emit(S.pool, lambda: nc.gpsimd.memset(prevbf[g].ap[:], 0.0), writes=[prevbf[g]])
        S.emit(S.pool, lambda: nc.gpsimd.memset(xnT2s[0].ap[:, :, 0:4], 0.0), writes=[xnT2s[0]])
        bkA = [C.ps(es, "bkA", [128, 1024], BF16) for _ in range(2)]
        pgen2 = [C.ps(es, "pgen", [128, 512], F32) for _ in range(2)]
        pzb1 = C.ps(es, "pzb", [128, 512], F32)
        pgen = pgen2 + [pzb1]
        pzb = [pzb1, pzb1]
        pys = [C.ps(es, "py", [128, 512], F32) for _ in range(2)]
        pst1 = C.ps(es, "pst", [128, 512], F32)
        psts = [pst1, pst1]
        psgs = [None] * 4
        cnt = {"g": 0, "tr": 0, "acc": 0}

        def nxt(lst, key):
            b = lst[cnt[key] % len(lst)]
            cnt[key] += 1
            return b

        def load(i):
            xt = xts[i % 2]
            S.dma(S.sp, xt.ap[:], src[i * T:(i + 1) * T, :].rearrange("(c p) d -> p c d", p=128), writes=[xt], chan_buf=xt)

        e1, dtT, adtT, acsT, nacsT, decT, dtdecT = [sm[n] for n in ("e1", "dtT", "adtT", "acsT", "nacsT", "decT", "dtdecT")]
        load(0)
        for i in range(NT):
            xt = xts[i % 2]
            xnT2 = xnT2s[i % 2]
            if i + 1 < NT:
                load(i + 1)
            for c in range(2):
                norm_transpose(C, K, xt.ap[:, c, :], xt, xn, ss, rstd, xn, bkA[c], xnT2.ap[:, :, 4 + c * 128:4 + (c + 1) * 128], xnT2)
            S.emit(S.pool, lambda: nc.gpsimd.tensor_copy(out=xnT2s[(i + 1) % 2].ap[:, :, 0:4], in_=xnT2.ap[:, :, T:T + 4]), reads=[xnT2], writes=[xnT2s[(i + 1) % 2]])
            pdt = nxt(pgen, "g")
            for kk in range(8):
                S.emit(S.pe, lambda: nc.tensor.matmul(pdt.ap[0:32, 0:T], lhsT=win.ap[:, kk, 6144:6176], rhs=xnT2.ap[:, kk, 4:T + 4], start=(kk == 0), stop=(kk == 7)),
                       reads=[win, xnT2], writes=[pdt], pe_acc=True, sig=(kk == 7))
            S.emit(S.act, lambda: nc.scalar.activation(out=e1.ap[:], in_=pdt.ap[0:32, 0:T], func=AF.Exp, bias=dtb.ap[:, 0:1]), reads=[pdt, dtb], writes=[e1])
            S.emit(S.act, lambda: nc.scalar.activation(out=dtT.ap[:], in_=e1.ap[:], func=AF.Ln, bias=1.0), reads=[e1], writes=[dtT])
            S.emit(S.dve, lambda: nc.vector.tensor_scalar(out=adtT.ap[:], in0=dtT.ap[:], scalar1=acol.ap[:, 0:1], scalar2=None, op0=ALU.mult), reads=[dtT, acol], writes=[adtT])
            for c in range(2):
                cs = slice(c * 128, (c + 1) * 128)
                S.emit(S.dve, lambda: nc.vector.tensor_tensor_scan(out=acsT.ap[:, cs], data0=ones32.ap[:], data1=adtT.ap[:, cs], initial=0.0, op0=ALU.mult, op1=ALU.add), reads=[ones32, adtT], writes=[acsT])
            S.emit(S.act, lambda: nc.scalar.mul(out=nacsT.ap[:], in_=acsT.ap[:], mul=-1.0), reads=[acsT], writes=[nacsT])
            for c in range(2):
                cs = slice(c * 128, (c + 1) * 128)
                last = acsT.ap[:, c * 128 + 127:c * 128 + 128]
                S.emit(S.act, lambda: nc.scalar.activation(out=decT.ap[:, cs], in_=acsT.ap[:, cs], func=AF.Exp, scale=-1.0, bias=last), reads=[acsT], writes=[decT])
                S.emit(S.act, lambda: nc.scalar.activation(out=ecol.ap[:, c:c + 1], in_=last, func=AF.Exp), reads=[acsT], writes=[ecol])
                S.emit(S.dve, lambda: nc.vector.tensor_scalar(out=dgs[c].ap[:], in0=identf.ap[0:32, 0:32], scalar1=ecol.ap[:, c:c + 1], scalar2=None, op0=ALU.mult), reads=[identf, ecol], writes=[dgs[c]])
            S.emit(S.dve, lambda: nc.vector.tensor_tensor(out=dtdecT.ap[:], in0=dtT.ap[:], in1=decT.ap[:], op=ALU.mult), reads=[dtT, decT], writes=[dtdecT])
            for c in range(2):
                cs = slice(c * 128, (c + 1) * 128)
                tok = toks[c]
                ptok = nxt(pgen, "g")
                for i4, (lh, rh) in enumerate(((dtT, None), (dtdecT, None), (nacsT, None), (ones32, dgs[c]))):
                    rhs_ap = identf.ap[0:32, 0:32] if rh is None else rh.ap[:]
                    lh_ap = lh.ap[:, cs] if rh is None else lh.ap[:]
                    S.emit(S.pe, lambda: nc.tensor.matmul(ptok.ap[:, i4 * 32:(i4 + 1) * 32], lhsT=lh_ap, rhs=rhs_ap, start=True, stop=True),
                           reads=[lh, identf] + ([rh] if rh is not None else []), writes=[ptok], pe_acc=True, sig=(i4 == 3))
                S.emit(S.dve, lambda: nc.vector.tensor_copy(out=tok.ap[:, 0:128], in_=ptok.ap[:, 0:128]), reads=[ptok], writes=[tok])
                S.emit(S.act, lambda: nc.scalar.activation(out=tok.ap[:, 128:160], in_=ptok.ap[:, 64:96], func=AF.Exp, scale=-1.0), reads=[ptok], writes=[tok])
            cring = [pgen2[0], pgen2[1], pzb1, pys[0], pys[1], pst1]
            ctmp = yts

            def c0(j):
                pj = cring[j % 6]
                for kk in range(8):
                    S.emit(S.pe, lambda: nc.tensor.matmul(pj.ap[:, 0:T + 4], lhsT=win.ap[:, kk, 2048 + j * 128:2048 + (j + 1) * 128], rhs=xnT2.ap[:, kk, :], start=(kk == 0), stop=(kk == 7)),
                           reads=[win, xnT2], writes=[pj], pe_acc=True, sig=(kk == 7))

            def c1(j):
                pj = cring[j % 6]
                acc, tm = accs[j % 3], ctmp[j % 2]
                S.emit(S.act, lambda: nc.scalar.activation(out=acc.ap[:], in_=pj.ap[:, 4:T + 4], func=AF.Identity, scale=cw.ap[:, j, 3:4], bias=cb.ap[:, j:j + 1]), reads=[pj, cw, cb], writes=[acc])
                S.emit(S.act, lambda: nc.scalar.mul(out=tm.ap[:], in_=pj.ap[:, 3:T + 3], mul=cw.ap[:, j, 2:3]), reads=[pj, cw], writes=[tm])

            def c2(j):
                pj = cring[j % 6]
                acc, tm = accs[j % 3], ctmp[j % 2]
                S.emit(S.pool, lambda: nc.gpsimd.tensor_tensor(out=acc.ap[:], in0=acc.ap[:], in1=tm.ap[:], op=ALU.add), reads=[acc, tm], writes=[acc])
                for tap in (1, 0):
                    S.emit(S.dve, lambda: nc.vector.scalar_tensor_tensor(out=acc.ap[:], in0=pj.ap[:, 1 + tap:1 + tap + T], scalar=cw.ap[:, j, tap:tap + 1], in1=acc.ap[:], op0=ALU.mult, op1=ALU.add), reads=[pj, cw, acc], writes=[acc])

            def c3(j):
                acc = accs[j % 3]
                S.emit(S.act, lambda: nc.scalar.activation(out=xbcT[j].ap[:], in_=acc.ap[:], func=AF.Silu), reads=[acc], writes=[xbcT[j]])

            cst = [c0, c1, c2, c3]
            for t in range(32 + len(cst) - 1):
                for k in reversed(range(len(cst))):
                    j = t - k
                    if 0 <= j < 32:
                        cst[k](j)
            def info(gi):
                c, g = gi // 8, gi % 8
                return c, g, slice(c * 128, (c + 1) * 128), toks[c]

            def st_ad(gi):
                c, g, cs, tok = info(gi)
                AD = ADs[gi % 2]
                S.emit(S.pool, lambda: nc.gpsimd.affine_select(out=AD.ap[:].rearrange("p (h l) -> p h l", h=4), in_=bc(acsT.ap[:, cs].unsqueeze(1), [32, 4, 128]), pattern=[[-1, 4], [0, 128]], compare_op=ALU.is_equal, fill=0.0, base=-4 * g, channel_multiplier=1), reads=[acsT], writes=[AD])

            def st0(gi):
                c, g, cs, tok = info(gi)
                bk = bkA[gi % 2]
                for jj, j in enumerate((2 * g, 2 * g + 1, 16 + g)):
                    S.emit(S.pe, lambda: nc.tensor.transpose(out=bk.ap[:, jj * 128:(jj + 1) * 128], in_=xbcT[j].ap[:, cs], identity=ident.ap[:]),
                           reads=[xbcT[j], ident], writes=[bk], pe_acc=True, sig=False)
                S.emit(S.pe, lambda: nc.tensor.matmul(bk.ap[:, 512:768].bitcast(F32), lhsT=xbcT[16 + g].ap[:, cs], rhs=xbcT[24 + g].ap[:, cs], start=True, stop=True), reads=[xbcT[16 + g], xbcT[24 + g]], writes=[bk], pe_acc=True)
                psg = pgen2[gi % 2]
                psgs[gi % 4] = psg
                AD = ADs[gi % 2]
                S.emit(S.pe, lambda: nc.tensor.matmul(psg.ap[:], lhsT=ones32.ap[:], rhs=AD.ap[:], start=True, stop=False), reads=[ones32, AD], writes=[psg], pe_acc=True, sig=False)
                S.emit(S.pe, lambda: nc.tensor.matmul(psg.ap[:], lhsT=ident.ap[:], rhs=K["nmc"].ap[:], start=False, stop=True), reads=[ident, K["nmc"]], writes=[psg], pe_acc=True)
                if g % 2 == 0:
                    pz = pzb[(gi // 2) % 2]
                    n = g // 2
                    for kk in range(8):
                        S.emit(S.pe, lambda: nc.tensor.matmul(pz.ap[:], lhsT=xnT2.ap[:, kk, 4 + c * 128:4 + (c + 1) * 128], rhs=win.ap[:, kk, n * 512:(n + 1) * 512], start=(kk == 0), stop=(kk == 7)),
                               reads=[xnT2, win], writes=[pz], pe_acc=True, sig=(kk == 7))

            def st1(gi):
                c, g, cs, tok = info(gi)
                ptr = bkA[gi % 2]
                psg = psgs[gi % 4]
                Eg = Egs[gi % 2]
                for h in range(4):
                    S.emit(S.act, lambda: nc.scalar.activation(out=Eg.ap[:, h * 128:(h + 1) * 128], in_=psg.ap[:, h * 128:(h + 1) * 128], func=AF.Exp, bias=tok.ap[:, 64 + 4 * g + h:64 + 4 * g + h + 1]), reads=[psg, tok], writes=[Eg])
                MT = MTs[gi % 2]
                S.emit(S.dve, lambda: nc.vector.tensor_tensor(out=MT.ap[:].rearrange("p (h l) -> p h l", h=4), in0=Eg.ap[:].rearrange("p (h l) -> p h l", h=4), in1=bc(ptr.ap[:, 512:768].bitcast(F32).unsqueeze(1), [128, 4, 128]), op=ALU.mult), reads=[Eg, ptr], writes=[MT])
                xcg, xcdg, Btg = xcgs[gi % 3], xcdgs[gi % 3], Btgs[gi % 3]
                pv = ptr.ap[:, 0:256].rearrange("p (h e) -> p h e", h=4)
                S.emit(S.dve, lambda: nc.vector.tensor_tensor(out=xcg.ap[:].rearrange("p (h e) -> p h e", h=4), in0=pv, in1=bc(tok.ap[:, 4 * g:4 * g + 4].unsqueeze(2), [128, 4, 64]), op=ALU.mult), reads=[ptr, tok], writes=[xcg])
                S.emit(S.dve, lambda: nc.vector.tensor_tensor(out=xcdg.ap[:].rearrange("p (h e) -> p h e", h=4), in0=pv, in1=bc(tok.ap[:, 32 + 4 * g:32 + 4 * g + 4].unsqueeze(2), [128, 4, 64]), op=ALU.mult), reads=[ptr, tok], writes=[xcdg])
                S.emit(S.act, lambda: nc.scalar.copy(out=Btg.ap[:], in_=ptr.ap[:, 256:384]), reads=[ptr], writes=[Btg])
                if g % 2 == 0:
                    pz = pzb[(gi // 2) % 2]
                    sz, szea = szs[(gi // 2) % 2], szeas[(gi // 2) % 2]
                    n = g // 2
                    S.emit(S.act, lambda: nc.scalar.activation(out=sz.ap[:], in_=pz.ap[:], func=AF.Tanh, scale=0.5), reads=[pz], writes=[sz])
                    S.emit(S.dve, lambda: nc.vector.scalar_tensor_tensor(out=sz.ap[:], in0=sz.ap[:], scalar=1.0, in1=pz.ap[:], op0=ALU.add, op1=ALU.mult), reads=[sz, pz], writes=[sz])
                    S.emit(S.pool, lambda: nc.gpsimd.tensor_tensor(out=szea.ap[:].rearrange("p (h e) -> p h e", h=8), in0=sz.ap[:].rearrange("p (h e) -> p h e", h=8), in1=bc(tok.ap[:, 128 + 8 * n:128 + 8 * n + 8].unsqueeze(2), [128, 8, 64]), op=ALU.mult), reads=[sz, tok], writes=[szea])
                pvw = prevT[g].ap[:]
                S.emit(S.pool, lambda: nc.gpsimd.tensor_tensor(out=pvw.rearrange("p (h e) -> p h e", h=4), in0=pvw.rearrange("p (h e) -> p h e", h=4), in1=bc(tok.ap[:, 96 + 4 * g:96 + 4 * g + 4].unsqueeze(2), [128, 4, 64]), op=ALU.mult), reads=[prevT[g], tok], writes=[prevT[g]])

            def st2(gi):
                c, g, cs, tok = info(gi)
                MT = MTs[gi % 2]
                xcg, xcdg, Btg = xcgs[gi % 3], xcdgs[gi % 3], Btgs[gi % 3]
                py = pys[gi % 2]
                pyr = py.ap[:, 0:256]
                pyo = py.ap[:, 256:512]
                for i2 in range(2):
                    S.emit(S.pe, lambda: nc.tensor.matmul(pyr[:, i2 * 128:(i2 + 1) * 128], lhsT=xbcT[2 * g + i2].ap[:, cs], rhs=diagD.ap[:, 2 * g + i2, :], start=True, stop=False),
                           reads=[xbcT[2 * g + i2], diagD], writes=[py], pe_acc=True, sig=False)
                    for hh in (2 * i2, 2 * i2 + 1):
                        S.emit(S.pe, lambda: nc.tensor.matmul(pyr[:, hh * 64:(hh + 1) * 64], lhsT=MT.ap[:, hh * 128:(hh + 1) * 128], rhs=xcg.ap[:, hh * 64:(hh + 1) * 64], start=False, stop=(hh % 2 == 1)),
                               reads=[MT, xcg], writes=[py], pe_acc=True, sig=False)
                S.emit(S.pe, lambda: nc.tensor.matmul(pyo, lhsT=xbcT[24 + g].ap[:, cs], rhs=prevbf[g].ap[:], start=True, stop=True), reads=[xbcT[24 + g], prevbf[g]], writes=[py], pe_acc=True)
                pst = psts[gi % 2]
                S.emit(S.pe, lambda: nc.tensor.matmul(pst.ap[:, 0:256], lhsT=Btg.ap[:], rhs=xcdg.ap[:], start=True, stop=True), reads=[Btg, xcdg], writes=[pst], pe_acc=True)

            def st3(gi):
                c, g, cs, tok = info(gi)
                sz, szea = szs[(gi // 2) % 2], szeas[(gi // 2) % 2]
                py = pys[gi % 2]
                yt, yu = yts[gi % 2], yus[gi % 2]
                gs = slice((g % 2) * 256, (g % 2 + 1) * 256)
                S.emit(S.dve, lambda: nc.vector.tensor_tensor(out=yt.ap[:], in0=py.ap[:, 256:512], in1=szea.ap[:, gs], op=ALU.mult), reads=[py, szea], writes=[yt])
                S.emit(S.dve, lambda: nc.vector.tensor_tensor(out=yu.ap[:], in0=py.ap[:, 0:256], in1=sz.ap[:, gs], op=ALU.mult), reads=[py, sz], writes=[yu])
                pst = psts[gi % 2]
                S.emit(S.dve, lambda: nc.vector.tensor_tensor(out=prevT[g].ap[:], in0=pst.ap[:, 0:256], in1=prevT[g].ap[:], op=ALU.add), reads=[pst, prevT[g]], writes=[prevT[g]])

            def st4(gi):
                c, g, cs, tok = info(gi)
                yt, yu, yz = yts[gi % 2], yus[gi % 2], yzs[gi % 4]
                S.emit(S.pool, lambda: nc.gpsimd.tensor_tensor(out=yz.ap[:], in0=yt.ap[:], in1=yu.ap[:], op=ALU.add), reads=[yt, yu], writes=[yz])
                S.emit(S.act, lambda: nc.scalar.copy(out=prevbf[g].ap[:], in_=prevT[g].ap[:]), reads=[prevT[g]], writes=[prevbf[g]])

            def st5(gi):
                yz, ssg = yzs[gi % 4], ssgs[gi % 4]
                S.emit(S.act, lambda: nc.scalar.activation(out=junkg.ap[:], in_=yz.ap[:], func=AF.Square, accum_out=ssg.ap[:, 0:1]), reads=[yz], writes=[junkg, ssg])

            def st6(gi):
                ssg, rg = ssgs[gi % 4], rgs[gi % 4]
                S.emit(S.pool, lambda: nc.gpsimd.tensor_scalar(out=ssg.ap[:, 0:1], in0=ssg.ap[:, 0:1], scalar1=1.0 / 256.0, scalar2=4e-5, op0=ALU.mult, op1=ALU.add), reads=[ssg], writes=[ssg])
                S.emit(S.pool, lambda: nc.gpsimd.tensor_tensor(out=rg.ap[:, 0:1], in0=ssg.ap[:, 0:1], in1=K["mhalf"].ap[:, 0:1], op=ALU.pow), reads=[ssg, K["mhalf"]], writes=[rg])

            def st7(gi):
                c, g, cs, tok = info(gi)
                yz, rg = yzs[gi % 4], rgs[gi % 4]
                yn = yns[c]
                S.emit(S.act, lambda: nc.scalar.mul(out=yn.ap[:, g * 256:(g + 1) * 256], in_=yz.ap[:], mul=rg.ap[:, 0:1]), reads=[yz, rg], writes=[yn])
                if g == 7:
                    S.dma(S.pool, ynd[i * T + c * 128:i * T + (c + 1) * 128, :], yn.ap[:], reads=[yn], chan_buf=yn)

            stages = [st_ad, st0, st1, st2, st3, st4, st5, st6, st7]
            NG = 16
            for t in range(NG + len(stages) - 1):
                for k in reversed(range(len(stages))):
                    gi = t - k
                    if 0 <= gi < NG:
                        stages[k](gi)
        S.barrier()
    with contextlib.ExitStack() as es:
        wout = C.sb(es, "wout", [128, 16, D], BF16)
        gout = C.sb(es, "gout", [128, 16], F32)
        load_cols(C, gout, I["ssm_norm"], 16)
        with contextlib.ExitStack() as es2:
            C.stg = [C.sb(es2, "stg", [128, 2048], F32) for _ in range(6)]
            prep_weight(C, es2, wout, I["ssm_w_out"], 16, D, gout)
            S.barrier()
        xts = [C.sb(es, "xt", [128, D], F32) for _ in range(2)]
        yls = [C.sb(es, "yl", [128, D_IN], BF16) for _ in range(2)]
        ynT = C.sb(es, "ynT", [128, 16, 128], BF16)
        ptrs = [C.ps(es, "ptr", [128, 1024], BF16) for _ in range(2)]
        pmm = [C.ps(es, "pmm", [128, 512], F32) for _ in range(2)]

        def load2(c):
            S.dma(S.sp, xts[c % 2].ap[:], src[c * 128:(c + 1) * 128, :], writes=[xts[c % 2]], chan_buf=xts[c % 2])
            S.dma(S.sp, yls[c % 2].ap[:], ynd[c * 128:(c + 1) * 128, :], writes=[yls[c % 2]], chan_buf=yls[c % 2])

        load2(0)
        nmm = 0
        for c in range(NCH):
            if c + 1 < NCH:
                load2(c + 1)
            xt, yl = xts[c % 2], yls[c % 2]
            for half in range(2):
                ptr = ptrs[half]
                for jj in range(8):
                    j = 8 * half + jj
                    S.emit(S.pe, lambda: nc.tensor.transpose(out=ptr.ap[:, jj * 128:(jj + 1) * 128], in_=yl.ap[:, j * 128:(j + 1) * 128], identity=ident.ap[:]),
                           reads=[yl, ident], writes=[ptr], pe_acc=True, sig=(jj == 7))
                if half == 0:
                    S.emit(S.act, lambda: nc.scalar.copy(out=ynT.ap[:, 0:8, :], in_=ptr.ap[:, 0:1024].rearrange("p (k t) -> p k t", k=8)), reads=[ptr], writes=[ynT])
                else:
                    S.emit(S.dve, lambda: nc.vector.tensor_copy(out=ynT.ap[:, 8:16, :], in_=ptr.ap[:, 0:1024].rearrange("p (k t) -> p k t", k=8)), reads=[ptr], writes=[ynT])
            for n2 in range(2):
                po = pmm[nmm % 2]
                nmm += 1
                for kk in range(16):
                    S.emit(S.pe, lambda: nc.tensor.matmul(po.ap[:], lhsT=ynT.ap[:, kk, :], rhs=wout.ap[:, kk, n2 * 512:(n2 + 1) * 512], start=(kk == 0), stop=(kk == 15)),
                           reads=[ynT, wout], writes=[po], pe_acc=True, sig=(kk == 15))
                S.emit(S.dve, lambda: nc.vector.tensor_tensor(out=xt.ap[:, n2 * 512:(n2 + 1) * 512], in0=po.ap[:], in1=xt.ap[:, n2 * 512:(n2 + 1) * 512], op=ALU.add), reads=[po, xt], writes=[xt])
            S.dma(S.pool, dst[c * 128:(c + 1) * 128, :], xt.ap[:], reads=[xt], chan_buf=xt)
        S.barrier()

ATT_PATTERNS = ((128, 1), (512, 4), (2048, 16))
ATT_MAXSTAGE = 7
ATT_DBG = 0


def rotary(C, src, dst, rp, tmps, nh):
    nc, S = C.nc, C.S
    sv = src.ap[:, 0:nh * 128].rearrange("p (h e) -> p h e", h=nh)
    dv = dst.ap[:, 0:nh * 128].rearrange("p (h e) -> p h e", h=nh)
    cos = bc(rp.ap[:, 0:16].unsqueeze(1), [128, nh, 16])
    sin = bc(rp.ap[:, 16:32].unsqueeze(1), [128, nh, 16])
    t1, t2, t3, t4 = [t.ap[:, 0:nh * 16].rearrange("p (h e) -> p h e", h=nh) for t in tmps]
    x1, x2 = sv[:, :, 0:16], sv[:, :, 16:32]
    S.emit(S.dve, lambda: nc.vector.tensor_tensor(out=t1, in0=x1, in1=cos, op=ALU.mult), reads=[src, rp], writes=[tmps[0]])
    S.emit(S.pool, lambda: nc.gpsimd.tensor_tensor(out=t2, in0=x2, in1=sin, op=ALU.mult), reads=[src, rp], writes=[tmps[1]])
    S.emit(S.dve, lambda: nc.vector.tensor_tensor(out=t3, in0=x2, in1=cos, op=ALU.mult), reads=[src, rp], writes=[tmps[2]])
    S.emit(S.pool, lambda: nc.gpsimd.tensor_tensor(out=t4, in0=x1, in1=sin, op=ALU.mult), reads=[src, rp], writes=[tmps[3]])
    S.emit(S.dve, lambda: nc.vector.tensor_tensor(out=dv[:, :, 0:16], in0=t1, in1=t2, op=ALU.subtract), reads=[tmps[0], tmps[1]], writes=[dst])
    S.emit(S.pool, lambda: nc.gpsimd.tensor_tensor(out=dv[:, :, 16:32], in0=t3, in1=t4, op=ALU.add), reads=[tmps[2], tmps[3]], writes=[dst])
    S.emit(S.act, lambda: nc.scalar.copy(out=dv[:, :, 32:128], in_=sv[:, :, 32:128]), reads=[src], writes=[dst])


def phase_attn(C, K, I, src, dst):
    nc, S = C.nc, C.S
    nd = [nc.dram_tensor("numden%d" % g, [S_LEN, 8 * 129], F32, kind="Internal").ap() for g in range(3)]
    rope = I["rope"]
    with contextlib.ExitStack() as es:
        wkv = C.sb(es, "wkv", [128, 8, 1536], BF16)
        wq = C.sb(es, "wq", [128, 8, 3072], BF16)
        wo = C.sb(es, "wo", [128, 8, D], BF16)
        gkv = C.sb(es, "gkv", [128, 8], F32)
        gq = C.sb(es, "gq", [128, 8], F32)
        load_cols(C, gkv, I["kv_norm"], 8)
        load_cols(C, gq, I["b_norm"], 8)
        with contextlib.ExitStack() as es2:
            C.stg = [C.sb(es2, "stg", [128, 2048], F32) for _ in range(6)]
            prep_weight(C, es2, wkv, I["w_kv"], 8, 1536, gkv)
            prep_weight(C, es2, wq, I["att_w_q"], 8, 3072, gq)
            prep_weight(C, es2, wo, I["att_w_o"], 8, D, None)
            S.barrier()
        xts = [C.sb(es, "xt", [128, D], F32) for _ in range(3)]
        rps = [C.sb(es, "rp", [128, 32], F32) for _ in range(8)]
        xns = [C.sb(es, "xn", [128, D], BF16) for _ in range(2)]
        xnTs = [C.sb(es, "xnT", [128, 8, 128], BF16) for _ in range(2)]
        sss = [C.sb(es, "ss", [128, 1], F32) for _ in range(2)]
        rstds = [C.sb(es, "rstd", [128, 1], F32) for _ in range(2)]
        qfs = [C.sb(es, "qf", [128, 1024], F32) for _ in range(2)]
        qbs = [C.sb(es, "qb", [128, 1024], BF16) for _ in range(2)]
        kfs = [C.sb(es, "kf", [128, 256], F32) for _ in range(2)]
        kbs = [C.sb(es, "kb", [128, 256], BF16) for _ in range(2)]
        tq = [C.sb(es, "tq", [128, 128], F32) for _ in range(4)]
        tk = [C.sb(es, "tk", [128, 32], F32) for _ in range(4)]
        QTs = [C.sb(es, "QT", [128, 8, 128], BF16) for _ in range(3)]
        KTs = [C.sb(es, "KT", [128, 2, 128], BF16) for _ in range(4)]
        vaugs = [C.sb(es, "vaug", [128, 2, 130], BF16) for _ in range(6)]
        PTs = [C.sb(es, "PT", [128, 512], BF16) for _ in range(8)]
        obs = [C.sb(es, "ob", [128, 8, 129], F32) for _ in range(2)]
        for v in vaugs:
            S.emit(S.pool, lambda: nc.gpsimd.memset(v.ap[:], 1.0), writes=[v])
        ptr = C.ps(es, "ptr", [128, 1024], BF16)
        pmm = [C.ps(es, "pmm", [128, 512], F32) for _ in range(2)]
        pS = [C.ps(es, "pS", [128, 512], F32) for _ in range(2)]
        po = C.ps(es, "po", [128, 3 * 512], F32)
        scale = 1.0 / np.sqrt(128.0)

        blocks = []
        for g, (win, dil) in enumerate(ATT_PATTERNS):
            for r in range(dil):
                for n in range(S_LEN // dil // 128):
                    blocks.append((g, dil, r, n))
        NB = len(blocks)

        def rows(ap, dil, r, n):
            return ap.rearrange("(m dd) c -> dd m c", dd=dil)[r][n * 128:(n + 1) * 128, :]

        def load(bi):
            g, dil, r, n = blocks[bi]
            S.dma(S.sp, xts[bi % 3].ap[:], rows(src, dil, r, n), writes=[xts[bi % 3]], chan_buf=xts[bi % 3])
            S.dma(S.sp, rps[bi % 8].ap[:], rows(rope, dil, r, n), writes=[rps[bi % 8]], chan_buf=rps[bi % 8])

        cnt = {"mm": 0, "S": 0}

        def A1(bi):
            xt = xts[bi % 3]
            norm_only(C, K, xt.ap[:], xt, xns[bi % 2], sss[bi % 2], rstds[bi % 2], xns[bi % 2])

        def A2(bi):
            transpose_only(C, K, xns[bi % 2], ptr, xnTs[bi % 2].ap[:], xnTs[bi % 2])

        def A3(bi):
            g, dil, r, n = blocks[bi]
            xnT, qf, kf, vaug = xnTs[bi % 2], qfs[bi % 2], kfs[bi % 2], vaugs[bi % 6]
            for n2 in range(2):
                pq = pmm[cnt["mm"] % 2]
                cnt["mm"] += 1
                for kk in range(8):
                    S.emit(S.pe, lambda: nc.tensor.matmul(pq.ap[:], lhsT=xnT.ap[:, kk, :], rhs=wq.ap[:, kk, g * 1024 + n2 * 512:g * 1024 + (n2 + 1) * 512], start=(kk == 0), stop=(kk == 7)),
                           reads=[xnT, wq], writes=[pq], pe_acc=True, sig=(kk == 7))
                S.emit(S.act, lambda: nc.scalar.copy(out=qf.ap[:, n2 * 512:(n2 + 1) * 512], in_=pq.ap[:]), reads=[pq], writes=[qf])
            if ATT_DBG == 1:
                return
            pkv = pmm[cnt["mm"] % 2]
            cnt["mm"] += 1
            for half, c0 in ((0, g * 256), (1, 768 + g * 256)):
                for kk in range(8):
                    S.emit(S.pe, lambda: nc.tensor.matmul(pkv.ap[:, half * 256:(half + 1) * 256], lhsT=xnT.ap[:, kk, :], rhs=wkv.ap[:, kk, c0:c0 + 256], start=(kk == 0), stop=(kk == 7)),
                           reads=[xnT, wkv], writes=[pkv], pe_acc=True, sig=(kk == 7 and half == 1))
            S.emit(S.act, lambda: nc.scalar.copy(out=kf.ap[:], in_=pkv.ap[:, 0:256]), reads=[pkv], writes=[kf])
            if ATT_DBG == 2:
                return
            for hv in range(2):
                S.emit(S.act, lambda: nc.scalar.copy(out=vaug.ap[:, hv, 0:128], in_=pkv.ap[:, 256 + hv * 128:256 + (hv + 1) * 128]), reads=[pkv], writes=[vaug])

        def A4(bi):
            rp = rps[bi % 8]
            rotary(C, qfs[bi % 2], qbs[bi % 2], rp, tq, 8)
            rotary(C, kfs[bi % 2], kbs[bi % 2], rp, tk, 2)

        def A5(bi):
            qb, kb, QT, KT = qbs[bi % 2], kbs[bi % 2], QTs[bi % 3], KTs[bi % 4]
            for h in range(8):
                S.emit(S.pe, lambda: nc.tensor.transpose(out=ptr.ap[:, h * 128:(h + 1) * 128], in_=qb.ap[:, h * 128:(h + 1) * 128], identity=K["ident"].ap[:]),
                       reads=[qb, K["ident"]], writes=[ptr], pe_acc=True, sig=(h == 7))
            S.emit(S.dve, lambda: nc.vector.tensor_copy(out=QT.ap[:], in_=ptr.ap[:, 0:1024].rearrange("p (k t) -> p k t", k=8)), reads=[ptr], writes=[QT])
            for h in range(2):
                S.emit(S.pe, lambda: nc.tensor.transpose(out=ptr.ap[:, h * 128:(h + 1) * 128], in_=kb.ap[:, h * 128:(h + 1) * 128], identity=K["ident"].ap[:]),
                       reads=[kb, K["ident"]], writes=[ptr], pe_acc=True, sig=(h == 1))
            S.emit(S.act, lambda: nc.scalar.copy(out=KT.ap[:], in_=ptr.ap[:, 0:256].rearrange("p (k t) -> p k t", k=2)), reads=[ptr], writes=[KT])

        def kb_list(bi):
            g, dil, r, n = blocks[bi]
            kbl = []
            if n > 0:
                kbl.append((KTs[(bi - 1) % 4], vaugs[(bi - 1) % 6], K["nmp"]))
            kbl.append((KTs[bi % 4], vaugs[bi % 6], K["nmc"]))
            return kbl

        def B1(bi):
            QT = QTs[bi % 3]
            kbl = kb_list(bi)
            for jk in range(2):
                for i2, (kt, va, nm) in enumerate(kbl):
                    ps_ = pS[cnt["S"] % 2]
                    cnt["S"] += 1
                    pt_ = PTs[(4 * bi + 2 * jk + i2) % 8]
                    S.emit(S.pe, lambda: nc.tensor.matmul(ps_.ap[:], lhsT=kt.ap[:, jk, :], rhs=QT.ap[:, 4 * jk:4 * jk + 4, :], start=True, stop=False),
                           reads=[kt, QT], writes=[ps_], pe_acc=True, sig=False)
                    S.emit(S.pe, lambda: nc.tensor.matmul(ps_.ap[:], lhsT=K["ident"].ap[:], rhs=nm.ap[:], start=False, stop=True),
                           reads=[K["ident"], nm], writes=[ps_], pe_acc=True)
                    S.emit(S.act, lambda: nc.scalar.activation(out=pt_.ap[:], in_=ps_.ap[:], func=AF.Exp, scale=float(scale)), reads=[ps_], writes=[pt_])

        def B2(bi):
            g, dil, r, n = blocks[bi]
            kbl = kb_list(bi)
            ob = obs[bi % 2]
            for jk in range(2):
                for hl in range(4):
                    h = 4 * jk + hl
                    oslice = po.ap[:, (h // 3) * 512 + (h % 3) * 129:(h // 3) * 512 + (h % 3) * 129 + 129]
                    for i2, (kt, va, nm) in enumerate(kbl):
                        pt_ = PTs[(4 * bi + 2 * jk + i2) % 8]
                        S.emit(S.pe, lambda: nc.tensor.matmul(oslice, lhsT=pt_.ap[:, hl * 128:(hl + 1) * 128], rhs=va.ap[:, jk, 0:129], start=(i2 == 0), stop=(i2 == len(kbl) - 1)),
                               reads=[pt_, va], writes=[po], pe_acc=True, sig=(i2 == len(kbl) - 1 and hl == 3))
            for b3 in range(3):
                nh = 3 if b3 < 2 else 2
                o_ap = ob.ap[:, b3 * 3:b3 * 3 + nh, :]
                i_ap = po.ap[:, b3 * 512:b3 * 512 + nh * 129].rearrange("p (h e) -> p h e", h=nh)
                if b3 != 1:
                    S.emit(S.act, lambda: nc.scalar.copy(out=o_ap, in_=i_ap), reads=[po], writes=[ob])
                else:
                    S.emit(S.dve, lambda: nc.vector.tensor_copy(out=o_ap, in_=i_ap), reads=[po], writes=[ob])
            S.dma(S.pool, rows(nd[g], dil, r, n), ob.ap[:].rearrange("p h e -> p (h e)"), reads=[ob], chan_buf=ob)

        stages = [A1, A2, A3, A4, A5, B1, B2][:ATT_MAXSTAGE]
        load(0)
        load(1)
        for t in range(NB + len(stages) - 1):
            if t + 2 < NB:
                load(t + 2)
            for k in reversed(range(len(stages))):
                bi = t - k
                if 0 <= bi < NB:
                    stages[k](bi)
        S.barrier()
        nmm = cnt["mm"]
        nds = [[C.sb(es, "ndl", [128, 8, 129], F32) for _ in range(3)] for _ in range(2)]
        rden = C.sb(es, "rden", [128, 8], F32)
        o16 = C.sb(es, "o16", [128, 1024], BF16)
        oT = C.sb(es, "oT", [128, 8, 128], BF16)

        def load2(c):
            S.dma(S.sp, xts[c % 2].ap[:], src[c * 128:(c + 1) * 128, :], writes=[xts[c % 2]], chan_buf=xts[c % 2])
            for g in range(3):
                b = nds[c % 2][g]
                S.dma(S.sp, b.ap[:].rearrange("p h e -> p (h e)"), nd[g][c * 128:(c + 1) * 128, :], writes=[b], chan_buf=b)

        load2(0)
        for c in range(NCH):
            if c + 1 < NCH:
                load2(c + 1)
            xt = xts[c % 2]
            a0, a1, a2 = nds[c % 2]
            S.emit(S.pool, lambda: nc.gpsimd.tensor_tensor(out=a0.ap[:], in0=a0.ap[:], in1=a1.ap[:], op=ALU.add), reads=[a0, a1], writes=[a0])
            S.emit(S.pool, lambda: nc.gpsimd.tensor_tensor(out=a0.ap[:], in0=a0.ap[:], in1=a2.ap[:], op=ALU.add), reads=[a0, a2], writes=[a0])
            S.emit(S.dve, lambda: nc.vector.reciprocal(out=rden.ap[:].unsqueeze(2), in_=a0.ap[:, :, 128:129]), reads=[a0], writes=[rden])
            S.emit(S.dve, lambda: nc.vector.tensor_tensor(out=o16.ap[:].rearrange("p (h e) -> p h e", h=8), in0=a0.ap[:, :, 0:128], in1=bc(rden.ap[:].unsqueeze(2), [128, 8, 128]), op=ALU.mult), reads=[a0, rden], writes=[o16])
            for h in range(8):
                S.emit(S.pe, lambda: nc.tensor.transpose(out=ptr.ap[:, h * 128:(h + 1) * 128], in_=o16.ap[:, h * 128:(h + 1) * 128], identity=K["ident"].ap[:]),
                       reads=[o16, K["ident"]], writes=[ptr], pe_acc=True, sig=(h == 7))
            S.emit(S.act, lambda: nc.scalar.copy(out=oT.ap[:], in_=ptr.ap[:, 0:1024].rearrange("p (k t) -> p k t", k=8)), reads=[ptr], writes=[oT])
            for n2 in range(2):
                pq = pmm[nmm % 2]
                nmm += 1
                for kk in range(8):
                    S.emit(S.pe, lambda: nc.tensor.matmul(pq.ap[:], lhsT=oT.ap[:, kk, :], rhs=wo.ap[:, kk, n2 * 512:(n2 + 1) * 512], start=(kk == 0), stop=(kk == 7)),
                           reads=[oT, wo], writes=[pq], pe_acc=True, sig=(kk == 7))
                S.emit(S.dve, lambda: nc.vector.tensor_tensor(out=xt.ap[:, n2 * 512:(n2 + 1) * 512], in0=pq.ap[:], in1=xt.ap[:, n2 * 512:(n2 + 1) * 512], op=ALU.add), reads=[pq, xt], writes=[xt])
            S.dma(S.pool, dst[c * 128:(c + 1) * 128, :], xt.ap[:], reads=[xt], chan_buf=xt)
        S.barrier()

def build(phases=("m", "f0", "a", "f1"), debug=False):
    nc = bass.Bass("TRN2", target_bir_lowering=False)
    I = {}

    def inp(name, shape):
        I[name] = nc.dram_tensor(name, shape, F32, kind="ExternalInput").ap()

    inp("x", [S_LEN, D])
    inp("a_norm", [D]); inp("ssm_w_in", [D, IN_DIM]); inp("ssm_conv_w", [4, XBC]); inp("ssm_conv_b", [XBC])
    inp("ssm_dt_bias", [32]); inp("ssm_a_log", [32]); inp("ssm_d", [32]); inp("ssm_norm", [D_IN]); inp("ssm_w_out", [D_IN, D])
    inp("kv_norm", [D]); inp("w_kv", [D, 1536]); inp("b_norm", [D]); inp("att_w_q", [D, 3072]); inp("att_w_o", [D, D])
    inp("ffn_norm", [2, D]); inp("ffn_w_up", [2, D, 2 * FF]); inp("ffn_conv_w", [2, 3, 2 * FF]); inp("ffn_w_down", [2, FF, D])
    inp("final_norm", [D]); inp("rope", [S_LEN, 32])
    out = nc.dram_tensor("out", [S_LEN, D], F32, kind="ExternalOutput").ap()
    kind = "ExternalOutput" if debug else "Internal"
    xa = nc.dram_tensor("xa", [S_LEN, D], F32, kind=kind).ap()
    xb = nc.dram_tensor("xb", [S_LEN, D], F32, kind=kind).ap()
    xc = nc.dram_tensor("xc", [S_LEN, D], F32, kind=kind).ap()
    with contextlib.ExitStack() as es:
        C = Ctx(nc, es)
        K = setup_consts(C, es)
        C.S.barrier()
        cur = I["x"]
        if "m" in phases:
            phase_mamba(C, K, I, cur, xa)
            cur = xa
        if "f0" in phases:
            phase_ffn(C, K, 0, cur, xb, I["ffn_w_up"][0], I["ffn_conv_w"][0], I["ffn_w_down"][0], I["ffn_norm"][0])
            cur = xb
        if "a" in phases:
            phase_attn(C, K, I, cur, xc)
            cur = xc
        if "f1" in phases:
            phase_ffn(C, K, 1, cur, out, I["ffn_w_up"][1], I["ffn_conv_w"][1], I["ffn_w_down"][1], I["ffn_norm"][1], final_gain=I["final_norm"])
        C.S.barrier()
    return nc


def rope_table():
    half = 16
    inv_freq = np.power(np.float32(500000.0), -np.arange(0, 32, 2, dtype=np.float32) / np.float32(32)).astype(np.float32)
    ang = (np.arange(S_LEN, dtype=np.float32)[:, None] * inv_freq[None, :]).astype(np.float32)
    return np.concatenate([np.cos(ang.astype(np.float64)), np.sin(ang.astype(np.float64))], axis=1).astype(np.float32)


def make_in_maps(inputs, n_cores):
    f = lambda a: np.ascontiguousarray(np.asarray(a, dtype=np.float32))
    shared = {
        "a_norm": f(inputs["a_norm"][0]), "ssm_w_in": f(inputs["ssm_w_in"][0]), "ssm_conv_w": f(inputs["ssm_conv_w"][0]),
        "ssm_conv_b": f(inputs["ssm_conv_b"][0]), "ssm_dt_bias": f(inputs["ssm_dt_bias"][0]), "ssm_a_log": f(inputs["ssm_a_log"][0]),
        "ssm_d": f(inputs["ssm_d"][0]), "ssm_norm": f(inputs["ssm_norm"][0]), "ssm_w_out": f(inputs["ssm_w_out"][0]),
        "kv_norm": f(inputs["kv_norm"]), "w_kv": f(inputs["w_kv"]), "b_norm": f(inputs["b_norm"][0]),
        "att_w_q": f(inputs["att_w_q"][0]), "att_w_o": f(inputs["att_w_o"][0]), "ffn_norm": f(inputs["ffn_norm"]),
        "ffn_w_up": f(inputs["ffn_w_up"]), "ffn_conv_w": f(inputs["ffn_conv_w"]), "ffn_w_down": f(inputs["ffn_w_down"]),
        "final_norm": f(inputs["final_norm"]), "rope": rope_table(),
    }
    x = f(inputs["x"])
    maps = []
    for c in range(n_cores):
        m = dict(shared)
        m["x"] = x[c]
        maps.append(m)
    return maps


def kernel(**inputs):
    nc = build()
    maps = make_in_maps(inputs, 8)
    res = run_bass_kernel_spmd(nc, maps, core_ids=list(range(8)))
    return np.stack([np.asarray(r["out"]) for r in res.results], axis=0).astype(np.float32)
```

```python
import contextlib
import numpy as np
import concourse.bass as bass
import concourse.mybir as mybir
from concourse.bass_utils import run_bass_kernel_spmd

F32 = mybir.dt.float32
BF16 = mybir.dt.bfloat16
AF = mybir.ActivationFunctionType
ALU = mybir.AluOpType

S_LEN = 4096
D = 1024
NCH = S_LEN // 128
FF = 2816
NF = FF // 128
D_IN = 2048
XBC = 4096
IN_DIM = 6176
NEG = -30000.0


class Buf:
    __slots__ = ("ap", "lw", "rd", "name", "chan", "chan_sw")

    def __init__(self, ap, name=""):
        self.ap = ap
        self.lw = None
        self.rd = {}
        self.name = name
        self.chan = None
        self.chan_sw = None


class Eng:
    def __init__(self, name, q, sem):
        self.name = name
        self.q = q
        self.sem = sem
        self.count = 0
        self.seen = {}


class Sched:
    def __init__(self, nc, es):
        self.nc = nc
        self.es = es
        mk = lambda n, q: Eng(n, q, es.enter_context(nc.semaphore("sem_" + n)))
        self.pe = mk("pe", nc.tensor)
        self.act = mk("act", nc.scalar)
        self.dve = mk("dve", nc.vector)
        self.pool = mk("pool", nc.gpsimd)
        self.sp = mk("sp", nc.sync)
        self.engs = [self.pe, self.act, self.dve, self.pool, self.sp]
        self.chans = []
        self.free_chans = []

    def new_chan(self, name):
        if self.free_chans:
            c = self.free_chans.pop()
            return c
        c = Eng("ch%d_%s" % (len(self.chans), name), None, self.es.enter_context(self.nc.semaphore("semch%d" % len(self.chans))))
        self.chans.append(c)
        return c

    def _waits(self, eng, reads, writes, pe_acc):
        waits = {}

        def need(dep):
            e, c = dep
            if eng.seen.get(e.name, 0) < c:
                if waits.get(e.name, (None, 0))[1] < c:
                    waits[e.name] = (e, c)

        for b in reads:
            if b.lw is not None:
                need(b.lw)
        for b in writes:
            if b.lw is not None and not (pe_acc and b.lw[0] is eng):
                need(b.lw)
            for dep in b.rd.values():
                need(dep)
        for e, c in waits.values():
            eng.q.wait_ge(e.sem, c)
            eng.seen[e.name] = c

    def emit(self, eng, fn, reads=(), writes=(), pe_acc=False, sig=True):
        self._waits(eng, reads, writes, pe_acc)
        ins = fn()
        if sig:
            eng.count += 1
            ins.then_inc(eng.sem, 1)
            cnt = eng.count
        else:
            cnt = eng.count + 1
        for b in reads:
            b.rd[eng.name] = (eng, cnt)
        for b in writes:
            b.lw = (eng, cnt)
            b.rd = {}
        return ins

    def dma(self, eng, out_ap, in_ap, reads=(), writes=(), chan_buf=None):
        if eng is self.pool:
            if chan_buf.chan_sw is None:
                chan_buf.chan_sw = self.new_chan(chan_buf.name + "_sw")
            ch = chan_buf.chan_sw
        else:
            if chan_buf.chan is None:
                chan_buf.chan = self.new_chan(chan_buf.name)
            ch = chan_buf.chan
        self._waits(eng, reads, writes, False)
        ins = eng.q.dma_start(out=out_ap, in_=in_ap)
        ch.count += 16
        ins.then_inc(ch.sem, 16)
        for b in reads:
            b.rd[ch.name] = (ch, ch.count)
        for b in writes:
            b.lw = (ch, ch.count)
            b.rd = {}
        return ins

    def barrier(self):
        allsrc = self.engs + self.chans
        for e in self.engs:
            for o in allsrc:
                if o is e or o.count == 0:
                    continue
                if e.seen.get(o.name, 0) < o.count:
                    e.q.wait_ge(o.sem, o.count)
                    e.seen[o.name] = o.count


class Ctx:
    def __init__(self, nc, es):
        self.nc = nc
        self.S = Sched(nc, es)
        self.n = 0
        self.rr = 0

    def sb(self, es, name, shape, dt):
        self.n += 1
        t = es.enter_context(self.nc.sbuf_tensor("%s_%d" % (name, self.n), shape, dt))
        return Buf(t, name)

    def ps(self, es, name, shape, dt):
        self.n += 1
        t = es.enter_context(self.nc.psum_tensor("%s_%d" % (name, self.n), shape, dt))
        return Buf(t, name)


def bc(ap, shape):
    return ap.broadcast_to(shape)


def setup_consts(C, es):
    nc, S = C.nc, C.S
    k = {}
    identf = C.sb(es, "identf", [128, 128], F32)
    ident = C.sb(es, "ident", [128, 128], BF16)
    S.emit(S.pool, lambda: nc.gpsimd.memset(identf.ap[:], 1.0), writes=[identf])
    S.emit(S.pool, lambda: nc.gpsimd.affine_select(out=identf.ap[:], in_=identf.ap[:], pattern=[[-1, 128]], compare_op=ALU.is_equal, fill=0.0, base=0, channel_multiplier=1), reads=[identf], writes=[identf])
    S.emit(S.dve, lambda: nc.vector.tensor_copy(out=ident.ap[:], in_=identf.ap[:]), reads=[identf], writes=[ident])
    mhalf = C.sb(es, "mhalf", [128, 8], F32)
    S.emit(S.pool, lambda: nc.gpsimd.memset(mhalf.ap[:], -0.5), writes=[mhalf])
    nmf = C.sb(es, "nmf", [128, 128], F32)
    nmc = C.sb(es, "nmc", [128, 512], BF16)
    nmp = C.sb(es, "nmp", [128, 512], BF16)
    S.emit(S.pool, lambda: nc.gpsimd.memset(nmf.ap[:], 0.0), writes=[nmf])
    S.emit(S.pool, lambda: nc.gpsimd.affine_select(out=nmf.ap[:], in_=nmf.ap[:], pattern=[[1, 128]], compare_op=ALU.is_ge, fill=NEG, base=0, channel_multiplier=-1), reads=[nmf], writes=[nmf])
    S.emit(S.dve, lambda: nc.vector.tensor_copy(out=nmc.ap[:].rearrange("p (h l) -> p h l", h=4), in_=bc(nmf.ap[:].unsqueeze(1), [128, 4, 128])), reads=[nmf], writes=[nmc])
    S.emit(S.pool, lambda: nc.gpsimd.memset(nmf.ap[:], 0.0), reads=[nmf], writes=[nmf])
    S.emit(S.pool, lambda: nc.gpsimd.affine_select(out=nmf.ap[:], in_=nmf.ap[:], pattern=[[-1, 128]], compare_op=ALU.is_ge, fill=NEG, base=0, channel_multiplier=1), reads=[nmf], writes=[nmf])
    S.emit(S.dve, lambda: nc.vector.tensor_copy(out=nmp.ap[:].rearrange("p (h l) -> p h l", h=4), in_=bc(nmf.ap[:].unsqueeze(1), [128, 4, 128])), reads=[nmf], writes=[nmp])
    k.update(identf=identf, ident=ident, mhalf=mhalf, nmc=nmc, nmp=nmp)
    return k


def prep_weight(C, es_stage, dst, src, nk, ncols, gain=None, col0=0):
    nc, S = C.nc, C.S
    stg = C.stg
    CW = 2048
    for kk in range(nk):
        for c0 in range(0, ncols, CW):
            cw = min(CW, ncols - c0)
            st = stg[C.rr % len(stg)]
            S.dma(S.sp, st.ap[:, 0:cw], src[kk * 128:(kk + 1) * 128, c0:c0 + cw], writes=[st], chan_buf=st)
            o = dst.ap[:, kk, col0 + c0:col0 + c0 + cw]
            which = (C.rr % 3) if gain is None else (C.rr % 2)
            C.rr += 1
            rds = [st] + ([gain] if gain is not None else [])
            if gain is None:
                if which == 0:
                    S.emit(S.act, lambda: nc.scalar.copy(out=o, in_=st.ap[:, 0:cw]), reads=rds, writes=[dst])
                elif which == 1:
                    S.emit(S.dve, lambda: nc.vector.tensor_copy(out=o, in_=st.ap[:, 0:cw]), reads=rds, writes=[dst])
                else:
                    S.emit(S.pool, lambda: nc.gpsimd.tensor_copy(out=o, in_=st.ap[:, 0:cw]), reads=rds, writes=[dst])
            else:
                g = gain.ap[:, kk:kk + 1]
                if which == 0:
                    S.emit(S.act, lambda: nc.scalar.mul(out=o, in_=st.ap[:, 0:cw], mul=g), reads=rds, writes=[dst])
                elif which == 1:
                    S.emit(S.dve, lambda: nc.vector.tensor_scalar(out=o, in0=st.ap[:, 0:cw], scalar1=g, scalar2=None, op0=ALU.mult), reads=rds, writes=[dst])
                else:
                    S.emit(S.pool, lambda: nc.gpsimd.tensor_scalar(out=o, in0=st.ap[:, 0:cw], scalar1=g, scalar2=None, op0=ALU.mult), reads=rds, writes=[dst])


def load_cols(C, dst, src_vec, nk):
    S = C.S
    with C.nc.allow_non_contiguous_dma(reason="small param vector"):
        S.dma(S.sp, dst.ap[:, 0:nk], src_vec.rearrange("(k p) -> p k", p=128), writes=[dst], chan_buf=dst)


def rms_rstd(C, x_ap, xbuf, junk, ss, rstd, K, n_feat, eps):
    nc, S = C.nc, C.S
    S.emit(S.act, lambda: nc.scalar.activation(out=junk.ap[:, 0:n_feat], in_=x_ap, func=AF.Square, accum_out=ss.ap[:, 0:1]), reads=[xbuf], writes=[junk, ss])
    S.emit(S.dve, lambda: nc.vector.tensor_scalar(out=ss.ap[:, 0:1], in0=ss.ap[:, 0:1], scalar1=1.0 / n_feat, scalar2=eps, op0=ALU.mult, op1=ALU.add), reads=[ss], writes=[ss])
    S.emit(S.pool, lambda: nc.gpsimd.tensor_tensor(out=rstd.ap[:, 0:1], in0=ss.ap[:, 0:1], in1=K["mhalf"].ap[:, 0:1], op=ALU.pow), reads=[ss, K["mhalf"]], writes=[rstd])


def norm_transpose(C, K, x_ap, xbuf, junk, ss, rstd, xn, ptr, xnT_ap, xnT, extra=()):
    nc, S = C.nc, C.S
    norm_only(C, K, x_ap, xbuf, junk, ss, rstd, xn)
    transpose_only(C, K, xn, ptr, xnT_ap, xnT, extra)


def norm_only(C, K, x_ap, xbuf, junk, ss, rstd, xn, dve_mul=False):
    nc, S = C.nc, C.S
    rms_rstd(C, x_ap, xbuf, junk, ss, rstd, K, D, 1e-6)
    if dve_mul:
        S.emit(S.dve, lambda: nc.vector.tensor_scalar(out=xn.ap[:, 0:D], in0=x_ap, scalar1=rstd.ap[:, 0:1], scalar2=None, op0=ALU.mult), reads=[xbuf, rstd], writes=[xn])
    else:
        S.emit(S.act, lambda: nc.scalar.mul(out=xn.ap[:, 0:D], in_=x_ap, mul=rstd.ap[:, 0:1]), reads=[xbuf, rstd], writes=[xn])


def transpose_only(C, K, xn, ptr, xnT_ap, xnT, extra=()):
    nc, S = C.nc, C.S
    for kk in range(8):
        S.emit(S.pe, lambda: nc.tensor.transpose(out=ptr.ap[:, kk * 128:(kk + 1) * 128], in_=xn.ap[:, kk * 128:(kk + 1) * 128], identity=K["ident"].ap[:]),
               reads=[xn, K["ident"]], writes=[ptr] + list(extra), pe_acc=True, sig=(kk == 7))
    S.emit(S.dve, lambda: nc.vector.tensor_copy(out=xnT_ap, in_=ptr.ap[:, 0:1024].rearrange("p (k t) -> p k t", k=8)), reads=[ptr], writes=[xnT])


def phase_ffn(C, K, layer, src, dst, w_up, conv_w, w_dn, gain_vec, final_gain=None):
    nc, S = C.nc, C.S
    T = 256
    NT = S_LEN // T
    with contextlib.ExitStack() as es:
        wup = C.sb(es, "wup", [128, 8, 2 * FF], BF16)
        wdn = C.sb(es, "wdn", [128, NF, D], BF16)
        gain = C.sb(es, "gain", [128, 8], F32)
        cw = C.sb(es, "cw", [128, 2 * NF, 3], F32)
        load_cols(C, gain, gain_vec, 8)
        with nc.allow_non_contiguous_dma(reason="conv weights"):
            for kk in range(3):
                S.dma(S.sp, cw.ap[:, :, kk], conv_w[kk].rearrange("(j p) -> p j", p=128), writes=[cw], chan_buf=cw)
        with contextlib.ExitStack() as es2:
            C.stg = [C.sb(es2, "stg", [128, 2048], F32) for _ in range(6)]
            prep_weight(C, es2, wup, w_up, 8, 2 * FF, gain)
            prep_weight(C, es2, wdn, w_dn, NF, D, None)
            S.barrier()
        xts = [C.sb(es, "xt", [128, 2, D], F32) for _ in range(2)]
        xn = C.sb(es, "xn", [128, D], BF16)
        xnTs = [C.sb(es, "xnT", [128, 8, T + 2], BF16) for _ in range(2)]
        ss = C.sb(es, "ss", [128, 1], F32)
        rstd = C.sb(es, "rstd", [128, 1], F32)
        ags = [C.sb(es, "ag", [128, T], F32) for _ in range(3)]
        avs = [C.sb(es, "av", [128, T], F32) for _ in range(3)]
        sgs = [C.sb(es, "sg", [128, T], F32) for _ in range(3)]
        h2_t = es.enter_context(nc.sbuf_tensor("h2T_all%d" % layer, [128, NF, T], BF16))
        h2T = [Buf(h2_t[:, j, :], "h2T%d" % j) for j in range(NF)]
        if final_gain is not None:
            gfin = C.sb(es, "gfin", [128, D], F32)
            obuf = [C.sb(es, "obuf", [128, D], F32) for _ in range(2)]
            S.dma(S.sp, gfin.ap[:], final_gain.partition_broadcast(128), writes=[gfin], chan_buf=gfin)
        S.emit(S.pool, lambda: nc.gpsimd.memset(xnTs[0].ap[:, :, 0:2], 0.0), writes=[xnTs[0]])
        ptrs = [C.ps(es, "ptr", [128, 1024], BF16) for _ in range(2)]
        pus = [C.ps(es, "pu", [128, 512], F32) for _ in range(4)]
        pds = [C.ps(es, "pd", [128, 512], F32) for _ in range(2)]

        def load(i):
            xt = xts[i % 2]
            S.dma(S.sp, xt.ap[:], src[i * T:(i + 1) * T, :].rearrange("(c p) d -> p c d", p=128), writes=[xt], chan_buf=xt)

        xn2 = [xn, C.sb(es, "xnb", [128, D], BF16)]
        ss2 = [ss, C.sb(es, "ssb", [128, 1], F32)]
        rstd2 = [rstd, C.sb(es, "rstdb", [128, 1], F32)]

        def head_norm(i):
            xt = xts[i % 2]
            for c in range(2):
                norm_only(C, K, xt.ap[:, c, :], xt, xn2[c], ss2[c], rstd2[c], xn2[c])

        def head_tr(i):
            xnT = xnTs[i % 2]
            if i > 0:
                S.emit(S.pool, lambda: nc.gpsimd.tensor_copy(out=xnT.ap[:, :, 0:2], in_=xnTs[(i - 1) % 2].ap[:, :, T:T + 2]), reads=[xnTs[(i - 1) % 2]], writes=[xnT])
            for c in range(2):
                transpose_only(C, K, xn2[c], ptrs[c], xnT.ap[:, :, 2 + c * 128:2 + (c + 1) * 128], xnT)

        def head(i):
            head_norm(i)
            head_tr(i)

        load(0)
        head(0)
        npu = 0
        npd = 0
        nb = 0
        for i in range(NT):
            xt = xts[i % 2]
            xnT = xnTs[i % 2]
            if i + 1 < NT:
                load(i + 1)
            for j in range(NF):
                pg, pv = pus[npu % 4], pus[(npu + 1) % 4]
                npu += 2
                ag, av, sgb = ags[nb % 3], avs[nb % 3], sgs[nb % 3]
                nb += 1
                for (pp, col) in ((pg, j), (pv, NF + j)):
                    for kk in range(8):
                        S.emit(S.pe, lambda: nc.tensor.matmul(pp.ap[:, 0:T + 2], lhsT=wup.ap[:, kk, col * 128:(col + 1) * 128], rhs=xnT.ap[:, kk, :], start=(kk == 0), stop=(kk == 7)),
                               reads=[wup, xnT], writes=[pp], pe_acc=True, sig=(kk == 7))
                S.emit(S.act, lambda: nc.scalar.mul(out=ag.ap[:], in_=pg.ap[:, 2:T + 2], mul=cw.ap[:, j, 2:3]), reads=[pg, cw], writes=[ag])
                S.emit(S.act, lambda: nc.scalar.mul(out=av.ap[:], in_=pv.ap[:, 2:T + 2], mul=cw.ap[:, NF + j, 2:3]), reads=[pv, cw], writes=[av])
                for tap in (1, 0):
                    S.emit(S.dve, lambda: nc.vector.scalar_tensor_tensor(out=ag.ap[:], in0=pg.ap[:, tap:tap + T], scalar=cw.ap[:, j, tap:tap + 1], in1=ag.ap[:], op0=ALU.mult, op1=ALU.add), reads=[pg, cw, ag], writes=[ag])
                for tap in (1, 0):
                    S.emit(S.dve, lambda: nc.vector.scalar_tensor_tensor(out=av.ap[:], in0=pv.ap[:, tap:tap + T], scalar=cw.ap[:, NF + j, tap:tap + 1], in1=av.ap[:], op0=ALU.mult, op1=ALU.add), reads=[pv, cw, av], writes=[av])
                S.emit(S.act, lambda: nc.scalar.activation(out=sgb.ap[:], in_=ag.ap[:], func=AF.Silu), reads=[ag], writes=[sgb])
                S.emit(S.pool, lambda: nc.gpsimd.tensor_tensor(out=h2T[j].ap[:], in0=sgb.ap[:], in1=av.ap[:], op=ALU.mult), reads=[sgb, av], writes=[h2T[j]])
                if j == 3 and i + 1 < NT:
                    head_norm(i + 1)
                if j == 12 and i + 1 < NT:
                    head_tr(i + 1)
            for c in range(2):
                for n in range(2):
                    pd = pds[npd % 2]
                    npd += 1
                    for j in range(NF):
                        S.emit(S.pe, lambda: nc.tensor.matmul(pd.ap[:], lhsT=h2T[j].ap[:, c * 128:(c + 1) * 128], rhs=wdn.ap[:, j, n * 512:(n + 1) * 512], start=(j == 0), stop=(j == NF - 1)),
                               reads=[h2T[j], wdn], writes=[pd], pe_acc=True, sig=(j == NF - 1))
                    S.emit(S.dve, lambda: nc.vector.tensor_tensor(out=xt.ap[:, c, n * 512:(n + 1) * 512], in0=pd.ap[:], in1=xt.ap[:, c, n * 512:(n + 1) * 512], op=ALU.add), reads=[pd, xt], writes=[xt])
                if final_gain is not None:
                    ob = obuf[c]
                    rms_rstd(C, xt.ap[:, c, :], xt, xn, ss, rstd, K, D, 1e-6)
                    S.emit(S.dve, lambda: nc.vector.scalar_tensor_tensor(out=ob.ap[:], in0=xt.ap[:, c, :], scalar=rstd.ap[:, 0:1], in1=gfin.ap[:], op0=ALU.mult, op1=ALU.mult), reads=[xt, rstd, gfin], writes=[ob])
                    S.dma(S.pool, dst[i * T + c * 128:i * T + (c + 1) * 128, :], ob.ap[:], reads=[ob], chan_buf=ob)
            if final_gain is None:
                S.dma(S.pool, dst[i * T:(i + 1) * T, :].rearrange("(c p) d -> p c d", p=128), xt.ap[:], reads=[xt], chan_buf=xt)
        S.barrier()


def phase_mamba(C, K, I, src, dst):
    nc, S = C.nc, C.S
    ynd = nc.dram_tensor("ynd", [S_LEN, D_IN], BF16, kind="Internal").ap()
    ident, identf = K["ident"], K["identf"]
    T = 256
    NT = S_LEN // T
    with contextlib.ExitStack() as es:
        win = C.sb(es, "win", [128, 8, IN_DIM], BF16)
        gain = C.sb(es, "gain", [128, 8], F32)
        cw = C.sb(es, "cw", [128, 32, 4], F32)
        cb = C.sb(es, "cb", [128, 32], F32)
        dcol = C.sb(es, "dcol", [128, 16], F32)
        diagD = C.sb(es, "diagD", [128, 16, 128], BF16)
        dtb = C.sb(es, "dtb", [32, 1], F32)
        acol = C.sb(es, "acol", [32, 1], F32)
        ones32 = C.sb(es, "ones32", [32, 128], F32)
        load_cols(C, gain, I["a_norm"], 8)
        load_cols(C, cb, I["ssm_conv_b"], 32)
        with nc.allow_non_contiguous_dma(reason="small params"):
            for kk in range(4):
                S.dma(S.sp, cw.ap[:, :, kk], I["ssm_conv_w"][kk].rearrange("(j p) -> p j", p=128), writes=[cw], chan_buf=cw)
            S.dma(S.sp, dtb.ap[:, 0:1], I["ssm_dt_bias"].rearrange("(h o) -> h o", o=1), writes=[dtb], chan_buf=dtb)
            S.dma(S.sp, acol.ap[:, 0:1], I["ssm_a_log"].rearrange("(h o) -> h o", o=1), writes=[acol], chan_buf=acol)
            dv = I["ssm_d"].rearrange("(k two) -> two k", two=2)
            for hh in range(2):
                S.dma(S.sp, dcol.ap[hh * 64:(hh + 1) * 64, :], dv[hh].partition_broadcast(64), writes=[dcol], chan_buf=dcol)
        S.emit(S.act, lambda: nc.scalar.activation(out=acol.ap[:], in_=acol.ap[:], func=AF.Exp), reads=[acol], writes=[acol])
        S.emit(S.dve, lambda: nc.vector.tensor_scalar(out=acol.ap[:], in0=acol.ap[:], scalar1=-1.0, scalar2=None, op0=ALU.mult), reads=[acol], writes=[acol])
        S.emit(S.pool, lambda: nc.gpsimd.memset(ones32.ap[:], 1.0), writes=[ones32])
        for kk in range(16):
            S.emit(S.dve, lambda: nc.vector.tensor_scalar(out=diagD.ap[:, kk, :], in0=identf.ap[:], scalar1=dcol.ap[:, kk:kk + 1], scalar2=None, op0=ALU.mult), reads=[identf, dcol], writes=[diagD])
        with contextlib.ExitStack() as es2:
            C.stg = [C.sb(es2, "stg", [128, 2048], F32) for _ in range(6)]
            prep_weight(C, es2, win, I["ssm_w_in"], 8, IN_DIM, gain)
            S.barrier()
        xts = [C.sb(es, "xt", [128, 2, D], F32) for _ in range(2)]
        xn = C.sb(es, "xn", [128, D], BF16)
        xnT2s = [C.sb(es, "xnT2", [128, 8, T + 4], BF16) for _ in range(2)]
        ss = C.sb(es, "ss", [128, 1], F32)
        rstd = C.sb(es, "rstd", [128, 1], F32)
        szs = [C.sb(es, "sz", [128, 512], F32) for _ in range(2)]
        szeas = [C.sb(es, "szea", [128, 512], F32) for _ in range(2)]
        xbcT_t = es.enter_context(nc.sbuf_tensor("xbcT_all", [128, 32, T], BF16))
        xbcT = [Buf(xbcT_t[:, j, :], "xbcT%d" % j) for j in range(32)]
        accs = [C.sb(es, "acc", [128, T], F32) for _ in range(3)]
        xcgs = [C.sb(es, "xcg", [128, 256], BF16) for _ in range(3)]
        xcdgs = [C.sb(es, "xcdg", [128, 256], BF16) for _ in range(3)]
        Btgs = [C.sb(es, "Btg", [128, 128], BF16) for _ in range(3)]
        ADs = [C.sb(es, "AD", [32, 512], F32) for _ in range(2)]
        Egs = [C.sb(es, "Eg", [128, 512], F32) for _ in range(2)]
        MTs = [C.sb(es, "MT", [128, 512], BF16) for _ in range(2)]
        prevT_t = es.enter_context(nc.sbuf_tensor("prevT_all", [128, D_IN], F32))
        prevbf_t = es.enter_context(nc.sbuf_tensor("prevbf_all", [128, D_IN], BF16))
        prevT = [Buf(prevT_t[:, g * 256:(g + 1) * 256], "prevT%d" % g) for g in range(8)]
        prevbf = [Buf(prevbf_t[:, g * 256:(g + 1) * 256], "prevbf%d" % g) for g in range(8)]
        yts = [C.sb(es, "yt", [128, 256], F32) for _ in range(2)]
        yus = [C.sb(es, "yu", [128, 256], F32) for _ in range(2)]
        yzs = [C.sb(es, "yz", [128, 256], F32) for _ in range(4)]
        junkg = C.sb(es, "junkg", [128, 256], BF16)
        ssgs = [C.sb(es, "ssg", [128, 1], F32) for _ in range(4)]
        rgs = [C.sb(es, "rg", [128, 1], F32) for _ in range(4)]
        yns = [C.sb(es, "yn", [128, D_IN], BF16) for _ in range(2)]
        sm = {n: C.sb(es, n, [32, T], F32) for n in ("dtT", "adtT", "acsT", "nacsT", "decT")}
        sm["e1"] = sm["dtT"]
        sm["dtdecT"] = sm["decT"]
        ecol = C.sb(es, "ecol", [32, 2], F32)
        dgs = [C.sb(es, "dg", [32, 32], F32) for _ in range(2)]
        toks = [C.sb(es, "tok", [128, 160], F32) for _ in range(2)]
        for g in range(8):
            S.emit(S.pool, lambda: nc.gpsimd.memset(prevT[g].ap[:], 0.0), writes=[prevT[g]])
            S.emit(S.pool, lambda: nc.gpsimd.memset(prevbf[g].ap[:], 0.0), writes=[prevbf[g]])
        S.emit(S.pool, lambda: nc.gpsimd.memset(xnT2s[0].ap[:, :, 0:4], 0.0), writes=[xnT2s[0]])
        bkA = [C.ps(es, "bkA", [128, 1024], BF16) for _ in range(2)]
        pgen2 = [C.ps(es, "pgen", [128, 512], F32) for _ in range(2)]
        pzb1 = C.ps(es, "pzb", [128, 512], F32)
        pgen = pgen2 + [pzb1]
        pzb = [pzb1, pzb1]
        pys = [C.ps(es, "py", [128, 512], F32) for _ in range(2)]
        pst1 = C.ps(es, "pst", [128, 512], F32)
        psts = [pst1, pst1]
        psgs = [None] * 4
        cnt = {"g": 0, "tr": 0, "acc": 0}

        def nxt(lst, key):
            b = lst[cnt[key] % len(lst)]
            cnt[key] += 1
            return b

        def load(i):
            xt = xts[i % 2]
            S.dma(S.sp, xt.ap[:], src[i * T:(i + 1) * T, :].rearrange("(c p) d -> p c d", p=128), writes=[xt], chan_buf=xt)

        e1, dtT, adtT, acsT, nacsT, decT, dtdecT = [sm[n] for n in ("e1", "dtT", "adtT", "acsT", "nacsT", "decT", "dtdecT")]
        load(0)
        for i in range(NT):
            xt = xts[i % 2]
            xnT2 = xnT2s[i % 2]
            if i + 1 < NT:
                load(i + 1)
            for c in range(2):
                norm_transpose(C, K, xt.ap[:, c, :], xt, xn, ss, rstd, xn, bkA[c], xnT2.ap[:, :, 4 + c * 128:4 + (c + 1) * 128], xnT2)
            S.emit(S.pool, lambda: nc.gpsimd.tensor_copy(out=xnT2s[(i + 1) % 2].ap[:, :, 0:4], in_=xnT2.ap[:, :, T:T + 4]), reads=[xnT2], writes=[xnT2s[(i + 1) % 2]])
            pdt = nxt(pgen, "g")
            for kk in range(8):
                S.emit(S.pe, lambda: nc.tensor.matmul(pdt.ap[0:32, 0:T], lhsT=win.ap[:, kk, 6144:6176], rhs=xnT2.ap[:, kk, 4:T + 4], start=(kk == 0), stop=(kk == 7)),
                       reads=[win, xnT2], writes=[pdt], pe_acc=True, sig=(kk == 7))
            S.emit(S.act, lambda: nc.scalar.activation(out=e1.ap[:], in_=pdt.ap[0:32, 0:T], func=AF.Exp, bias=dtb.ap[:, 0:1]), reads=[pdt, dtb], writes=[e1])
            S.emit(S.act, lambda: nc.scalar.activation(out=dtT.ap[:], in_=e1.ap[:], func=AF.Ln, bias=1.0), reads=[e1], writes=[dtT])
            S.emit(S.dve, lambda: nc.vector.tensor_scalar(out=adtT.ap[:], in0=dtT.ap[:], scalar1=acol.ap[:, 0:1], scalar2=None, op0=ALU.mult), reads=[dtT, acol], writes=[adtT])
            for c in range(2):
                cs = slice(c * 128, (c + 1) * 128)
                S.emit(S.dve, lambda: nc.vector.tensor_tensor_scan(out=acsT.ap[:, cs], data0=ones32.ap[:], data1=adtT.ap[:, cs], initial=0.0, op0=ALU.mult, op1=ALU.add), reads=[ones32, adtT], writes=[acsT])
            S.emit(S.act, lambda: nc.scalar.mul(out=nacsT.ap[:], in_=acsT.ap[:], mul=-1.0), reads=[acsT], writes=[nacsT])
            for c in range(2):
                cs = slice(c * 128, (c + 1) * 128)
                last = acsT.ap[:, c * 128 + 127:c * 128 + 128]
                S.emit(S.act, lambda: nc.scalar.activation(out=decT.ap[:, cs], in_=acsT.ap[:, cs], func=AF.Exp, scale=-1.0, bias=last), reads=[acsT], writes=[decT])
                S.emit(S.act, lambda: nc.scalar.activation(out=ecol.ap[:, c:c + 1], in_=last, func=AF.Exp), reads=[acsT], writes=[ecol])
                S.emit(S.dve, lambda: nc.vector.tensor_scalar(out=dgs[c].ap[:], in0=identf.ap[0:32, 0:32], scalar1=ecol.ap[:, c:c + 1], scalar2=None, op0=ALU.mult), reads=[identf, ecol], writes=[dgs[c]])
            S.emit(S.dve, lambda: nc.vector.tensor_tensor(out=dtdecT.ap[:], in0=dtT.ap[:], in1=decT.ap[:], op=ALU.mult), reads=[dtT, decT], writes=[dtdecT])
            for c in range(2):
                cs = slice(c * 128, (c + 1) * 128)
                tok = toks[c]
                ptok = nxt(pgen, "g")
                for i4, (lh, rh) in enumerate(((dtT, None), (dtdecT, None), (nacsT, None), (ones32, dgs[c]))):
                    rhs_ap = identf.ap[0:32, 0:32] if rh is None else rh.ap[:]
                    lh_ap = lh.ap[:, cs] if rh is None else lh.ap[:]
                    S.emit(S.pe, lambda: nc.tensor.matmul(ptok.ap[:, i4 * 32:(i4 + 1) * 32], lhsT=lh_ap, rhs=rhs_ap, start=True, stop=True),
                           reads=[lh, identf] + ([rh] if rh is not None else []), writes=[ptok], pe_acc=True, sig=(i4 == 3))
                S.emit(S.dve, lambda: nc.vector.tensor_copy(out=tok.ap[:, 0:128], in_=ptok.ap[:, 0:128]), reads=[ptok], writes=[tok])
                S.emit(S.act, lambda: nc.scalar.activation(out=tok.ap[:, 128:160], in_=ptok.ap[:, 64:96], func=AF.Exp, scale=-1.0), reads=[ptok], writes=[tok])
            cring = [pgen2[0], pgen2[1], pzb1, pys[0], pys[1], pst1]
            ctmp = yts

            def c0(j):
                pj = cring[j % 6]
                for kk in range(8):
                    S.emit(S.pe, lambda: nc.tensor.matmul(pj.ap[:, 0:T + 4], lhsT=win.ap[:, kk, 2048 + j * 128:2048 + (j + 1) * 128], rhs=xnT2.ap[:, kk, :], start=(kk == 0), stop=(kk == 7)),
                           reads=[win, xnT2], writes=[pj], pe_acc=True, sig=(kk == 7))

            def c1(j):
                pj = cring[j % 6]
                acc, tm = accs[j % 3], ctmp[j % 2]
                S.emit(S.act, lambda: nc.scalar.activation(out=acc.ap[:], in_=pj.ap[:, 4:T + 4], func=AF.Identity, scale=cw.ap[:, j, 3:4], bias=cb.ap[:, j:j + 1]), reads=[pj, cw, cb], writes=[acc])
                S.emit(S.act, lambda: nc.scalar.mul(out=tm.ap[:], in_=pj.ap[:, 3:T + 3], mul=cw.ap[:, j, 2:3]), reads=[pj, cw], writes=[tm])

            def c2(j):
                pj = cring[j % 6]
                acc, tm = accs[j % 3], ctmp[j % 2]
                S.emit(S.pool, lambda: nc.gpsimd.tensor_tensor(out=acc.ap[:], in0=acc.ap[:], in1=tm.ap[:], op=ALU.add), reads=[acc, tm], writes=[acc])
                for tap in (1, 0):
                    S.emit(S.dve, lambda: nc.vector.scalar_tensor_tensor(out=acc.ap[:], in0=pj.ap[:, 1 + tap:1 + tap + T], scalar=cw.ap[:, j, tap:tap + 1], in1=acc.ap[:], op0=ALU.mult, op1=ALU.add), reads=[pj, cw, acc], writes=[acc])

            def c3(j):
                acc = accs[j % 3]
                S.emit(S.act, lambda: nc.scalar.activation(out=xbcT[j].ap[:], in_=acc.ap[:], func=AF.Silu), reads=[acc], writes=[xbcT[j]])

            cst = [c0, c1, c2, c3]
            for t in range(32 + len(cst) - 1):
                for k in reversed(range(len(cst))):
                    j = t - k
                    if 0 <= j < 32:
                        cst[k](j)
            def info(gi):
                c, g = gi // 8, gi % 8
                return c, g, slice(c * 128, (c + 1) * 128), toks[c]

            def st_ad(gi):
                c, g, cs, tok = info(gi)
                AD = ADs[gi % 2]
                S.emit(S.pool, lambda: nc.gpsimd.affine_select(out=AD.ap[:].rearrange("p (h l) -> p h l", h=4), in_=bc(acsT.ap[:, cs].unsqueeze(1), [32, 4, 128]), pattern=[[-1, 4], [0, 128]], compare_op=ALU.is_equal, fill=0.0, base=-4 * g, channel_multiplier=1), reads=[acsT], writes=[AD])

            def st0(gi):
                c, g, cs, tok = info(gi)
                bk = bkA[gi % 2]
                for jj, j in enumerate((2 * g, 2 * g + 1, 16 + g)):
                    S.emit(S.pe, lambda: nc.tensor.transpose(out=bk.ap[:, jj * 128:(jj + 1) * 128], in_=xbcT[j].ap[:, cs], identity=ident.ap[:]),
                           reads=[xbcT[j], ident], writes=[bk], pe_acc=True, sig=False)
                S.emit(S.pe, lambda: nc.tensor.matmul(bk.ap[:, 512:768].bitcast(F32), lhsT=xbcT[16 + g].ap[:, cs], rhs=xbcT[24 + g].ap[:, cs], start=True, stop=True), reads=[xbcT[16 + g], xbcT[24 + g]], writes=[bk], pe_acc=True)
                psg = pgen2[gi % 2]
                psgs[gi % 4] = psg
                AD = ADs[gi % 2]
                S.emit(S.pe, lambda: nc.tensor.matmul(psg.ap[:], lhsT=ones32.ap[:], rhs=AD.ap[:], start=True, stop=False), reads=[ones32, AD], writes=[psg], pe_acc=True, sig=False)
                S.emit(S.pe, lambda: nc.tensor.matmul(psg.ap[:], lhsT=ident.ap[:], rhs=K["nmc"].ap[:], start=False, stop=True), reads=[ident, K["nmc"]], writes=[psg], pe_acc=True)
                if g % 2 == 0:
                    pz = pzb[(gi // 2) % 2]
                    n = g // 2
                    for kk in range(8):
                        S.emit(S.pe, lambda: nc.tensor.matmul(pz.ap[:], lhsT=xnT2.ap[:, kk, 4 + c * 128:4 + (c + 1) * 128], rhs=win.ap[:, kk, n * 512:(n + 1) * 512], start=(kk == 0), stop=(kk == 7)),
                               reads=[xnT2, win], writes=[pz], pe_acc=True, sig=(kk == 7))

            def st1(gi):
                c, g, cs, tok = info(gi)
                ptr = bkA[gi % 2]
                psg = psgs[gi % 4]
                Eg = Egs[gi % 2]
                for h in range(4):
                    S.emit(S.act, lambda: nc.scalar.activation(out=Eg.ap[:, h * 128:(h + 1) * 128], in_=psg.ap[:, h * 128:(h + 1) * 128], func=AF.Exp, bias=tok.ap[:, 64 + 4 * g + h:64 + 4 * g + h + 1]), reads=[psg, tok], writes=[Eg])
                MT = MTs[gi % 2]
                S.emit(S.dve, lambda: nc.vector.tensor_tensor(out=MT.ap[:].rearrange("p (h l) -> p h l", h=4), in0=Eg.ap[:].rearrange("p (h l) -> p h l", h=4), in1=bc(ptr.ap[:, 512:768].bitcast(F32).unsqueeze(1), [128, 4, 128]), op=ALU.mult), reads=[Eg, ptr], writes=[MT])
                xcg, xcdg, Btg = xcgs[gi % 3], xcdgs[gi % 3], Btgs[gi % 3]
                pv = ptr.ap[:, 0:256].rearrange("p (h e) -> p h e", h=4)
                S.emit(S.dve, lambda: nc.vector.tensor_tensor(out=xcg.ap[:].rearrange("p (h e) -> p h e", h=4), in0=pv, in1=bc(tok.ap[:, 4 * g:4 * g + 4].unsqueeze(2), [128, 4, 64]), op=ALU.mult), reads=[ptr, tok], writes=[xcg])
                S.emit(S.dve, lambda: nc.vector.tensor_tensor(out=xcdg.ap[:].rearrange("p (h e) -> p h e", h=4), in0=pv, in1=bc(tok.ap[:, 32 + 4 * g:32 + 4 * g + 4].unsqueeze(2), [128, 4, 64]), op=ALU.mult), reads=[ptr, tok], writes=[xcdg])
                S.emit(S.act, lambda: nc.scalar.copy(out=Btg.ap[:], in_=ptr.ap[:, 256:384]), reads=[ptr], writes=[Btg])
                if g % 2 == 0:
                    pz = pzb[(gi // 2) % 2]
                    sz, szea = szs[(gi // 2) % 2], szeas[(gi // 2) % 2]
                    n = g // 2
                    S.emit(S.act, lambda: nc.scalar.activation(out=sz.ap[:], in_=pz.ap[:], func=AF.Tanh, scale=0.5), reads=[pz], writes=[sz])
                    S.emit(S.dve, lambda: nc.vector.scalar_tensor_tensor(out=sz.ap[:], in0=sz.ap[:], scalar=1.0, in1=pz.ap[:], op0=ALU.add, op1=ALU.mult), reads=[sz, pz], writes=[sz])
                    S.emit(S.pool, lambda: nc.gpsimd.tensor_tensor(out=szea.ap[:].rearrange("p (h e) -> p h e", h=8), in0=sz.ap[:].rearrange("p (h e) -> p h e", h=8), in1=bc(tok.ap[:, 128 + 8 * n:128 + 8 * n + 8].unsqueeze(2), [128, 8, 64]), op=ALU.mult), reads=[sz, tok], writes=[szea])
                pvw = prevT[g].ap[:]
                S.emit(S.pool, lambda: nc.gpsimd.tensor_tensor(out=pvw.rearrange("p (h e) -> p h e", h=4), in0=pvw.rearrange("p (h e) -> p h e", h=4), in1=bc(tok.ap[:, 96 + 4 * g:96 + 4 * g + 4].unsqueeze(2), [128, 4, 64]), op=ALU.mult), reads=[prevT[g], tok], writes=[prevT[g]])

            def st2(gi):
                c, g, cs, tok = info(gi)
                MT = MTs[gi % 2]
                xcg, xcdg, Btg = xcgs[gi % 3], xcdgs[gi % 3], Btgs[gi % 3]
                py = pys[gi % 2]
                pyr = py.ap[:, 0:256]
                pyo = py.ap[:, 256:512]
                for i2 in range(2):
                    S.emit(S.pe, lambda: nc.tensor.matmul(pyr[:, i2 * 128:(i2 + 1) * 128], lhsT=xbcT[2 * g + i2].ap[:, cs], rhs=diagD.ap[:, 2 * g + i2, :], start=True, stop=False),
                           reads=[xbcT[2 * g + i2], diagD], writes=[py], pe_acc=True, sig=False)
                    for hh in (2 * i2, 2 * i2 + 1):
                        S.emit(S.pe, lambda: nc.tensor.matmul(pyr[:, hh * 64:(hh + 1) * 64], lhsT=MT.ap[:, hh * 128:(hh + 1) * 128], rhs=xcg.ap[:, hh * 64:(hh + 1) * 64], start=False, stop=(hh % 2 == 1)),
                               reads=[MT, xcg], writes=[py], pe_acc=True, sig=False)
                S.emit(S.pe, lambda: nc.tensor.matmul(pyo, lhsT=xbcT[24 + g].ap[:, cs], rhs=prevbf[g].ap[:], start=True, stop=True), reads=[xbcT[24 + g], prevbf[g]], writes=[py], pe_acc=True)
                pst = psts[gi % 2]
                S.emit(S.pe, lambda: nc.tensor.matmul(pst.ap[:, 0:256], lhsT=Btg.ap[:], rhs=xcdg.ap[:], start=True, stop=True), reads=[Btg, xcdg], writes=[pst], pe_acc=True)

            def st3(gi):
                c, g, cs, tok = info(gi)
                sz, szea = szs[(gi // 2) % 2], szeas[(gi // 2) % 2]
                py = pys[gi % 2]
                yt, yu = yts[gi % 2], yus[gi % 2]
                gs = slice((g % 2) * 256, (g % 2 + 1) * 256)
                S.emit(S.dve, lambda: nc.vector.tensor_tensor(out=yt.ap[:], in0=py.ap[:, 256:512], in1=szea.ap[:, gs], op=ALU.mult), reads=[py, szea], writes=[yt])
                S.emit(S.dve, lambda: nc.vector.tensor_tensor(out=yu.ap[:], in0=py.ap[:, 0:256], in1=sz.ap[:, gs], op=ALU.mult), reads=[py, sz], writes=[yu])
                pst = psts[gi % 2]
                S.emit(S.dve, lambda: nc.vector.tensor_tensor(out=prevT[g].ap[:], in0=pst.ap[:, 0:256], in1=prevT[g].ap[:], op=ALU.add), reads=[pst, prevT[g]], writes=[prevT[g]])

            def st4(gi):
                c, g, cs, tok = info(gi)
                yt, yu, yz = yts[gi % 2], yus[gi % 2], yzs[gi % 4]
                S.emit(S.pool, lambda: nc.gpsimd.tensor_tensor(out=yz.ap[:], in0=yt.ap[:], in1=yu.ap[:], op=ALU.add), reads=[yt, yu], writes=[yz])
                S.emit(S.act, lambda: nc.scalar.copy(out=prevbf[g].ap[:], in_=prevT[g].ap[:]), reads=[prevT[g]], writes=[prevbf[g]])

            def st5(gi):
                yz, ssg = yzs[gi % 4], ssgs[gi % 4]
                S.emit(S.act, lambda: nc.scalar.activation(out=junkg.ap[:], in_=yz.ap[:], func=AF.Square, accum_out=ssg.ap[:, 0:1]), reads=[yz], writes=[junkg, ssg])

            def st6(gi):
                ssg, rg = ssgs[gi % 4], rgs[gi % 4]
                S.emit(S.pool, lambda: nc.gpsimd.tensor_scalar(out=ssg.ap[:, 0:1], in0=ssg.ap[:, 0:1], scalar1=1.0 / 256.0, scalar2=4e-5, op0=ALU.mult, op1=ALU.add), reads=[ssg], writes=[ssg])
                S.emit(S.pool, lambda: nc.gpsimd.tensor_tensor(out=rg.ap[:, 0:1], in0=ssg.ap[:, 0:1], in1=K["mhalf"].ap[:, 0:1], op=ALU.pow), reads=[ssg, K["mhalf"]], writes=[rg])

            def st7(gi):
                c, g, cs, tok = info(gi)
                yz, rg = yzs[gi % 4], rgs[gi % 4]
                yn = yns[c]
                S.emit(S.act, lambda: nc.scalar.mul(out=yn.ap[:, g * 256:(g + 1) * 256], in_=yz.ap[:], mul=rg.ap[:, 0:1]), reads=[yz, rg], writes=[yn])
                if g == 7:
                    S.dma(S.pool, ynd[i * T + c * 128:i * T + (c + 1) * 128, :], yn.ap[:], reads=[yn], chan_buf=yn)

            stages = [st_ad, st0, st1, st2, st3, st4, st5, st6, st7]
            NG = 16
            for t in range(NG + len(stages) - 1):
                for k in reversed(range(len(stages))):
                    gi = t - k
                    if 0 <= gi < NG:
                        stages[k](gi)
        S.barrier()
    with contextlib.ExitStack() as es:
        wout = C.sb(es, "wout", [128, 16, D], BF16)
        gout = C.sb(es, "gout", [128, 16], F32)
        load_cols(C, gout, I["ssm_norm"], 16)
        with contextlib.ExitStack() as es2:
            C.stg = [C.sb(es2, "stg", [128, 2048], F32) for _ in range(6)]
            prep_weight(C, es2, wout, I["ssm_w_out"], 16, D, gout)
            S.barrier()
        xts = [C.sb(es, "xt", [128, D], F32) for _ in range(2)]
        yls = [C.sb(es, "yl", [128, D_IN], BF16) for _ in range(2)]
        ynT = C.sb(es, "ynT", [128, 16, 128], BF16)
        ptrs = [C.ps(es, "ptr", [128, 1024], BF16) for _ in range(2)]
        pmm = [C.ps(es, "pmm", [128, 512], F32) for _ in range(2)]

        def load2(c):
            S.dma(S.sp, xts[c % 2].ap[:], src[c * 128:(c + 1) * 128, :], writes=[xts[c % 2]], chan_buf=xts[c % 2])
            S.dma(S.sp, yls[c % 2].ap[:], ynd[c * 128:(c + 1) * 128, :], writes=[yls[c % 2]], chan_buf=yls[c % 2])

        load2(0)
        nmm = 0
        for c in range(NCH):
            if c + 1 < NCH:
                load2(c + 1)
            xt, yl = xts[c % 2], yls[c % 2]
            for half in range(2):
                ptr = ptrs[half]
                for jj in range(8):
                    j = 8 * half + jj
                    S.emit(S.pe, lambda: nc.tensor.transpose(out=ptr.ap[:, jj * 128:(jj + 1) * 128], in_=yl.ap[:, j * 128:(j + 1) * 128], identity=ident.ap[:]),
                           reads=[yl, ident], writes=[ptr], pe_acc=True, sig=(jj == 7))
                if half == 0:
                    S.emit(S.act, lambda: nc.scalar.copy(out=ynT.ap[:, 0:8, :], in_=ptr.ap[:, 0:1024].rearrange("p (k t) -> p k t", k=8)), reads=[ptr], writes=[ynT])
                else:
                    S.emit(S.dve, lambda: nc.vector.tensor_copy(out=ynT.ap[:, 8:16, :], in_=ptr.ap[:, 0:1024].rearrange("p (k t) -> p k t", k=8)), reads=[ptr], writes=[ynT])
            for n2 in range(2):
                po = pmm[nmm % 2]
                nmm += 1
                for kk in range(16):
                    S.emit(S.pe, lambda: nc.tensor.matmul(po.ap[:], lhsT=ynT.ap[:, kk, :], rhs=wout.ap[:, kk, n2 * 512:(n2 + 1) * 512], start=(kk == 0), stop=(kk == 15)),
                           reads=[ynT, wout], writes=[po], pe_acc=True, sig=(kk == 15))
                S.emit(S.dve, lambda: nc.vector.tensor_tensor(out=xt.ap[:, n2 * 512:(n2 + 1) * 512], in0=po.ap[:], in1=xt.ap[:, n2 * 512:(n2 + 1) * 512], op=ALU.add), reads=[po, xt], writes=[xt])
            S.dma(S.pool, dst[c * 128:(c + 1) * 128, :], xt.ap[:], reads=[xt], chan_buf=xt)
        S.barrier()

ATT_PATTERNS = ((128, 1), (512, 4), (2048, 16))
ATT_MAXSTAGE = 7
ATT_DBG = 0


def rotary(C, src, dst, rp, tmps, nh):
    nc, S = C.nc, C.S
    sv = src.ap[:, 0:nh * 128].rearrange("p (h e) -> p h e", h=nh)
    dv = dst.ap[:, 0:nh * 128].rearrange("p (h e) -> p h e", h=nh)
    cos = bc(rp.ap[:, 0:16].unsqueeze(1), [128, nh, 16])
    sin = bc(rp.ap[:, 16:32].unsqueeze(1), [128, nh, 16])
    t1, t2, t3, t4 = [t.ap[:, 0:nh * 16].rearrange("p (h e) -> p h e", h=nh) for t in tmps]
    x1, x2 = sv[:, :, 0:16], sv[:, :, 16:32]
    S.emit(S.dve, lambda: nc.vector.tensor_tensor(out=t1, in0=x1, in1=cos, op=ALU.mult), reads=[src, rp], writes=[tmps[0]])
    S.emit(S.pool, lambda: nc.gpsimd.tensor_tensor(out=t2, in0=x2, in1=sin, op=ALU.mult), reads=[src, rp], writes=[tmps[1]])
    S.emit(S.dve, lambda: nc.vector.tensor_tensor(out=t3, in0=x2, in1=cos, op=ALU.mult), reads=[src, rp], writes=[tmps[2]])
    S.emit(S.pool, lambda: nc.gpsimd.tensor_tensor(out=t4, in0=x1, in1=sin, op=ALU.mult), reads=[src, rp], writes=[tmps[3]])
    S.emit(S.dve, lambda: nc.vector.tensor_tensor(out=dv[:, :, 0:16], in0=t1, in1=t2, op=ALU.subtract), reads=[tmps[0], tmps[1]], writes=[dst])
    S.emit(S.pool, lambda: nc.gpsimd.tensor_tensor(out=dv[:, :, 16:32], in0=t3, in1=t4, op=ALU.add), reads=[tmps[2], tmps[3]], writes=[dst])
    if nh > 2:
        S.emit(S.dve, lambda: nc.vector.tensor_copy(out=dv[:, :, 32:128], in_=sv[:, :, 32:128]), reads=[src], writes=[dst])
    else:
        S.emit(S.act, lambda: nc.scalar.copy(out=dv[:, :, 32:128], in_=sv[:, :, 32:128]), reads=[src], writes=[dst])


def phase_attn(C, K, I, src, dst):
    nc, S = C.nc, C.S
    nd = [nc.dram_tensor("numden%d" % g, [S_LEN, 8 * 129], F32, kind="Internal").ap() for g in range(3)]
    rope = I["rope"]
    with contextlib.ExitStack() as es:
        wkv = C.sb(es, "wkv", [128, 8, 1536], BF16)
        wq = C.sb(es, "wq", [128, 8, 3072], BF16)
        wo = C.sb(es, "wo", [128, 8, D], BF16)
        gkv = C.sb(es, "gkv", [128, 8], F32)
        gq = C.sb(es, "gq", [128, 8], F32)
        load_cols(C, gkv, I["kv_norm"], 8)
        load_cols(C, gq, I["b_norm"], 8)
        with contextlib.ExitStack() as es2:
            C.stg = [C.sb(es2, "stg", [128, 2048], F32) for _ in range(6)]
            prep_weight(C, es2, wkv, I["w_kv"], 8, 1536, gkv)
            prep_weight(C, es2, wq, I["att_w_q"], 8, 3072, gq)
            prep_weight(C, es2, wo, I["att_w_o"], 8, D, None)
            S.barrier()
        xts = [C.sb(es, "xt", [128, D], F32) for _ in range(3)]
        rps = [C.sb(es, "rp", [128, 32], F32) for _ in range(8)]
        xns = [C.sb(es, "xn", [128, D], BF16) for _ in range(2)]
        xnTs = [C.sb(es, "xnT", [128, 8, 128], BF16) for _ in range(2)]
        sss = [C.sb(es, "ss", [128, 1], F32) for _ in range(2)]
        rstds = [C.sb(es, "rstd", [128, 1], F32) for _ in range(2)]
        qfs = [C.sb(es, "qf", [128, 1024], F32) for _ in range(2)]
        qbs = [C.sb(es, "qb", [128, 1024], BF16) for _ in range(2)]
        kfs = [C.sb(es, "kf", [128, 256], F32) for _ in range(2)]
        kbs = [C.sb(es, "kb", [128, 256], BF16) for _ in range(2)]
        tq = [C.sb(es, "tq", [128, 128], F32) for _ in range(4)]
        tk = [C.sb(es, "tk", [128, 32], F32) for _ in range(4)]
        QTs = [C.sb(es, "QT", [128, 8, 128], BF16) for _ in range(3)]
        KTs = [C.sb(es, "KT", [128, 2, 128], BF16) for _ in range(4)]
        vaugs = [C.sb(es, "vaug", [128, 2, 130], BF16) for _ in range(6)]
        PTs = [C.sb(es, "PT", [128, 512], BF16) for _ in range(8)]
        obs = [C.sb(es, "ob", [128, 8, 129], F32) for _ in range(2)]
        for v in vaugs:
            S.emit(S.pool, lambda: nc.gpsimd.memset(v.ap[:], 1.0), writes=[v])
        ptr = C.ps(es, "ptr", [128, 1024], BF16)
        pmm = [C.ps(es, "pmm", [128, 512], F32) for _ in range(2)]
        pS = [C.ps(es, "pS", [128, 512], F32) for _ in range(2)]
        po = C.ps(es, "po", [128, 3 * 512], F32)
        scale = 1.0 / np.sqrt(128.0)

        blocks = []
        for g, (win, dil) in enumerate(ATT_PATTERNS):
            for r in range(dil):
                for n in range(S_LEN // dil // 128):
                    blocks.append((g, dil, r, n))
        NB = len(blocks)

        def rows(ap, dil, r, n):
            return ap.rearrange("(m dd) c -> dd m c", dd=dil)[r][n * 128:(n + 1) * 128, :]

        def load(bi):
            g, dil, r, n = blocks[bi]
            S.dma(S.sp, xts[bi % 3].ap[:], rows(src, dil, r, n), writes=[xts[bi % 3]], chan_buf=xts[bi % 3])
            S.dma(S.sp, rps[bi % 8].ap[:], rows(rope, dil, r, n), writes=[rps[bi % 8]], chan_buf=rps[bi % 8])

        cnt = {"mm": 0, "S": 0}

        def A1(bi):
            xt = xts[bi % 3]
            norm_only(C, K, xt.ap[:], xt, xns[bi % 2], sss[bi % 2], rstds[bi % 2], xns[bi % 2], dve_mul=True)

        def A2(bi):
            transpose_only(C, K, xns[bi % 2], ptr, xnTs[bi % 2].ap[:], xnTs[bi % 2])

        def A3(bi):
            g, dil, r, n = blocks[bi]
            xnT, qf, kf, vaug = xnTs[bi % 2], qfs[bi % 2], kfs[bi % 2], vaugs[bi % 6]
            for n2 in range(2):
                pq = pmm[cnt["mm"] % 2]
                cnt["mm"] += 1
                for kk in range(8):
                    S.emit(S.pe, lambda: nc.tensor.matmul(pq.ap[:], lhsT=xnT.ap[:, kk, :], rhs=wq.ap[:, kk, g * 1024 + n2 * 512:g * 1024 + (n2 + 1) * 512], start=(kk == 0), stop=(kk == 7)),
                           reads=[xnT, wq], writes=[pq], pe_acc=True, sig=(kk == 7))
                if n2 == 0:
                    S.emit(S.act, lambda: nc.scalar.copy(out=qf.ap[:, n2 * 512:(n2 + 1) * 512], in_=pq.ap[:]), reads=[pq], writes=[qf])
                else:
                    S.emit(S.dve, lambda: nc.vector.tensor_copy(out=qf.ap[:, n2 * 512:(n2 + 1) * 512], in_=pq.ap[:]), reads=[pq], writes=[qf])
            if ATT_DBG == 1:
                return
            pkv = pmm[cnt["mm"] % 2]
            cnt["mm"] += 1
            for half, c0 in ((0, g * 256), (1, 768 + g * 256)):
                for kk in range(8):
                    S.emit(S.pe, lambda: nc.tensor.matmul(pkv.ap[:, half * 256:(half + 1) * 256], lhsT=xnT.ap[:, kk, :], rhs=wkv.ap[:, kk, c0:c0 + 256], start=(kk == 0), stop=(kk == 7)),
                           reads=[xnT, wkv], writes=[pkv], pe_acc=True, sig=(kk == 7 and half == 1))
            S.emit(S.act, lambda: nc.scalar.copy(out=kf.ap[:], in_=pkv.ap[:, 0:256]), reads=[pkv], writes=[kf])
            if ATT_DBG == 2:
                return
            for hv in range(2):
                S.emit(S.act, lambda: nc.scalar.copy(out=vaug.ap[:, hv, 0:128], in_=pkv.ap[:, 256 + hv * 128:256 + (hv + 1) * 128]), reads=[pkv], writes=[vaug])

        def A4(bi):
            rp = rps[bi % 8]
            rotary(C, qfs[bi % 2], qbs[bi % 2], rp, tq, 8)
            rotary(C, kfs[bi % 2], kbs[bi % 2], rp, tk, 2)

        def A5(bi):
            qb, kb, QT, KT = qbs[bi % 2], kbs[bi % 2], QTs[bi % 3], KTs[bi % 4]
            for h in range(8):
                S.emit(S.pe, lambda: nc.tensor.transpose(out=ptr.ap[:, h * 128:(h + 1) * 128], in_=qb.ap[:, h * 128:(h + 1) * 128], identity=K["ident"].ap[:]),
                       reads=[qb, K["ident"]], writes=[ptr], pe_acc=True, sig=(h == 7))
            S.emit(S.dve, lambda: nc.vector.tensor_copy(out=QT.ap[:], in_=ptr.ap[:, 0:1024].rearrange("p (k t) -> p k t", k=8)), reads=[ptr], writes=[QT])
            for h in range(2):
                S.emit(S.pe, lambda: nc.tensor.transpose(out=ptr.ap[:, h * 128:(h + 1) * 128], in_=kb.ap[:, h * 128:(h + 1) * 128], identity=K["ident"].ap[:]),
                       reads=[kb, K["ident"]], writes=[ptr], pe_acc=True, sig=(h == 1))
            S.emit(S.act, lambda: nc.scalar.copy(out=KT.ap[:], in_=ptr.ap[:, 0:256].rearrange("p (k t) -> p k t", k=2)), reads=[ptr], writes=[KT])

        def kb_list(bi):
            g, dil, r, n = blocks[bi]
            kbl = []
            if n > 0:
                kbl.append((KTs[(bi - 1) % 4], vaugs[(bi - 1) % 6], K["nmp"]))
            kbl.append((KTs[bi % 4], vaugs[bi % 6], K["nmc"]))
            return kbl

        def B1(bi):
            QT = QTs[bi % 3]
            kbl = kb_list(bi)
            for jk in range(2):
                for i2, (kt, va, nm) in enumerate(kbl):
                    ps_ = pS[cnt["S"] % 2]
                    cnt["S"] += 1
                    pt_ = PTs[(4 * bi + 2 * jk + i2) % 8]
                    S.emit(S.pe, lambda: nc.tensor.matmul(ps_.ap[:], lhsT=kt.ap[:, jk, :], rhs=QT.ap[:, 4 * jk:4 * jk + 4, :], start=True, stop=False),
                           reads=[kt, QT], writes=[ps_], pe_acc=True, sig=False)
                    S.emit(S.pe, lambda: nc.tensor.matmul(ps_.ap[:], lhsT=K["ident"].ap[:], rhs=nm.ap[:], start=False, stop=True),
                           reads=[K["ident"], nm], writes=[ps_], pe_acc=True)
                    S.emit(S.act, lambda: nc.scalar.activation(out=pt_.ap[:], in_=ps_.ap[:], func=AF.Exp, scale=float(scale)), reads=[ps_], writes=[pt_])

        def B2(bi):
            g, dil, r, n = blocks[bi]
            kbl = kb_list(bi)
            ob = obs[bi % 2]
            for jk in range(2):
                for hl in range(4):
                    h = 4 * jk + hl
                    oslice = po.ap[:, (h // 3) * 512 + (h % 3) * 129:(h // 3) * 512 + (h % 3) * 129 + 129]
                    for i2, (kt, va, nm) in enumerate(kbl):
                        pt_ = PTs[(4 * bi + 2 * jk + i2) % 8]
                        S.emit(S.pe, lambda: nc.tensor.matmul(oslice, lhsT=pt_.ap[:, hl * 128:(hl + 1) * 128], rhs=va.ap[:, jk, 0:129], start=(i2 == 0), stop=(i2 == len(kbl) - 1)),
                               reads=[pt_, va], writes=[po], pe_acc=True, sig=(i2 == len(kbl) - 1 and hl == 3))
            for b3 in range(3):
                nh = 3 if b3 < 2 else 2
                o_ap = ob.ap[:, b3 * 3:b3 * 3 + nh, :]
                i_ap = po.ap[:, b3 * 512:b3 * 512 + nh * 129].rearrange("p (h e) -> p h e", h=nh)
                if b3 != 1:
                    S.emit(S.act, lambda: nc.scalar.copy(out=o_ap, in_=i_ap), reads=[po], writes=[ob])
                else:
                    S.emit(S.dve, lambda: nc.vector.tensor_copy(out=o_ap, in_=i_ap), reads=[po], writes=[ob])
            S.dma(S.pool, rows(nd[g], dil, r, n), ob.ap[:].rearrange("p h e -> p (h e)"), reads=[ob], chan_buf=ob)

        stages = [A1, A2, A3, A4, A5, B1, B2][:ATT_MAXSTAGE]
        load(0)
        load(1)
        for t in range(NB + len(stages) - 1):
            if t + 2 < NB:
                load(t + 2)
            for k in reversed(range(len(stages))):
                bi = t - k
                if 0 <= bi < NB:
                    stages[k](bi)
        S.barrier()
        nmm = cnt["mm"]
        nds = [[C.sb(es, "ndl", [128, 8, 129], F32) for _ in range(3)] for _ in range(2)]
        rden = C.sb(es, "rden", [128, 8], F32)
        o16 = C.sb(es, "o16", [128, 1024], BF16)
        oT = C.sb(es, "oT", [128, 8, 128], BF16)

        def load2(c):
            S.dma(S.sp, xts[c % 2].ap[:], src[c * 128:(c + 1) * 128, :], writes=[xts[c % 2]], chan_buf=xts[c % 2])
            for g in range(3):
                b = nds[c % 2][g]
                S.dma(S.sp, b.ap[:].rearrange("p h e -> p (h e)"), nd[g][c * 128:(c + 1) * 128, :], writes=[b], chan_buf=b)

        load2(0)
        for c in range(NCH):
            if c + 1 < NCH:
                load2(c + 1)
            xt = xts[c % 2]
            a0, a1, a2 = nds[c % 2]
            S.emit(S.pool, lambda: nc.gpsimd.tensor_tensor(out=a0.ap[:], in0=a0.ap[:], in1=a1.ap[:], op=ALU.add), reads=[a0, a1], writes=[a0])
            S.emit(S.pool, lambda: nc.gpsimd.tensor_tensor(out=a0.ap[:], in0=a0.ap[:], in1=a2.ap[:], op=ALU.add), reads=[a0, a2], writes=[a0])
            S.emit(S.dve, lambda: nc.vector.reciprocal(out=rden.ap[:].unsqueeze(2), in_=a0.ap[:, :, 128:129]), reads=[a0], writes=[rden])
            S.emit(S.dve, lambda: nc.vector.tensor_tensor(out=o16.ap[:].rearrange("p (h e) -> p h e", h=8), in0=a0.ap[:, :, 0:128], in1=bc(rden.ap[:].unsqueeze(2), [128, 8, 128]), op=ALU.mult), reads=[a0, rden], writes=[o16])
            for h in range(8):
                S.emit(S.pe, lambda: nc.tensor.transpose(out=ptr.ap[:, h * 128:(h + 1) * 128], in_=o16.ap[:, h * 128:(h + 1) * 128], identity=K["ident"].ap[:]),
                       reads=[o16, K["ident"]], writes=[ptr], pe_acc=True, sig=(h == 7))
            S.emit(S.act, lambda: nc.scalar.copy(out=oT.ap[:], in_=ptr.ap[:, 0:1024].rearrange("p (k t) -> p k t", k=8)), reads=[ptr], writes=[oT])
            for n2 in range(2):
                pq = pmm[nmm % 2]
                nmm += 1
                for kk in range(8):
                    S.emit(S.pe, lambda: nc.tensor.matmul(pq.ap[:], lhsT=oT.ap[:, kk, :], rhs=wo.ap[:, kk, n2 * 512:(n2 + 1) * 512], start=(kk == 0), stop=(kk == 7)),
                           reads=[oT, wo], writes=[pq], pe_acc=True, sig=(kk == 7))
                S.emit(S.dve, lambda: nc.vector.tensor_tensor(out=xt.ap[:, n2 * 512:(n2 + 1) * 512], in0=pq.ap[:], in1=xt.ap[:, n2 * 512:(n2 + 1) * 512], op=ALU.add), reads=[pq, xt], writes=[xt])
            S.dma(S.pool, dst[c * 128:(c + 1) * 128, :], xt.ap[:], reads=[xt], chan_buf=xt)
        S.barrier()

def build(phases=("m", "f0", "a", "f1"), debug=False):
    nc = bass.Bass("TRN2", target_bir_lowering=False)
    I = {}

    def inp(name, shape):
        I[name] = nc.dram_tensor(name, shape, F32, kind="ExternalInput").ap()

    inp("x", [S_LEN, D])
    inp("a_norm", [D]); inp("ssm_w_in", [D, IN_DIM]); inp("ssm_conv_w", [4, XBC]); inp("ssm_conv_b", [XBC])
    inp("ssm_dt_bias", [32]); inp("ssm_a_log", [32]); inp("ssm_d", [32]); inp("ssm_norm", [D_IN]); inp("ssm_w_out", [D_IN, D])
    inp("kv_norm", [D]); inp("w_kv", [D, 1536]); inp("b_norm", [D]); inp("att_w_q", [D, 3072]); inp("att_w_o", [D, D])
    inp("ffn_norm", [2, D]); inp("ffn_w_up", [2, D, 2 * FF]); inp("ffn_conv_w", [2, 3, 2 * FF]); inp("ffn_w_down", [2, FF, D])
    inp("final_norm", [D]); inp("rope", [S_LEN, 32])
    out = nc.dram_tensor("out", [S_LEN, D], F32, kind="ExternalOutput").ap()
    kind = "ExternalOutput" if debug else "Internal"
    xa = nc.dram_tensor("xa", [S_LEN, D], F32, kind=kind).ap()
    xb = nc.dram_tensor("xb", [S_LEN, D], F32, kind=kind).ap()
    xc = nc.dram_tensor("xc", [S_LEN, D], F32, kind=kind).ap()
    with contextlib.ExitStack() as es:
        C = Ctx(nc, es)
        K = setup_consts(C, es)
        C.S.barrier()
        cur = I["x"]
        if "m" in phases:
            phase_mamba(C, K, I, cur, xa)
            cur = xa
        if "f0" in phases:
            phase_ffn(C, K, 0, cur, xb, I["ffn_w_up"][0], I["ffn_conv_w"][0], I["ffn_w_down"][0], I["ffn_norm"][0])
            cur = xb
        if "a" in phases:
            phase_attn(C, K, I, cur, xc)
            cur = xc
        if "f1" in phases:
            phase_ffn(C, K, 1, cur, out, I["ffn_w_up"][1], I["ffn_conv_w"][1], I["ffn_w_down"][1], I["ffn_norm"][1], final_gain=I["final_norm"])
        C.S.barrier()
    return nc


def rope_table():
    half = 16
    inv_freq = np.power(np.float32(500000.0), -np.arange(0, 32, 2, dtype=np.float32) / np.float32(32)).astype(np.float32)
    ang = (np.arange(S_LEN, dtype=np.float32)[:, None] * inv_freq[None, :]).astype(np.float32)
    return np.concatenate([np.cos(ang.astype(np.float64)), np.sin(ang.astype(np.float64))], axis=1).astype(np.float32)


def make_in_maps(inputs, n_cores):
    f = lambda a: np.ascontiguousarray(np.asarray(a, dtype=np.float32))
    shared = {
        "a_norm": f(inputs["a_norm"][0]), "ssm_w_in": f(inputs["ssm_w_in"][0]), "ssm_conv_w": f(inputs["ssm_conv_w"][0]),
        "ssm_conv_b": f(inputs["ssm_conv_b"][0]), "ssm_dt_bias": f(inputs["ssm_dt_bias"][0]), "ssm_a_log": f(inputs["ssm_a_log"][0]),
        "ssm_d": f(inputs["ssm_d"][0]), "ssm_norm": f(inputs["ssm_norm"][0]), "ssm_w_out": f(inputs["ssm_w_out"][0]),
        "kv_norm": f(inputs["kv_norm"]), "w_kv": f(inputs["w_kv"]), "b_norm": f(inputs["b_norm"][0]),
        "att_w_q": f(inputs["att_w_q"][0]), "att_w_o": f(inputs["att_w_o"][0]), "ffn_norm": f(inputs["ffn_norm"]),
        "ffn_w_up": f(inputs["ffn_w_up"]), "ffn_conv_w": f(inputs["ffn_conv_w"]), "ffn_w_down": f(inputs["ffn_w_down"]),
        "final_norm": f(inputs["final_norm"]), "rope": rope_table(),
    }
    x = f(inputs["x"])
    maps = []
    for c in range(n_cores):
        m = dict(shared)
        m["x"] = x[c]
        maps.append(m)
    return maps


def kernel(**inputs):
    nc = build()
    maps = make_in_maps(inputs, 8)
    res = run_bass_kernel_spmd(nc, maps, core_ids=list(range(8)))
    return np.stack([np.asarray(r["out"]) for r in res.results], axis=0).astype(np.float32)
```
